# Optimizing a Trainium2 kernel written in Bass

```python
import jax, jax.numpy as jnp
from jax import lax
import numpy as np

D_MODEL = 1024
BATCH = 2
SEQ = 8192
DEPTH = 4
DEC_BATCH = 16
DEC_SEQ = 32
PAST_LEN = 1024

CHUNK = 64
N_EVEN = (DEPTH + 1) // 2
N_ODD = DEPTH // 2
EPS = 1e-6

POOL_WINDOWS = (2, 4, 8, 16)
POOL_GROUPS = len(POOL_WINDOWS)
POOL_HIST = max(POOL_WINDOWS) - 1
W_A = D_MODEL
POOL_GDIM = W_A // POOL_GROUPS

W_B = D_MODEL
SGU_HEADS = 4
SGU_HDIM = W_B // SGU_HEADS
SGU_CHUNK = 128

EVEN_MIX = W_A + W_B
EVEN_IN = W_A + 2 * W_B + EVEN_MIX

N_HEADS_C = D_MODEL // 128
QK_NOPE = 128
QK_ROPE = 64
V_DIM = 128
KV_LORA = D_MODEL // 4
Q_LORA = 3 * D_MODEL // 8
C_MIX = N_HEADS_C * V_DIM
ODD_IN = Q_LORA + KV_LORA + QK_ROPE + C_MIX
ROPE_BASE = 10000.0
Q_BLOCK = 128
ATTN_SCALE = (QK_NOPE + QK_ROPE) ** -0.5

kernel_name = "hybrid_pool_sgu_mla_streaming_step"


def rms_norm(x, g):
    x32 = x.astype(jnp.float32)
    y = x32 * lax.rsqrt(jnp.mean(x32 * x32, axis=-1, keepdims=True) + EPS)
    return (y * g.astype(jnp.float32)).astype(x.dtype)


def layer_norm(x, g, b):
    x32 = x.astype(jnp.float32)
    mu = jnp.mean(x32, axis=-1, keepdims=True)
    xc = x32 - mu
    var = jnp.mean(xc * xc, axis=-1, keepdims=True)
    y = xc * lax.rsqrt(var + EPS) * g.astype(jnp.float32) + b.astype(jnp.float32)
    return y.astype(x.dtype)


def rope_tables(pos):
    half = QK_ROPE // 2
    freqs = ROPE_BASE ** (-jnp.arange(half, dtype=jnp.float32) / half)
    ang = pos.astype(jnp.float32)[:, None] * freqs[None, :]
    return jnp.cos(ang), jnp.sin(ang)


def apply_rope(x, cos, sin):
    half = QK_ROPE // 2
    x32 = x.astype(jnp.float32)
    x1, x2 = x32[..., :half], x32[..., half:]
    out = jnp.concatenate([x1 * cos - x2 * sin, x2 * cos + x1 * sin], axis=-1)
    return out.astype(x.dtype)


def pool_mixer(a, hist, pos0, w_pool, scale):
    B, T, W = a.shape
    P = hist.shape[1]
    full = jnp.concatenate([hist, a], axis=1).astype(jnp.float32)
    cs = jnp.concatenate([jnp.zeros((B, 1, W), jnp.float32), jnp.cumsum(full, axis=1)], axis=1)
    pos = pos0 + jnp.arange(T)
    means = []
    for g, w in enumerate(POOL_WINDOWS):
        sl = slice(g * POOL_GDIM, (g + 1) * POOL_GDIM)
        wsum = cs[:, P + 1:P + 1 + T, sl] - cs[:, P + 1 - w:P + 1 - w + T, sl]
        cnt = jnp.minimum(pos + 1, w).astype(jnp.float32)[None, :, None]
        means.append(wsum / cnt)
    d = (jnp.concatenate(means, axis=-1) - a.astype(jnp.float32)).astype(a.dtype)
    d = d.reshape(B, T, POOL_GROUPS, POOL_GDIM)
    y = jnp.einsum('btgi,gio->btgo', d, w_pool).reshape(B, T, W)
    return y * scale


def sgu_mixer(uv, ln_g, ln_b, w_s, b_s):
    uv = jax.nn.gelu(uv, approximate=False)
    u, v = uv[..., :W_B], uv[..., W_B:]
    v = layer_norm(v, ln_g, ln_b)
    B, T, W = v.shape
    L = min(T, SGU_CHUNK)
    idx = jnp.arange(L)
    mask = (idx[None, :] // CHUNK) <= (idx[:, None] // CHUNK)
    ws = jnp.where(mask[None], w_s[:, :L, :L], jnp.zeros((), w_s.dtype))
    vc = v.reshape(B, T // L, L, SGU_HEADS, SGU_HDIM)
    mixed = jnp.einsum('gij,bcjgd->bcigd', ws, vc)
    mixed = mixed + jnp.transpose(b_s[:, :L])[None, None, :, :, None]
    return u * mixed.reshape(B, T, W), v


def even_layer(x, pool_hist, pos0, g_pre, g_post, w_in, w_pool, pool_scale,
               ln_g, ln_b, w_s, b_s, w_out):
    h = rms_norm(x, g_pre)
    z = jnp.einsum('btd,de->bte', h, w_in)
    a = z[..., :W_A]
    uv = z[..., W_A:W_A + 2 * W_B]
    gate = z[..., W_A + 2 * W_B:]
    y_a = pool_mixer(a, pool_hist, pos0, w_pool, pool_scale)
    y_b, v = sgu_mixer(uv, ln_g, ln_b, w_s, b_s)
    mix = jnp.concatenate([y_a, y_b], axis=-1) * jax.nn.silu(gate)
    y = jnp.einsum('bte,ed->btd', mix, w_out)
    x = x + rms_norm(y, g_post)
    new_hist = jnp.concatenate([pool_hist, a], axis=1)[:, -POOL_HIST:]
    return x, new_hist, v


def mla_project(x, pos, g_pre, w_in, q_norm, kv_norm, w_q_up, w_kv_up):
    h = rms_norm(x, g_pre)
    z = jnp.einsum('btd,de->bte', h, w_in)
    q_c = z[..., :Q_LORA]
    kv_c = z[..., Q_LORA:Q_LORA + KV_LORA]
    k_r = z[..., Q_LORA + KV_LORA:Q_LORA + KV_LORA + QK_ROPE]
    gate = z[..., Q_LORA + KV_LORA + QK_ROPE:]
    q = jnp.einsum('btc,chd->bthd', rms_norm(q_c, q_norm), w_q_up)
    cos, sin = rope_tables(pos)
    q_rope = apply_rope(q[..., QK_NOPE:], cos[:, None, :], sin[:, None, :])
    k_rope = apply_rope(k_r, cos, sin)
    ckv = rms_norm(kv_c, kv_norm)
    q_abs = jnp.einsum('bthn,chn->bthc', q[..., :QK_NOPE], w_kv_up[..., :QK_NOPE])
    return q_abs, q_rope, ckv, k_rope, gate


def mla_attend(q_abs, q_rope, ckv, krope, q_pos, k_pos):
    s = (jnp.einsum('bqhc,bkc->bhqk', q_abs, ckv)
         + jnp.einsum('bqhr,bkr->bhqk', q_rope, krope)).astype(jnp.float32) * ATTN_SCALE
    mask = (k_pos[None, :] // CHUNK) <= (q_pos[:, None] // CHUNK)
    s = jnp.where(mask[None, None], s, -jnp.inf)
    p = jax.nn.softmax(s, axis=-1).astype(ckv.dtype)
    return jnp.einsum('bhqk,bkc->bqhc', p, ckv)


def mla_attend_blocked(q_abs, q_rope, ckv, krope):
    B, T, H, C = q_abs.shape
    nb = T // Q_BLOCK
    k_pos = jnp.arange(T)
    qa = jnp.swapaxes(q_abs.reshape(B, nb, Q_BLOCK, H, C), 0, 1)
    qr = jnp.swapaxes(q_rope.reshape(B, nb, Q_BLOCK, H, QK_ROPE), 0, 1)
    starts = jnp.arange(nb) * Q_BLOCK

    def one_block(args):
        qa_b, qr_b, s0 = args
        return mla_attend(qa_b, qr_b, ckv, krope, s0 + jnp.arange(Q_BLOCK), k_pos)

    o = lax.map(one_block, (qa, qr, starts))
    return jnp.swapaxes(o, 0, 1).reshape(B, T, H, C)


def mla_output(o_lat, gate, w_kv_up, w_o):
    B, T = o_lat.shape[:2]
    o = jnp.einsum('bthc,chv->bthv', o_lat, w_kv_up[..., QK_NOPE:]).reshape(B, T, C_MIX)
    return jnp.einsum('bte,ed->btd', o * jax.nn.silu(gate), w_o)


def setup_inputs(seed: int = 0) -> dict:
    key = jax.random.key(seed)
    ks = jax.random.split(key, 24)
    f32 = jnp.float32

    def nrm(k, shape, scale):
        return scale * jax.random.normal(k, shape, f32)

    return {
        "x_prompt": nrm(ks[0], (BATCH, SEQ, D_MODEL), 1.0),
        "x_sample": nrm(ks[1], (DEC_BATCH, DEC_SEQ, D_MODEL), 1.0),
        "cache_pool": nrm(ks[2], (N_EVEN, DEC_BATCH, POOL_HIST, W_A), 1.0),
        "cache_ckv": nrm(ks[3], (N_ODD, DEC_BATCH, PAST_LEN, KV_LORA), 1.0),
        "cache_krope": nrm(ks[4], (N_ODD, DEC_BATCH, PAST_LEN, QK_ROPE), 1.0),
        "norm_pre": 1.0 + nrm(ks[5], (DEPTH, D_MODEL), 0.05),
        "norm_post": 1.0 + nrm(ks[6], (DEPTH, D_MODEL), 0.05),
        "w_in_even": nrm(ks[7], (N_EVEN, D_MODEL, EVEN_IN), D_MODEL ** -0.5),
        "w_pool": nrm(ks[8], (N_EVEN, POOL_GROUPS, POOL_GDIM, POOL_GDIM), POOL_GDIM ** -0.5),
        "pool_scale": 1.0 + nrm(ks[9], (N_EVEN, W_A), 0.1),
        "sgu_ln_g": 1.0 + nrm(ks[10], (N_EVEN, W_B), 0.05),
        "sgu_ln_b": nrm(ks[11], (N_EVEN, W_B), 0.02),
        "w_spatial": nrm(ks[12], (N_EVEN, SGU_HEADS, SGU_CHUNK, SGU_CHUNK), 0.5 * SGU_CHUNK ** -0.5),
        "b_spatial": 1.0 + nrm(ks[13], (N_EVEN, SGU_HEADS, SGU_CHUNK), 0.02),
        "w_out_even": nrm(ks[14], (N_EVEN, EVEN_MIX, D_MODEL), EVEN_MIX ** -0.5),
        "w_in_odd": nrm(ks[15], (N_ODD, D_MODEL, ODD_IN), D_MODEL ** -0.5),
        "q_norm": 1.0 + nrm(ks[16], (N_ODD, Q_LORA), 0.05),
        "kv_norm": 1.0 + nrm(ks[17], (N_ODD, KV_LORA), 0.05),
        "w_q_up": nrm(ks[18], (N_ODD, Q_LORA, N_HEADS_C, QK_NOPE + QK_ROPE), Q_LORA ** -0.5),
        "w_kv_up": nrm(ks[19], (N_ODD, KV_LORA, N_HEADS_C, QK_NOPE + V_DIM), KV_LORA ** -0.5),
        "w_o": nrm(ks[20], (N_ODD, C_MIX, D_MODEL), C_MIX ** -0.5),
    }


def reference(x_prompt, x_sample, cache_pool, cache_ckv, cache_krope, norm_pre, norm_post,
              w_in_even, w_pool, pool_scale, sgu_ln_g, sgu_ln_b, w_spatial, b_spatial, w_out_even,
              w_in_odd, q_norm, kv_norm, w_q_up, w_kv_up, w_o):
    past = cache_ckv.shape[2]
    T = x_prompt.shape[1]
    S = x_sample.shape[1]
    pos_p = jnp.arange(T)
    pos_s = past + jnp.arange(S)
    k_pos_s = jnp.arange(past + S)
    xp, xs = x_prompt, x_sample
    pool_p, pool_s, sgu_s = [], [], []
    ckv_p, kr_p, ckv_s, kr_s = [], [], [], []
    for l in range(DEPTH):
        if l % 2 == 0:
            e = l // 2
            ew = (norm_pre[l], norm_post[l], w_in_even[e], w_pool[e], pool_scale[e],
                  sgu_ln_g[e], sgu_ln_b[e], w_spatial[e], b_spatial[e], w_out_even[e])
            zero_hist = jnp.zeros((xp.shape[0], POOL_HIST, W_A), xp.dtype)
            xp, hp, _ = even_layer(xp, zero_hist, 0, *ew)
            xs, hs, vs = even_layer(xs, cache_pool[e], past, *ew)
            pool_p.append(hp)
            pool_s.append(hs)
            sgu_s.append(vs)
        else:
            o = l // 2
            qa, qr, ckv, kr, g = mla_project(xp, pos_p, norm_pre[l], w_in_odd[o], q_norm[o],
                                             kv_norm[o], w_q_up[o], w_kv_up[o])
            ol = mla_attend_blocked(qa, qr, ckv, kr)
            xp = xp + rms_norm(mla_output(ol, g, w_kv_up[o], w_o[o]), norm_post[l])
            ckv_p.append(ckv)
            kr_p.append(kr)
            qa, qr, ckv, kr, g = mla_project(xs, pos_s, norm_pre[l], w_in_odd[o], q_norm[o],
                                             kv_norm[o], w_q_up[o], w_kv_up[o])
            ckv_all = jnp.concatenate([cache_ckv[o], ckv], axis=1)
            kr_all = jnp.concatenate([cache_krope[o], kr], axis=1)
            ol = mla_attend(qa, qr, ckv_all, kr_all, pos_s, k_pos_s)
            xs = xs + rms_norm(mla_output(ol, g, w_kv_up[o], w_o[o]), norm_post[l])
            ckv_s.append(ckv)
            kr_s.append(kr)
    return (xp, xs, jnp.stack(pool_p), jnp.stack(pool_s), jnp.stack(sgu_s),
            jnp.stack(ckv_p), jnp.stack(kr_p), jnp.stack(ckv_s), jnp.stack(kr_s))
```

```python
import contextlib
import numpy as np
import concourse.bass as bass
import concourse.mybir as mybir
from concourse.bass_utils import run_bass_kernel_spmd

F32 = mybir.dt.float32
BF16 = mybir.dt.bfloat16
AF = mybir.ActivationFunctionType
ALU = mybir.AluOpType
AX = mybir.AxisListType

D = 1024
KD = 8
EPS = 1e-6
ATTN_SCALE = 192.0 ** -0.5
NEG = -30000.0


class Op:
    __slots__ = ("eng", "fn", "waits", "signal", "count", "kind", "chan", "chan_val")

    def __init__(self, eng, fn, kind):
        self.eng = eng
        self.fn = fn
        self.kind = kind
        self.waits = []
        self.signal = False
        self.count = None
        self.chan = None
        self.chan_val = None


class Prog:
    ENGS = ("pe", "act", "dve", "pool", "sp")

    def __init__(self, nc):
        self.nc = nc
        self.ops = {e: [] for e in self.ENGS}
        self.res = {}
        self.chan_tot = {}
        self.chan_sem = {}
        self.eng_sem = {}
        self.out_ops = []

    def _deps(self, op, reads, writes, join=()):
        deps = []
        for r in reads:
            st = self.res.get(r)
            if st is None:
                st = self.res[r] = [[], [], []]
            for wop in st[0]:
                deps.append((wop, "raw"))
        for w in list(writes) + list(join):
            st = self.res.get(w)
            if st is None:
                st = self.res[w] = [[], [], []]
            if w not in join:
                for wop in st[0]:
                    deps.append((wop, "waw"))
            else:
                for rd in st[2]:
                    deps.append((rd, "war"))
            for rd in st[1]:
                deps.append((rd, "war"))
        seen = set()
        for p, kind in deps:
            if p is op or id(p) in seen:
                continue
            if p.kind == "c" and p.eng == op.eng and op.kind == "c":
                if op.eng == "pe" or kind == "war":
                    continue
            seen.add(id(p))
            op.waits.append(p)
            if p.kind == "c":
                p.signal = True
        for r in reads:
            self.res[r][1].append(op)
        for w in writes:
            self.res[w] = [[op], [], self.res[w][1]]
        for w in join:
            self.res[w][0].append(op)

    def op(self, eng, fn, reads=(), writes=(), join=()):
        o = Op(eng, fn, "c")
        self._deps(o, reads, writes, join)
        self.ops[eng].append(o)
        return o

    def dma(self, eng, fn, reads=(), writes=(), chan=None, n=1, is_out=False):
        o = Op(eng, fn, "d")
        self._deps(o, reads, writes)
        tot = self.chan_tot.get(chan, 0) + 16 * n
        self.chan_tot[chan] = tot
        o.chan = chan
        o.chan_val = tot
        self.ops[eng].append(o)
        if is_out:
            self.out_ops.append(o)
        return o

    def collective(self, fn, reads=(), writes=(), chan=None):
        o = Op("pool", fn, "x")
        self._deps(o, reads, writes)
        assert chan not in self.chan_tot
        self.chan_tot[chan] = 1
        o.chan = chan
        o.chan_val = 1
        self.ops["pool"].append(o)
        return o

    def barrier(self):
        o = Op("sp", lambda e: e.nop(), "c")
        for e in self.ENGS:
            if e == "sp":
                continue
            for p in reversed(self.ops[e]):
                if p.kind == "c":
                    p.signal = True
                    o.waits.append(p)
                    break
        lastd = {}
        for e in self.ENGS:
            for p in self.ops[e]:
                if p.kind != "c":
                    lastd[p.chan] = p
        o.waits.extend(lastd.values())
        o.signal = True
        self.ops["sp"].append(o)
        for e in self.ENGS:
            if e == "sp":
                continue
            o2 = Op(e, lambda q: q.nop(), "c")
            o2.waits.append(o)
            self.ops[e].append(o2)
        self.res = {}

    def finish(self):
        o = Op("sp", lambda e: e.nop(), "c")
        for p in self.out_ops:
            o.waits.append(p)
        for e in self.ENGS:
            if e == "sp":
                continue
            for p in reversed(self.ops[e]):
                if p.kind == "c":
                    p.signal = True
                    o.waits.append(p)
                    break
        self.ops["sp"].append(o)

    def replay(self):
        nc = self.nc
        engobj = {"pe": nc.tensor, "act": nc.scalar, "dve": nc.vector, "pool": nc.gpsimd, "sp": nc.sync}
        EPOCH = 6000
        for i, c in enumerate(self.chan_tot):
            self.chan_sem[c] = nc.alloc_semaphore(name="c%d" % i)
        for e in self.ENGS:
            cnt = 0
            for o in self.ops[e]:
                if o.kind == "c" and o.signal:
                    ep = cnt // EPOCH
                    if (e, ep) not in self.eng_sem:
                        self.eng_sem[(e, ep)] = nc.alloc_semaphore(name="s_%s%d" % (e, ep))
                    o.count = (ep, cnt % EPOCH + 1)
                    cnt += 1
        prog = self

        def run(e):
            eng = engobj[e]
            seen = {}
            for o in prog.ops[e]:
                for p in o.waits:
                    if p.kind == "c":
                        sem, val = prog.eng_sem[(p.eng, p.count[0])], p.count[1]
                    else:
                        sem, val = prog.chan_sem[p.chan], p.chan_val
                    k = id(sem)
                    if seen.get(k, 0) >= val:
                        continue
                    seen[k] = val
                    eng.wait_ge(sem, val)
                r = o.fn(eng)
                if o.kind == "c":
                    if o.signal:
                        r.then_inc(prog.eng_sem[(e, o.count[0])], 1)
                elif o.kind == "d":
                    for ins in r:
                        ins.then_inc(prog.chan_sem[o.chan], 16)
                else:
                    r.then_inc(prog.chan_sem[o.chan])

        with nc.Block() as block:
            @block.tensor
            def _(t):
                run("pe")

            @block.scalar
            def _(t):
                run("act")

            @block.vector
            def _(t):
                run("dve")

            @block.gpsimd
            def _(t):
                run("pool")

            @block.sync
            def _(t):
                run("sp")


def build(NBLK, NLAYERS):
    NTOK = NBLK * 128
    NS = 64
    TOT = NTOK + NS
    NT = NBLK // 2
    nc = bass.Bass("TRN2", target_bir_lowering=False)

    def din(name, shape):
        return nc.dram_tensor(name, list(shape), F32, kind="ExternalInput").ap()

    def dout(name, shape):
        return nc.dram_tensor(name, list(shape), F32, kind="ExternalOutput").ap()

    xp = din("xp", [NTOK, D])
    xs = din("xs", [NS, D])
    cpool = din("cpool", [2, 2, 15, D])
    cckv = din("cckv", [2, 2, 1024, 256])
    ckr = din("ckr", [2, 2, 1024, 64])
    w_in_even = din("w_in_even", [2, D, 5120])
    w_pool = din("w_pool", [2, 4, 256, 256])
    ln_g = din("ln_g", [2, D])
    ln_b = din("ln_b", [2, D])
    w_sp = din("w_sp", [2, 4, 128, 128])
    b_sp = din("b_sp", [2, 4, 128])
    w_out_even = din("w_out_even", [2, 2048, D])
    w_in_odd = din("w_in_odd", [2, D, 1728])
    kv_norm = din("kv_norm", [2, 256])
    w_q_up = din("w_q_up", [2, 384, 8 * 192])
    w_kv_up = din("w_kv_up", [2, 256, 8 * 256])
    w_o = din("w_o", [2, D, D])
    vecs = din("vecs", [128, 96])
    ident_d = din("ident", [128, 128])
    mask_d = din("mask", [128, 512])
    selw_d = din("selw", [128, 8])
    rcnt_d = din("rcnt", [128, 4 * 16])
    cs_tm = din("cs_tm", [128, (NBLK + 1) * 64])
    cs_fm = din("cs_fm", [64, 2 * TOT])

    y_p = dout("y_p", [NTOK, D])
    y_s = dout("y_s", [NS, D])
    o_pool_p = dout("o_pool_p", [2, 15, D])
    o_pool_s = dout("o_pool_s", [2, 2, 15, D])
    o_sgu_s = dout("o_sgu_s", [2, NS, D])
    o_ckv_p = dout("o_ckv_p", [2, NTOK, 256])
    o_kr_p = dout("o_kr_p", [2, NTOK, 64])
    o_ckv_s = dout("o_ckv_s", [2, NS, 256])
    o_kr_s = dout("o_kr_s", [2, NS, 64])

    HW = 8 * NBLK * 16
    bh_in = [nc.dram_tensor("bh_in%d" % e, [128, HW], F32) for e in range(2)]
    bh_out = [nc.dram_tensor("bh_out%d" % e, [4 * 128, HW], F32) for e in range(2)]
    NSPL = max(1, NBLK // 8)
    BPS = NBLK // NSPL
    TPS = BPS * 128
    bk_in = [[nc.dram_tensor("bk_in%d_%d" % (o, sp), [TPS, 320], BF16) for sp in range(NSPL)] for o in range(2)]
    bk_out = [[nc.dram_tensor("bk_out%d_%d" % (o, sp), [4 * TPS, 320], BF16) for sp in range(NSPL)] for o in range(2)]

    P = Prog(nc)
    es = contextlib.ExitStack()

    def S(name, shape, dt):
        return es.enter_context(nc.sbuf_tensor("t_" + name, list(shape), dt))

    with es:
        ps = [es.enter_context(nc.psum_tensor("ps%d" % i, [128, 512], F32)) for i in range(8)]
        bankctr = [0]

        def nbank():
            b = bankctr[0] % 6
            bankctr[0] += 1
            return b

        xT = S("xT", [128, KD, TOT], F32)
        identf = S("identf", [128, 128], F32)
        identb = S("identb", [128, 128], BF16)
        onesb = S("onesb", [128, 128], BF16)
        epsc = S("epsc", [128, 1], F32)
        vec = S("vec", [128, 96], F32)
        selw = S("selw", [128, 8], F32)
        rcnt = S("rcnt", [128, 4, 16], F32)
        NWB = 4
        wst = [S("wst%d" % i, [128, 256], F32) for i in range(1)]
        wbf = [S("wbf%d" % i, [128, 8 * 128], BF16) for i in range(NWB)]
        hT = S("hT", [128, KD, 256], BF16)
        sqb = S("sqb", [128, KD, 256], BF16)
        rstd = S("rstd", [128, 258], F32)
        yT = S("yT", [128, KD, 256], F32)
        tmpT = S("tmpT", [128, KD, 256], F32)
        RB = max(81920, 24 * NTOK + 4 * NBLK * 516)
        R = S("R", [128, RB], mybir.dt.uint8)

        R2 = S("R2", [128, 23040], mybir.dt.uint8)

        def r2view(off, shape, dt, parts=128):
            n = 1
            for s_ in shape[1:]:
                n *= s_
            esz = 4 if dt == F32 else 2
            v = R2[0:parts, off:off + n * esz].bitcast(dt)
            if len(shape) == 3:
                v = v.rearrange("p (a b) -> p a b", a=shape[1])
            elif len(shape) == 4:
                v = v.rearrange("p (a b c) -> p a b c", a=shape[1], b=shape[2])
            return v

        def rview(off, shape, dt):
            n = 1
            for s in shape[1:]:
                n *= s
            esz = 4 if dt == F32 else 2
            v = R[:, off:off + n * esz].bitcast(dt)
            if len(shape) == 3:
                v = v.rearrange("p (a b) -> p a b", a=shape[1])
            elif len(shape) == 4:
                v = v.rearrange("p (a b c) -> p a b c", a=shape[1], b=shape[2])
            return v

        def ld(dst, src, name, eng="sp"):
            P.dma(eng, lambda q: [q.dma_start(out=dst, in_=src)], writes=[name], chan=name)

        ld(identf[:], ident_d[:, :], "identf")
        ld(vec[:], vecs[:, :], "vec")
        ld(selw[:], selw_d[:, :], "selw")
        ld(rcnt[:].rearrange("p a b -> p (a b)"), rcnt_d[:, :], "rcnt")
        P.op("dve", lambda q: q.tensor_copy(out=identb[:], in_=identf[:]), reads=["identf"], writes=["identb"])
        P.op("pool", lambda q: q.memset(onesb[:], 1.0 / 1024.0), writes=["onesb"])
        P.op("pool", lambda q: q.memset(epsc[:], EPS), writes=["epsc"])

        evq = [0]

        def evac_copy(out, in_, bank, wres, rres=(), scale=None, join=()):
            evq[0] += 1
            if evq[0] % 2 == 0:
                P.op("act", lambda q: q.activation(out=out, in_=in_, func=AF.Copy, scale=(1.0 if scale is None else scale)),
                     reads=list(rres), writes=[("ps", bank)] + list(wres), join=join)
            else:
                if scale is None:
                    P.op("dve", lambda q: q.tensor_copy(out=out, in_=in_), reads=list(rres), writes=[("ps", bank)] + list(wres), join=join)
                else:
                    P.op("dve", lambda q: q.tensor_scalar(out=out, in0=in_, scalar1=scale, scalar2=None, op0=ALU.mult),
                         reads=list(rres), writes=[("ps", bank)] + list(wres), join=join)

        wctr = [0]

        WBA = {}
        CH = {}
        rlctr = [0]

        class WV:
            def __init__(self, name):
                self.name = name

            def __getitem__(self, idx):
                _, ks, cs = idx
                return (self.name, ks.start or 0, cs.start)

        PIECE = {}

        def conv(name, src2d, rows, cols, piece=None):
            A = nc.dram_tensor("wbA_" + name, [rows, cols], BF16)
            WBA[name] = A
            piece = piece or cols
            PIECE[name] = piece
            for pi in range((cols + piece - 1) // piece):
                cs = slice(pi * piece, min(cols, (pi + 1) * piece))
                P.dma("pool", lambda q, cs=cs: [q.dma_start(out=A.ap()[:, cs], in_=src2d[:, cs])], writes=[("wbA", name, pi)],
                      chan=("cv", name, pi))

        def relay(name, k0, kdim, c0, cols):
            i = rlctr[0]
            rlctr[0] += 1
            B = nc.dram_tensor("wbB_%s_%d_%d" % (name, k0, c0), [128, kdim * cols], BF16)
            CH[(name, k0, c0)] = (B, kdim, cols)
            src = WBA[name].ap().rearrange("(k p) c -> p k c", p=128)[:, k0:k0 + kdim, c0:c0 + cols]
            slot = ("rlslot", i % 8)
            P.dma("sp", lambda q: [q.dma_start(out=B.ap().rearrange("p (k c) -> p k c", k=kdim), in_=src)],
                  reads=[("wbA", name, c0 // PIECE[name]), slot], writes=[("wbB", name, k0, c0), slot], chan=slot)

        def relay_layer(layer):
            if layer % 2 == 0:
                e = layer // 2
                for j in range(40):
                    relay("win%d" % e, 0, 8, j * 128, 128)
                for dc in range(8):
                    relay("wo%d" % e, 0, 8, dc * 128, 128)
                    relay("wo%d" % e, 8, 8, dc * 128, 128)
            else:
                o = layer // 2
                for h in range(8):
                    relay("wku%d" % o, 0, 2, h * 256, 256)
                for (c0, w) in [(384, 128), (512, 128), (640, 64), (0, 128), (128, 128), (256, 128)] + [(704 + ec * 128, 128) for ec in range(8)]:
                    relay("wio%d" % o, 0, 8, c0, w)
                for h in range(8):
                    relay("wqu%d" % o, 0, 3, h * 192, 192)
                for dc in range(8):
                    relay("wov%d" % o, 0, 8, dc * 128, 128)

        def conv_layer(layer):
            if layer % 2 == 0:
                e = layer // 2
                conv("win%d" % e, w_in_even[e], D, 5120, piece=1024)
                conv("wo%d" % e, w_out_even[e], 2048, D)
            else:
                o = layer // 2
                conv("wku%d" % o, w_kv_up[o], 256, 2048)
                conv("wio%d" % o, w_in_odd[o], D, 1728)
                conv("wqu%d" % o, w_q_up[o], 384, 1536)
                conv("wov%d" % o, w_o[o], D, D)

        def wload(ref, kdim, cols):
            i = wctr[0]
            wctr[0] += 1
            B, kd_, cols_ = CH[ref]
            assert kd_ == kdim and cols_ == cols, (ref, kdim, cols)
            wb = wbf[i % NWB]
            bname = "wbf%d" % (i % NWB)
            n = kdim * cols
            wbv = wb[:, 0:n].rearrange("p (k c) -> p k c", k=kdim)
            P.dma("sp", lambda q: [q.dma_start(out=wb[:, 0:n], in_=B.ap()[:, :])], reads=[("wbB",) + ref], writes=[bname], chan=bname)
            return wbv, bname

        xin = tmpT[:].rearrange("p k n -> p (k n)")[:, 0:D]

        def load_x(src_rows, c0, n):
            P.dma("sp", lambda q: [q.dma_start(out=xin[0:n, :], in_=src_rows)], writes=["tmpT"], chan="xin")
            for half in range(2):
                b = nbank()

                def tr(q, half=half, b=b):
                    r = None
                    for j in range(4):
                        k = half * 4 + j
                        r = q.transpose(ps[b][:, j * 128:j * 128 + n], xin[0:n, k * 128:(k + 1) * 128], identf[0:n, 0:n])
                    return r
                P.op("pe", tr, reads=["tmpT", "identf"], writes=[("ps", b)])
                src = ps[b][:].rearrange("p (j t) -> p j t", j=4)[:, :, 0:n]
                evac_copy(xT[:, half * 4:half * 4 + 4, c0:c0 + n], src, b, [("xT", c0 // 256)])

        yout = yT[:].rearrange("p k n -> p (k n)")[:, 0:D]

        def store_x(dst_rows, c0, n):
            for half in range(2):
                b = nbank()

                def tr(q, half=half, b=b):
                    r = None
                    for j in range(4):
                        k = half * 4 + j
                        r = q.transpose(ps[b][0:n, j * 128:(j + 1) * 128], xT[:, k, c0:c0 + n], identf[:, :])
                    return r
                P.op("pe", tr, reads=[("xT", c0 // 256), "identf"], writes=[("ps", b)])
                evac_copy(yout[0:n, half * 512:(half + 1) * 512], ps[b][0:n, :], b, ["yT"])
            P.dma("sp", lambda q: [q.dma_start(out=dst_rows, in_=yout[0:n, :])], reads=["yT"],
                  chan="yout", is_out=True)

        for b in range(NBLK):
            load_x(xp[b * 128:(b + 1) * 128, :], b * 128, 128)
        load_x(xs[:, :], NTOK, NS)

        def rms_stats(src4, n, srcres, outname="rstd", kdim=KD, scale=1.0):
            P.op("act", lambda q: q.activation(out=sqb[:, 0:kdim, 0:n], in_=src4, func=AF.Square), reads=list(srcres), writes=["sqb"])
            b = nbank()

            def mm(q):
                r = None
                for k in range(kdim):
                    r = q.matmul(ps[b][:, 0:n], lhsT=onesb[:], rhs=sqb[:, k, 0:n], start=(k == 0), stop=(k == kdim - 1))
                return r
            P.op("pe", mm, reads=["sqb", "onesb"], writes=[("ps", b)])
            P.op("act", lambda q: q.activation(out=rstd[:, 0:n], in_=ps[b][:, 0:n], func=AF.Sqrt, bias=epsc[:, 0:1], scale=scale),
                 reads=["epsc"], writes=[("ps", b), outname])
            P.op("dve", lambda q: q.reciprocal(out=rstd[:, 0:n], in_=rstd[:, 0:n]), reads=[outname], writes=[outname])

        KS = 5

        def pre_norm(layer, xview, n, xres):
            gb = vec[:, layer * 8:layer * 8 + 8].unsqueeze(2).to_broadcast([128, KD, n])
            P.op("pool", lambda q: q.tensor_tensor(out=tmpT[:, :, 0:n], in0=xview, in1=gb, op=ALU.mult),
                 reads=list(xres) + ["vec"], writes=["tmpT"])
            rms_stats(xview, n, xres)
            rb1 = rstd[:, 0:n].unsqueeze(1).to_broadcast([128, KS, n])
            rb2 = rstd[:, 0:n].unsqueeze(1).to_broadcast([128, KD - KS, n])
            P.op("dve", lambda q: q.tensor_tensor(out=hT[:, 0:KS, 0:n], in0=tmpT[:, 0:KS, 0:n], in1=rb1, op=ALU.mult),
                 reads=["tmpT", "rstd"], writes=["hT"])
            P.op("pool", lambda q: q.tensor_tensor(out=hT[:, KS:KD, 0:n], in0=tmpT[:, KS:KD, 0:n], in1=rb2, op=ALU.mult),
                 reads=["tmpT", "rstd"], join=["hT"])

        def pre_norm_lite(layer, xview, n, xres):
            P.op("act", lambda q: q.activation(out=hT[:, :, 0:n], in_=xview, func=AF.Square), reads=list(xres), writes=["hT"])
            b = nbank()

            def mm(q):
                r = None
                for k in range(KD):
                    r = q.matmul(ps[b][:, 0:n], lhsT=onesb[:], rhs=hT[:, k, 0:n], start=(k == 0), stop=(k == KD - 1))
                return r
            P.op("pe", mm, reads=["hT", "onesb"], writes=[("ps", b)])
            P.op("act", lambda q: q.activation(out=rstd[:, 0:n], in_=ps[b][:, 0:n], func=AF.Sqrt, bias=epsc[:, 0:1], scale=1.0),
                 reads=["epsc"], writes=[("ps", b), "rstd"])
            P.op("dve", lambda q: q.reciprocal(out=rstd[:, 0:n], in_=rstd[:, 0:n]), reads=["rstd"], writes=["rstd"])
            for k in range(KD):
                P.op("dve", lambda q, k=k: q.scalar_tensor_tensor(out=hT[:, k, 0:n], in0=xview[:, k, :], scalar=vec[:, layer * 8 + k:layer * 8 + k + 1],
                                                                  in1=rstd[:, 0:n], op0=ALU.mult, op1=ALU.mult),
                     reads=list(xres) + ["vec", "rstd"], **({"writes": ["hT"]} if k == 0 else {"join": ["hT"]}))

        def post_norm_update(layer, c0, n):
            xres = [("xT", c0 // 256)]
            gb = vec[:, 32 + layer * 8:32 + layer * 8 + 8].unsqueeze(2).to_broadcast([128, KD, n])
            P.op("pool", lambda q: q.tensor_tensor(out=tmpT[:, :, 0:n], in0=yT[:, :, 0:n], in1=gb, op=ALU.mult),
                 reads=["yT", "vec"], writes=["tmpT"])
            rms_stats(yT[:, :, 0:n], n, ["yT"])
            rb1 = rstd[:, 0:n].unsqueeze(1).to_broadcast([128, KS, n])
            rb2 = rstd[:, 0:n].unsqueeze(1).to_broadcast([128, KD - KS, n])
            P.op("dve", lambda q: q.tensor_tensor(out=tmpT[:, 0:KS, 0:n], in0=tmpT[:, 0:KS, 0:n], in1=rb1, op=ALU.mult),
                 reads=["tmpT", "rstd"], writes=["tmpTa"])
            P.op("pool", lambda q: q.tensor_tensor(out=tmpT[:, KS:KD, 0:n], in0=tmpT[:, KS:KD, 0:n], in1=rb2, op=ALU.mult),
                 reads=["tmpT", "rstd"], writes=["tmpTb"])
            P.op("dve", lambda q: q.tensor_tensor(out=xT[:, 0:KS, c0:c0 + n], in0=xT[:, 0:KS, c0:c0 + n], in1=tmpT[:, 0:KS, 0:n], op=ALU.add),
                 reads=["tmpTa", "tmpT"] + xres, writes=xres)
            P.op("pool", lambda q: q.tensor_tensor(out=xT[:, KS:KD, c0:c0 + n], in0=xT[:, KS:KD, c0:c0 + n], in1=tmpT[:, KS:KD, 0:n], op=ALU.add),
                 reads=["tmpTb", "tmpT"] + xres, join=xres)

        def mm_fm(bank, wv, col0, ncol, rhs3, n, kdim=KD, first=True, last=True):
            def f(q):
                r = None
                for k in range(kdim):
                    r = q.matmul(ps[bank][0:ncol, 0:n], lhsT=wv[:, k, col0:col0 + ncol], rhs=rhs3[:, k, 0:n],
                                 start=(first and k == 0), stop=(last and k == kdim - 1))
                return r
            return f

        HWB = HW * 4
        halo_s = rview(0, [128, 8, NBLK, 16], F32)
        aT = rview(8192, [128, 8, 2, 144], F32)
        sA = rview(17408, [128, 8, 2, 144], F32)
        sB = rview(26624, [128, 8, 2, 144], F32)
        wp_st = rview(35840, [128, 8, 256], F32)
        gT = rview(44032, [128, 16, 256], BF16)
        hb = R[:, 44032:44032 + HWB].bitcast(F32)
        mixT = rview(52224, [128, 16, 256], BF16)
        vtm = rview(60416, [128, 2, D], F32)
        uT = rview(68608, [128, KD, 256], BF16)
        halo_c = rview(72704, [128, 8, NBLK, 16], F32)
        dT = sqb
        vbf = r2view(0, [128, 2, D], BF16)
        atm = yT[:].rearrange("p k n -> p (k n)")[:, 0:D]
        histtm = tmpT[:].rearrange("p k n -> p (k n)")[:, 0:D]
        st4 = S("st4", [128, 16], F32)
        wp_bf = r2view(4096, [128, 8, 256], BF16)
        ws_f = r2view(8192, [128, 4, 128], F32)
        wsT = r2view(10240, [128, 4, 128], BF16)
        wsd_f = r2view(11264, [64, 4, 64], F32, parts=64)
        wsdT = r2view(12288, [64, 4, 64], BF16, parts=64)
        lng = r2view(12800, [128, D], F32)
        lnb = r2view(16896, [128, D], F32)
        bsp = r2view(20992, [128, 4, 128], F32)

        def even_layer(e):
            layer = 2 * e
            win = WV("win%d" % e)
            for t in range(NT):
                xv = xT[:, :, t * 256:(t + 1) * 256].rearrange("p k (b w) -> p k b w", b=2)[:, :, :, 112:128]
                P.op("pool", lambda q, xv=xv: q.tensor_copy(out=yT[:, :, 0:32].rearrange("p k (b w) -> p k b w", b=2), in_=xv),
                     reads=[("xT", t)], writes=["yT"])
                pre_norm(layer, yT[:, :, 0:32], 32, ["yT"])
                for ec in range(8):
                    wv, wn = wload(win[:, :, ec * 128:(ec + 1) * 128], KD, 128)
                    b = nbank()
                    P.op("pe", mm_fm(b, wv, 0, 128, hT, 32), reads=[wn, "hT"], writes=[("ps", b)])
                    evac_copy(halo_c[:, ec, t * 2:t * 2 + 2, :], ps[b][:, 0:32].rearrange("p (b w) -> p b w", b=2), b, ["halo_c"])
            P.dma("sp", lambda q: [q.dma_start(out=bh_in[e].ap()[:, :], in_=halo_c.rearrange("p k b w -> p (k b w)"))],
                  reads=["halo_c"], writes=[("bh_in", e)], chan=("bh_in", e))
            P.collective(lambda q: q.collective_compute("AllGather", ALU.bypass, replica_groups=[[0, 1, 2, 3], [4, 5, 6, 7]],
                                                        ins=[bh_in[e].ap().opt()], outs=[bh_out[e].ap().opt()]),
                         reads=[("bh_in", e)], writes=[("bh_out", e)], chan=("ccH", e))
            hg = hb.rearrange("p (k b w) -> p k b w", k=8, b=NBLK)
            for r in range(4):
                P.dma("sp", lambda q, r=r: [q.dma_start(out=hb, in_=bh_out[e].ap()[r * 128:(r + 1) * 128, :])],
                      reads=[("bh_out", e)], writes=["gT"], chan="hb")
                if r == 0:
                    P.op("dve", lambda q: q.tensor_scalar(out=halo_s, in0=hg, scalar1=selw[:, 0:1], scalar2=None, op0=ALU.mult),
                         reads=["gT", "selw"], writes=["halo_s"])
                else:
                    P.op("dve", lambda q, r=r: q.scalar_tensor_tensor(out=halo_s, in0=hg, scalar=selw[:, r:r + 1], in1=halo_s,
                                                                     op0=ALU.mult, op1=ALU.add),
                         reads=["gT", "selw", "halo_s"], writes=["halo_s"])
                if r == 3 and NBLK > 1:
                    P.op("dve", lambda q: q.scalar_tensor_tensor(out=halo_s[:, :, 1:NBLK, :], in0=hg[:, :, 0:NBLK - 1, :],
                                                                 scalar=selw[:, 4:5], in1=halo_s[:, :, 1:NBLK, :],
                                                                 op0=ALU.mult, op1=ALU.add),
                         reads=["gT", "selw", "halo_s"], writes=["halo_s"])
            P.dma("sp", lambda q: [q.dma_start(out=wp_st, in_=w_pool[e].rearrange("g (i p) o -> p (g i) o", p=128))],
                  writes=["wp_st"], chan="wp_st")
            P.op("dve", lambda q: q.tensor_copy(out=wp_bf[:], in_=wp_st), reads=["wp_st"], writes=["wp_bf"])
            P.dma("sp", lambda q: [q.dma_start(out=ws_f[:], in_=w_sp[e].rearrange("g i j -> i g j"))], writes=["ws_f"], chan="ws_f")
            b = nbank()

            def trw(q):
                r = None
                for g in range(4):
                    r = q.transpose(ps[b][:, g * 128:(g + 1) * 128], ws_f[:, g, :], identf[:])
                return r
            P.op("pe", trw, reads=["ws_f", "identf"], writes=[("ps", b)])
            P.op("dve", lambda q: q.tensor_copy(out=wsT[:].rearrange("p g i -> p (g i)"), in_=ps[b][:, :]), writes=[("ps", b), "wsT"])
            P.op("pool", lambda q: q.memset(wsT[64:128, :, 0:64], 0.0), reads=["wsT"], writes=["wsT"])
            P.op("pool", lambda q: q.memset(wsd_f[:], 0.0), writes=["wsd_f"])
            P.dma("sp", lambda q: [q.dma_start(out=wsd_f[0:32, :, 0:32], in_=w_sp[e, :, 0:32, 0:32].rearrange("g i j -> i g j")),
                                   q.dma_start(out=wsd_f[32:64, :, 32:64], in_=w_sp[e, :, 0:32, 0:32].rearrange("g i j -> i g j"))],
                  reads=["wsd_f"], writes=["wsd_f"], chan="wsd_f", n=2)
            b2 = nbank()

            def trw2(q):
                r = None
                for g in range(4):
                    r = q.transpose(ps[b2][0:64, g * 64:(g + 1) * 64], wsd_f[:, g, :], identf[0:64, 0:64])
                return r
            P.op("pe", trw2, reads=["wsd_f", "identf"], writes=[("ps", b2)])
            P.op("dve", lambda q: q.tensor_copy(out=wsdT[:].rearrange("p g i -> p (g i)"), in_=ps[b2][0:64, 0:256]),
                 writes=[("ps", b2), "wsdT"])
            ld(lng[:], ln_g[e:e + 1, :].broadcast_to([128, D]), "lng")
            ld(lnb[:], ln_b[e:e + 1, :].broadcast_to([128, D]), "lnb")
            ld(bsp[:].rearrange("p g i -> p (g i)"), b_sp[e:e + 1].rearrange("o g i -> o (g i)").broadcast_to([128, 512]), "bsp")
            return dict(win=win)

        def even_tile(e, ctx, t):
            layer = 2 * e
            sample = (t == NT)
            NB, W = (2, 32) if sample else (2, 128)
            n = NB * W
            c0 = NTOK if sample else t * 256
            xres = [("xT", c0 // 256)]
            win = ctx["win"]
            pre_norm(layer, xT[:, :, c0:c0 + n], n, xres)
            if sample:
                for s in range(2):
                    P.dma("sp", lambda q, s=s: [q.dma_start(out=histtm[0:15, :], in_=cpool[e, s])], writes=["tmpT"], chan="histtm")
                    for half in range(2):
                        b = nbank()

                        def trh(q, half=half, b=b):
                            r = None
                            for j in range(4):
                                k = half * 4 + j
                                r = q.transpose(ps[b][:, j * 16:j * 16 + 15], histtm[0:15, k * 128:(k + 1) * 128], identf[0:15, 0:15])
                            return r
                        P.op("pe", trh, reads=["tmpT", "identf"], writes=[("ps", b)])
                        P.op("dve", lambda q, half=half, b=b, s=s: q.tensor_copy(
                            out=aT[:, half * 4:half * 4 + 4, s, 1:16], in_=ps[b][:, 0:64].rearrange("p (j w) -> p j w", j=4)[:, :, 0:15]),
                            writes=[("ps", b), "aT"])
            else:
                P.op("pool", lambda q: q.tensor_copy(out=aT[:, :, :, 0:16], in_=halo_s[:, :, t * 2:t * 2 + 2, :]),
                     reads=["halo_s"], writes=["aT"])
            need_tm = sample or (t == NT - 1)
            m = 64 if sample else 128
            for ec in range(8):
                wv, wn = wload(win[:, :, ec * 128:(ec + 1) * 128], KD, 128)
                b = nbank()
                P.op("pe", mm_fm(b, wv, 0, 128, hT, n), reads=[wn, "hT"], writes=[("ps", b)])
                evac_copy(aT[:, ec, 0:NB, 16:16 + W], ps[b][:, 0:n].rearrange("p (b w) -> p b w", b=NB), b, ["aT"])
                if need_tm:
                    b = nbank()
                    tc0 = 0 if sample else 128

                    def mmtm(q, b=b, wv=wv, tc0=tc0):
                        r = None
                        for k in range(KD):
                            r = q.matmul(ps[b][0:m, 0:128], lhsT=hT[:, k, tc0:tc0 + m], rhs=wv[:, k, :], start=(k == 0), stop=(k == KD - 1))
                        return r
                    P.op("pe", mmtm, reads=[wn, "hT"], writes=[("ps", b)])
                    evac_copy(atm[0:m, ec * 128:(ec + 1) * 128], ps[b][0:m, 0:128], b, ["yT"])
            if need_tm:
                if sample:
                    P.dma("sp", lambda q: [q.dma_start(out=o_pool_s[e, 0], in_=atm[17:32, :]),
                                           q.dma_start(out=o_pool_s[e, 1], in_=atm[49:64, :])],
                          reads=["yT"], chan="atm", n=2, is_out=True)
                else:
                    P.dma("sp", lambda q: [q.dma_start(out=o_pool_p[e], in_=atm[113:128, :])], reads=["yT"], chan="atm", is_out=True)
            def sh(dst, src, k0, d):
                return lambda q: q.tensor_tensor(out=dst[:, k0:8, 0:NB, d:16 + W], in0=src[:, k0:8, 0:NB, d:16 + W],
                                                 in1=src[:, k0:8, 0:NB, 0:16 + W - d], op=ALU.add)
            P.op("dve", sh(sA, aT, 0, 1), reads=["aT"], writes=["sA", "sA2"])
            P.op("pool", sh(sB, sA, 2, 2), reads=["sA"], writes=["sB", "sB2"])
            P.op("dve", sh(sA, sB, 4, 4), reads=["sB"], writes=["sA2"])
            P.op("pool", sh(sB, sA, 6, 8), reads=["sA2"], writes=["sB2"])
            srcs = [(sA, ["sA"]), (sB, ["sB"]), (sA, ["sA2"]), (sB, ["sB2"])]
            for g in range(4):
                sbuf_, sres = srcs[g]
                w = 2 ** (g + 1)
                P.op("dve", lambda q, g=g, sbuf_=sbuf_, w=w: q.scalar_tensor_tensor(
                    out=dT[:, 2 * g:2 * g + 2, 0:n].rearrange("p k (b w) -> p k b w", b=NB),
                    in0=sbuf_[:, 2 * g:2 * g + 2, 0:NB, 16:16 + W], scalar=1.0 / w, in1=aT[:, 2 * g:2 * g + 2, 0:NB, 16:16 + W],
                    op0=ALU.mult, op1=ALU.subtract), reads=sres + ["aT"], writes=["sqb"])
                if (not sample) and t == 0:
                    rc = rcnt[:, g, :].unsqueeze(1).to_broadcast([128, 2, 16])
                    P.op("pool", lambda q, g=g, sbuf_=sbuf_, rc=rc: q.tensor_tensor(
                        out=tmpT[:, 0:2, 0:16], in0=sbuf_[:, 2 * g:2 * g + 2, 0, 16:32], in1=rc, op=ALU.mult),
                        reads=sres + ["rcnt"], writes=["tmpT"])
                    P.op("dve", lambda q, g=g: q.tensor_tensor(out=dT[:, 2 * g:2 * g + 2, 0:16], in0=tmpT[:, 0:2, 0:16],
                                                               in1=aT[:, 2 * g:2 * g + 2, 0, 16:32], op=ALU.subtract),
                         reads=["tmpT", "aT"], writes=["sqb"])
            nblk_tm = 1 if sample else 2
            for ec in range(8):
                wv, wn = wload(win[:, :, 2048 + ec * 128:2048 + (ec + 1) * 128], KD, 128)
                for blk in range(nblk_tm):
                    b = nbank()

                    def mmv(q, b=b, wv=wv, blk=blk):
                        r = None
                        for k in range(KD):
                            r = q.matmul(ps[b][0:m, 0:128], lhsT=hT[:, k, blk * 128:blk * 128 + m], rhs=wv[:, k, :],
                                         start=(k == 0), stop=(k == KD - 1))
                        return r
                    P.op("pe", mmv, reads=[wn, "hT"], writes=[("ps", b)])
                    P.op("act", lambda q, b=b, blk=blk, ec=ec: q.activation(out=vtm[0:m, blk, ec * 128:(ec + 1) * 128], in_=ps[b][0:m, 0:128],
                                                                         func=AF.Gelu), writes=[("ps", b), ("vtm", blk)])
            sqscr = tmpT[:].rearrange("p k n -> p (k n)")[:, 0:D]
            for blk in range(nblk_tm):
                vb = vtm[0:m, blk, :]
                vr = ("vtm", blk)
                P.op("dve", lambda q, vb=vb: q.reduce_sum(out=st4[0:m, 0:1], in_=vb, axis=AX.X), reads=[vr], writes=["st_a"])
                P.op("act", lambda q, vb=vb: q.activation(out=sqscr[0:m, :], in_=vb, func=AF.Square, accum_out=st4[0:m, 1:2]),
                     reads=[vr], writes=["tmpT", "st_b"])
                P.op("pool", lambda q: q.tensor_scalar(out=st4[0:m, 2:3], in0=st4[0:m, 0:1], scalar1=1.0 / D, scalar2=None, op0=ALU.mult),
                     reads=["st_a"], writes=["st_c"])
                P.op("dve", lambda q: q.tensor_tensor(out=st4[0:m, 3:4], in0=st4[0:m, 2:3], in1=st4[0:m, 2:3], op=ALU.mult),
                     reads=["st_c"], writes=["st_d"])
                P.op("pool", lambda q: q.tensor_scalar(out=st4[0:m, 4:5], in0=st4[0:m, 1:2], scalar1=1.0 / D, scalar2=None, op0=ALU.mult),
                     reads=["st_b"], writes=["st_e"])
                P.op("dve", lambda q: q.tensor_tensor(out=st4[0:m, 5:6], in0=st4[0:m, 4:5], in1=st4[0:m, 3:4], op=ALU.subtract),
                     reads=["st_e", "st_d"], writes=["st_f"])
                P.op("act", lambda q: q.activation(out=st4[0:m, 6:7], in_=st4[0:m, 5:6], func=AF.Sqrt, bias=epsc[0:m, 0:1], scale=1.0),
                     reads=["st_f", "epsc"], writes=["st_g"])
                P.op("dve", lambda q: q.reciprocal(out=st4[0:m, 7:8], in_=st4[0:m, 6:7]), reads=["st_g"], writes=["st_h"])
                P.op("dve", lambda q, vb=vb: q.tensor_scalar(out=vb, in0=vb, scalar1=st4[0:m, 2:3], scalar2=st4[0:m, 7:8],
                                                          op0=ALU.subtract, op1=ALU.mult), reads=[vr, "st_c", "st_h"], writes=[vr])
                P.op("pool", lambda q, vb=vb: q.tensor_tensor(out=vb, in0=vb, in1=lng[0:m, :], op=ALU.mult), reads=[vr, "lng"], writes=[vr])
                P.op("dve", lambda q, vb=vb: q.tensor_tensor(out=vb, in0=vb, in1=lnb[0:m, :], op=ALU.add), reads=[vr, "lnb"], writes=[vr])
                P.op("pool", lambda q, vb=vb, blk=blk: q.tensor_copy(out=vbf[0:m, blk, :], in_=vb), reads=[vr], writes=[("vbf", blk)])
                if sample:
                    P.dma("sp", lambda q: [q.dma_start(out=o_sgu_s[e], in_=vtm[0:64, 0, :])], reads=[vr], chan="vout", is_out=True)
            for ec in range(8):
                wv, wn = wload(win[:, :, 1024 + ec * 128:1024 + (ec + 1) * 128], KD, 128)
                b = nbank()
                P.op("pe", mm_fm(b, wv, 0, 128, hT, n), reads=[wn, "hT"], writes=[("ps", b)])
                P.op("act", lambda q, b=b, ec=ec: q.activation(out=uT[:, ec, 0:n], in_=ps[b][:, 0:n], func=AF.Gelu),
                     writes=[("ps", b), "uT"])
            for ec in range(16):
                wv, wn = wload(win[:, :, 3072 + ec * 128:3072 + (ec + 1) * 128], KD, 128)
                b = nbank()
                P.op("pe", mm_fm(b, wv, 0, 128, hT, n), reads=[wn, "hT"], writes=[("ps", b)])
                P.op("act", lambda q, b=b, ec=ec: q.activation(out=gT[:, ec, 0:n], in_=ps[b][:, 0:n], func=AF.Silu),
                     writes=[("ps", b), "gT"])
            P.op("pool", lambda q: q.tensor_tensor(out=uT[:, :, 0:n], in0=uT[:, :, 0:n], in1=gT[:, 8:16, 0:n], op=ALU.mult),
                 reads=["uT", "gT"], writes=["uT"])
            wI = 64 if sample else 128
            for blk in range(nblk_tm):
                for half in range(2):
                    b = nbank()

                    def mms(q, b=b, blk=blk, half=half):
                        r = None
                        for j in range(4):
                            dk = half * 4 + j
                            g = dk // 2
                            rhs = wsdT[0:64, g, :] if sample else wsT[:, g, :]
                            r = q.matmul(ps[b][:, j * 128:j * 128 + wI], lhsT=vbf[0:m, blk, dk * 128:(dk + 1) * 128], rhs=rhs,
                                         start=True, stop=True)
                        return r
                    P.op("pe", mms, reads=[("vbf", blk), "wsT", "wsdT"], writes=[("ps", b)])
                    ps4 = ps[b][:].rearrange("p (g c i) -> p g c i", g=2, c=2)
                    if sample:
                        for s in range(2):
                            bb = bsp[:, half * 2:half * 2 + 2, 0:32].unsqueeze(2).to_broadcast([128, 2, 2, 32])
                            P.op("dve", lambda q, s=s, half=half, bb=bb, ps4=ps4: q.tensor_tensor(
                                out=tmpT[:, half * 4:half * 4 + 4, s * 32:(s + 1) * 32].rearrange("p (g c) i -> p g c i", g=2),
                                in0=ps4[:, :, :, s * 32:(s + 1) * 32], in1=bb, op=ALU.add),
                                reads=["bsp"], writes=[("ps", b), "tmpT"])
                    else:
                        bb = bsp[:, half * 2:half * 2 + 2, :].unsqueeze(2).to_broadcast([128, 2, 2, 128])
                        P.op("dve", lambda q, half=half, blk=blk, bb=bb, ps4=ps4: q.tensor_tensor(
                            out=tmpT[:, half * 4:half * 4 + 4, blk * 128:(blk + 1) * 128].rearrange("p (g c) i -> p g c i", g=2),
                            in0=ps4, in1=bb, op=ALU.add),
                            reads=["bsp"], writes=[("ps", b), "tmpT"])
            P.op("pool", lambda q: q.tensor_tensor(out=mixT[:, 8:16, 0:n], in0=tmpT[:, :, 0:n], in1=uT[:, :, 0:n], op=ALU.mult),
                 reads=["tmpT", "uT"], writes=["mixT_b"])
            for g in range(4):
                for oc in range(2):
                    b = nbank()

                    def mmp(q, b=b, g=g, oc=oc):
                        r = None
                        for ic in range(2):
                            r = q.matmul(ps[b][:, 0:n], lhsT=wp_bf[:, g * 2 + ic, oc * 128:(oc + 1) * 128], rhs=dT[:, 2 * g + ic, 0:n],
                                         start=(ic == 0), stop=(ic == 1))
                        return r
                    P.op("pe", mmp, reads=["wp_bf", "sqb"], writes=[("ps", b)])
                    ch = 2 * g + oc
                    P.op("dve", lambda q, b=b, ch=ch: q.scalar_tensor_tensor(
                        out=mixT[:, ch, 0:n], in0=ps[b][:, 0:n], scalar=vec[:, 64 + e * 8 + ch:64 + e * 8 + ch + 1], in1=gT[:, ch, 0:n],
                        op0=ALU.mult, op1=ALU.mult), reads=["vec", "gT"], writes=[("ps", b), "mixT_a"])
            wo = WV("wo%d" % e)
            mixres = ["mixT_b", "mixT_a"]
            for dc in range(8):
                wva, wna = wload(wo[:, 0:8, dc * 128:(dc + 1) * 128], 8, 128)
                wvb, wnb = wload(wo[:, 8:16, dc * 128:(dc + 1) * 128], 8, 128)
                b = nbank()
                P.op("pe", mm_fm(b, wva, 0, 128, mixT[:, 0:8, :], n, first=True, last=False), reads=[wna] + mixres, writes=[("ps", b)])
                P.op("pe", mm_fm(b, wvb, 0, 128, mixT[:, 8:16, :], n, first=False, last=True), reads=[wnb] + mixres, writes=[("ps", b)])
                evac_copy(yT[:, dc, 0:n], ps[b][:, 0:n], b, ["yT"])
            post_norm_update(layer, c0, n)

        NK = 4 * NTOK
        KT = rview(0, [128, 2, NK], BF16)
        krT = R[0:64, 4 * NK:6 * NK].bitcast(BF16)
        Vt = rview(6 * NK, [128, 4 * NBLK, 258], BF16)
        WkvT = r2view(0, [128, 8, 256], BF16)
        Wv = r2view(4096, [128, 2, 8, 128], BF16)
        gateT = r2view(8192, [128, 8, 128], BF16)
        qn = r2view(10240, [128, 8, 128], BF16)
        qaT = r2view(12288, [128, 2, 8, 128], BF16)
        qrope = r2view(16384, [64, 8, 128], BF16, parts=64)
        kvn_bc = r2view(18432, [128, 256], F32)
        qcn = r2view(19456, [128, 3, 128], BF16)
        Pb = r2view(20224, [128, 2, 512], BF16)
        maskb = S("maskb", [128, 512], BF16)
        permf = S("permf", [64, 64], F32)
        cst = S("cst", [128, 64], F32)
        csfO = S("csfO", [128, 258], F32)
        csf = csfO[0:64, 0:256].rearrange("p (a t) -> p a t", a=2)
        stt = S("stt", [128, 16], F32)
        psb = [p[:].bitcast(BF16) for p in ps]
        tmp2d = tmpT[:].rearrange("p k n -> p (k n)")
        tmpb = tmp2d.bitcast(BF16)
        yT2d = yT[:].rearrange("p k n -> p (k n)")
        sqb2d = sqb[:].rearrange("p k n -> p (k n)")
        PTs = [tmpb[:, 0:512], tmpb[:, 512:1024]]
        krtm_s = tmpb[:, 1024:1600].rearrange("p (a b) -> p a b", b=64)
        olat = tmpb[:, 2048:4096].rearrange("p (h c) -> p h c", h=8)
        qas = tmpb[:, 1600:1856].rearrange("p (c t) -> p c t", c=2)
        qrs = tmpb[0:64, 1856:1984]
        qc = yT[:, 0:3, 128:256]
        Oacc = yT[:, 3:5, 128:256]
        qrf = yT[0:64, 5, 128:256]
        qt1 = yT[0:64, 6, 128:256]
        qt2 = yT[0:64, 7, 128:256]
        olT = sqb[:].rearrange("p k n -> p (k n)").rearrange("p (c h q) -> p c h q", c=2, h=8)
        oT = hT[:, :, 128:256]
        bks = [nc.dram_tensor("bks%d" % o, [64, 320], BF16) for o in range(2)]

        ld(tmp2d[:, 0:512], mask_d[:, :], "tmpT")
        P.op("dve", lambda q: q.tensor_copy(out=maskb[:], in_=tmp2d[:, 0:512]), reads=["tmpT"], writes=["maskb"])
        P.op("dve", lambda q: q.tensor_copy(out=permf[:, 0:32], in_=identf[0:64, 32:64]), reads=["identf"], writes=["permf"])
        P.op("pool", lambda q: q.tensor_copy(out=permf[:, 32:64], in_=identf[0:64, 0:32]), reads=["identf", "permf"], writes=["permf"])

        def stage_cast(src_ap, dst_ap, w, dres, parts=128):
            stv = wst[0][0:parts, 0:w]
            P.dma("sp", lambda q: [q.dma_start(out=stv, in_=src_ap)], writes=["wst0"], chan="wst0")
            P.op("pool", lambda q: q.tensor_copy(out=dst_ap, in_=stv), reads=["wst0"], writes=list(dres))

        def kt_transposes(kb, vidx, parts, krt_src, col0, w):
            b = nbank()

            def tr(q):
                q.transpose(psb[b][:, 0:w], Vt[0:parts, vidx, 0:128], identb[0:parts, 0:parts])
                q.transpose(psb[b][:, 128:128 + w], Vt[0:parts, vidx, 128:256], identb[0:parts, 0:parts])
                return q.transpose(psb[b][0:64, 256:256 + w], krt_src, identb[0:parts, 0:parts])
            P.op("pe", tr, reads=["V", "tmpT", "identb"], writes=[("ps", b)])
            P.op("act", lambda q: q.copy(out=KT[:, :, col0:col0 + w], in_=psb[b][:, 0:256].rearrange("p (c k) -> p c k", c=2)[:, :, 0:w]),
                 writes=[("ps", b), "KT"])
            P.op("dve", lambda q: q.tensor_copy(out=krT[:, col0:col0 + w], in_=psb[b][0:64, 256:256 + w]), writes=[("ps", b), "krT"])

        stt2 = S("stt2", [128, 24], F32)
        P.op("pool", lambda q: q.memset(stt2[:, 0:1], 30000.0 * ATTN_SCALE), writes=["nm_init"])
        Oaccs = [(csfO[:, 0:257], "csf"), (rstd[:, 0:257], "rstd")]
        Pbs = [Pb[:, 0, :], Pb[:, 1, :], wst[0][:, :].bitcast(BF16)]
        Pbn = [("Pb", 0), ("Pb", 1), "wst0"]

        uctr = [0]

        def attn_stream(units, hook=None):
            steps = []
            for ui, U in enumerate(units):
                ng = len(U["groups"])
                U["_k"] = uctr[0] % 4
                U["_bO"] = 6 + uctr[0] % 2
                uctr[0] += 1
                for gi, G in enumerate(U["groups"]):
                    steps.append((ui, gi, ng, U, G))
            N = len(steps)
            st = [dict() for _ in range(N)]

            def col(base, k):
                return stt2[:, base + k:base + k + 1]

            def stA(i):
                ui, gi, ng, U, G = steps[i]
                w = G["w"]
                bS = nbank()
                st[i]["bS"] = bS

                def mmS(q):
                    q.matmul(ps[bS][:, 0:w], lhsT=U["qa0"], rhs=G["kt0"], start=True, stop=False)
                    q.matmul(ps[bS][:, 0:w], lhsT=U["qa1"], rhs=G["kt1"], start=False, stop=False)
                    r = q.matmul(ps[bS][:, 0:w], lhsT=U["qr"], rhs=G["krt"], start=False, stop=not G["diag"])
                    if G["diag"]:
                        r = q.matmul(ps[bS][:, 0:w], lhsT=identb[:], rhs=maskb[:, 0:w], start=False, stop=True)
                    return r
                P.op("pe", mmS, reads=["qaT", "qrope", "tmpT", "KT", "krT", "maskb", "identb"], writes=[("ps", bS)])

            def stB(i):
                ui, gi, ng, U, G = steps[i]
                w = G["w"]
                bS = st[i]["bS"]
                k = U["_k"]
                ib = i % 3
                if gi == 0:
                    P.op("dve", lambda q: q.reduce_max(out=col(6, k), in_=ps[bS][:, 0:w], axis=AX.X), writes=[("ps", bS), ("gmax", k)])
                    P.op("dve", lambda q: q.tensor_scalar(out=col(2, k), in0=col(6, k), scalar1=-ATTN_SCALE, scalar2=None, op0=ALU.mult),
                         reads=[("gmax", k)], writes=[("nm", k)])
                P.op("act", lambda q: q.activation(out=Pbs[ib][:, 0:w], in_=ps[bS][:, 0:w], func=AF.Exp, bias=col(2, k), scale=ATTN_SCALE),
                     reads=[("nm", k)], writes=[("ps", bS), Pbn[ib]])

            def stC(i):
                ui, gi, ng, U, G = steps[i]
                w = G["w"]
                bT = nbank()
                nj = (w + 127) // 128
                ib = i % 3
                it = i % 2

                def trP(q):
                    r = None
                    for j in range(nj):
                        wj = min(128, w - j * 128)
                        r = q.transpose(psb[bT][0:wj, j * 128:(j + 1) * 128], Pbs[ib][:, j * 128:j * 128 + wj], identb[:])
                    return r
                P.op("pe", trP, reads=[Pbn[ib], "identb"], writes=[("ps", bT)])
                kp = min(128, w)
                if i % 3 == 0:
                    P.op("act", lambda q: q.copy(out=PTs[it][0:kp, 0:nj * 128], in_=psb[bT][0:kp, 0:nj * 128]),
                         writes=[("ps", bT), ("PTs", it)])
                else:
                    P.op("dve", lambda q: q.tensor_copy(out=PTs[it][0:kp, 0:nj * 128], in_=psb[bT][0:kp, 0:nj * 128]),
                         writes=[("ps", bT), ("PTs", it)])

            def stD(i):
                ui, gi, ng, U, G = steps[i]
                bO = U["_bO"]
                it = i % 2

                def mmO(q):
                    r = None
                    nv = len(G["v"])
                    for j, (vap, K) in enumerate(G["v"]):
                        r = q.matmul(ps[bO][:, 0:257], lhsT=PTs[it][0:K, j * 128:(j + 1) * 128], rhs=vap,
                                     start=(gi == 0 and j == 0), stop=(gi == ng - 1 and j == nv - 1))
                    return r
                P.op("pe", mmO, reads=[("PTs", it), "V"], writes=[("ps", bO)])
                if gi == ng - 1:
                    P.op("dve", lambda q: q.reciprocal(out=stt2[:, 16:17], in_=ps[bO][:, 256:257]), writes=[("ps", bO), "rl"])
                    P.op("act", lambda q: q.activation(out=U["out_ap"], in_=ps[bO][:, 0:256], func=AF.Copy, scale=stt2[:, 16:17]),
                         reads=["rl"], writes=[("ps", bO), "olat"])
                    if U.get("post") is not None:
                        U["post"]()

            for i in range(N + 3):
                if hook is not None and i == max(0, (2 * N) // 3):
                    hook()
                if i < N:
                    stA(i)
                    stB(i)
                if 0 <= i - 2 < N:
                    stC(i - 2)
                if 0 <= i - 3 < N:
                    stD(i - 3)

        def odd_layer(o):
            layer = 2 * o + 1
            wio = WV("wio%d" % o)
            wqu = WV("wqu%d" % o)
            wku = WV("wku%d" % o)
            wov = WV("wov%d" % o)
            ld(kvn_bc, kv_norm[o:o + 1, :].broadcast_to([128, 256]), "kvn_bc")
            P.op("pool", lambda q: q.memset(Vt[:, :, 256:258], 1.0), writes=["V"])
            for h in range(8):
                wv, wn = wload(wku[:, :, h * 256:(h + 1) * 256], 2, 256)
                b = nbank()

                def trk(q, b=b, wv=wv):
                    q.transpose(psb[b][:, 0:128], wv[:, 0, 0:128], identb[:])
                    return q.transpose(psb[b][:, 128:256], wv[:, 1, 0:128], identb[:])
                P.op("pe", trk, reads=[wn, "identb"], writes=[("ps", b)])
                P.op("act", lambda q, b=b, h=h: q.copy(out=WkvT[:, h, :], in_=psb[b][:, 0:256]), writes=[("ps", b), "WkvT"])
                P.op("pool", lambda q, wv=wv, h=h: q.tensor_copy(out=Wv[:, :, h, :], in_=wv[:, :, 128:256]), reads=[wn], writes=["Wv"])
            units = [(u * 128, 128, u) for u in range(NBLK)] + [(NTOK, 64, NBLK)]
            for (c0, n, u) in units:
                sample = (u == NBLK)
                xres = [("xT", c0 // 256)]
                pre_norm(layer, xT[:, :, c0:c0 + n], n, xres)
                P.dma("sp", lambda q, u=u: [q.dma_start(out=cst[:], in_=cs_tm[:, u * 64:(u + 1) * 64])], writes=["cst"], chan="cst")
                b = nbank()
                for ci, (col, w) in enumerate([(384, 128), (512, 128), (640, 64)]):
                    wv, wn = wload(wio[:, :, col:col + w], KD, w)

                    def mmk(q, b=b, wv=wv, ci=ci, w=w, n=n):
                        r = None
                        for k in range(KD):
                            r = q.matmul(ps[b][0:n, ci * 128:ci * 128 + w], lhsT=hT[:, k, 0:n], rhs=wv[:, k, :], start=(k == 0), stop=(k == KD - 1))
                        return r
                    P.op("pe", mmk, reads=[wn, "hT"], writes=[("ps", b)])
                kvc = tmp2d[0:n, 0:320]
                P.op("act", lambda q, b=b, n=n, kvc=kvc: q.copy(out=kvc, in_=ps[b][0:n, 0:320]), writes=[("ps", b), "tmpT"])
                P.op("act", lambda q, n=n: q.activation(out=tmp2d[0:n, 512:768], in_=tmp2d[0:n, 0:256], func=AF.Square, accum_out=stt[0:n, 8:9]),
                     reads=["tmpT"], writes=["tmpT2", "ss"])
                P.op("act", lambda q, n=n: q.activation(out=stt[0:n, 9:10], in_=stt[0:n, 8:9], func=AF.Sqrt, bias=epsc[0:n, 0:1], scale=1.0 / 256.0),
                     reads=["ss", "epsc"], writes=["ss2"])
                P.op("dve", lambda q, n=n: q.reciprocal(out=stt[0:n, 10:11], in_=stt[0:n, 9:10]), reads=["ss2"], writes=["ss3"])
                P.op("dve", lambda q, n=n: q.scalar_tensor_tensor(out=yT2d[0:n, 0:256], in0=tmp2d[0:n, 0:256], scalar=stt[0:n, 10:11],
                                                                  in1=kvn_bc[0:n, :], op0=ALU.mult, op1=ALU.mult),
                     reads=["tmpT", "ss3", "kvn_bc"], writes=["yT"])
                x1, x2 = tmp2d[0:n, 256:288], tmp2d[0:n, 288:320]
                cs_, sn_ = cst[0:n, 0:32], cst[0:n, 32:64]
                t1, t2, t3, t4 = (tmp2d[0:n, 1024 + 32 * j:1056 + 32 * j] for j in range(4))
                P.op("pool", lambda q, x1=x1, cs_=cs_, t1=t1: q.tensor_tensor(out=t1, in0=x1, in1=cs_, op=ALU.mult), reads=["tmpT", "cst"], writes=["t1"])
                P.op("dve", lambda q, x2=x2, sn_=sn_, t2=t2: q.tensor_tensor(out=t2, in0=x2, in1=sn_, op=ALU.mult), reads=["tmpT", "cst"], writes=["t2"])
                P.op("pool", lambda q, x2=x2, cs_=cs_, t3=t3: q.tensor_tensor(out=t3, in0=x2, in1=cs_, op=ALU.mult), reads=["tmpT", "cst"], writes=["t3"])
                P.op("dve", lambda q, x1=x1, sn_=sn_, t4=t4: q.tensor_tensor(out=t4, in0=x1, in1=sn_, op=ALU.mult), reads=["tmpT", "cst"], writes=["t4"])
                P.op("pool", lambda q, n=n, t1=t1, t2=t2: q.tensor_tensor(out=yT2d[0:n, 256:288], in0=t1, in1=t2, op=ALU.subtract),
                     reads=["t1", "t2"], writes=["yTr1"])
                P.op("dve", lambda q, n=n, t3=t3, t4=t4: q.tensor_tensor(out=yT2d[0:n, 288:320], in0=t3, in1=t4, op=ALU.add),
                     reads=["t3", "t4"], writes=["yTr2"])
                P.op("pool", lambda q, n=n: q.tensor_copy(out=sqb2d[0:n, 0:320], in_=yT2d[0:n, 0:320]), reads=["yT", "yTr1", "yTr2"], writes=["sqb"])
                if sample:
                    P.dma("sp", lambda q: [q.dma_start(out=o_ckv_s[o], in_=yT2d[0:64, 0:256]), q.dma_start(out=o_kr_s[o], in_=yT2d[0:64, 256:320])],
                          reads=["yT", "yTr1", "yTr2"], chan="kvout", n=2, is_out=True)
                    P.dma("sp", lambda q: [q.dma_start(out=bks[o].ap()[:, :], in_=sqb2d[0:64, 0:320])], reads=["sqb"], writes=[("bks", o)], chan="bks")
                else:
                    P.dma("sp", lambda q, c0=c0: [q.dma_start(out=o_ckv_p[o, c0:c0 + 128, :], in_=yT2d[:, 0:256]),
                                                 q.dma_start(out=o_kr_p[o, c0:c0 + 128, :], in_=yT2d[:, 256:320])],
                          reads=["yT", "yTr1", "yTr2"], chan="kvout", n=2, is_out=True)
                    P.dma("sp", lambda q, u=u: [q.dma_start(out=bk_in[o][u // BPS].ap()[(u % BPS) * 128:(u % BPS) * 128 + 128, :], in_=sqb2d[:, 0:320])],
                          reads=["sqb"], writes=[("bk_in", o, u)], chan="bkin")
            for sp in range(NSPL):
                P.collective(lambda q, sp=sp: q.collective_compute("AllGather", ALU.bypass, replica_groups=[[0, 1, 2, 3], [4, 5, 6, 7]],
                                                                   ins=[bk_in[o][sp].ap().opt()], outs=[bk_out[o][sp].ap().opt()]),
                             reads=[("bk_in", o, u) for u in range(sp * BPS, (sp + 1) * BPS)], writes=[("bk_out", o, sp)], chan=("ccK", o, sp))
            if layer + 1 < NLAYERS:
                relay_layer(layer + 1)
            V4 = Vt.rearrange("p (m r) c -> p m r c", r=4)
            krtm = tmpb[:, 0:NK // 2].rearrange("p (m r c) -> p m r c", r=4, c=64)
            for sp in range(NSPL):
                bko = bk_out[o][sp].ap()
                ms = slice(sp * BPS, (sp + 1) * BPS)
                for r in range(4):
                    P.dma("sp", lambda q, r=r, bko=bko, ms=ms: [q.dma_start(out=V4[:, ms, r, 0:256], in_=bko[r * TPS:(r + 1) * TPS, 0:256].rearrange("(m p) c -> p m c", p=128))],
                          reads=[("bk_out", o, sp)], writes=["V"], chan="Vld")
                    P.dma("sp", lambda q, r=r, bko=bko, ms=ms: [q.dma_start(out=krtm[:, ms, r, :], in_=bko[r * TPS:(r + 1) * TPS, 256:320].rearrange("(m p) c -> p m c", p=128))],
                          reads=[("bk_out", o, sp)], writes=["tmpT"], chan="krld")
            krtm3 = tmpb[:, 0:NK // 2].rearrange("p (k c) -> p k c", c=64)
            for kb in range(4 * NBLK):
                kt_transposes(kb, kb, 128, krtm3[:, kb, :], kb * 128, 128)

            qrfs = [yT[0:64, 5, 128:256], yT[0:64, 3, 128:256]]
            qt1s = [yT[0:64, 6, 128:256], yT[0:64, 4, 128:256]]

            def q_path(c0, n, do_norm=True):
                xres = [("xT", c0 // 256)]
                if do_norm:
                    pre_norm(layer, xT[:, :, c0:c0 + n], n, xres)
                P.dma("sp", lambda q: [q.dma_start(out=csf[:, :, 0:n], in_=cs_fm.rearrange("p (a t) -> p a t", a=2)[:, :, c0:c0 + n])],
                      writes=["csf"], chan="csf")
                for kc in range(3):
                    wv, wn = wload(wio[:, :, kc * 128:(kc + 1) * 128], KD, 128)
                    b = nbank()
                    P.op("pe", mm_fm(b, wv, 0, 128, hT, n), reads=[wn, "hT"], writes=[("ps", b)])
                    evac_copy(qc[:, kc, 0:n], ps[b][:, 0:n], b, ["qc"])
                rms_stats(qc[:, :, 0:n], n, ["qc"], kdim=3, scale=1024.0 / 384.0)
                for kc in range(3):
                    P.op("dve", lambda q, kc=kc: q.scalar_tensor_tensor(out=qcn[:, kc, 0:n], in0=qc[:, kc, 0:n],
                                                                        scalar=vec[:, 80 + o * 3 + kc:80 + o * 3 + kc + 1], in1=rstd[:, 0:n],
                                                                        op0=ALU.mult, op1=ALU.mult), reads=["qc", "vec", "rstd"], writes=["qcn"])
                for ec in range(8):
                    wv, wn = wload(wio[:, :, 704 + ec * 128:704 + (ec + 1) * 128], KD, 128)
                    b = nbank()
                    P.op("pe", mm_fm(b, wv, 0, 128, hT, n), reads=[wn, "hT"], writes=[("ps", b)])
                    P.op("act", lambda q, b=b, ec=ec: q.activation(out=gateT[:, ec, 0:n], in_=ps[b][:, 0:n], func=AF.Silu),
                         writes=[("ps", b), "gateT"])

                def stage_a(h):
                    j = h % 2
                    wv, wn = wload(wqu[:, :, h * 192:(h + 1) * 192], 3, 192)
                    b1 = nbank()
                    P.op("pe", mm_fm(b1, wv, 0, 128, qcn, n, kdim=3), reads=[wn, "qcn"], writes=[("ps", b1)])
                    evac_copy(qn[:, h, 0:n], ps[b1][:, 0:n], b1, [("qn", h)])
                    b2 = nbank()
                    P.op("pe", mm_fm(b2, wv, 128, 64, qcn, n, kdim=3), reads=[wn, "qcn"], writes=[("ps", b2)])
                    P.op("act", lambda q: q.copy(out=qrfs[j][:, 0:n], in_=ps[b2][0:64, 0:n]), writes=[("ps", b2), ("qrf", j)])

                def stage_b(h):
                    j = h % 2
                    jw = {"writes": ["qrope"]} if h == 0 else {"join": ["qrope"]}
                    jq = {"wres": ["qaT"]} if h == 0 else {"wres": [], "join": ["qaT"]}
                    b3 = nbank()
                    P.op("pe", lambda q: q.matmul(ps[b3][0:64, 0:n], lhsT=permf[:, :], rhs=qrfs[j][:, 0:n], start=True, stop=True),
                         reads=["permf", ("qrf", j)], writes=[("ps", b3)])
                    P.op("dve", lambda q: q.tensor_tensor(out=qt1s[j][:, 0:n], in0=ps[b3][0:64, 0:n], in1=csf[:, 1, 0:n], op=ALU.mult),
                         reads=["csf"], writes=[("ps", b3), ("qt1", j)])
                    P.op("pool", lambda q: q.tensor_tensor(out=qt2[:, 0:n], in0=qrfs[j][:, 0:n], in1=csf[:, 0, 0:n], op=ALU.mult),
                         reads=["csf", ("qrf", j)], writes=["qt2"])
                    P.op("dve", lambda q: q.tensor_tensor(out=qrope[:, h, 0:n], in0=qt1s[j][:, 0:n], in1=qt2[:, 0:n], op=ALU.add),
                         reads=[("qt1", j), "qt2"], **jw)
                    b4 = nbank()

                    def mma(q):
                        q.matmul(ps[b4][:, 0:n], lhsT=WkvT[:, h, 0:128], rhs=qn[:, h, 0:n], start=True, stop=True)
                        return q.matmul(ps[b4][:, 128:128 + n], lhsT=WkvT[:, h, 128:256], rhs=qn[:, h, 0:n], start=True, stop=True)
                    P.op("pe", mma, reads=["WkvT", ("qn", h)], writes=[("ps", b4)])
                    evac_copy(qaT[:, :, h, 0:n], ps[b4][:, 0:256].rearrange("p (c t) -> p c t", c=2)[:, :, 0:n], b4, jq["wres"], join=jq.get("join", ()))

                for i in range(9):
                    if i < 8:
                        stage_a(i)
                    if i >= 1:
                        stage_b(i - 1)

            def head_out(h, n):
                jo = {"writes": ["oT"]} if h == 0 else {"join": ["oT"]}
                b = nbank()

                def tro(q):
                    q.transpose(psb[b][:, 0:128], olat[:, h, 0:128], identb[:])
                    return q.transpose(psb[b][:, 128:256], olat[:, h, 128:256], identb[:])
                P.op("pe", tro, reads=["olat", "identb"], writes=[("ps", b)])
                evac_copy(olT[:, :, h, :], psb[b][:, 0:256].rearrange("p (c q) -> p c q", c=2), b, [("olT", h)], join=["sqb"])
                b2 = nbank()

                def mmo(q):
                    q.matmul(ps[b2][:, 0:n], lhsT=Wv[:, 0, h, :], rhs=olT[:, 0, h, 0:n], start=True, stop=False)
                    return q.matmul(ps[b2][:, 0:n], lhsT=Wv[:, 1, h, :], rhs=olT[:, 1, h, 0:n], start=False, stop=True)
                P.op("pe", mmo, reads=["Wv", ("olT", h), "sqb"], writes=[("ps", b2)])
                P.op("dve", lambda q: q.tensor_tensor(out=oT[:, h, 0:n], in0=ps[b2][:, 0:n], in1=gateT[:, h, 0:n], op=ALU.mult),
                     reads=["gateT"], writes=[("ps", b2)] + (["oT"] if h == 0 else []), join=([] if h == 0 else ["oT"]))

            def out_path(c0, n, heads_done=False):
                for h in range(0 if heads_done else 8):
                    b = nbank()

                    def mmo(q, b=b, h=h):
                        q.matmul(ps[b][:, 0:n], lhsT=Wv[:, 0, h, :], rhs=olT[:, 0, h, 0:n], start=True, stop=False)
                        return q.matmul(ps[b][:, 0:n], lhsT=Wv[:, 1, h, :], rhs=olT[:, 1, h, 0:n], start=False, stop=True)
                    P.op("pe", mmo, reads=["Wv", "sqb"], writes=[("ps", b)])
                    P.op("dve", lambda q, b=b, h=h: q.tensor_tensor(out=oT[:, h, 0:n], in0=ps[b][:, 0:n], in1=gateT[:, h, 0:n], op=ALU.mult),
                         reads=["gateT"], writes=[("ps", b), "oT"])
                for dc in range(8):
                    wv, wn = wload(wov[:, :, dc * 128:(dc + 1) * 128], KD, 128)
                    b = nbank()
                    P.op("pe", mm_fm(b, wv, 0, 128, oT, n), reads=[wn, "oT"], writes=[("ps", b)])
                    evac_copy(yT[:, dc, 0:n], ps[b][:, 0:n], b, ["yT"])
                post_norm_update(layer, c0, n)

            for blk in range(NBLK):
                c0 = blk * 128
                q_path(c0, 128, do_norm=(blk == 0))
                units_ = []
                for h in range(8):
                    groups = []
                    for g in range(blk + 1):
                        ks = slice(g * 512, (g + 1) * 512)
                        groups.append(dict(kt0=KT[:, 0, ks], kt1=KT[:, 1, ks], krt=krT[:, ks], w=512,
                                           v=[(Vt[:, g * 4 + j, 0:257], 128) for j in range(4)], diag=(g == blk)))
                    units_.append(dict(qa0=qaT[:, 0, h, :], qa1=qaT[:, 1, h, :], qr=qrope[:, h, :], groups=groups, out_ap=olat[:, h, :],
                                       post=(lambda h=h: head_out(h, 128))))
                nc0, nn = ((blk + 1) * 128, 128) if blk + 1 < NBLK else (NTOK, 64)
                attn_stream(units_, hook=(lambda nc0=nc0, nn=nn: pre_norm_lite(layer, xT[:, :, nc0:nc0 + nn], nn, [("xT", nc0 // 256)])))
                out_path(c0, 128, heads_done=True)
            q_path(NTOK, 64, do_norm=False)
            for s_ in range(2):
                for kb in range(8):
                    stage_cast(cckv[o, s_, kb * 128:(kb + 1) * 128, :], Vt[:, kb, 0:256], 256, ["V"])
                    stage_cast(ckr[o, s_, kb * 128:(kb + 1) * 128, :], krtm_s[:, kb, :], 64, ["tmpT"])
                P.dma("sp", lambda q, s_=s_: [q.dma_start(out=Vt[0:32, 8, 0:256], in_=bks[o].ap()[s_ * 32:(s_ + 1) * 32, 0:256]),
                                             q.dma_start(out=krtm_s[0:32, 8, :], in_=bks[o].ap()[s_ * 32:(s_ + 1) * 32, 256:320])],
                      reads=[("bks", o)], writes=["V", "tmpT"], chan="bksld", n=2)
                for kb in range(8):
                    kt_transposes(kb, kb, 128, krtm_s[:, kb, :], kb * 128, 128)
                kt_transposes(8, 8, 32, krtm_s[0:32, 8, :], 1024, 32)
                units_ = []
                for hq in range(2):
                    ts = slice(s_ * 32, (s_ + 1) * 32)
                    hs = slice(hq * 4, hq * 4 + 4)
                    groups = []
                    for g in range(2):
                        ks = slice(g * 512, (g + 1) * 512)
                        groups.append(dict(kt0=KT[:, 0, ks], kt1=KT[:, 1, ks], krt=krT[:, ks], w=512,
                                           v=[(Vt[:, g * 4 + j, 0:257], 128) for j in range(4)], diag=False))
                    groups.append(dict(kt0=KT[:, 0, 1024:1056], kt1=KT[:, 1, 1024:1056], krt=krT[:, 1024:1056], w=32,
                                       v=[(Vt[0:32, 8, 0:257], 32)], diag=False))
                    P.op("dve", lambda q, hs=hs, ts=ts: q.tensor_copy(out=qas.rearrange("p c (h t) -> p c h t", h=4), in_=qaT[:, :, hs, ts]),
                         reads=["qaT"], writes=["tmpT"])
                    P.op("pool", lambda q, hs=hs, ts=ts: q.tensor_copy(out=qrs.rearrange("p (h t) -> p h t", h=4), in_=qrope[:, hs, ts]),
                         reads=["qrope"], writes=["tmpT"])

                    def post(hs=hs, ts=ts):
                        b = nbank()

                        def tro2(q):
                            q.transpose(psb[b][:, 0:128], olat[:, 0, 0:128], identb[:])
                            return q.transpose(psb[b][:, 128:256], olat[:, 0, 128:256], identb[:])
                        P.op("pe", tro2, reads=["olat", "identb"], writes=[("ps", b)])
                        evac_copy(olT[:, :, hs, ts], psb[b][:, 0:256].rearrange("p (c h t) -> p c h t", c=2, h=4), b, ["sqb"])
                    attn_stream([dict(qa0=qas[:, 0, :], qa1=qas[:, 1, :], qr=qrs, groups=groups, out_ap=olat[:, 0, :], post=post)])
            out_path(NTOK, 64)

        for layer in range(NLAYERS):
            conv_layer(layer)
        relay_layer(0)
        for layer in range(NLAYERS):
            if layer % 2 == 0:
                e = layer // 2
                ctx = even_layer(e)
                for t in range(NT + 1):
                    if t == 1 and layer + 1 < NLAYERS:
                        relay_layer(layer + 1)
                    even_tile(e, ctx, t)
            else:
                P.barrier()
                odd_layer(layer // 2)
                P.barrier()

        for b in range(NBLK):
            store_x(y_p[b * 128:(b + 1) * 128, :], b * 128, 128)
        store_x(y_s[:, :], NTOK, NS)
        P.finish()
        P.replay()
    return nc


def _host_consts(c, NBLK):
    NTOK = NBLK * 128
    TOT = NTOK + 64
    r = c % 4
    mask = np.zeros((128, 512), np.float32)
    qi = np.arange(128)[:, None]
    for i in range(4):
        blk = mask[:, i * 128:(i + 1) * 128]
        if i > r:
            blk[:] = NEG
        elif i == r:
            kj = np.arange(128)[None, :]
            blk[:] = np.where((kj // 64) <= (qi // 64), 0.0, NEG)
    selw = np.zeros((128, 8), np.float32)
    if r > 0:
        selw[:, r - 1] = 1.0
    else:
        selw[:, 4] = 1.0
    rc = np.zeros((128, 4, 16), np.float32)
    for g in range(4):
        w = 2 ** (g + 1)
        for p in range(16):
            rc[:, g, p] = (1.0 / min(p + 1, w)) if r == 0 else 1.0 / w
    half = 32
    freqs = (10000.0 ** (-np.arange(half, dtype=np.float32) / half)).astype(np.float32)
    pos = np.zeros(TOT, np.float32)
    for m in range(NBLK):
        pos[m * 128:(m + 1) * 128] = (4 * m + r) * 128 + np.arange(128)
    pos[NTOK:NTOK + 32] = 1024 + np.arange(32)
    pos[NTOK + 32:] = 1024 + np.arange(32)
    ang = pos[:, None].astype(np.float32) * freqs[None, :]
    cos, sin = np.cos(ang).astype(np.float32), np.sin(ang).astype(np.float32)
    cs_tm = np.zeros((128, NBLK + 1, 64), np.float32)
    for m in range(NBLK):
        cs_tm[:, m, 0:32] = cos[m * 128:(m + 1) * 128]
        cs_tm[:, m, 32:64] = sin[m * 128:(m + 1) * 128]
    cs_tm[0:64, NBLK, 0:32] = cos[NTOK:]
    cs_tm[0:64, NBLK, 32:64] = sin[NTOK:]
    cs_fm = np.zeros((64, 2, TOT), np.float32)
    cs_fm[0:32, 0] = cos.T
    cs_fm[32:64, 0] = cos.T
    cs_fm[0:32, 1] = -sin.T
    cs_fm[32:64, 1] = sin.T
    return dict(mask=mask, selw=selw, rcnt=rc.reshape(128, 64), cs_tm=cs_tm.reshape(128, -1), cs_fm=cs_fm.reshape(64, -1),
                ident=np.eye(128, dtype=np.float32))


def _fm(v):
    return np.ascontiguousarray(np.asarray(v, np.float32).reshape(-1, 128).T)


_NC_CACHE = {}


def kernel(x_prompt, x_sample, cache_pool, cache_ckv, cache_krope, norm_pre, norm_post,
           w_in_even, w_pool, pool_scale, sgu_ln_g, sgu_ln_b, w_spatial, b_spatial, w_out_even,
           w_in_odd, q_norm, kv_norm, w_q_up, w_kv_up, w_o, _nlayers=4):
    f = lambda a: np.ascontiguousarray(np.asarray(a, dtype=np.float32))
    x_prompt = f(x_prompt)
    x_sample = f(x_sample)
    B, T, _ = x_prompt.shape
    NBLK = T // 512
    NTOK = NBLK * 128
    vecs = np.zeros((128, 96), np.float32)
    for l in range(4):
        vecs[:, l * 8:(l + 1) * 8] = _fm(f(norm_pre)[l])
        vecs[:, 32 + l * 8:32 + (l + 1) * 8] = _fm(f(norm_post)[l])
    for e in range(2):
        vecs[:, 64 + e * 8:64 + (e + 1) * 8] = _fm(f(pool_scale)[e])
        vecs[:, 80 + e * 3:80 + (e + 1) * 3] = _fm(f(q_norm)[e])
    shared = dict(w_in_even=f(w_in_even), w_pool=f(w_pool), ln_g=f(sgu_ln_g), ln_b=f(sgu_ln_b), w_sp=f(w_spatial),
                  b_sp=f(b_spatial), w_out_even=f(w_out_even), w_in_odd=f(w_in_odd), kv_norm=f(kv_norm),
                  w_q_up=f(w_q_up).reshape(2, 384, 8 * 192), w_kv_up=f(w_kv_up).reshape(2, 256, 8 * 256), w_o=f(w_o), vecs=vecs)
    cache_pool, cache_ckv, cache_krope = f(cache_pool), f(cache_ckv), f(cache_krope)
    in_maps = []
    for c in range(8):
        b, r = c // 4, c % 4
        xb = x_prompt[b].reshape(NBLK, 4, 128, D)[:, r].reshape(NTOK, D)
        m = dict(shared)
        m.update(_host_consts(c, NBLK))
        m["xp"] = np.ascontiguousarray(xb)
        m["xs"] = np.ascontiguousarray(x_sample[2 * c:2 * c + 2].reshape(64, D))
        m["cpool"] = np.ascontiguousarray(cache_pool[:, 2 * c:2 * c + 2])
        m["cckv"] = np.ascontiguousarray(cache_ckv[:, 2 * c:2 * c + 2])
        m["ckr"] = np.ascontiguousarray(cache_krope[:, 2 * c:2 * c + 2])
        in_maps.append(m)
    key = (NBLK, _nlayers)
    if key not in _NC_CACHE:
        _NC_CACHE[key] = build(NBLK, _nlayers)
    nc = _NC_CACHE[key]
    res = run_bass_kernel_spmd(nc, in_maps, core_ids=list(range(8))).results

    def unshard(name, width):
        out = np.zeros((B, NBLK, 4, 128, width), np.float32)
        for c in range(8):
            out[c // 4, :, c % 4] = res[c][name].reshape(NBLK, 128, width)
        return out.reshape(B, T, width)

    def unshard_l(name, width):
        out = np.zeros((2, B, NBLK, 4, 128, width), np.float32)
        for c in range(8):
            out[:, c // 4, :, c % 4] = res[c][name].reshape(2, NBLK, 128, width)
        return out.reshape(2, B, T, width)

    y_prompt = unshard("y_p", D)
    y_sample = np.concatenate([res[c]["y_s"].reshape(2, 32, D) for c in range(8)], 0)
    pool_p = np.stack([res[3]["o_pool_p"], res[7]["o_pool_p"]], 1)
    pool_s = np.concatenate([res[c]["o_pool_s"] for c in range(8)], 1)
    sgu_s = np.concatenate([res[c]["o_sgu_s"].reshape(2, 2, 32, D) for c in range(8)], 1)
    ckv_p = unshard_l("o_ckv_p", 256)
    kr_p = unshard_l("o_kr_p", 64)
    ckv_s = np.concatenate([res[c]["o_ckv_s"].reshape(2, 2, 32, 256) for c in range(8)], 1)
    kr_s = np.concatenate([res[c]["o_kr_s"].reshape(2, 2, 32, 64) for c in range(8)], 1)
    return (y_prompt, y_sample, pool_p, pool_s, sgu_s, ckv_p, kr_p, ckv_s, kr_s)
```

```python
import contextlib
import numpy as np
import concourse.bass as bass
import concourse.mybir as mybir
from concourse.bass_utils import run_bass_kernel_spmd

F32 = mybir.dt.float32
BF16 = mybir.dt.bfloat16
AF = mybir.ActivationFunctionType
ALU = mybir.AluOpType
AX = mybir.AxisListType

D = 1024
KD = 8
EPS = 1e-6
ATTN_SCALE = 192.0 ** -0.5
NEG = -30000.0


class Op:
    __slots__ = ("eng", "fn", "waits", "signal", "count", "kind", "chan", "chan_val")

    def __init__(self, eng, fn, kind):
        self.eng = eng
        self.fn = fn
        self.kind = kind
        self.waits = []
        self.signal = False
        self.count = None
        self.chan = None
        self.chan_val = None


class Prog:
    ENGS = ("pe", "act", "dve", "pool", "sp")

    def __init__(self, nc):
        self.nc = nc
        self.ops = {e: [] for e in self.ENGS}
        self.res = {}
        self.chan_tot = {}
        self.chan_sem = {}
        self.eng_sem = {}
        self.out_ops = []

    def _deps(self, op, reads, writes, join=()):
        deps = []
        for r in reads:
            st = self.res.get(r)
            if st is None:
                st = self.res[r] = [[], [], []]
            for wop in st[0]:
                deps.append((wop, "raw"))
        for w in list(writes) + list(join):
            st = self.res.get(w)
            if st is None:
                st = self.res[w] = [[], [], []]
            if w not in join:
                for wop in st[0]:
                    deps.append((wop, "waw"))
            else:
                for rd in st[2]:
                    deps.append((rd, "war"))
            for rd in st[1]:
                deps.append((rd, "war"))
        seen = set()
        for p, kind in deps:
            if p is op or id(p) in seen:
                continue
            if p.kind == "c" and p.eng == op.eng and op.kind == "c":
                if op.eng == "pe" or kind == "war":
                    continue
            seen.add(id(p))
            op.waits.append(p)
            if p.kind == "c":
                p.signal = True
        for r in reads:
            self.res[r][1].append(op)
        for w in writes:
            self.res[w] = [[op], [], self.res[w][1]]
        for w in join:
            self.res[w][0].append(op)

    def op(self, eng, fn, reads=(), writes=(), join=()):
        o = Op(eng, fn, "c")
        self._deps(o, reads, writes, join)
        self.ops[eng].append(o)
        return o

    def dma(self, eng, fn, reads=(), writes=(), chan=None, n=1, is_out=False):
        o = Op(eng, fn, "d")
        self._deps(o, reads, writes)
        tot = self.chan_tot.get(chan, 0) + 16 * n
        self.chan_tot[chan] = tot
        o.chan = chan
        o.chan_val = tot
        self.ops[eng].append(o)
        if is_out:
            self.out_ops.append(o)
        return o

    def collective(self, fn, reads=(), writes=(), chan=None):
        o = Op("pool", fn, "x")
        self._deps(o, reads, writes)
        assert chan not in self.chan_tot
        self.chan_tot[chan] = 1
        o.chan = chan
        o.chan_val = 1
        self.ops["pool"].append(o)
        return o

    def barrier(self):
        o = Op("sp", lambda e: e.nop(), "c")
        for e in self.ENGS:
            if e == "sp":
                continue
            for p in reversed(self.ops[e]):
                if p.kind == "c":
                    p.signal = True
                    o.waits.append(p)
                    break
        lastd = {}
        for e in self.ENGS:
            for p in self.ops[e]:
                if p.kind != "c":
                    lastd[p.chan] = p
        o.waits.extend(lastd.values())
        o.signal = True
        self.ops["sp"].append(o)
        for e in self.ENGS:
            if e == "sp":
                continue
            o2 = Op(e, lambda q: q.nop(), "c")
            o2.waits.append(o)
            self.ops[e].append(o2)
        self.res = {}

    def finish(self):
        o = Op("sp", lambda e: e.nop(), "c")
        for p in self.out_ops:
            o.waits.append(p)
        for e in self.ENGS:
            if e == "sp":
                continue
            for p in reversed(self.ops[e]):
                if p.kind == "c":
                    p.signal = True
                    o.waits.append(p)
                    break
        self.ops["sp"].append(o)

    def replay(self):
        nc = self.nc
        engobj = {"pe": nc.tensor, "act": nc.scalar, "dve": nc.vector, "pool": nc.gpsimd, "sp": nc.sync}
        EPOCH = 6000
        for i, c in enumerate(self.chan_tot):
            self.chan_sem[c] = nc.alloc_semaphore(name="c%d" % i)
        for e in self.ENGS:
            cnt = 0
            for o in self.ops[e]:
                if o.kind == "c" and o.signal:
                    ep = cnt // EPOCH
                    if (e, ep) not in self.eng_sem:
                        self.eng_sem[(e, ep)] = nc.alloc_semaphore(name="s_%s%d" % (e, ep))
                    o.count = (ep, cnt % EPOCH + 1)
                    cnt += 1
        prog = self

        def run(e):
            eng = engobj[e]
            seen = {}
            for o in prog.ops[e]:
                for p in o.waits:
                    if p.kind == "c":
                        sem, val = prog.eng_sem[(p.eng, p.count[0])], p.count[1]
                    else:
                        sem, val = prog.chan_sem[p.chan], p.chan_val
                    k = id(sem)
                    if seen.get(k, 0) >= val:
                        continue
                    seen[k] = val
                    eng.wait_ge(sem, val)
                r = o.fn(eng)
                if o.kind == "c":
                    if o.signal:
                        r.then_inc(prog.eng_sem[(e, o.count[0])], 1)
                elif o.kind == "d":
                    for ins in r:
                        ins.then_inc(prog.chan_sem[o.chan], 16)
                else:
                    r.then_inc(prog.chan_sem[o.chan])

        with nc.Block() as block:
            @block.tensor
            def _(t):
                run("pe")

            @block.scalar
            def _(t):
                run("act")

            @block.vector
            def _(t):
                run("dve")

            @block.gpsimd
            def _(t):
                run("pool")

            @block.sync
            def _(t):
                run("sp")


def build(NBLK, NLAYERS):
    NTOK = NBLK * 128
    NS = 64
    TOT = NTOK + NS
    NT = NBLK // 2
    nc = bass.Bass("TRN2", target_bir_lowering=False)

    def din(name, shape):
        return nc.dram_tensor(name, list(shape), F32, kind="ExternalInput").ap()

    def dout(name, shape):
        return nc.dram_tensor(name, list(shape), F32, kind="ExternalOutput").ap()

    xp = din("xp", [NTOK, D])
    xs = din("xs", [NS, D])
    cpool = din("cpool", [2, 2, 15, D])
    cckv = din("cckv", [2, 2, 1024, 256])
    ckr = din("ckr", [2, 2, 1024, 64])
    w_in_even = din("w_in_even", [2, D, 5120])
    w_pool = din("w_pool", [2, 4, 256, 256])
    ln_g = din("ln_g", [2, D])
    ln_b = din("ln_b", [2, D])
    w_sp = din("w_sp", [2, 4, 128, 128])
    b_sp = din("b_sp", [2, 4, 128])
    w_out_even = din("w_out_even", [2, 2048, D])
    w_in_odd = din("w_in_odd", [2, D, 1728])
    kv_norm = din("kv_norm", [2, 256])
    w_q_up = din("w_q_up", [2, 384, 8 * 192])
    w_kv_up = din("w_kv_up", [2, 256, 8 * 256])
    w_o = din("w_o", [2, D, D])
    vecs = din("vecs", [128, 96])
    ident_d = din("ident", [128, 128])
    mask_d = din("mask", [128, 512])
    selw_d = din("selw", [128, 8])
    rcnt_d = din("rcnt", [128, 4 * 16])
    cs_tm = din("cs_tm", [128, (NBLK + 1) * 64])
    cs_fm = din("cs_fm", [64, 2 * TOT])

    y_p = dout("y_p", [NTOK, D])
    y_s = dout("y_s", [NS, D])
    o_pool_p = dout("o_pool_p", [2, 15, D])
    o_pool_s = dout("o_pool_s", [2, 2, 15, D])
    o_sgu_s = dout("o_sgu_s", [2, NS, D])
    o_ckv_p = dout("o_ckv_p", [2, NTOK, 256])
    o_kr_p = dout("o_kr_p", [2, NTOK, 64])
    o_ckv_s = dout("o_ckv_s", [2, NS, 256])
    o_kr_s = dout("o_kr_s", [2, NS, 64])

    HW = 8 * NBLK * 16
    bh_in = [nc.dram_tensor("bh_in%d" % e, [128, HW], F32) for e in range(2)]
    bh_out = [nc.dram_tensor("bh_out%d" % e, [4 * 128, HW], F32) for e in range(2)]
    NSPL = max(1, NBLK // 8)
    BPS = NBLK // NSPL
    TPS = BPS * 128
    bk_in = [[nc.dram_tensor("bk_in%d_%d" % (o, sp), [TPS, 320], BF16) for sp in range(NSPL)] for o in range(2)]
    bk_out = [[nc.dram_tensor("bk_out%d_%d" % (o, sp), [4 * TPS, 320], BF16) for sp in range(NSPL)] for o in range(2)]

    P = Prog(nc)
    es = contextlib.ExitStack()

    def S(name, shape, dt):
        return es.enter_context(nc.sbuf_tensor("t_" + name, list(shape), dt))

    with es:
        ps = [es.enter_context(nc.psum_tensor("ps%d" % i, [128, 512], F32)) for i in range(8)]
        bankctr = [0]

        def nbank():
            b = bankctr[0] % 6
            bankctr[0] += 1
            return b

        xT = S("xT", [128, KD, TOT], F32)
        identf = S("identf", [128, 128], F32)
        identb = S("identb", [128, 128], BF16)
        onesb = S("onesb", [128, 128], BF16)
        epsc = S("epsc", [128, 1], F32)
        vec = S("vec", [128, 96], F32)
        selw = S("selw", [128, 8], F32)
        rcnt = S("rcnt", [128, 4, 16], F32)
        NWB = 4
        wst = [S("wst%d" % i, [128, 256], F32) for i in range(1)]
        wbf = [S("wbf%d" % i, [128, 8 * 128], BF16) for i in range(NWB)]
        hT = S("hT", [128, KD, 256], BF16)
        sqb = S("sqb", [128, KD, 256], BF16)
        rstd = S("rstd", [128, 258], F32)
        yT = S("yT", [128, KD, 256], F32)
        tmpT = S("tmpT", [128, KD, 256], F32)
        RB = max(81920, 24 * NTOK + 4 * NBLK * 516)
        R = S("R", [128, RB], mybir.dt.uint8)

        R2 = S("R2", [128, 23040], mybir.dt.uint8)

        def r2view(off, shape, dt, parts=128):
            n = 1
            for s_ in shape[1:]:
                n *= s_
            esz = 4 if dt == F32 else 2
            v = R2[0:parts, off:off + n * esz].bitcast(dt)
            if len(shape) == 3:
                v = v.rearrange("p (a b) -> p a b", a=shape[1])
            elif len(shape) == 4:
                v = v.rearrange("p (a b c) -> p a b c", a=shape[1], b=shape[2])
            return v

        def rview(off, shape, dt):
            n = 1
            for s in shape[1:]:
                n *= s
            esz = 4 if dt == F32 else 2
            v = R[:, off:off + n * esz].bitcast(dt)
            if len(shape) == 3:
                v = v.rearrange("p (a b) -> p a b", a=shape[1])
            elif len(shape) == 4:
                v = v.rearrange("p (a b c) -> p a b c", a=shape[1], b=shape[2])
            return v

        def ld(dst, src, name, eng="sp"):
            P.dma(eng, lambda q: [q.dma_start(out=dst, in_=src)], writes=[name], chan=name)

        ld(identf[:], ident_d[:, :], "identf")
        ld(vec[:], vecs[:, :], "vec")
        ld(selw[:], selw_d[:, :], "selw")
        ld(rcnt[:].rearrange("p a b -> p (a b)"), rcnt_d[:, :], "rcnt")
        P.op("dve", lambda q: q.tensor_copy(out=identb[:], in_=identf[:]), reads=["identf"], writes=["identb"])
        P.op("pool", lambda q: q.memset(onesb[:], 1.0 / 1024.0), writes=["onesb"])
        P.op("pool", lambda q: q.memset(epsc[:], EPS), writes=["epsc"])

        evq = [0]

        def evac_copy(out, in_, bank, wres, rres=(), scale=None, join=()):
            evq[0] += 1
            if evq[0] % 2 == 0:
                P.op("act", lambda q: q.activation(out=out, in_=in_, func=AF.Copy, scale=(1.0 if scale is None else scale)),
                     reads=list(rres), writes=[("ps", bank)] + list(wres), join=join)
            else:
                if scale is None:
                    P.op("dve", lambda q: q.tensor_copy(out=out, in_=in_), reads=list(rres), writes=[("ps", bank)] + list(wres), join=join)
                else:
                    P.op("dve", lambda q: q.tensor_scalar(out=out, in0=in_, scalar1=scale, scalar2=None, op0=ALU.mult),
                         reads=list(rres), writes=[("ps", bank)] + list(wres), join=join)

        wctr = [0]

        WBA = {}
        CH = {}
        rlctr = [0]

        class WV:
            def __init__(self, name):
                self.name = name

            def __getitem__(self, idx):
                _, ks, cs = idx
                return (self.name, ks.start or 0, cs.start)

        PIECE = {}

        def conv(name, src2d, rows, cols, piece=None):
            A = nc.dram_tensor("wbA_" + name, [rows, cols], BF16)
            WBA[name] = A
            piece = piece or cols
            PIECE[name] = piece
            for pi in range((cols + piece - 1) // piece):
                cs = slice(pi * piece, min(cols, (pi + 1) * piece))
                P.dma("pool", lambda q, cs=cs: [q.dma_start(out=A.ap()[:, cs], in_=src2d[:, cs])], writes=[("wbA", name, pi)],
                      chan=("cv", name, pi))

        def relay(name, k0, kdim, c0, cols):
            i = rlctr[0]
            rlctr[0] += 1
            B = nc.dram_tensor("wbB_%s_%d_%d" % (name, k0, c0), [128, kdim * cols], BF16)
            CH[(name, k0, c0)] = (B, kdim, cols)
            src = WBA[name].ap().rearrange("(k p) c -> p k c", p=128)[:, k0:k0 + kdim, c0:c0 + cols]
            slot = ("rlslot", i % 8)
            P.dma("sp", lambda q: [q.dma_start(out=B.ap().rearrange("p (k c) -> p k c", k=kdim), in_=src)],
                  reads=[("wbA", name, c0 // PIECE[name]), slot], writes=[("wbB", name, k0, c0), slot], chan=slot)

        def relay_layer(layer):
            if layer % 2 == 0:
                e = layer // 2
                for j in range(40):
                    relay("win%d" % e, 0, 8, j * 128, 128)
                for dc in range(8):
                    relay("wo%d" % e, 0, 8, dc * 128, 128)
                    relay("wo%d" % e, 8, 8, dc * 128, 128)
            else:
                o = layer // 2
                for h in range(8):
                    relay("wku%d" % o, 0, 2, h * 256, 256)
                for (c0, w) in [(384, 128), (512, 128), (640, 64), (0, 128), (128, 128), (256, 128)] + [(704 + ec * 128, 128) for ec in range(8)]:
                    relay("wio%d" % o, 0, 8, c0, w)
                for h in range(8):
                    relay("wqu%d" % o, 0, 3, h * 192, 192)
                for dc in range(8):
                    relay("wov%d" % o, 0, 8, dc * 128, 128)

        def conv_layer(layer):
            if layer % 2 == 0:
                e = layer // 2
                conv("win%d" % e, w_in_even[e], D, 5120, piece=1024)
                conv("wo%d" % e, w_out_even[e], 2048, D)
            else:
                o = layer // 2
                conv("wku%d" % o, w_kv_up[o], 256, 2048)
                conv("wio%d" % o, w_in_odd[o], D, 1728)
                conv("wqu%d" % o, w_q_up[o], 384, 1536)
                conv("wov%d" % o, w_o[o], D, D)

        def wload(ref, kdim, cols):
            i = wctr[0]
            wctr[0] += 1
            B, kd_, cols_ = CH[ref]
            assert kd_ == kdim and cols_ == cols, (ref, kdim, cols)
            wb = wbf[i % NWB]
            bname = "wbf%d" % (i % NWB)
            n = kdim * cols
            wbv = wb[:, 0:n].rearrange("p (k c) -> p k c", k=kdim)
            P.dma("sp", lambda q: [q.dma_start(out=wb[:, 0:n], in_=B.ap()[:, :])], reads=[("wbB",) + ref], writes=[bname], chan=bname)
            return wbv, bname

        xin = tmpT[:].rearrange("p k n -> p (k n)")[:, 0:D]

        def load_x(src_rows, c0, n):
            P.dma("sp", lambda q: [q.dma_start(out=xin[0:n, :], in_=src_rows)], writes=["tmpT"], chan="xin")
            for half in range(2):
                b = nbank()

                def tr(q, half=half, b=b):
                    r = None
                    for j in range(4):
                        k = half * 4 + j
                        r = q.transpose(ps[b][:, j * 128:j * 128 + n], xin[0:n, k * 128:(k + 1) * 128], identf[0:n, 0:n])
                    return r
                P.op("pe", tr, reads=["tmpT", "identf"], writes=[("ps", b)])
                src = ps[b][:].rearrange("p (j t) -> p j t", j=4)[:, :, 0:n]
                evac_copy(xT[:, half * 4:half * 4 + 4, c0:c0 + n], src, b, [("xT", c0 // 256)])

        yout = yT[:].rearrange("p k n -> p (k n)")[:, 0:D]

        def store_x(dst_rows, c0, n):
            for half in range(2):
                b = nbank()

                def tr(q, half=half, b=b):
                    r = None
                    for j in range(4):
                        k = half * 4 + j
                        r = q.transpose(ps[b][0:n, j * 128:(j + 1) * 128], xT[:, k, c0:c0 + n], identf[:, :])
                    return r
                P.op("pe", tr, reads=[("xT", c0 // 256), "identf"], writes=[("ps", b)])
                evac_copy(yout[0:n, half * 512:(half + 1) * 512], ps[b][0:n, :], b, ["yT"])
            P.dma("sp", lambda q: [q.dma_start(out=dst_rows, in_=yout[0:n, :])], reads=["yT"],
                  chan="yout", is_out=True)

        for b in range(NBLK):
            load_x(xp[b * 128:(b + 1) * 128, :], b * 128, 128)
        load_x(xs[:, :], NTOK, NS)

        def rms_stats(src4, n, srcres, outname="rstd", kdim=KD, scale=1.0):
            P.op("act", lambda q: q.activation(out=sqb[:, 0:kdim, 0:n], in_=src4, func=AF.Square), reads=list(srcres), writes=["sqb"])
            b = nbank()

            def mm(q):
                r = None
                for k in range(kdim):
                    r = q.matmul(ps[b][:, 0:n], lhsT=onesb[:], rhs=sqb[:, k, 0:n], start=(k == 0), stop=(k == kdim - 1))
                return r
            P.op("pe", mm, reads=["sqb", "onesb"], writes=[("ps", b)])
            P.op("act", lambda q: q.activation(out=rstd[:, 0:n], in_=ps[b][:, 0:n], func=AF.Sqrt, bias=epsc[:, 0:1], scale=scale),
                 reads=["epsc"], writes=[("ps", b), outname])
            P.op("dve", lambda q: q.reciprocal(out=rstd[:, 0:n], in_=rstd[:, 0:n]), reads=[outname], writes=[outname])

        KS = 5

        def pre_norm(layer, xview, n, xres):
            gb = vec[:, layer * 8:layer * 8 + 8].unsqueeze(2).to_broadcast([128, KD, n])
            P.op("pool", lambda q: q.tensor_tensor(out=tmpT[:, :, 0:n], in0=xview, in1=gb, op=ALU.mult),
                 reads=list(xres) + ["vec"], writes=["tmpT"])
            rms_stats(xview, n, xres)
            rb1 = rstd[:, 0:n].unsqueeze(1).to_broadcast([128, KS, n])
            rb2 = rstd[:, 0:n].unsqueeze(1).to_broadcast([128, KD - KS, n])
            P.op("dve", lambda q: q.tensor_tensor(out=hT[:, 0:KS, 0:n], in0=tmpT[:, 0:KS, 0:n], in1=rb1, op=ALU.mult),
                 reads=["tmpT", "rstd"], writes=["hT"])
            P.op("pool", lambda q: q.tensor_tensor(out=hT[:, KS:KD, 0:n], in0=tmpT[:, KS:KD, 0:n], in1=rb2, op=ALU.mult),
                 reads=["tmpT", "rstd"], join=["hT"])

        def pre_norm_lite_stages(layer, xview, n, xres):
            def s1():
                P.op("act", lambda q: q.activation(out=hT[:, :, 0:n], in_=xview, func=AF.Square), reads=list(xres), writes=["hT"])
            bb = [None]

            def s2():
                pre_norm_lite_mid(n, bb)

            def s3():
                for k in range(KD):
                    P.op("dve", lambda q, k=k: q.scalar_tensor_tensor(out=hT[:, k, 0:n], in0=xview[:, k, :], scalar=vec[:, layer * 8 + k:layer * 8 + k + 1],
                                                                      in1=rstd[:, 0:n], op0=ALU.mult, op1=ALU.mult),
                         reads=list(xres) + ["vec", "rstd"], **({"writes": ["hT"]} if k == 0 else {"join": ["hT"]}))
            return [s1, s2, s3]

        def pre_norm_lite_mid(n, bb):
            b = nbank()

            def mm(q):
                r = None
                for k in range(KD):
                    r = q.matmul(ps[b][:, 0:n], lhsT=onesb[:], rhs=hT[:, k, 0:n], start=(k == 0), stop=(k == KD - 1))
                return r
            P.op("pe", mm, reads=["hT", "onesb"], writes=[("ps", b)])
            P.op("act", lambda q: q.activation(out=rstd[:, 0:n], in_=ps[b][:, 0:n], func=AF.Sqrt, bias=epsc[:, 0:1], scale=1.0),
                 reads=["epsc"], writes=[("ps", b), "rstd"])
            P.op("dve", lambda q: q.reciprocal(out=rstd[:, 0:n], in_=rstd[:, 0:n]), reads=["rstd"], writes=["rstd"])

        def post_norm_update(layer, c0, n):
            xres = [("xT", c0 // 256)]
            gb = vec[:, 32 + layer * 8:32 + layer * 8 + 8].unsqueeze(2).to_broadcast([128, KD, n])
            P.op("pool", lambda q: q.tensor_tensor(out=tmpT[:, :, 0:n], in0=yT[:, :, 0:n], in1=gb, op=ALU.mult),
                 reads=["yT", "vec"], writes=["tmpT"])
            rms_stats(yT[:, :, 0:n], n, ["yT"])
            rb1 = rstd[:, 0:n].unsqueeze(1).to_broadcast([128, KS, n])
            rb2 = rstd[:, 0:n].unsqueeze(1).to_broadcast([128, KD - KS, n])
            P.op("dve", lambda q: q.tensor_tensor(out=tmpT[:, 0:KS, 0:n], in0=tmpT[:, 0:KS, 0:n], in1=rb1, op=ALU.mult),
                 reads=["tmpT", "rstd"], writes=["tmpTa"])
            P.op("pool", lambda q: q.tensor_tensor(out=tmpT[:, KS:KD, 0:n], in0=tmpT[:, KS:KD, 0:n], in1=rb2, op=ALU.mult),
                 reads=["tmpT", "rstd"], writes=["tmpTb"])
            P.op("dve", lambda q: q.tensor_tensor(out=xT[:, 0:KS, c0:c0 + n], in0=xT[:, 0:KS, c0:c0 + n], in1=tmpT[:, 0:KS, 0:n], op=ALU.add),
                 reads=["tmpTa", "tmpT"] + xres, writes=xres)
            P.op("pool", lambda q: q.tensor_tensor(out=xT[:, KS:KD, c0:c0 + n], in0=xT[:, KS:KD, c0:c0 + n], in1=tmpT[:, KS:KD, 0:n], op=ALU.add),
                 reads=["tmpTb", "tmpT"] + xres, join=xres)

        def mm_fm(bank, wv, col0, ncol, rhs3, n, kdim=KD, first=True, last=True):
            def f(q):
                r = None
                for k in range(kdim):
                    r = q.matmul(ps[bank][0:ncol, 0:n], lhsT=wv[:, k, col0:col0 + ncol], rhs=rhs3[:, k, 0:n],
                                 start=(first and k == 0), stop=(last and k == kdim - 1))
                return r
            return f

        HWB = HW * 4
        halo_s = rview(0, [128, 8, NBLK, 16], F32)
        aT = rview(8192, [128, 8, 2, 144], F32)
        sA = rview(17408, [128, 8, 2, 144], F32)
        sB = rview(26624, [128, 8, 2, 144], F32)
        wp_st = rview(35840, [128, 8, 256], F32)
        gT = rview(44032, [128, 16, 256], BF16)
        hb = R[:, 44032:44032 + HWB].bitcast(F32)
        mixT = rview(52224, [128, 16, 256], BF16)
        vtm = rview(60416, [128, 2, D], F32)
        uT = rview(68608, [128, KD, 256], BF16)
        halo_c = rview(72704, [128, 8, NBLK, 16], F32)
        dT = sqb
        vbf = r2view(0, [128, 2, D], BF16)
        atm = yT[:].rearrange("p k n -> p (k n)")[:, 0:D]
        histtm = tmpT[:].rearrange("p k n -> p (k n)")[:, 0:D]
        st4 = S("st4", [128, 16], F32)
        wp_bf = r2view(4096, [128, 8, 256], BF16)
        ws_f = r2view(8192, [128, 4, 128], F32)
        wsT = r2view(10240, [128, 4, 128], BF16)
        wsd_f = r2view(11264, [64, 4, 64], F32, parts=64)
        wsdT = r2view(12288, [64, 4, 64], BF16, parts=64)
        lng = r2view(12800, [128, D], F32)
        lnb = r2view(16896, [128, D], F32)
        bsp = r2view(20992, [128, 4, 128], F32)

        def even_layer(e):
            layer = 2 * e
            win = WV("win%d" % e)
            for t in range(NT):
                xv = xT[:, :, t * 256:(t + 1) * 256].rearrange("p k (b w) -> p k b w", b=2)[:, :, :, 112:128]
                P.op("pool", lambda q, xv=xv: q.tensor_copy(out=yT[:, :, 0:32].rearrange("p k (b w) -> p k b w", b=2), in_=xv),
                     reads=[("xT", t)], writes=["yT"])
                pre_norm(layer, yT[:, :, 0:32], 32, ["yT"])
                for ec in range(8):
                    wv, wn = wload(win[:, :, ec * 128:(ec + 1) * 128], KD, 128)
                    b = nbank()
                    P.op("pe", mm_fm(b, wv, 0, 128, hT, 32), reads=[wn, "hT"], writes=[("ps", b)])
                    evac_copy(halo_c[:, ec, t * 2:t * 2 + 2, :], ps[b][:, 0:32].rearrange("p (b w) -> p b w", b=2), b, ["halo_c"])
            P.dma("sp", lambda q: [q.dma_start(out=bh_in[e].ap()[:, :], in_=halo_c.rearrange("p k b w -> p (k b w)"))],
                  reads=["halo_c"], writes=[("bh_in", e)], chan=("bh_in", e))
            P.collective(lambda q: q.collective_compute("AllGather", ALU.bypass, replica_groups=[[0, 1, 2, 3], [4, 5, 6, 7]],
                                                        ins=[bh_in[e].ap().opt()], outs=[bh_out[e].ap().opt()]),
                         reads=[("bh_in", e)], writes=[("bh_out", e)], chan=("ccH", e))
            hg = hb.rearrange("p (k b w) -> p k b w", k=8, b=NBLK)
            for r in range(4):
                P.dma("sp", lambda q, r=r: [q.dma_start(out=hb, in_=bh_out[e].ap()[r * 128:(r + 1) * 128, :])],
                      reads=[("bh_out", e)], writes=["gT"], chan="hb")
                if r == 0:
                    P.op("dve", lambda q: q.tensor_scalar(out=halo_s, in0=hg, scalar1=selw[:, 0:1], scalar2=None, op0=ALU.mult),
                         reads=["gT", "selw"], writes=["halo_s"])
                else:
                    P.op("dve", lambda q, r=r: q.scalar_tensor_tensor(out=halo_s, in0=hg, scalar=selw[:, r:r + 1], in1=halo_s,
                                                                     op0=ALU.mult, op1=ALU.add),
                         reads=["gT", "selw", "halo_s"], writes=["halo_s"])
                if r == 3 and NBLK > 1:
                    P.op("dve", lambda q: q.scalar_tensor_tensor(out=halo_s[:, :, 1:NBLK, :], in0=hg[:, :, 0:NBLK - 1, :],
                                                                 scalar=selw[:, 4:5], in1=halo_s[:, :, 1:NBLK, :],
                                                                 op0=ALU.mult, op1=ALU.add),
                         reads=["gT", "selw", "halo_s"], writes=["halo_s"])
            P.dma("sp", lambda q: [q.dma_start(out=wp_st, in_=w_pool[e].rearrange("g (i p) o -> p (g i) o", p=128))],
                  writes=["wp_st"], chan="wp_st")
            P.op("dve", lambda q: q.tensor_copy(out=wp_bf[:], in_=wp_st), reads=["wp_st"], writes=["wp_bf"])
            P.dma("sp", lambda q: [q.dma_start(out=ws_f[:], in_=w_sp[e].rearrange("g i j -> i g j"))], writes=["ws_f"], chan="ws_f")
            b = nbank()

            def trw(q):
                r = None
                for g in range(4):
                    r = q.transpose(ps[b][:, g * 128:(g + 1) * 128], ws_f[:, g, :], identf[:])
                return r
            P.op("pe", trw, reads=["ws_f", "identf"], writes=[("ps", b)])
            P.op("dve", lambda q: q.tensor_copy(out=wsT[:].rearrange("p g i -> p (g i)"), in_=ps[b][:, :]), writes=[("ps", b), "wsT"])
            P.op("pool", lambda q: q.memset(wsT[64:128, :, 0:64], 0.0), reads=["wsT"], writes=["wsT"])
            P.op("pool", lambda q: q.memset(wsd_f[:], 0.0), writes=["wsd_f"])
            P.dma("sp", lambda q: [q.dma_start(out=wsd_f[0:32, :, 0:32], in_=w_sp[e, :, 0:32, 0:32].rearrange("g i j -> i g j")),
                                   q.dma_start(out=wsd_f[32:64, :, 32:64], in_=w_sp[e, :, 0:32, 0:32].rearrange("g i j -> i g j"))],
                  reads=["wsd_f"], writes=["wsd_f"], chan="wsd_f", n=2)
            b2 = nbank()

            def trw2(q):
                r = None
                for g in range(4):
                    r = q.transpose(ps[b2][0:64, g * 64:(g + 1) * 64], wsd_f[:, g, :], identf[0:64, 0:64])
                return r
            P.op("pe", trw2, reads=["wsd_f", "identf"], writes=[("ps", b2)])
            P.op("dve", lambda q: q.tensor_copy(out=wsdT[:].rearrange("p g i -> p (g i)"), in_=ps[b2][0:64, 0:256]),
                 writes=[("ps", b2), "wsdT"])
            ld(lng[:], ln_g[e:e + 1, :].broadcast_to([128, D]), "lng")
            ld(lnb[:], ln_b[e:e + 1, :].broadcast_to([128, D]), "lnb")
            ld(bsp[:].rearrange("p g i -> p (g i)"), b_sp[e:e + 1].rearrange("o g i -> o (g i)").broadcast_to([128, 512]), "bsp")
            return dict(win=win)

        def even_tile(e, ctx, t):
            layer = 2 * e
            sample = (t == NT)
            NB, W = (2, 32) if sample else (2, 128)
            n = NB * W
            c0 = NTOK if sample else t * 256
            xres = [("xT", c0 // 256)]
            win = ctx["win"]
            pre_norm(layer, xT[:, :, c0:c0 + n], n, xres)
            if sample:
                for s in range(2):
                    P.dma("sp", lambda q, s=s: [q.dma_start(out=histtm[0:15, :], in_=cpool[e, s])], writes=["tmpT"], chan="histtm")
                    for half in range(2):
                        b = nbank()

                        def trh(q, half=half, b=b):
                            r = None
                            for j in range(4):
                                k = half * 4 + j
                                r = q.transpose(ps[b][:, j * 16:j * 16 + 15], histtm[0:15, k * 128:(k + 1) * 128], identf[0:15, 0:15])
                            return r
                        P.op("pe", trh, reads=["tmpT", "identf"], writes=[("ps", b)])
                        P.op("dve", lambda q, half=half, b=b, s=s: q.tensor_copy(
                            out=aT[:, half * 4:half * 4 + 4, s, 1:16], in_=ps[b][:, 0:64].rearrange("p (j w) -> p j w", j=4)[:, :, 0:15]),
                            writes=[("ps", b), "aT"])
            else:
                P.op("pool", lambda q: q.tensor_copy(out=aT[:, :, :, 0:16], in_=halo_s[:, :, t * 2:t * 2 + 2, :]),
                     reads=["halo_s"], writes=["aT"])
            need_tm = sample or (t == NT - 1)
            m = 64 if sample else 128
            for ec in range(8):
                wv, wn = wload(win[:, :, ec * 128:(ec + 1) * 128], KD, 128)
                b = nbank()
                P.op("pe", mm_fm(b, wv, 0, 128, hT, n), reads=[wn, "hT"], writes=[("ps", b)])
                evac_copy(aT[:, ec, 0:NB, 16:16 + W], ps[b][:, 0:n].rearrange("p (b w) -> p b w", b=NB), b, ["aT"])
                if need_tm:
                    b = nbank()
                    tc0 = 0 if sample else 128

                    def mmtm(q, b=b, wv=wv, tc0=tc0):
                        r = None
                        for k in range(KD):
                            r = q.matmul(ps[b][0:m, 0:128], lhsT=hT[:, k, tc0:tc0 + m], rhs=wv[:, k, :], start=(k == 0), stop=(k == KD - 1))
                        return r
                    P.op("pe", mmtm, reads=[wn, "hT"], writes=[("ps", b)])
                    evac_copy(atm[0:m, ec * 128:(ec + 1) * 128], ps[b][0:m, 0:128], b, ["yT"])
            if need_tm:
                if sample:
                    P.dma("sp", lambda q: [q.dma_start(out=o_pool_s[e, 0], in_=atm[17:32, :]),
                                           q.dma_start(out=o_pool_s[e, 1], in_=atm[49:64, :])],
                          reads=["yT"], chan="atm", n=2, is_out=True)
                else:
                    P.dma("sp", lambda q: [q.dma_start(out=o_pool_p[e], in_=atm[113:128, :])], reads=["yT"], chan="atm", is_out=True)
            def sh(dst, src, k0, d):
                return lambda q: q.tensor_tensor(out=dst[:, k0:8, 0:NB, d:16 + W], in0=src[:, k0:8, 0:NB, d:16 + W],
                                                 in1=src[:, k0:8, 0:NB, 0:16 + W - d], op=ALU.add)
            P.op("dve", sh(sA, aT, 0, 1), reads=["aT"], writes=["sA", "sA2"])
            P.op("pool", sh(sB, sA, 2, 2), reads=["sA"], writes=["sB", "sB2"])
            P.op("dve", sh(sA, sB, 4, 4), reads=["sB"], writes=["sA2"])
            P.op("pool", sh(sB, sA, 6, 8), reads=["sA2"], writes=["sB2"])
            srcs = [(sA, ["sA"]), (sB, ["sB"]), (sA, ["sA2"]), (sB, ["sB2"])]
            for g in range(4):
                sbuf_, sres = srcs[g]
                w = 2 ** (g + 1)
                P.op("dve", lambda q, g=g, sbuf_=sbuf_, w=w: q.scalar_tensor_tensor(
                    out=dT[:, 2 * g:2 * g + 2, 0:n].rearrange("p k (b w) -> p k b w", b=NB),
                    in0=sbuf_[:, 2 * g:2 * g + 2, 0:NB, 16:16 + W], scalar=1.0 / w, in1=aT[:, 2 * g:2 * g + 2, 0:NB, 16:16 + W],
                    op0=ALU.mult, op1=ALU.subtract), reads=sres + ["aT"], writes=["sqb"])
                if (not sample) and t == 0:
                    rc = rcnt[:, g, :].unsqueeze(1).to_broadcast([128, 2, 16])
                    P.op("pool", lambda q, g=g, sbuf_=sbuf_, rc=rc: q.tensor_tensor(
                        out=tmpT[:, 0:2, 0:16], in0=sbuf_[:, 2 * g:2 * g + 2, 0, 16:32], in1=rc, op=ALU.mult),
                        reads=sres + ["rcnt"], writes=["tmpT"])
                    P.op("dve", lambda q, g=g: q.tensor_tensor(out=dT[:, 2 * g:2 * g + 2, 0:16], in0=tmpT[:, 0:2, 0:16],
                                                               in1=aT[:, 2 * g:2 * g + 2, 0, 16:32], op=ALU.subtract),
                         reads=["tmpT", "aT"], writes=["sqb"])
            nblk_tm = 1 if sample else 2
            for ec in range(8):
                wv, wn = wload(win[:, :, 2048 + ec * 128:2048 + (ec + 1) * 128], KD, 128)
                for blk in range(nblk_tm):
                    b = nbank()

                    def mmv(q, b=b, wv=wv, blk=blk):
                        r = None
                        for k in range(KD):
                            r = q.matmul(ps[b][0:m, 0:128], lhsT=hT[:, k, blk * 128:blk * 128 + m], rhs=wv[:, k, :],
                                         start=(k == 0), stop=(k == KD - 1))
                        return r
                    P.op("pe", mmv, reads=[wn, "hT"], writes=[("ps", b)])
                    P.op("act", lambda q, b=b, blk=blk, ec=ec: q.activation(out=vtm[0:m, blk, ec * 128:(ec + 1) * 128], in_=ps[b][0:m, 0:128],
                                                                         func=AF.Gelu), writes=[("ps", b), ("vtm", blk)])
            sqscr = tmpT[:].rearrange("p k n -> p (k n)")[:, 0:D]
            for blk in range(nblk_tm):
                vb = vtm[0:m, blk, :]
                vr = ("vtm", blk)
                P.op("dve", lambda q, vb=vb: q.reduce_sum(out=st4[0:m, 0:1], in_=vb, axis=AX.X), reads=[vr], writes=["st_a"])
                P.op("act", lambda q, vb=vb: q.activation(out=sqscr[0:m, :], in_=vb, func=AF.Square, accum_out=st4[0:m, 1:2]),
                     reads=[vr], writes=["tmpT", "st_b"])
                P.op("pool", lambda q: q.tensor_scalar(out=st4[0:m, 2:3], in0=st4[0:m, 0:1], scalar1=1.0 / D, scalar2=None, op0=ALU.mult),
                     reads=["st_a"], writes=["st_c"])
                P.op("dve", lambda q: q.tensor_tensor(out=st4[0:m, 3:4], in0=st4[0:m, 2:3], in1=st4[0:m, 2:3], op=ALU.mult),
                     reads=["st_c"], writes=["st_d"])
                P.op("pool", lambda q: q.tensor_scalar(out=st4[0:m, 4:5], in0=st4[0:m, 1:2], scalar1=1.0 / D, scalar2=None, op0=ALU.mult),
                     reads=["st_b"], writes=["st_e"])
                P.op("dve", lambda q: q.tensor_tensor(out=st4[0:m, 5:6], in0=st4[0:m, 4:5], in1=st4[0:m, 3:4], op=ALU.subtract),
                     reads=["st_e", "st_d"], writes=["st_f"])
                P.op("act", lambda q: q.activation(out=st4[0:m, 6:7], in_=st4[0:m, 5:6], func=AF.Sqrt, bias=epsc[0:m, 0:1], scale=1.0),
                     reads=["st_f", "epsc"], writes=["st_g"])
                P.op("dve", lambda q: q.reciprocal(out=st4[0:m, 7:8], in_=st4[0:m, 6:7]), reads=["st_g"], writes=["st_h"])
                P.op("dve", lambda q, vb=vb: q.tensor_scalar(out=vb, in0=vb, scalar1=st4[0:m, 2:3], scalar2=st4[0:m, 7:8],
                                                          op0=ALU.subtract, op1=ALU.mult), reads=[vr, "st_c", "st_h"], writes=[vr])
                P.op("pool", lambda q, vb=vb: q.tensor_tensor(out=vb, in0=vb, in1=lng[0:m, :], op=ALU.mult), reads=[vr, "lng"], writes=[vr])
                P.op("dve", lambda q, vb=vb: q.tensor_tensor(out=vb, in0=vb, in1=lnb[0:m, :], op=ALU.add), reads=[vr, "lnb"], writes=[vr])
                P.op("pool", lambda q, vb=vb, blk=blk: q.tensor_copy(out=vbf[0:m, blk, :], in_=vb), reads=[vr], writes=[("vbf", blk)])
                if sample:
                    P.dma("sp", lambda q: [q.dma_start(out=o_sgu_s[e], in_=vtm[0:64, 0, :])], reads=[vr], chan="vout", is_out=True)
            for ec in range(8):
                wv, wn = wload(win[:, :, 1024 + ec * 128:1024 + (ec + 1) * 128], KD, 128)
                b = nbank()
                P.op("pe", mm_fm(b, wv, 0, 128, hT, n), reads=[wn, "hT"], writes=[("ps", b)])
                P.op("act", lambda q, b=b, ec=ec: q.activation(out=uT[:, ec, 0:n], in_=ps[b][:, 0:n], func=AF.Gelu),
                     writes=[("ps", b), "uT"])
            for ec in range(16):
                wv, wn = wload(win[:, :, 3072 + ec * 128:3072 + (ec + 1) * 128], KD, 128)
                b = nbank()
                P.op("pe", mm_fm(b, wv, 0, 128, hT, n), reads=[wn, "hT"], writes=[("ps", b)])
                P.op("act", lambda q, b=b, ec=ec: q.activation(out=gT[:, ec, 0:n], in_=ps[b][:, 0:n], func=AF.Silu),
                     writes=[("ps", b), "gT"])
            P.op("pool", lambda q: q.tensor_tensor(out=uT[:, :, 0:n], in0=uT[:, :, 0:n], in1=gT[:, 8:16, 0:n], op=ALU.mult),
                 reads=["uT", "gT"], writes=["uT"])
            wI = 64 if sample else 128
            for blk in range(nblk_tm):
                for half in range(2):
                    b = nbank()

                    def mms(q, b=b, blk=blk, half=half):
                        r = None
                        for j in range(4):
                            dk = half * 4 + j
                            g = dk // 2
                            rhs = wsdT[0:64, g, :] if sample else wsT[:, g, :]
                            r = q.matmul(ps[b][:, j * 128:j * 128 + wI], lhsT=vbf[0:m, blk, dk * 128:(dk + 1) * 128], rhs=rhs,
                                         start=True, stop=True)
                        return r
                    P.op("pe", mms, reads=[("vbf", blk), "wsT", "wsdT"], writes=[("ps", b)])
                    ps4 = ps[b][:].rearrange("p (g c i) -> p g c i", g=2, c=2)
                    if sample:
                        for s in range(2):
                            bb = bsp[:, half * 2:half * 2 + 2, 0:32].unsqueeze(2).to_broadcast([128, 2, 2, 32])
                            P.op("dve", lambda q, s=s, half=half, bb=bb, ps4=ps4: q.tensor_tensor(
                                out=tmpT[:, half * 4:half * 4 + 4, s * 32:(s + 1) * 32].rearrange("p (g c) i -> p g c i", g=2),
                                in0=ps4[:, :, :, s * 32:(s + 1) * 32], in1=bb, op=ALU.add),
                                reads=["bsp"], writes=[("ps", b), "tmpT"])
                    else:
                        bb = bsp[:, half * 2:half * 2 + 2, :].unsqueeze(2).to_broadcast([128, 2, 2, 128])
                        P.op("dve", lambda q, half=half, blk=blk, bb=bb, ps4=ps4: q.tensor_tensor(
                            out=tmpT[:, half * 4:half * 4 + 4, blk * 128:(blk + 1) * 128].rearrange("p (g c) i -> p g c i", g=2),
                            in0=ps4, in1=bb, op=ALU.add),
                            reads=["bsp"], writes=[("ps", b), "tmpT"])
            P.op("pool", lambda q: q.tensor_tensor(out=mixT[:, 8:16, 0:n], in0=tmpT[:, :, 0:n], in1=uT[:, :, 0:n], op=ALU.mult),
                 reads=["tmpT", "uT"], writes=["mixT_b"])
            for g in range(4):
                for oc in range(2):
                    b = nbank()

                    def mmp(q, b=b, g=g, oc=oc):
                        r = None
                        for ic in range(2):
                            r = q.matmul(ps[b][:, 0:n], lhsT=wp_bf[:, g * 2 + ic, oc * 128:(oc + 1) * 128], rhs=dT[:, 2 * g + ic, 0:n],
                                         start=(ic == 0), stop=(ic == 1))
                        return r
                    P.op("pe", mmp, reads=["wp_bf", "sqb"], writes=[("ps", b)])
                    ch = 2 * g + oc
                    P.op("dve", lambda q, b=b, ch=ch: q.scalar_tensor_tensor(
                        out=mixT[:, ch, 0:n], in0=ps[b][:, 0:n], scalar=vec[:, 64 + e * 8 + ch:64 + e * 8 + ch + 1], in1=gT[:, ch, 0:n],
                        op0=ALU.mult, op1=ALU.mult), reads=["vec", "gT"], writes=[("ps", b), "mixT_a"])
            wo = WV("wo%d" % e)
            mixres = ["mixT_b", "mixT_a"]
            for dc in range(8):
                wva, wna = wload(wo[:, 0:8, dc * 128:(dc + 1) * 128], 8, 128)
                wvb, wnb = wload(wo[:, 8:16, dc * 128:(dc + 1) * 128], 8, 128)
                b = nbank()
                P.op("pe", mm_fm(b, wva, 0, 128, mixT[:, 0:8, :], n, first=True, last=False), reads=[wna] + mixres, writes=[("ps", b)])
                P.op("pe", mm_fm(b, wvb, 0, 128, mixT[:, 8:16, :], n, first=False, last=True), reads=[wnb] + mixres, writes=[("ps", b)])
                evac_copy(yT[:, dc, 0:n], ps[b][:, 0:n], b, ["yT"])
            post_norm_update(layer, c0, n)

        NK = 4 * NTOK
        KT = rview(0, [128, 2, NK], BF16)
        krT = R[0:64, 4 * NK:6 * NK].bitcast(BF16)
        Vt = rview(6 * NK, [128, 4 * NBLK, 258], BF16)
        WkvT = r2view(0, [128, 8, 256], BF16)
        Wv = r2view(4096, [128, 2, 8, 128], BF16)
        gateT = r2view(8192, [128, 8, 128], BF16)
        qn = r2view(10240, [128, 8, 128], BF16)
        qaT = r2view(12288, [128, 2, 8, 128], BF16)
        qrope = r2view(16384, [64, 8, 128], BF16, parts=64)
        kvn_bc = r2view(18432, [128, 256], F32)
        qcn = r2view(19456, [128, 3, 128], BF16)
        Pb = r2view(20224, [128, 2, 512], BF16)
        maskb = S("maskb", [128, 512], BF16)
        permf = S("permf", [64, 64], F32)
        cst = S("cst", [128, 64], F32)
        csfO = S("csfO", [128, 258], F32)
        csf = csfO[0:64, 0:256].rearrange("p (a t) -> p a t", a=2)
        stt = S("stt", [128, 16], F32)
        psb = [p[:].bitcast(BF16) for p in ps]
        tmp2d = tmpT[:].rearrange("p k n -> p (k n)")
        tmpb = tmp2d.bitcast(BF16)
        yT2d = yT[:].rearrange("p k n -> p (k n)")
        sqb2d = sqb[:].rearrange("p k n -> p (k n)")
        PTs = [tmpb[:, 0:512], tmpb[:, 512:1024]]
        krtm_s = tmpb[:, 1024:1600].rearrange("p (a b) -> p a b", b=64)
        olat = tmpb[:, 2048:4096].rearrange("p (h c) -> p h c", h=8)
        qas = tmpb[:, 1600:1856].rearrange("p (c t) -> p c t", c=2)
        qrs = tmpb[0:64, 1856:1984]
        qc = yT[:, 0:3, 128:256]
        Oacc = yT[:, 3:5, 128:256]
        qrf = yT[0:64, 5, 128:256]
        qt1 = yT[0:64, 6, 128:256]
        qt2 = yT[0:64, 7, 128:256]
        olT = sqb[:].rearrange("p k n -> p (k n)").rearrange("p (c h q) -> p c h q", c=2, h=8)
        oT = hT[:, :, 128:256]
        bks = [nc.dram_tensor("bks%d" % o, [64, 320], BF16) for o in range(2)]

        ld(tmp2d[:, 0:512], mask_d[:, :], "tmpT")
        P.op("dve", lambda q: q.tensor_copy(out=maskb[:], in_=tmp2d[:, 0:512]), reads=["tmpT"], writes=["maskb"])
        P.op("dve", lambda q: q.tensor_copy(out=permf[:, 0:32], in_=identf[0:64, 32:64]), reads=["identf"], writes=["permf"])
        P.op("pool", lambda q: q.tensor_copy(out=permf[:, 32:64], in_=identf[0:64, 0:32]), reads=["identf", "permf"], writes=["permf"])

        def stage_cast(src_ap, dst_ap, w, dres, parts=128):
            stv = wst[0][0:parts, 0:w]
            P.dma("sp", lambda q: [q.dma_start(out=stv, in_=src_ap)], writes=["wst0"], chan="wst0")
            P.op("pool", lambda q: q.tensor_copy(out=dst_ap, in_=stv), reads=["wst0"], writes=list(dres))

        def kt_transposes(kb, vidx, parts, krt_src, col0, w):
            b = nbank()

            def tr(q):
                q.transpose(psb[b][:, 0:w], Vt[0:parts, vidx, 0:128], identb[0:parts, 0:parts])
                q.transpose(psb[b][:, 128:128 + w], Vt[0:parts, vidx, 128:256], identb[0:parts, 0:parts])
                return q.transpose(psb[b][0:64, 256:256 + w], krt_src, identb[0:parts, 0:parts])
            P.op("pe", tr, reads=["V", "tmpT", "identb"], writes=[("ps", b)])
            P.op("act", lambda q: q.copy(out=KT[:, :, col0:col0 + w], in_=psb[b][:, 0:256].rearrange("p (c k) -> p c k", c=2)[:, :, 0:w]),
                 writes=[("ps", b), "KT"])
            P.op("dve", lambda q: q.tensor_copy(out=krT[:, col0:col0 + w], in_=psb[b][0:64, 256:256 + w]), writes=[("ps", b), "krT"])

        stt2 = S("stt2", [128, 24], F32)
        P.op("pool", lambda q: q.memset(stt2[:, 0:1], 30000.0 * ATTN_SCALE), writes=["nm_init"])
        Oaccs = [(csfO[:, 0:257], "csf"), (rstd[:, 0:257], "rstd")]
        Pbs = [Pb[:, 0, :], Pb[:, 1, :], wst[0][:, :].bitcast(BF16)]
        Pbn = [("Pb", 0), ("Pb", 1), "wst0"]

        uctr = [0]
        self_defer = [None]

        def attn_stream(units, hook=None):
            steps = []
            for ui, U in enumerate(units):
                ng = len(U["groups"])
                U["_k"] = uctr[0] % 4
                U["_bO"] = 6 + uctr[0] % 2
                uctr[0] += 1
                for gi, G in enumerate(U["groups"]):
                    steps.append((ui, gi, ng, U, G))
            N = len(steps)
            st = [dict() for _ in range(N)]

            def col(base, k):
                return stt2[:, base + k:base + k + 1]

            def stA(i):
                ui, gi, ng, U, G = steps[i]
                w = G["w"]
                bS = nbank()
                st[i]["bS"] = bS

                def mmS(q):
                    q.matmul(ps[bS][:, 0:w], lhsT=U["qa0"], rhs=G["kt0"], start=True, stop=False)
                    q.matmul(ps[bS][:, 0:w], lhsT=U["qa1"], rhs=G["kt1"], start=False, stop=False)
                    r = q.matmul(ps[bS][:, 0:w], lhsT=U["qr"], rhs=G["krt"], start=False, stop=not G["diag"])
                    if G["diag"]:
                        r = q.matmul(ps[bS][:, 0:w], lhsT=identb[:], rhs=maskb[:, 0:w], start=False, stop=True)
                    return r
                P.op("pe", mmS, reads=["qaT", "qrope", "tmpT", "KT", "krT", "maskb", "identb"], writes=[("ps", bS)])

            def stB(i):
                ui, gi, ng, U, G = steps[i]
                w = G["w"]
                bS = st[i]["bS"]
                k = U["_k"]
                ib = i % 3
                if gi == 0:
                    P.op("dve", lambda q: q.reduce_max(out=col(6, k), in_=ps[bS][:, 0:w], axis=AX.X), writes=[("ps", bS), ("gmax", k)])
                    P.op("dve", lambda q: q.tensor_scalar(out=col(2, k), in0=col(6, k), scalar1=-ATTN_SCALE, scalar2=None, op0=ALU.mult),
                         reads=[("gmax", k)], writes=[("nm", k)])
                P.op("act", lambda q: q.activation(out=Pbs[ib][:, 0:w], in_=ps[bS][:, 0:w], func=AF.Exp, bias=col(2, k), scale=ATTN_SCALE),
                     reads=[("nm", k)], writes=[("ps", bS), Pbn[ib]])

            def stC(i):
                ui, gi, ng, U, G = steps[i]
                w = G["w"]
                bT = nbank()
                nj = (w + 127) // 128
                ib = i % 3
                it = i % 2

                def trP(q):
                    r = None
                    for j in range(nj):
                        wj = min(128, w - j * 128)
                        r = q.transpose(psb[bT][0:wj, j * 128:(j + 1) * 128], Pbs[ib][:, j * 128:j * 128 + wj], identb[:])
                    return r
                P.op("pe", trP, reads=[Pbn[ib], "identb"], writes=[("ps", bT)])
                kp = min(128, w)
                if i % 3 == 0:
                    P.op("act", lambda q: q.copy(out=PTs[it][0:kp, 0:nj * 128], in_=psb[bT][0:kp, 0:nj * 128]),
                         writes=[("ps", bT), ("PTs", it)])
                else:
                    P.op("dve", lambda q: q.tensor_copy(out=PTs[it][0:kp, 0:nj * 128], in_=psb[bT][0:kp, 0:nj * 128]),
                         writes=[("ps", bT), ("PTs", it)])

            def stD(i):
                ui, gi, ng, U, G = steps[i]
                bO = U["_bO"]
                it = i % 2

                def mmO(q):
                    r = None
                    nv = len(G["v"])
                    for j, (vap, K) in enumerate(G["v"]):
                        r = q.matmul(ps[bO][:, 0:257], lhsT=PTs[it][0:K, j * 128:(j + 1) * 128], rhs=vap,
                                     start=(gi == 0 and j == 0), stop=(gi == ng - 1 and j == nv - 1))
                    return r
                P.op("pe", mmO, reads=[("PTs", it), "V"], writes=[("ps", bO)])
                if gi == ng - 1:
                    P.op("dve", lambda q: q.reciprocal(out=stt2[:, 16:17], in_=ps[bO][:, 256:257]), writes=[("ps", bO), "rl"])
                    P.op("act", lambda q: q.activation(out=U["out_ap"], in_=ps[bO][:, 0:256], func=AF.Copy, scale=stt2[:, 16:17]),
                         reads=["rl"], writes=[("ps", bO), "olat"])
                    if U.get("post") is not None:
                        for j, fn in enumerate(U["post"]):
                            self_defer[0](2 + 2 * j, fn)

            pend = []
            st_cur = [0]

            def defer(delay, fn):
                pend.append((st_cur[0] + delay, fn))
            self_defer[0] = defer
            h0 = max(0, N // 2)
            for i in range(N + 3):
                st_cur[0] = i
                if hook is not None and i == h0:
                    for j, fn in enumerate(hook):
                        defer(2 * j, fn)
                due = [p for p in pend if p[0] <= i]
                pend[:] = [p for p in pend if p[0] > i]
                for _, fn in due:
                    fn()
                if i < N:
                    stA(i)
                    stB(i)
                if 0 <= i - 2 < N:
                    stC(i - 2)
                if 0 <= i - 3 < N:
                    stD(i - 3)
            for _, fn in sorted(pend, key=lambda p: p[0]):
                fn()

        def odd_layer(o):
            layer = 2 * o + 1
            wio = WV("wio%d" % o)
            wqu = WV("wqu%d" % o)
            wku = WV("wku%d" % o)
            wov = WV("wov%d" % o)
            ld(kvn_bc, kv_norm[o:o + 1, :].broadcast_to([128, 256]), "kvn_bc")
            P.op("pool", lambda q: q.memset(Vt[:, :, 256:258], 1.0), writes=["V"])
            for h in range(8):
                wv, wn = wload(wku[:, :, h * 256:(h + 1) * 256], 2, 256)
                b = nbank()

                def trk(q, b=b, wv=wv):
                    q.transpose(psb[b][:, 0:128], wv[:, 0, 0:128], identb[:])
                    return q.transpose(psb[b][:, 128:256], wv[:, 1, 0:128], identb[:])
                P.op("pe", trk, reads=[wn, "identb"], writes=[("ps", b)])
                P.op("act", lambda q, b=b, h=h: q.copy(out=WkvT[:, h, :], in_=psb[b][:, 0:256]), writes=[("ps", b), "WkvT"])
                P.op("pool", lambda q, wv=wv, h=h: q.tensor_copy(out=Wv[:, :, h, :], in_=wv[:, :, 128:256]), reads=[wn], writes=["Wv"])
            units = [(u * 128, 128, u) for u in range(NBLK)] + [(NTOK, 64, NBLK)]
            for (c0, n, u) in units:
                sample = (u == NBLK)
                xres = [("xT", c0 // 256)]
                pre_norm(layer, xT[:, :, c0:c0 + n], n, xres)
                P.dma("sp", lambda q, u=u: [q.dma_start(out=cst[:], in_=cs_tm[:, u * 64:(u + 1) * 64])], writes=["cst"], chan="cst")
                b = nbank()
                for ci, (col, w) in enumerate([(384, 128), (512, 128), (640, 64)]):
                    wv, wn = wload(wio[:, :, col:col + w], KD, w)

                    def mmk(q, b=b, wv=wv, ci=ci, w=w, n=n):
                        r = None
                        for k in range(KD):
                            r = q.matmul(ps[b][0:n, ci * 128:ci * 128 + w], lhsT=hT[:, k, 0:n], rhs=wv[:, k, :], start=(k == 0), stop=(k == KD - 1))
                        return r
                    P.op("pe", mmk, reads=[wn, "hT"], writes=[("ps", b)])
                kvc = tmp2d[0:n, 0:320]
                P.op("act", lambda q, b=b, n=n, kvc=kvc: q.copy(out=kvc, in_=ps[b][0:n, 0:320]), writes=[("ps", b), "tmpT"])
                P.op("act", lambda q, n=n: q.activation(out=tmp2d[0:n, 512:768], in_=tmp2d[0:n, 0:256], func=AF.Square, accum_out=stt[0:n, 8:9]),
                     reads=["tmpT"], writes=["tmpT2", "ss"])
                P.op("act", lambda q, n=n: q.activation(out=stt[0:n, 9:10], in_=stt[0:n, 8:9], func=AF.Sqrt, bias=epsc[0:n, 0:1], scale=1.0 / 256.0),
                     reads=["ss", "epsc"], writes=["ss2"])
                P.op("dve", lambda q, n=n: q.reciprocal(out=stt[0:n, 10:11], in_=stt[0:n, 9:10]), reads=["ss2"], writes=["ss3"])
                P.op("dve", lambda q, n=n: q.scalar_tensor_tensor(out=yT2d[0:n, 0:256], in0=tmp2d[0:n, 0:256], scalar=stt[0:n, 10:11],
                                                                  in1=kvn_bc[0:n, :], op0=ALU.mult, op1=ALU.mult),
                     reads=["tmpT", "ss3", "kvn_bc"], writes=["yT"])
                x1, x2 = tmp2d[0:n, 256:288], tmp2d[0:n, 288:320]
                cs_, sn_ = cst[0:n, 0:32], cst[0:n, 32:64]
                t1, t2, t3, t4 = (tmp2d[0:n, 1024 + 32 * j:1056 + 32 * j] for j in range(4))
                P.op("pool", lambda q, x1=x1, cs_=cs_, t1=t1: q.tensor_tensor(out=t1, in0=x1, in1=cs_, op=ALU.mult), reads=["tmpT", "cst"], writes=["t1"])
                P.op("dve", lambda q, x2=x2, sn_=sn_, t2=t2: q.tensor_tensor(out=t2, in0=x2, in1=sn_, op=ALU.mult), reads=["tmpT", "cst"], writes=["t2"])
                P.op("pool", lambda q, x2=x2, cs_=cs_, t3=t3: q.tensor_tensor(out=t3, in0=x2, in1=cs_, op=ALU.mult), reads=["tmpT", "cst"], writes=["t3"])
                P.op("dve", lambda q, x1=x1, sn_=sn_, t4=t4: q.tensor_tensor(out=t4, in0=x1, in1=sn_, op=ALU.mult), reads=["tmpT", "cst"], writes=["t4"])
                P.op("pool", lambda q, n=n, t1=t1, t2=t2: q.tensor_tensor(out=yT2d[0:n, 256:288], in0=t1, in1=t2, op=ALU.subtract),
                     reads=["t1", "t2"], writes=["yTr1"])
                P.op("dve", lambda q, n=n, t3=t3, t4=t4: q.tensor_tensor(out=yT2d[0:n, 288:320], in0=t3, in1=t4, op=ALU.add),
                     reads=["t3", "t4"], writes=["yTr2"])
                P.op("pool", lambda q, n=n: q.tensor_copy(out=sqb2d[0:n, 0:320], in_=yT2d[0:n, 0:320]), reads=["yT", "yTr1", "yTr2"], writes=["sqb"])
                if sample:
                    P.dma("sp", lambda q: [q.dma_start(out=o_ckv_s[o], in_=yT2d[0:64, 0:256]), q.dma_start(out=o_kr_s[o], in_=yT2d[0:64, 256:320])],
                          reads=["yT", "yTr1", "yTr2"], chan="kvout", n=2, is_out=True)
                    P.dma("sp", lambda q: [q.dma_start(out=bks[o].ap()[:, :], in_=sqb2d[0:64, 0:320])], reads=["sqb"], writes=[("bks", o)], chan="bks")
                else:
                    P.dma("sp", lambda q, c0=c0: [q.dma_start(out=o_ckv_p[o, c0:c0 + 128, :], in_=yT2d[:, 0:256]),
                                                 q.dma_start(out=o_kr_p[o, c0:c0 + 128, :], in_=yT2d[:, 256:320])],
                          reads=["yT", "yTr1", "yTr2"], chan="kvout", n=2, is_out=True)
                    P.dma("sp", lambda q, u=u: [q.dma_start(out=bk_in[o][u // BPS].ap()[(u % BPS) * 128:(u % BPS) * 128 + 128, :], in_=sqb2d[:, 0:320])],
                          reads=["sqb"], writes=[("bk_in", o, u)], chan="bkin")
            for sp in range(NSPL):
                P.collective(lambda q, sp=sp: q.collective_compute("AllGather", ALU.bypass, replica_groups=[[0, 1, 2, 3], [4, 5, 6, 7]],
                                                                   ins=[bk_in[o][sp].ap().opt()], outs=[bk_out[o][sp].ap().opt()]),
                             reads=[("bk_in", o, u) for u in range(sp * BPS, (sp + 1) * BPS)], writes=[("bk_out", o, sp)], chan=("ccK", o, sp))
            if layer + 1 < NLAYERS:
                relay_layer(layer + 1)
            V4 = Vt.rearrange("p (m r) c -> p m r c", r=4)
            krtm = tmpb[:, 0:NK // 2].rearrange("p (m r c) -> p m r c", r=4, c=64)
            for sp in range(NSPL):
                bko = bk_out[o][sp].ap()
                ms = slice(sp * BPS, (sp + 1) * BPS)
                for r in range(4):
                    P.dma("sp", lambda q, r=r, bko=bko, ms=ms: [q.dma_start(out=V4[:, ms, r, 0:256], in_=bko[r * TPS:(r + 1) * TPS, 0:256].rearrange("(m p) c -> p m c", p=128))],
                          reads=[("bk_out", o, sp)], writes=["V"], chan="Vld")
                    P.dma("sp", lambda q, r=r, bko=bko, ms=ms: [q.dma_start(out=krtm[:, ms, r, :], in_=bko[r * TPS:(r + 1) * TPS, 256:320].rearrange("(m p) c -> p m c", p=128))],
                          reads=[("bk_out", o, sp)], writes=["tmpT"], chan="krld")
            krtm3 = tmpb[:, 0:NK // 2].rearrange("p (k c) -> p k c", c=64)
            for kb in range(4 * NBLK):
                kt_transposes(kb, kb, 128, krtm3[:, kb, :], kb * 128, 128)

            qrfs = [yT[0:64, 5, 128:256], yT[0:64, 3, 128:256]]
            qt1s = [yT[0:64, 6, 128:256], yT[0:64, 4, 128:256]]

            def q_path(c0, n, do_norm=True):
                xres = [("xT", c0 // 256)]
                if do_norm:
                    pre_norm(layer, xT[:, :, c0:c0 + n], n, xres)
                P.dma("sp", lambda q: [q.dma_start(out=csf[:, :, 0:n], in_=cs_fm.rearrange("p (a t) -> p a t", a=2)[:, :, c0:c0 + n])],
                      writes=["csf"], chan="csf")
                for kc in range(3):
                    wv, wn = wload(wio[:, :, kc * 128:(kc + 1) * 128], KD, 128)
                    b = nbank()
                    P.op("pe", mm_fm(b, wv, 0, 128, hT, n), reads=[wn, "hT"], writes=[("ps", b)])
                    evac_copy(qc[:, kc, 0:n], ps[b][:, 0:n], b, ["qc"])
                rms_stats(qc[:, :, 0:n], n, ["qc"], kdim=3, scale=1024.0 / 384.0)
                for kc in range(3):
                    P.op("dve", lambda q, kc=kc: q.scalar_tensor_tensor(out=qcn[:, kc, 0:n], in0=qc[:, kc, 0:n],
                                                                        scalar=vec[:, 80 + o * 3 + kc:80 + o * 3 + kc + 1], in1=rstd[:, 0:n],
                                                                        op0=ALU.mult, op1=ALU.mult), reads=["qc", "vec", "rstd"], writes=["qcn"])
                for ec in range(8):
                    wv, wn = wload(wio[:, :, 704 + ec * 128:704 + (ec + 1) * 128], KD, 128)
                    b = nbank()
                    P.op("pe", mm_fm(b, wv, 0, 128, hT, n), reads=[wn, "hT"], writes=[("ps", b)])
                    P.op("act", lambda q, b=b, ec=ec: q.activation(out=gateT[:, ec, 0:n], in_=ps[b][:, 0:n], func=AF.Silu),
                         writes=[("ps", b), "gateT"])

                def stage_a(h):
                    j = h % 2
                    wv, wn = wload(wqu[:, :, h * 192:(h + 1) * 192], 3, 192)
                    b1 = nbank()
                    P.op("pe", mm_fm(b1, wv, 0, 128, qcn, n, kdim=3), reads=[wn, "qcn"], writes=[("ps", b1)])
                    evac_copy(qn[:, h, 0:n], ps[b1][:, 0:n], b1, [("qn", h)])
                    b2 = nbank()
                    P.op("pe", mm_fm(b2, wv, 128, 64, qcn, n, kdim=3), reads=[wn, "qcn"], writes=[("ps", b2)])
                    P.op("act", lambda q: q.copy(out=qrfs[j][:, 0:n], in_=ps[b2][0:64, 0:n]), writes=[("ps", b2), ("qrf", j)])

                def stage_b(h):
                    j = h % 2
                    jw = {"writes": ["qrope"]} if h == 0 else {"join": ["qrope"]}
                    jq = {"wres": ["qaT"]} if h == 0 else {"wres": [], "join": ["qaT"]}
                    b3 = nbank()
                    P.op("pe", lambda q: q.matmul(ps[b3][0:64, 0:n], lhsT=permf[:, :], rhs=qrfs[j][:, 0:n], start=True, stop=True),
                         reads=["permf", ("qrf", j)], writes=[("ps", b3)])
                    P.op("dve", lambda q: q.tensor_tensor(out=qt1s[j][:, 0:n], in0=ps[b3][0:64, 0:n], in1=csf[:, 1, 0:n], op=ALU.mult),
                         reads=["csf"], writes=[("ps", b3), ("qt1", j)])
                    P.op("pool", lambda q: q.tensor_tensor(out=qt2[:, 0:n], in0=qrfs[j][:, 0:n], in1=csf[:, 0, 0:n], op=ALU.mult),
                         reads=["csf", ("qrf", j)], writes=["qt2"])
                    P.op("dve", lambda q: q.tensor_tensor(out=qrope[:, h, 0:n], in0=qt1s[j][:, 0:n], in1=qt2[:, 0:n], op=ALU.add),
                         reads=[("qt1", j), "qt2"], **jw)
                    b4 = nbank()

                    def mma(q):
                        q.matmul(ps[b4][:, 0:n], lhsT=WkvT[:, h, 0:128], rhs=qn[:, h, 0:n], start=True, stop=True)
                        return q.matmul(ps[b4][:, 128:128 + n], lhsT=WkvT[:, h, 128:256], rhs=qn[:, h, 0:n], start=True, stop=True)
                    P.op("pe", mma, reads=["WkvT", ("qn", h)], writes=[("ps", b4)])
                    evac_copy(qaT[:, :, h, 0:n], ps[b4][:, 0:256].rearrange("p (c t) -> p c t", c=2)[:, :, 0:n], b4, jq["wres"], join=jq.get("join", ()))

                for i in range(9):
                    if i < 8:
                        stage_a(i)
                    if i >= 1:
                        stage_b(i - 1)

            def head_out_a(h):
                b = nbank()

                def tro(q):
                    q.transpose(psb[b][:, 0:128], olat[:, h, 0:128], identb[:])
                    return q.transpose(psb[b][:, 128:256], olat[:, h, 128:256], identb[:])
                P.op("pe", tro, reads=["olat", "identb"], writes=[("ps", b)])
                evac_copy(olT[:, :, h, :], psb[b][:, 0:256].rearrange("p (c q) -> p c q", c=2), b, [("olT", h)], join=["sqb"])

            def head_out_b(h, n):
                b2 = nbank()

                def mmo(q):
                    q.matmul(ps[b2][:, 0:n], lhsT=Wv[:, 0, h, :], rhs=olT[:, 0, h, 0:n], start=True, stop=False)
                    return q.matmul(ps[b2][:, 0:n], lhsT=Wv[:, 1, h, :], rhs=olT[:, 1, h, 0:n], start=False, stop=True)
                P.op("pe", mmo, reads=["Wv", ("olT", h), "sqb"], writes=[("ps", b2)])
                P.op("dve", lambda q: q.tensor_tensor(out=oT[:, h, 0:n], in0=ps[b2][:, 0:n], in1=gateT[:, h, 0:n], op=ALU.mult),
                     reads=["gateT"], writes=[("ps", b2)] + (["oT"] if h == 0 else []), join=([] if h == 0 else ["oT"]))

            def out_path(c0, n, heads_done=False):
                for h in range(0 if heads_done else 8):
                    b = nbank()

                    def mmo(q, b=b, h=h):
                        q.matmul(ps[b][:, 0:n], lhsT=Wv[:, 0, h, :], rhs=olT[:, 0, h, 0:n], start=True, stop=False)
                        return q.matmul(ps[b][:, 0:n], lhsT=Wv[:, 1, h, :], rhs=olT[:, 1, h, 0:n], start=False, stop=True)
                    P.op("pe", mmo, reads=["Wv", "sqb"], writes=[("ps", b)])
                    P.op("dve", lambda q, b=b, h=h: q.tensor_tensor(out=oT[:, h, 0:n], in0=ps[b][:, 0:n], in1=gateT[:, h, 0:n], op=ALU.mult),
                         reads=["gateT"], writes=[("ps", b), "oT"])
                for dc in range(8):
                    wv, wn = wload(wov[:, :, dc * 128:(dc + 1) * 128], KD, 128)
                    b = nbank()
                    P.op("pe", mm_fm(b, wv, 0, 128, oT, n), reads=[wn, "oT"], writes=[("ps", b)])
                    evac_copy(yT[:, dc, 0:n], ps[b][:, 0:n], b, ["yT"])
                post_norm_update(layer, c0, n)

            for blk in range(NBLK):
                c0 = blk * 128
                q_path(c0, 128, do_norm=(blk == 0))
                units_ = []
                for h in range(8):
                    groups = []
                    for g in range(blk + 1):
                        ks = slice(g * 512, (g + 1) * 512)
                        groups.append(dict(kt0=KT[:, 0, ks], kt1=KT[:, 1, ks], krt=krT[:, ks], w=512,
                                           v=[(Vt[:, g * 4 + j, 0:257], 128) for j in range(4)], diag=(g == blk)))
                    units_.append(dict(qa0=qaT[:, 0, h, :], qa1=qaT[:, 1, h, :], qr=qrope[:, h, :], groups=groups, out_ap=olat[:, h, :],
                                       post=[(lambda h=h: head_out_a(h)), (lambda h=h: head_out_b(h, 128))]))
                nc0, nn = ((blk + 1) * 128, 128) if blk + 1 < NBLK else (NTOK, 64)
                attn_stream(units_, hook=pre_norm_lite_stages(layer, xT[:, :, nc0:nc0 + nn], nn, [("xT", nc0 // 256)]))
                out_path(c0, 128, heads_done=True)
            q_path(NTOK, 64, do_norm=False)
            for s_ in range(2):
                for kb in range(8):
                    stage_cast(cckv[o, s_, kb * 128:(kb + 1) * 128, :], Vt[:, kb, 0:256], 256, ["V"])
                    stage_cast(ckr[o, s_, kb * 128:(kb + 1) * 128, :], krtm_s[:, kb, :], 64, ["tmpT"])
                P.dma("sp", lambda q, s_=s_: [q.dma_start(out=Vt[0:32, 8, 0:256], in_=bks[o].ap()[s_ * 32:(s_ + 1) * 32, 0:256]),
                                             q.dma_start(out=krtm_s[0:32, 8, :], in_=bks[o].ap()[s_ * 32:(s_ + 1) * 32, 256:320])],
                      reads=[("bks", o)], writes=["V", "tmpT"], chan="bksld", n=2)
                for kb in range(8):
                    kt_transposes(kb, kb, 128, krtm_s[:, kb, :], kb * 128, 128)
                kt_transposes(8, 8, 32, krtm_s[0:32, 8, :], 1024, 32)
                units_ = []
                for hq in range(2):
                    ts = slice(s_ * 32, (s_ + 1) * 32)
                    hs = slice(hq * 4, hq * 4 + 4)
                    groups = []
                    for g in range(2):
                        ks = slice(g * 512, (g + 1) * 512)
                        groups.append(dict(kt0=KT[:, 0, ks], kt1=KT[:, 1, ks], krt=krT[:, ks], w=512,
                                           v=[(Vt[:, g * 4 + j, 0:257], 128) for j in range(4)], diag=False))
                    groups.append(dict(kt0=KT[:, 0, 1024:1056], kt1=KT[:, 1, 1024:1056], krt=krT[:, 1024:1056], w=32,
                                       v=[(Vt[0:32, 8, 0:257], 32)], diag=False))
                    P.op("dve", lambda q, hs=hs, ts=ts: q.tensor_copy(out=qas.rearrange("p c (h t) -> p c h t", h=4), in_=qaT[:, :, hs, ts]),
                         reads=["qaT"], writes=["tmpT"])
                    P.op("pool", lambda q, hs=hs, ts=ts: q.tensor_copy(out=qrs.rearrange("p (h t) -> p h t", h=4), in_=qrope[:, hs, ts]),
                         reads=["qrope"], writes=["tmpT"])

                    def post(hs=hs, ts=ts):
                        b = nbank()

                        def tro2(q):
                            q.transpose(psb[b][:, 0:128], olat[:, 0, 0:128], identb[:])
                            return q.transpose(psb[b][:, 128:256], olat[:, 0, 128:256], identb[:])
                        P.op("pe", tro2, reads=["olat", "identb"], writes=[("ps", b)])
                        evac_copy(olT[:, :, hs, ts], psb[b][:, 0:256].rearrange("p (c h t) -> p c h t", c=2, h=4), b, ["sqb"])
                    attn_stream([dict(qa0=qas[:, 0, :], qa1=qas[:, 1, :], qr=qrs, groups=groups, out_ap=olat[:, 0, :], post=[post])])
            out_path(NTOK, 64)

        for layer in range(NLAYERS):
            conv_layer(layer)
        relay_layer(0)
        for layer in range(NLAYERS):
            if layer % 2 == 0:
                e = layer // 2
                ctx = even_layer(e)
                for t in range(NT + 1):
                    if t == 1 and layer + 1 < NLAYERS:
                        relay_layer(layer + 1)
                    even_tile(e, ctx, t)
            else:
                P.barrier()
                odd_layer(layer // 2)
                P.barrier()

        for b in range(NBLK):
            store_x(y_p[b * 128:(b + 1) * 128, :], b * 128, 128)
        store_x(y_s[:, :], NTOK, NS)
        P.finish()
        P.replay()
    return nc


def _host_consts(c, NBLK):
    NTOK = NBLK * 128
    TOT = NTOK + 64
    r = c % 4
    mask = np.zeros((128, 512), np.float32)
    qi = np.arange(128)[:, None]
    for i in range(4):
        blk = mask[:, i * 128:(i + 1) * 128]
        if i > r:
            blk[:] = NEG
        elif i == r:
            kj = np.arange(128)[None, :]
            blk[:] = np.where((kj // 64) <= (qi // 64), 0.0, NEG)
    selw = np.zeros((128, 8), np.float32)
    if r > 0:
        selw[:, r - 1] = 1.0
    else:
        selw[:, 4] = 1.0
    rc = np.zeros((128, 4, 16), np.float32)
    for g in range(4):
        w = 2 ** (g + 1)
        for p in range(16):
            rc[:, g, p] = (1.0 / min(p + 1, w)) if r == 0 else 1.0 / w
    half = 32
    freqs = (10000.0 ** (-np.arange(half, dtype=np.float32) / half)).astype(np.float32)
    pos = np.zeros(TOT, np.float32)
    for m in range(NBLK):
        pos[m * 128:(m + 1) * 128] = (4 * m + r) * 128 + np.arange(128)
    pos[NTOK:NTOK + 32] = 1024 + np.arange(32)
    pos[NTOK + 32:] = 1024 + np.arange(32)
    ang = pos[:, None].astype(np.float32) * freqs[None, :]
    cos, sin = np.cos(ang).astype(np.float32), np.sin(ang).astype(np.float32)
    cs_tm = np.zeros((128, NBLK + 1, 64), np.float32)
    for m in range(NBLK):
        cs_tm[:, m, 0:32] = cos[m * 128:(m + 1) * 128]
        cs_tm[:, m, 32:64] = sin[m * 128:(m + 1) * 128]
    cs_tm[0:64, NBLK, 0:32] = cos[NTOK:]
    cs_tm[0:64, NBLK, 32:64] = sin[NTOK:]
    cs_fm = np.zeros((64, 2, TOT), np.float32)
    cs_fm[0:32, 0] = cos.T
    cs_fm[32:64, 0] = cos.T
    cs_fm[0:32, 1] = -sin.T
    cs_fm[32:64, 1] = sin.T
    return dict(mask=mask, selw=selw, rcnt=rc.reshape(128, 64), cs_tm=cs_tm.reshape(128, -1), cs_fm=cs_fm.reshape(64, -1),
                ident=np.eye(128, dtype=np.float32))


def _fm(v):
    return np.ascontiguousarray(np.asarray(v, np.float32).reshape(-1, 128).T)


_NC_CACHE = {}


def kernel(x_prompt, x_sample, cache_pool, cache_ckv, cache_krope, norm_pre, norm_post,
           w_in_even, w_pool, pool_scale, sgu_ln_g, sgu_ln_b, w_spatial, b_spatial, w_out_even,
           w_in_odd, q_norm, kv_norm, w_q_up, w_kv_up, w_o, _nlayers=4):
    f = lambda a: np.ascontiguousarray(np.asarray(a, dtype=np.float32))
    x_prompt = f(x_prompt)
    x_sample = f(x_sample)
    B, T, _ = x_prompt.shape
    NBLK = T // 512
    NTOK = NBLK * 128
    vecs = np.zeros((128, 96), np.float32)
    for l in range(4):
        vecs[:, l * 8:(l + 1) * 8] = _fm(f(norm_pre)[l])
        vecs[:, 32 + l * 8:32 + (l + 1) * 8] = _fm(f(norm_post)[l])
    for e in range(2):
        vecs[:, 64 + e * 8:64 + (e + 1) * 8] = _fm(f(pool_scale)[e])
        vecs[:, 80 + e * 3:80 + (e + 1) * 3] = _fm(f(q_norm)[e])
    shared = dict(w_in_even=f(w_in_even), w_pool=f(w_pool), ln_g=f(sgu_ln_g), ln_b=f(sgu_ln_b), w_sp=f(w_spatial),
                  b_sp=f(b_spatial), w_out_even=f(w_out_even), w_in_odd=f(w_in_odd), kv_norm=f(kv_norm),
                  w_q_up=f(w_q_up).reshape(2, 384, 8 * 192), w_kv_up=f(w_kv_up).reshape(2, 256, 8 * 256), w_o=f(w_o), vecs=vecs)
    cache_pool, cache_ckv, cache_krope = f(cache_pool), f(cache_ckv), f(cache_krope)
    in_maps = []
    for c in range(8):
        b, r = c // 4, c % 4
        xb = x_prompt[b].reshape(NBLK, 4, 128, D)[:, r].reshape(NTOK, D)
        m = dict(shared)
        m.update(_host_consts(c, NBLK))
        m["xp"] = np.ascontiguousarray(xb)
        m["xs"] = np.ascontiguousarray(x_sample[2 * c:2 * c + 2].reshape(64, D))
        m["cpool"] = np.ascontiguousarray(cache_pool[:, 2 * c:2 * c + 2])
        m["cckv"] = np.ascontiguousarray(cache_ckv[:, 2 * c:2 * c + 2])
        m["ckr"] = np.ascontiguousarray(cache_krope[:, 2 * c:2 * c + 2])
        in_maps.append(m)
    key = (NBLK, _nlayers)
    if key not in _NC_CACHE:
        _NC_CACHE[key] = build(NBLK, _nlayers)
    nc = _NC_CACHE[key]
    res = run_bass_kernel_spmd(nc, in_maps, core_ids=list(range(8))).results

    def unshard(name, width):
        out = np.zeros((B, NBLK, 4, 128, width), np.float32)
        for c in range(8):
            out[c // 4, :, c % 4] = res[c][name].reshape(NBLK, 128, width)
        return out.reshape(B, T, width)

    def unshard_l(name, width):
        out = np.zeros((2, B, NBLK, 4, 128, width), np.float32)
        for c in range(8):
            out[:, c // 4, :, c % 4] = res[c][name].reshape(2, NBLK, 128, width)
        return out.reshape(2, B, T, width)

    y_prompt = unshard("y_p", D)
    y_sample = np.concatenate([res[c]["y_s"].reshape(2, 32, D) for c in range(8)], 0)
    pool_p = np.stack([res[3]["o_pool_p"], res[7]["o_pool_p"]], 1)
    pool_s = np.concatenate([res[c]["o_pool_s"] for c in range(8)], 1)
    sgu_s = np.concatenate([res[c]["o_sgu_s"].reshape(2, 2, 32, D) for c in range(8)], 1)
    ckv_p = unshard_l("o_ckv_p", 256)
    kr_p = unshard_l("o_kr_p", 64)
    ckv_s = np.concatenate([res[c]["o_ckv_s"].reshape(2, 2, 32, 256) for c in range(8)], 1)
    kr_s = np.concatenate([res[c]["o_kr_s"].reshape(2, 2, 32, 64) for c in range(8)], 1)
    return (y_prompt, y_sample, pool_p, pool_s, sgu_s, ckv_p, kr_p, ckv_s, kr_s)
```

```python
import contextlib
import numpy as np
import concourse.bass as bass
import concourse.mybir as mybir
from concourse.bass_utils import run_bass_kernel_spmd

F32 = mybir.dt.float32
BF16 = mybir.dt.bfloat16
AF = mybir.ActivationFunctionType
ALU = mybir.AluOpType
AX = mybir.AxisListType

D = 1024
KD = 8
EPS = 1e-6
ATTN_SCALE = 192.0 ** -0.5
NEG = -30000.0


class Op:
    __slots__ = ("eng", "fn", "waits", "signal", "count", "kind", "chan", "chan_val")

    def __init__(self, eng, fn, kind):
        self.eng = eng
        self.fn = fn
        self.kind = kind
        self.waits = []
        self.signal = False
        self.count = None
        self.chan = None
        self.chan_val = None


class Prog:
    ENGS = ("pe", "act", "dve", "pool", "sp")

    def __init__(self, nc):
        self.nc = nc
        self.ops = {e: [] for e in self.ENGS}
        self.res = {}
        self.chan_tot = {}
        self.chan_sem = {}
        self.eng_sem = {}
        self.out_ops = []

    def _deps(self, op, reads, writes, join=()):
        deps = []
        for r in reads:
            st = self.res.get(r)
            if st is None:
                st = self.res[r] = [[], [], []]
            for wop in st[0]:
                deps.append((wop, "raw"))
        for w in list(writes) + list(join):
            st = self.res.get(w)
            if st is None:
                st = self.res[w] = [[], [], []]
            if w not in join:
                for wop in st[0]:
                    deps.append((wop, "waw"))
            else:
                for rd in st[2]:
                    deps.append((rd, "war"))
            for rd in st[1]:
                deps.append((rd, "war"))
        seen = set()
        for p, kind in deps:
            if p is op or id(p) in seen:
                continue
            if p.kind == "c" and p.eng == op.eng and op.kind == "c":
                if op.eng == "pe" or kind == "war":
                    continue
            seen.add(id(p))
            op.waits.append(p)
            if p.kind == "c":
                p.signal = True
        for r in reads:
            self.res[r][1].append(op)
        for w in writes:
            self.res[w] = [[op], [], self.res[w][1]]
        for w in join:
            self.res[w][0].append(op)

    def op(self, eng, fn, reads=(), writes=(), join=()):
        o = Op(eng, fn, "c")
        self._deps(o, reads, writes, join)
        self.ops[eng].append(o)
        return o

    def dma(self, eng, fn, reads=(), writes=(), chan=None, n=1, is_out=False):
        o = Op(eng, fn, "d")
        self._deps(o, reads, writes)
        tot = self.chan_tot.get(chan, 0) + 16 * n
        self.chan_tot[chan] = tot
        o.chan = chan
        o.chan_val = tot
        self.ops[eng].append(o)
        if is_out:
            self.out_ops.append(o)
        return o

    def collective(self, fn, reads=(), writes=(), chan=None):
        o = Op("pool", fn, "x")
        self._deps(o, reads, writes)
        assert chan not in self.chan_tot
        self.chan_tot[chan] = 1
        o.chan = chan
        o.chan_val = 1
        self.ops["pool"].append(o)
        return o

    def barrier(self):
        o = Op("sp", lambda e: e.nop(), "c")
        for e in self.ENGS:
            if e == "sp":
                continue
            for p in reversed(self.ops[e]):
                if p.kind == "c":
                    p.signal = True
                    o.waits.append(p)
                    break
        lastd = {}
        for e in self.ENGS:
            for p in self.ops[e]:
                if p.kind != "c":
                    lastd[p.chan] = p
        o.waits.extend(lastd.values())
        o.signal = True
        self.ops["sp"].append(o)
        for e in self.ENGS:
            if e == "sp":
                continue
            o2 = Op(e, lambda q: q.nop(), "c")
            o2.waits.append(o)
            self.ops[e].append(o2)
        self.res = {}

    def finish(self):
        o = Op("sp", lambda e: e.nop(), "c")
        for p in self.out_ops:
            o.waits.append(p)
        for e in self.ENGS:
            if e == "sp":
                continue
            for p in reversed(self.ops[e]):
                if p.kind == "c":
                    p.signal = True
                    o.waits.append(p)
                    break
        self.ops["sp"].append(o)

    def replay(self):
        nc = self.nc
        engobj = {"pe": nc.tensor, "act": nc.scalar, "dve": nc.vector, "pool": nc.gpsimd, "sp": nc.sync}
        EPOCH = 6000
        for i, c in enumerate(self.chan_tot):
            self.chan_sem[c] = nc.alloc_semaphore(name="c%d" % i)
        for e in self.ENGS:
            cnt = 0
            for o in self.ops[e]:
                if o.kind == "c" and o.signal:
                    ep = cnt // EPOCH
                    if (e, ep) not in self.eng_sem:
                        self.eng_sem[(e, ep)] = nc.alloc_semaphore(name="s_%s%d" % (e, ep))
                    o.count = (ep, cnt % EPOCH + 1)
                    cnt += 1
        prog = self

        def run(e):
            eng = engobj[e]
            seen = {}
            for o in prog.ops[e]:
                for p in o.waits:
                    if p.kind == "c":
                        sem, val = prog.eng_sem[(p.eng, p.count[0])], p.count[1]
                    else:
                        sem, val = prog.chan_sem[p.chan], p.chan_val
                    k = id(sem)
                    if seen.get(k, 0) >= val:
                        continue
                    seen[k] = val
                    eng.wait_ge(sem, val)
                r = o.fn(eng)
                if o.kind == "c":
                    if o.signal:
                        r.then_inc(prog.eng_sem[(e, o.count[0])], 1)
                elif o.kind == "d":
                    for ins in r:
                        ins.then_inc(prog.chan_sem[o.chan], 16)
                else:
                    r.then_inc(prog.chan_sem[o.chan])

        with nc.Block() as block:
            @block.tensor
            def _(t):
                run("pe")

            @block.scalar
            def _(t):
                run("act")

            @block.vector
            def _(t):
                run("dve")

            @block.gpsimd
            def _(t):
                run("pool")

            @block.sync
            def _(t):
                run("sp")


def build(NBLK, NLAYERS):
    NTOK = NBLK * 128
    NS = 64
    TOT = NTOK + NS
    NT = NBLK // 2
    nc = bass.Bass("TRN2", target_bir_lowering=False)

    def din(name, shape):
        return nc.dram_tensor(name, list(shape), F32, kind="ExternalInput").ap()

    def dout(name, shape):
        return nc.dram_tensor(name, list(shape), F32, kind="ExternalOutput").ap()

    xp = din("xp", [NTOK, D])
    xs = din("xs", [NS, D])
    cpool = din("cpool", [2, 2, 15, D])
    cckv = din("cckv", [2, 2, 1024, 256])
    ckr = din("ckr", [2, 2, 1024, 64])
    w_in_even = din("w_in_even", [2, D, 5120])
    w_pool = din("w_pool", [2, 4, 256, 256])
    ln_g = din("ln_g", [2, D])
    ln_b = din("ln_b", [2, D])
    w_sp = din("w_sp", [2, 4, 128, 128])
    b_sp = din("b_sp", [2, 4, 128])
    w_out_even = din("w_out_even", [2, 2048, D])
    w_in_odd = din("w_in_odd", [2, D, 1728])
    kv_norm = din("kv_norm", [2, 256])
    w_q_up = din("w_q_up", [2, 384, 8 * 192])
    w_kv_up = din("w_kv_up", [2, 256, 8 * 256])
    w_o = din("w_o", [2, D, D])
    vecs = din("vecs", [128, 96])
    ident_d = din("ident", [128, 128])
    mask_d = din("mask", [128, 512])
    selw_d = din("selw", [128, 8])
    rcnt_d = din("rcnt", [128, 4 * 16])
    cs_tm = din("cs_tm", [128, (NBLK + 1) * 64])
    cs_fm = din("cs_fm", [64, 2 * TOT])

    y_p = dout("y_p", [NTOK, D])
    y_s = dout("y_s", [NS, D])
    o_pool_p = dout("o_pool_p", [2, 15, D])
    o_pool_s = dout("o_pool_s", [2, 2, 15, D])
    o_sgu_s = dout("o_sgu_s", [2, NS, D])
    o_ckv_p = dout("o_ckv_p", [2, NTOK, 256])
    o_kr_p = dout("o_kr_p", [2, NTOK, 64])
    o_ckv_s = dout("o_ckv_s", [2, NS, 256])
    o_kr_s = dout("o_kr_s", [2, NS, 64])

    HW = 8 * NBLK * 16
    bh_in = [nc.dram_tensor("bh_in%d" % e, [128, HW], F32) for e in range(2)]
    bh_out = [nc.dram_tensor("bh_out%d" % e, [4 * 128, HW], F32) for e in range(2)]
    NSPL = max(1, NBLK // 8)
    BPS = NBLK // NSPL
    TPS = BPS * 128
    bk_in = [[nc.dram_tensor("bk_in%d_%d" % (o, sp), [TPS, 320], BF16) for sp in range(NSPL)] for o in range(2)]
    bk_out = [[nc.dram_tensor("bk_out%d_%d" % (o, sp), [4 * TPS, 320], BF16) for sp in range(NSPL)] for o in range(2)]

    P = Prog(nc)
    es = contextlib.ExitStack()

    def S(name, shape, dt):
        return es.enter_context(nc.sbuf_tensor("t_" + name, list(shape), dt))

    with es:
        ps = [es.enter_context(nc.psum_tensor("ps%d" % i, [128, 512], F32)) for i in range(8)]
        bankctr = [0]

        def nbank():
            b = bankctr[0] % 6
            bankctr[0] += 1
            return b

        xT = S("xT", [128, KD, TOT], F32)
        identf = S("identf", [128, 128], F32)
        identb = S("identb", [128, 128], BF16)
        onesb = S("onesb", [128, 128], BF16)
        epsc = S("epsc", [128, 1], F32)
        vec = S("vec", [128, 96], F32)
        selw = S("selw", [128, 8], F32)
        rcnt = S("rcnt", [128, 4, 16], F32)
        NWB = 4
        wst = [S("wst%d" % i, [128, 256], F32) for i in range(1)]
        wbf = [S("wbf%d" % i, [128, 8 * 128], BF16) for i in range(NWB)]
        hT = S("hT", [128, KD, 256], BF16)
        sqb = S("sqb", [128, KD, 256], BF16)
        rstd = S("rstd", [128, 258], F32)
        yT = S("yT", [128, KD, 256], F32)
        tmpT = S("tmpT", [128, KD, 256], F32)
        RB = max(81920, 24 * NTOK + 4 * NBLK * 516)
        R = S("R", [128, RB], mybir.dt.uint8)

        R2 = S("R2", [128, 23040], mybir.dt.uint8)

        def r2view(off, shape, dt, parts=128):
            n = 1
            for s_ in shape[1:]:
                n *= s_
            esz = 4 if dt == F32 else 2
            v = R2[0:parts, off:off + n * esz].bitcast(dt)
            if len(shape) == 3:
                v = v.rearrange("p (a b) -> p a b", a=shape[1])
            elif len(shape) == 4:
                v = v.rearrange("p (a b c) -> p a b c", a=shape[1], b=shape[2])
            return v

        def rview(off, shape, dt):
            n = 1
            for s in shape[1:]:
                n *= s
            esz = 4 if dt == F32 else 2
            v = R[:, off:off + n * esz].bitcast(dt)
            if len(shape) == 3:
                v = v.rearrange("p (a b) -> p a b", a=shape[1])
            elif len(shape) == 4:
                v = v.rearrange("p (a b c) -> p a b c", a=shape[1], b=shape[2])
            return v

        def ld(dst, src, name, eng="sp"):
            P.dma(eng, lambda q: [q.dma_start(out=dst, in_=src)], writes=[name], chan=name)

        ld(identf[:], ident_d[:, :], "identf")
        ld(vec[:], vecs[:, :], "vec")
        ld(selw[:], selw_d[:, :], "selw")
        ld(rcnt[:].rearrange("p a b -> p (a b)"), rcnt_d[:, :], "rcnt")
        P.op("dve", lambda q: q.tensor_copy(out=identb[:], in_=identf[:]), reads=["identf"], writes=["identb"])
        P.op("pool", lambda q: q.memset(onesb[:], 1.0 / 1024.0), writes=["onesb"])
        P.op("pool", lambda q: q.memset(epsc[:], EPS), writes=["epsc"])

        evq = [0]

        def evac_copy(out, in_, bank, wres, rres=(), scale=None, join=()):
            evq[0] += 1
            if evq[0] % 2 == 0:
                P.op("act", lambda q: q.activation(out=out, in_=in_, func=AF.Copy, scale=(1.0 if scale is None else scale)),
                     reads=list(rres), writes=[("ps", bank)] + list(wres), join=join)
            else:
                if scale is None:
                    P.op("dve", lambda q: q.tensor_copy(out=out, in_=in_), reads=list(rres), writes=[("ps", bank)] + list(wres), join=join)
                else:
                    P.op("dve", lambda q: q.tensor_scalar(out=out, in0=in_, scalar1=scale, scalar2=None, op0=ALU.mult),
                         reads=list(rres), writes=[("ps", bank)] + list(wres), join=join)

        wctr = [0]

        WBA = {}
        CH = {}
        rlctr = [0]

        class WV:
            def __init__(self, name):
                self.name = name

            def __getitem__(self, idx):
                _, ks, cs = idx
                return (self.name, ks.start or 0, cs.start)

        PIECE = {}

        def conv(name, src2d, rows, cols, piece=None):
            A = nc.dram_tensor("wbA_" + name, [rows, cols], BF16)
            WBA[name] = A
            piece = piece or cols
            PIECE[name] = piece
            for pi in range((cols + piece - 1) // piece):
                cs = slice(pi * piece, min(cols, (pi + 1) * piece))
                P.dma("pool", lambda q, cs=cs: [q.dma_start(out=A.ap()[:, cs], in_=src2d[:, cs])], writes=[("wbA", name, pi)],
                      chan=("cv", name, pi))

        def relay(name, k0, kdim, c0, cols):
            i = rlctr[0]
            rlctr[0] += 1
            B = nc.dram_tensor("wbB_%s_%d_%d" % (name, k0, c0), [128, kdim * cols], BF16)
            CH[(name, k0, c0)] = (B, kdim, cols)
            src = WBA[name].ap().rearrange("(k p) c -> p k c", p=128)[:, k0:k0 + kdim, c0:c0 + cols]
            slot = ("rlslot", i % 8)
            P.dma("sp", lambda q: [q.dma_start(out=B.ap().rearrange("p (k c) -> p k c", k=kdim), in_=src)],
                  reads=[("wbA", name, c0 // PIECE[name]), slot], writes=[("wbB", name, k0, c0), slot], chan=slot)

        def relay_layer(layer):
            if layer % 2 == 0:
                e = layer // 2
                for j in range(40):
                    relay("win%d" % e, 0, 8, j * 128, 128)
                for dc in range(8):
                    relay("wo%d" % e, 0, 8, dc * 128, 128)
                    relay("wo%d" % e, 8, 8, dc * 128, 128)
            else:
                o = layer // 2
                for h in range(8):
                    relay("wku%d" % o, 0, 2, h * 256, 256)
                for (c0, w) in [(384, 128), (512, 128), (640, 64), (0, 128), (128, 128), (256, 128)] + [(704 + ec * 128, 128) for ec in range(8)]:
                    relay("wio%d" % o, 0, 8, c0, w)
                for h in range(8):
                    relay("wqu%d" % o, 0, 3, h * 192, 192)
                for dc in range(8):
                    relay("wov%d" % o, 0, 8, dc * 128, 128)

        def conv_layer(layer):
            if layer % 2 == 0:
                e = layer // 2
                conv("win%d" % e, w_in_even[e], D, 5120, piece=1024)
                conv("wo%d" % e, w_out_even[e], 2048, D)
            else:
                o = layer // 2
                conv("wku%d" % o, w_kv_up[o], 256, 2048)
                conv("wio%d" % o, w_in_odd[o], D, 1728)
                conv("wqu%d" % o, w_q_up[o], 384, 1536)
                conv("wov%d" % o, w_o[o], D, D)

        def wload(ref, kdim, cols):
            i = wctr[0]
            wctr[0] += 1
            B, kd_, cols_ = CH[ref]
            assert kd_ == kdim and cols_ == cols, (ref, kdim, cols)
            wb = wbf[i % NWB]
            bname = "wbf%d" % (i % NWB)
            n = kdim * cols
            wbv = wb[:, 0:n].rearrange("p (k c) -> p k c", k=kdim)
            P.dma("sp", lambda q: [q.dma_start(out=wb[:, 0:n], in_=B.ap()[:, :])], reads=[("wbB",) + ref], writes=[bname], chan=bname)
            return wbv, bname

        xin = tmpT[:].rearrange("p k n -> p (k n)")[:, 0:D]

        def load_x(src_rows, c0, n):
            P.dma("sp", lambda q: [q.dma_start(out=xin[0:n, :], in_=src_rows)], writes=["tmpT"], chan="xin")
            for half in range(2):
                b = nbank()

                def tr(q, half=half, b=b):
                    r = None
                    for j in range(4):
                        k = half * 4 + j
                        r = q.transpose(ps[b][:, j * 128:j * 128 + n], xin[0:n, k * 128:(k + 1) * 128], identf[0:n, 0:n])
                    return r
                P.op("pe", tr, reads=["tmpT", "identf"], writes=[("ps", b)])
                src = ps[b][:].rearrange("p (j t) -> p j t", j=4)[:, :, 0:n]
                evac_copy(xT[:, half * 4:half * 4 + 4, c0:c0 + n], src, b, [("xT", c0 // 256)])

        yout = yT[:].rearrange("p k n -> p (k n)")[:, 0:D]

        def store_x(dst_rows, c0, n):
            for half in range(2):
                b = nbank()

                def tr(q, half=half, b=b):
                    r = None
                    for j in range(4):
                        k = half * 4 + j
                        r = q.transpose(ps[b][0:n, j * 128:(j + 1) * 128], xT[:, k, c0:c0 + n], identf[:, :])
                    return r
                P.op("pe", tr, reads=[("xT", c0 // 256), "identf"], writes=[("ps", b)])
                evac_copy(yout[0:n, half * 512:(half + 1) * 512], ps[b][0:n, :], b, ["yT"])
            P.dma("sp", lambda q: [q.dma_start(out=dst_rows, in_=yout[0:n, :])], reads=["yT"],
                  chan="yout", is_out=True)

        for b in range(NBLK):
            load_x(xp[b * 128:(b + 1) * 128, :], b * 128, 128)
        load_x(xs[:, :], NTOK, NS)

        def rms_stats(src4, n, srcres, outname="rstd", kdim=KD, scale=1.0):
            P.op("act", lambda q: q.activation(out=sqb[:, 0:kdim, 0:n], in_=src4, func=AF.Square), reads=list(srcres), writes=["sqb"])
            b = nbank()

            def mm(q):
                r = None
                for k in range(kdim):
                    r = q.matmul(ps[b][:, 0:n], lhsT=onesb[:], rhs=sqb[:, k, 0:n], start=(k == 0), stop=(k == kdim - 1))
                return r
            P.op("pe", mm, reads=["sqb", "onesb"], writes=[("ps", b)])
            P.op("act", lambda q: q.activation(out=rstd[:, 0:n], in_=ps[b][:, 0:n], func=AF.Sqrt, bias=epsc[:, 0:1], scale=scale),
                 reads=["epsc"], writes=[("ps", b), outname])
            P.op("dve", lambda q: q.reciprocal(out=rstd[:, 0:n], in_=rstd[:, 0:n]), reads=[outname], writes=[outname])

        KS = 5

        def pre_norm(layer, xview, n, xres):
            gb = vec[:, layer * 8:layer * 8 + 8].unsqueeze(2).to_broadcast([128, KD, n])
            P.op("pool", lambda q: q.tensor_tensor(out=tmpT[:, :, 0:n], in0=xview, in1=gb, op=ALU.mult),
                 reads=list(xres) + ["vec"], writes=["tmpT"])
            rms_stats(xview, n, xres)
            rb1 = rstd[:, 0:n].unsqueeze(1).to_broadcast([128, KS, n])
            rb2 = rstd[:, 0:n].unsqueeze(1).to_broadcast([128, KD - KS, n])
            P.op("dve", lambda q: q.tensor_tensor(out=hT[:, 0:KS, 0:n], in0=tmpT[:, 0:KS, 0:n], in1=rb1, op=ALU.mult),
                 reads=["tmpT", "rstd"], writes=["hT"])
            P.op("pool", lambda q: q.tensor_tensor(out=hT[:, KS:KD, 0:n], in0=tmpT[:, KS:KD, 0:n], in1=rb2, op=ALU.mult),
                 reads=["tmpT", "rstd"], join=["hT"])

        def pre_norm_lite_stages(layer, xview, n, xres):
            def s1():
                P.op("act", lambda q: q.activation(out=hT[:, :, 0:n], in_=xview, func=AF.Square), reads=list(xres), writes=["hT"])
            bb = [None]

            def s2():
                pre_norm_lite_mid(n, bb)

            def s3():
                for k in range(KD):
                    P.op("dve", lambda q, k=k: q.scalar_tensor_tensor(out=hT[:, k, 0:n], in0=xview[:, k, :], scalar=vec[:, layer * 8 + k:layer * 8 + k + 1],
                                                                      in1=rstd[:, 0:n], op0=ALU.mult, op1=ALU.mult),
                         reads=list(xres) + ["vec", "rstd"], **({"writes": ["hT"]} if k == 0 else {"join": ["hT"]}))
            return [s1, s2, s3]

        def pre_norm_lite_mid(n, bb):
            b = nbank()

            def mm(q):
                r = None
                for k in range(KD):
                    r = q.matmul(ps[b][:, 0:n], lhsT=onesb[:], rhs=hT[:, k, 0:n], start=(k == 0), stop=(k == KD - 1))
                return r
            P.op("pe", mm, reads=["hT", "onesb"], writes=[("ps", b)])
            P.op("act", lambda q: q.activation(out=rstd[:, 0:n], in_=ps[b][:, 0:n], func=AF.Sqrt, bias=epsc[:, 0:1], scale=1.0),
                 reads=["epsc"], writes=[("ps", b), "rstd"])
            P.op("dve", lambda q: q.reciprocal(out=rstd[:, 0:n], in_=rstd[:, 0:n]), reads=["rstd"], writes=["rstd"])

        def post_norm_update(layer, c0, n):
            xres = [("xT", c0 // 256)]
            gb = vec[:, 32 + layer * 8:32 + layer * 8 + 8].unsqueeze(2).to_broadcast([128, KD, n])
            P.op("pool", lambda q: q.tensor_tensor(out=tmpT[:, :, 0:n], in0=yT[:, :, 0:n], in1=gb, op=ALU.mult),
                 reads=["yT", "vec"], writes=["tmpT"])
            rms_stats(yT[:, :, 0:n], n, ["yT"])
            rb1 = rstd[:, 0:n].unsqueeze(1).to_broadcast([128, KS, n])
            rb2 = rstd[:, 0:n].unsqueeze(1).to_broadcast([128, KD - KS, n])
            P.op("dve", lambda q: q.tensor_tensor(out=tmpT[:, 0:KS, 0:n], in0=tmpT[:, 0:KS, 0:n], in1=rb1, op=ALU.mult),
                 reads=["tmpT", "rstd"], writes=["tmpTa"])
            P.op("pool", lambda q: q.tensor_tensor(out=tmpT[:, KS:KD, 0:n], in0=tmpT[:, KS:KD, 0:n], in1=rb2, op=ALU.mult),
                 reads=["tmpT", "rstd"], writes=["tmpTb"])
            P.op("dve", lambda q: q.tensor_tensor(out=xT[:, 0:KS, c0:c0 + n], in0=xT[:, 0:KS, c0:c0 + n], in1=tmpT[:, 0:KS, 0:n], op=ALU.add),
                 reads=["tmpTa", "tmpT"] + xres, writes=xres)
            P.op("pool", lambda q: q.tensor_tensor(out=xT[:, KS:KD, c0:c0 + n], in0=xT[:, KS:KD, c0:c0 + n], in1=tmpT[:, KS:KD, 0:n], op=ALU.add),
                 reads=["tmpTb", "tmpT"] + xres, join=xres)

        def mm_fm(bank, wv, col0, ncol, rhs3, n, kdim=KD, first=True, last=True):
            def f(q):
                r = None
                for k in range(kdim):
                    r = q.matmul(ps[bank][0:ncol, 0:n], lhsT=wv[:, k, col0:col0 + ncol], rhs=rhs3[:, k, 0:n],
                                 start=(first and k == 0), stop=(last and k == kdim - 1))
                return r
            return f

        HWB = HW * 4
        halo_s = rview(0, [128, 8, NBLK, 16], F32)
        aT = rview(8192, [128, 8, 2, 144], F32)
        sA = rview(17408, [128, 8, 2, 144], F32)
        sB = rview(26624, [128, 8, 2, 144], F32)
        wp_st = rview(35840, [128, 8, 256], F32)
        gT = rview(44032, [128, 16, 256], BF16)
        hb = R[:, 44032:44032 + HWB].bitcast(F32)
        mixT = rview(52224, [128, 16, 256], BF16)
        vtm = rview(60416, [128, 2, D], F32)
        uT = rview(68608, [128, KD, 256], BF16)
        halo_c = rview(72704, [128, 8, NBLK, 16], F32)
        dT = sqb
        vbf = r2view(0, [128, 2, D], BF16)
        atm = yT[:].rearrange("p k n -> p (k n)")[:, 0:D]
        histtm = tmpT[:].rearrange("p k n -> p (k n)")[:, 0:D]
        st4 = S("st4", [128, 16], F32)
        wp_bf = r2view(4096, [128, 8, 256], BF16)
        ws_f = r2view(8192, [128, 4, 128], F32)
        wsT = r2view(10240, [128, 4, 128], BF16)
        wsd_f = r2view(11264, [64, 4, 64], F32, parts=64)
        wsdT = r2view(12288, [64, 4, 64], BF16, parts=64)
        lng = r2view(12800, [128, D], F32)
        lnb = r2view(16896, [128, D], F32)
        bsp = r2view(20992, [128, 4, 128], F32)

        def even_layer(e):
            layer = 2 * e
            win = WV("win%d" % e)
            nh = 16 * NBLK
            xv = xT[:, :, 0:NTOK].rearrange("p k (b w) -> p k b w", w=128)[:, :, :, 112:128]
            P.op("pool", lambda q: q.tensor_copy(out=yT[:, :, 0:nh].rearrange("p k (b w) -> p k b w", w=16), in_=xv),
                 reads=[("xT", t) for t in range(NT)], writes=["yT"])
            pre_norm(layer, yT[:, :, 0:nh], nh, ["yT"])
            for ec in range(8):
                wv, wn = wload(win[:, :, ec * 128:(ec + 1) * 128], KD, 128)
                b = nbank()
                P.op("pe", mm_fm(b, wv, 0, 128, hT, nh), reads=[wn, "hT"], writes=[("ps", b)])
                evac_copy(halo_c[:, ec, :, :], ps[b][:, 0:nh].rearrange("p (b w) -> p b w", w=16), b, ["halo_c"])
            P.dma("sp", lambda q: [q.dma_start(out=bh_in[e].ap()[:, :], in_=halo_c.rearrange("p k b w -> p (k b w)"))],
                  reads=["halo_c"], writes=[("bh_in", e)], chan=("bh_in", e))
            P.collective(lambda q: q.collective_compute("AllGather", ALU.bypass, replica_groups=[[0, 1, 2, 3], [4, 5, 6, 7]],
                                                        ins=[bh_in[e].ap().opt()], outs=[bh_out[e].ap().opt()]),
                         reads=[("bh_in", e)], writes=[("bh_out", e)], chan=("ccH", e))
            hg = hb.rearrange("p (k b w) -> p k b w", k=8, b=NBLK)
            for r in range(4):
                P.dma("sp", lambda q, r=r: [q.dma_start(out=hb, in_=bh_out[e].ap()[r * 128:(r + 1) * 128, :])],
                      reads=[("bh_out", e)], writes=["gT"], chan="hb")
                if r == 0:
                    P.op("dve", lambda q: q.tensor_scalar(out=halo_s, in0=hg, scalar1=selw[:, 0:1], scalar2=None, op0=ALU.mult),
                         reads=["gT", "selw"], writes=["halo_s"])
                else:
                    P.op("dve", lambda q, r=r: q.scalar_tensor_tensor(out=halo_s, in0=hg, scalar=selw[:, r:r + 1], in1=halo_s,
                                                                     op0=ALU.mult, op1=ALU.add),
                         reads=["gT", "selw", "halo_s"], writes=["halo_s"])
                if r == 3 and NBLK > 1:
                    P.op("dve", lambda q: q.scalar_tensor_tensor(out=halo_s[:, :, 1:NBLK, :], in0=hg[:, :, 0:NBLK - 1, :],
                                                                 scalar=selw[:, 4:5], in1=halo_s[:, :, 1:NBLK, :],
                                                                 op0=ALU.mult, op1=ALU.add),
                         reads=["gT", "selw", "halo_s"], writes=["halo_s"])
            P.dma("sp", lambda q: [q.dma_start(out=wp_st, in_=w_pool[e].rearrange("g (i p) o -> p (g i) o", p=128))],
                  writes=["wp_st"], chan="wp_st")
            P.op("dve", lambda q: q.tensor_copy(out=wp_bf[:], in_=wp_st), reads=["wp_st"], writes=["wp_bf"])
            P.dma("sp", lambda q: [q.dma_start(out=ws_f[:], in_=w_sp[e].rearrange("g i j -> i g j"))], writes=["ws_f"], chan="ws_f")
            b = nbank()

            def trw(q):
                r = None
                for g in range(4):
                    r = q.transpose(ps[b][:, g * 128:(g + 1) * 128], ws_f[:, g, :], identf[:])
                return r
            P.op("pe", trw, reads=["ws_f", "identf"], writes=[("ps", b)])
            P.op("dve", lambda q: q.tensor_copy(out=wsT[:].rearrange("p g i -> p (g i)"), in_=ps[b][:, :]), writes=[("ps", b), "wsT"])
            P.op("pool", lambda q: q.memset(wsT[64:128, :, 0:64], 0.0), reads=["wsT"], writes=["wsT"])
            P.op("pool", lambda q: q.memset(wsd_f[:], 0.0), writes=["wsd_f"])
            P.dma("sp", lambda q: [q.dma_start(out=wsd_f[0:32, :, 0:32], in_=w_sp[e, :, 0:32, 0:32].rearrange("g i j -> i g j")),
                                   q.dma_start(out=wsd_f[32:64, :, 32:64], in_=w_sp[e, :, 0:32, 0:32].rearrange("g i j -> i g j"))],
                  reads=["wsd_f"], writes=["wsd_f"], chan="wsd_f", n=2)
            b2 = nbank()

            def trw2(q):
                r = None
                for g in range(4):
                    r = q.transpose(ps[b2][0:64, g * 64:(g + 1) * 64], wsd_f[:, g, :], identf[0:64, 0:64])
                return r
            P.op("pe", trw2, reads=["wsd_f", "identf"], writes=[("ps", b2)])
            P.op("dve", lambda q: q.tensor_copy(out=wsdT[:].rearrange("p g i -> p (g i)"), in_=ps[b2][0:64, 0:256]),
                 writes=[("ps", b2), "wsdT"])
            ld(lng[:], ln_g[e:e + 1, :].broadcast_to([128, D]), "lng")
            ld(lnb[:], ln_b[e:e + 1, :].broadcast_to([128, D]), "lnb")
            ld(bsp[:].rearrange("p g i -> p (g i)"), b_sp[e:e + 1].rearrange("o g i -> o (g i)").broadcast_to([128, 512]), "bsp")
            return dict(win=win)

        def even_tile(e, ctx, t):
            layer = 2 * e
            sample = (t == NT)
            NB, W = (2, 32) if sample else (2, 128)
            n = NB * W
            c0 = NTOK if sample else t * 256
            xres = [("xT", c0 // 256)]
            win = ctx["win"]
            if t == 0:
                pre_norm(layer, xT[:, :, c0:c0 + n], n, xres)
            nxt = None
            if t < NT:
                c0n, nn = (NTOK, 64) if t + 1 == NT else ((t + 1) * 256, 256)
                nxt = pre_norm_lite_stages(layer, xT[:, :, c0n:c0n + nn], nn, [("xT", c0n // 256)])
            if sample:
                for s in range(2):
                    P.dma("sp", lambda q, s=s: [q.dma_start(out=histtm[0:15, :], in_=cpool[e, s])], writes=["tmpT"], chan="histtm")
                    for half in range(2):
                        b = nbank()

                        def trh(q, half=half, b=b):
                            r = None
                            for j in range(4):
                                k = half * 4 + j
                                r = q.transpose(ps[b][:, j * 16:j * 16 + 15], histtm[0:15, k * 128:(k + 1) * 128], identf[0:15, 0:15])
                            return r
                        P.op("pe", trh, reads=["tmpT", "identf"], writes=[("ps", b)])
                        P.op("dve", lambda q, half=half, b=b, s=s: q.tensor_copy(
                            out=aT[:, half * 4:half * 4 + 4, s, 1:16], in_=ps[b][:, 0:64].rearrange("p (j w) -> p j w", j=4)[:, :, 0:15]),
                            writes=[("ps", b), "aT"])
            else:
                P.op("pool", lambda q: q.tensor_copy(out=aT[:, :, :, 0:16], in_=halo_s[:, :, t * 2:t * 2 + 2, :]),
                     reads=["halo_s"], writes=["aT"])
            need_tm = sample or (t == NT - 1)
            m = 64 if sample else 128
            for ec in range(8):
                wv, wn = wload(win[:, :, ec * 128:(ec + 1) * 128], KD, 128)
                b = nbank()
                P.op("pe", mm_fm(b, wv, 0, 128, hT, n), reads=[wn, "hT"], writes=[("ps", b)])
                evac_copy(aT[:, ec, 0:NB, 16:16 + W], ps[b][:, 0:n].rearrange("p (b w) -> p b w", b=NB), b, ["aT"])
                if need_tm:
                    b = nbank()
                    tc0 = 0 if sample else 128

                    def mmtm(q, b=b, wv=wv, tc0=tc0):
                        r = None
                        for k in range(KD):
                            r = q.matmul(ps[b][0:m, 0:128], lhsT=hT[:, k, tc0:tc0 + m], rhs=wv[:, k, :], start=(k == 0), stop=(k == KD - 1))
                        return r
                    P.op("pe", mmtm, reads=[wn, "hT"], writes=[("ps", b)])
                    evac_copy(atm[0:m, ec * 128:(ec + 1) * 128], ps[b][0:m, 0:128], b, ["yT"])
            if need_tm:
                if sample:
                    P.dma("sp", lambda q: [q.dma_start(out=o_pool_s[e, 0], in_=atm[17:32, :]),
                                           q.dma_start(out=o_pool_s[e, 1], in_=atm[49:64, :])],
                          reads=["yT"], chan="atm", n=2, is_out=True)
                else:
                    P.dma("sp", lambda q: [q.dma_start(out=o_pool_p[e], in_=atm[113:128, :])], reads=["yT"], chan="atm", is_out=True)
            def sh(dst, src, k0, d):
                return lambda q: q.tensor_tensor(out=dst[:, k0:8, 0:NB, d:16 + W], in0=src[:, k0:8, 0:NB, d:16 + W],
                                                 in1=src[:, k0:8, 0:NB, 0:16 + W - d], op=ALU.add)
            P.op("dve", sh(sA, aT, 0, 1), reads=["aT"], writes=["sA", "sA2"])
            P.op("pool", sh(sB, sA, 2, 2), reads=["sA"], writes=["sB", "sB2"])
            P.op("dve", sh(sA, sB, 4, 4), reads=["sB"], writes=["sA2"])
            P.op("pool", sh(sB, sA, 6, 8), reads=["sA2"], writes=["sB2"])
            srcs = [(sA, ["sA"]), (sB, ["sB"]), (sA, ["sA2"]), (sB, ["sB2"])]
            for g in range(4):
                sbuf_, sres = srcs[g]
                w = 2 ** (g + 1)
                P.op("dve", lambda q, g=g, sbuf_=sbuf_, w=w: q.scalar_tensor_tensor(
                    out=dT[:, 2 * g:2 * g + 2, 0:n].rearrange("p k (b w) -> p k b w", b=NB),
                    in0=sbuf_[:, 2 * g:2 * g + 2, 0:NB, 16:16 + W], scalar=1.0 / w, in1=aT[:, 2 * g:2 * g + 2, 0:NB, 16:16 + W],
                    op0=ALU.mult, op1=ALU.subtract), reads=sres + ["aT"], writes=["sqb"])
                if (not sample) and t == 0:
                    rc = rcnt[:, g, :].unsqueeze(1).to_broadcast([128, 2, 16])
                    P.op("pool", lambda q, g=g, sbuf_=sbuf_, rc=rc: q.tensor_tensor(
                        out=tmpT[:, 0:2, 0:16], in0=sbuf_[:, 2 * g:2 * g + 2, 0, 16:32], in1=rc, op=ALU.mult),
                        reads=sres + ["rcnt"], writes=["tmpT"])
                    P.op("dve", lambda q, g=g: q.tensor_tensor(out=dT[:, 2 * g:2 * g + 2, 0:16], in0=tmpT[:, 0:2, 0:16],
                                                               in1=aT[:, 2 * g:2 * g + 2, 0, 16:32], op=ALU.subtract),
                         reads=["tmpT", "aT"], writes=["sqb"])
            nblk_tm = 1 if sample else 2
            for ec in range(8):
                wv, wn = wload(win[:, :, 2048 + ec * 128:2048 + (ec + 1) * 128], KD, 128)
                for blk in range(nblk_tm):
                    b = nbank()

                    def mmv(q, b=b, wv=wv, blk=blk):
                        r = None
                        for k in range(KD):
                            r = q.matmul(ps[b][0:m, 0:128], lhsT=hT[:, k, blk * 128:blk * 128 + m], rhs=wv[:, k, :],
                                         start=(k == 0), stop=(k == KD - 1))
                        return r
                    P.op("pe", mmv, reads=[wn, "hT"], writes=[("ps", b)])
                    P.op("act", lambda q, b=b, blk=blk, ec=ec: q.activation(out=vtm[0:m, blk, ec * 128:(ec + 1) * 128], in_=ps[b][0:m, 0:128],
                                                                         func=AF.Gelu), writes=[("ps", b), ("vtm", blk)])
            sqscr = tmpT[:].rearrange("p k n -> p (k n)")[:, 0:D]
            for blk in range(nblk_tm):
                vb = vtm[0:m, blk, :]
                vr = ("vtm", blk)
                P.op("dve", lambda q, vb=vb: q.reduce_sum(out=st4[0:m, 0:1], in_=vb, axis=AX.X), reads=[vr], writes=["st_a"])
                P.op("act", lambda q, vb=vb: q.activation(out=sqscr[0:m, :], in_=vb, func=AF.Square, accum_out=st4[0:m, 1:2]),
                     reads=[vr], writes=["tmpT", "st_b"])
                P.op("pool", lambda q: q.tensor_scalar(out=st4[0:m, 2:3], in0=st4[0:m, 0:1], scalar1=1.0 / D, scalar2=None, op0=ALU.mult),
                     reads=["st_a"], writes=["st_c"])
                P.op("dve", lambda q: q.tensor_tensor(out=st4[0:m, 3:4], in0=st4[0:m, 2:3], in1=st4[0:m, 2:3], op=ALU.mult),
                     reads=["st_c"], writes=["st_d"])
                P.op("pool", lambda q: q.tensor_scalar(out=st4[0:m, 4:5], in0=st4[0:m, 1:2], scalar1=1.0 / D, scalar2=None, op0=ALU.mult),
                     reads=["st_b"], writes=["st_e"])
                P.op("dve", lambda q: q.tensor_tensor(out=st4[0:m, 5:6], in0=st4[0:m, 4:5], in1=st4[0:m, 3:4], op=ALU.subtract),
                     reads=["st_e", "st_d"], writes=["st_f"])
                P.op("act", lambda q: q.activation(out=st4[0:m, 6:7], in_=st4[0:m, 5:6], func=AF.Sqrt, bias=epsc[0:m, 0:1], scale=1.0),
                     reads=["st_f", "epsc"], writes=["st_g"])
                P.op("dve", lambda q: q.reciprocal(out=st4[0:m, 7:8], in_=st4[0:m, 6:7]), reads=["st_g"], writes=["st_h"])
                P.op("dve", lambda q, vb=vb: q.tensor_scalar(out=vb, in0=vb, scalar1=st4[0:m, 2:3], scalar2=st4[0:m, 7:8],
                                                          op0=ALU.subtract, op1=ALU.mult), reads=[vr, "st_c", "st_h"], writes=[vr])
                P.op("pool", lambda q, vb=vb: q.tensor_tensor(out=vb, in0=vb, in1=lng[0:m, :], op=ALU.mult), reads=[vr, "lng"], writes=[vr])
                P.op("dve", lambda q, vb=vb: q.tensor_tensor(out=vb, in0=vb, in1=lnb[0:m, :], op=ALU.add), reads=[vr, "lnb"], writes=[vr])
                P.op("pool", lambda q, vb=vb, blk=blk: q.tensor_copy(out=vbf[0:m, blk, :], in_=vb), reads=[vr], writes=[("vbf", blk)])
                if sample:
                    P.dma("sp", lambda q: [q.dma_start(out=o_sgu_s[e], in_=vtm[0:64, 0, :])], reads=[vr], chan="vout", is_out=True)
            for ec in range(8):
                wv, wn = wload(win[:, :, 1024 + ec * 128:1024 + (ec + 1) * 128], KD, 128)
                b = nbank()
                P.op("pe", mm_fm(b, wv, 0, 128, hT, n), reads=[wn, "hT"], writes=[("ps", b)])
                P.op("act", lambda q, b=b, ec=ec: q.activation(out=uT[:, ec, 0:n], in_=ps[b][:, 0:n], func=AF.Gelu),
                     writes=[("ps", b), "uT"])
            for ec in range(16):
                wv, wn = wload(win[:, :, 3072 + ec * 128:3072 + (ec + 1) * 128], KD, 128)
                b = nbank()
                P.op("pe", mm_fm(b, wv, 0, 128, hT, n), reads=[wn, "hT"], writes=[("ps", b)])
                P.op("act", lambda q, b=b, ec=ec: q.activation(out=gT[:, ec, 0:n], in_=ps[b][:, 0:n], func=AF.Silu),
                     writes=[("ps", b), "gT"])
            if nxt:
                nxt[0]()
            P.op("pool", lambda q: q.tensor_tensor(out=uT[:, :, 0:n], in0=uT[:, :, 0:n], in1=gT[:, 8:16, 0:n], op=ALU.mult),
                 reads=["uT", "gT"], writes=["uT"])
            wI = 64 if sample else 128
            for blk in range(nblk_tm):
                for half in range(2):
                    b = nbank()

                    def mms(q, b=b, blk=blk, half=half):
                        r = None
                        for j in range(4):
                            dk = half * 4 + j
                            g = dk // 2
                            rhs = wsdT[0:64, g, :] if sample else wsT[:, g, :]
                            r = q.matmul(ps[b][:, j * 128:j * 128 + wI], lhsT=vbf[0:m, blk, dk * 128:(dk + 1) * 128], rhs=rhs,
                                         start=True, stop=True)
                        return r
                    P.op("pe", mms, reads=[("vbf", blk), "wsT", "wsdT"], writes=[("ps", b)])
                    ps4 = ps[b][:].rearrange("p (g c i) -> p g c i", g=2, c=2)
                    if sample:
                        for s in range(2):
                            bb = bsp[:, half * 2:half * 2 + 2, 0:32].unsqueeze(2).to_broadcast([128, 2, 2, 32])
                            P.op("dve", lambda q, s=s, half=half, bb=bb, ps4=ps4: q.tensor_tensor(
                                out=tmpT[:, half * 4:half * 4 + 4, s * 32:(s + 1) * 32].rearrange("p (g c) i -> p g c i", g=2),
                                in0=ps4[:, :, :, s * 32:(s + 1) * 32], in1=bb, op=ALU.add),
                                reads=["bsp"], writes=[("ps", b), "tmpT"])
                    else:
                        bb = bsp[:, half * 2:half * 2 + 2, :].unsqueeze(2).to_broadcast([128, 2, 2, 128])
                        P.op("dve", lambda q, half=half, blk=blk, bb=bb, ps4=ps4: q.tensor_tensor(
                            out=tmpT[:, half * 4:half * 4 + 4, blk * 128:(blk + 1) * 128].rearrange("p (g c) i -> p g c i", g=2),
                            in0=ps4, in1=bb, op=ALU.add),
                            reads=["bsp"], writes=[("ps", b), "tmpT"])
            P.op("pool", lambda q: q.tensor_tensor(out=mixT[:, 8:16, 0:n], in0=tmpT[:, :, 0:n], in1=uT[:, :, 0:n], op=ALU.mult),
                 reads=["tmpT", "uT"], writes=["mixT_b"])
            if nxt:
                nxt[1]()
            for g in range(4):
                for oc in range(2):
                    b = nbank()

                    def mmp(q, b=b, g=g, oc=oc):
                        r = None
                        for ic in range(2):
                            r = q.matmul(ps[b][:, 0:n], lhsT=wp_bf[:, g * 2 + ic, oc * 128:(oc + 1) * 128], rhs=dT[:, 2 * g + ic, 0:n],
                                         start=(ic == 0), stop=(ic == 1))
                        return r
                    P.op("pe", mmp, reads=["wp_bf", "sqb"], writes=[("ps", b)])
                    ch = 2 * g + oc
                    P.op("dve", lambda q, b=b, ch=ch: q.scalar_tensor_tensor(
                        out=mixT[:, ch, 0:n], in0=ps[b][:, 0:n], scalar=vec[:, 64 + e * 8 + ch:64 + e * 8 + ch + 1], in1=gT[:, ch, 0:n],
                        op0=ALU.mult, op1=ALU.mult), reads=["vec", "gT"], writes=[("ps", b), "mixT_a"])
            if nxt:
                nxt[2]()
            wo = WV("wo%d" % e)
            mixres = ["mixT_b", "mixT_a"]
            for dc in range(8):
                wva, wna = wload(wo[:, 0:8, dc * 128:(dc + 1) * 128], 8, 128)
                wvb, wnb = wload(wo[:, 8:16, dc * 128:(dc + 1) * 128], 8, 128)
                b = nbank()
                P.op("pe", mm_fm(b, wva, 0, 128, mixT[:, 0:8, :], n, first=True, last=False), reads=[wna] + mixres, writes=[("ps", b)])
                P.op("pe", mm_fm(b, wvb, 0, 128, mixT[:, 8:16, :], n, first=False, last=True), reads=[wnb] + mixres, writes=[("ps", b)])
                evac_copy(yT[:, dc, 0:n], ps[b][:, 0:n], b, ["yT"])
            post_norm_update(layer, c0, n)

        NK = 4 * NTOK
        KT = rview(0, [128, 2, NK], BF16)
        krT = R[0:64, 4 * NK:6 * NK].bitcast(BF16)
        Vt = rview(6 * NK, [128, 4 * NBLK, 258], BF16)
        WkvT = r2view(0, [128, 8, 256], BF16)
        Wv = r2view(4096, [128, 2, 8, 128], BF16)
        gateT = r2view(8192, [128, 8, 128], BF16)
        qn = r2view(10240, [128, 8, 128], BF16)
        qaT = r2view(12288, [128, 2, 8, 128], BF16)
        qrope = r2view(16384, [64, 8, 128], BF16, parts=64)
        kvn_bc = r2view(18432, [128, 256], F32)
        qcn = r2view(19456, [128, 3, 128], BF16)
        Pb = r2view(20224, [128, 2, 512], BF16)
        maskb = S("maskb", [128, 512], BF16)
        permf = S("permf", [64, 64], F32)
        cst = S("cst", [128, 64], F32)
        csfO = S("csfO", [128, 258], F32)
        csf = csfO[0:64, 0:256].rearrange("p (a t) -> p a t", a=2)
        stt = S("stt", [128, 16], F32)
        psb = [p[:].bitcast(BF16) for p in ps]
        tmp2d = tmpT[:].rearrange("p k n -> p (k n)")
        tmpb = tmp2d.bitcast(BF16)
        yT2d = yT[:].rearrange("p k n -> p (k n)")
        sqb2d = sqb[:].rearrange("p k n -> p (k n)")
        PTs = [tmpb[:, 0:512], tmpb[:, 512:1024]]
        krtm_s = tmpb[:, 1024:1600].rearrange("p (a b) -> p a b", b=64)
        olat = tmpb[:, 2048:4096].rearrange("p (h c) -> p h c", h=8)
        qas = tmpb[:, 1600:1856].rearrange("p (c t) -> p c t", c=2)
        qrs = tmpb[0:64, 1856:1984]
        qc = yT[:, 0:3, 128:256]
        Oacc = yT[:, 3:5, 128:256]
        qrf = yT[0:64, 5, 128:256]
        qt1 = yT[0:64, 6, 128:256]
        qt2 = yT[0:64, 7, 128:256]
        olT = sqb[:].rearrange("p k n -> p (k n)").rearrange("p (c h q) -> p c h q", c=2, h=8)
        oT = hT[:, :, 128:256]
        bks = [nc.dram_tensor("bks%d" % o, [64, 320], BF16) for o in range(2)]

        ld(tmp2d[:, 0:512], mask_d[:, :], "tmpT")
        P.op("dve", lambda q: q.tensor_copy(out=maskb[:], in_=tmp2d[:, 0:512]), reads=["tmpT"], writes=["maskb"])
        P.op("dve", lambda q: q.tensor_copy(out=permf[:, 0:32], in_=identf[0:64, 32:64]), reads=["identf"], writes=["permf"])
        P.op("pool", lambda q: q.tensor_copy(out=permf[:, 32:64], in_=identf[0:64, 0:32]), reads=["identf", "permf"], writes=["permf"])

        def stage_cast(src_ap, dst_ap, w, dres, parts=128):
            stv = wst[0][0:parts, 0:w]
            P.dma("sp", lambda q: [q.dma_start(out=stv, in_=src_ap)], writes=["wst0"], chan="wst0")
            P.op("pool", lambda q: q.tensor_copy(out=dst_ap, in_=stv), reads=["wst0"], writes=list(dres))

        def kt_transposes(kb, vidx, parts, krt_src, col0, w):
            b = nbank()

            def tr(q):
                q.transpose(psb[b][:, 0:w], Vt[0:parts, vidx, 0:128], identb[0:parts, 0:parts])
                q.transpose(psb[b][:, 128:128 + w], Vt[0:parts, vidx, 128:256], identb[0:parts, 0:parts])
                return q.transpose(psb[b][0:64, 256:256 + w], krt_src, identb[0:parts, 0:parts])
            P.op("pe", tr, reads=["V", "tmpT", "identb"], writes=[("ps", b)])
            P.op("act", lambda q: q.copy(out=KT[:, :, col0:col0 + w], in_=psb[b][:, 0:256].rearrange("p (c k) -> p c k", c=2)[:, :, 0:w]),
                 writes=[("ps", b), "KT"])
            P.op("dve", lambda q: q.tensor_copy(out=krT[:, col0:col0 + w], in_=psb[b][0:64, 256:256 + w]), writes=[("ps", b), "krT"])

        stt2 = S("stt2", [128, 24], F32)
        P.op("pool", lambda q: q.memset(stt2[:, 0:1], 30000.0 * ATTN_SCALE), writes=["nm_init"])
        Oaccs = [(csfO[:, 0:257], "csf"), (rstd[:, 0:257], "rstd")]
        Pbs = [Pb[:, 0, :], Pb[:, 1, :], wst[0][:, :].bitcast(BF16)]
        Pbn = [("Pb", 0), ("Pb", 1), "wst0"]

        uctr = [0]
        self_defer = [None]

        def kt_transposes2(kb, krt0, krt1):
            b = nbank()

            def tr(q):
                r = None
                for j, krt in enumerate((krt0, krt1)):
                    o = j * 384
                    q.transpose(psb[b][:, o:o + 128], Vt[:, kb + j, 0:128], identb[:])
                    q.transpose(psb[b][:, o + 128:o + 256], Vt[:, kb + j, 128:256], identb[:])
                    r = q.transpose(psb[b][0:64, o + 256:o + 384], krt, identb[:])
                return r
            P.op("pe", tr, reads=["V", "tmpT", "identb"], writes=[("ps", b)])
            src = psb[b][:, 0:768].rearrange("p (k x t) -> p k x t", k=2, x=3)
            c0 = kb * 128
            P.op("act", lambda q: q.copy(out=KT[:, :, c0:c0 + 256].rearrange("p c (k t) -> p k c t", k=2), in_=src[:, :, 0:2, :]),
                 writes=[("ps", b), "KT"])
            P.op("dve", lambda q: q.tensor_copy(out=krT[:, c0:c0 + 256].rearrange("p (k t) -> p k t", k=2), in_=src[0:64, :, 2, :]),
                 writes=[("ps", b), "krT"])

        def attn_stream(units, hook=None):
            steps = []
            for ui, U in enumerate(units):
                ng = len(U["groups"])
                U["_k"] = uctr[0] % 4
                U["_bO"] = 6 + uctr[0] % 2
                uctr[0] += 1
                for gi, G in enumerate(U["groups"]):
                    steps.append((ui, gi, ng, U, G))
            N = len(steps)
            st = [dict() for _ in range(N)]

            def col(base, k):
                return stt2[:, base + k:base + k + 1]

            def stA(i):
                ui, gi, ng, U, G = steps[i]
                w = G["w"]
                bS = nbank()
                st[i]["bS"] = bS

                def mmS(q):
                    q.matmul(ps[bS][:, 0:w], lhsT=U["qa0"], rhs=G["kt0"], start=True, stop=False)
                    q.matmul(ps[bS][:, 0:w], lhsT=U["qa1"], rhs=G["kt1"], start=False, stop=False)
                    r = q.matmul(ps[bS][:, 0:w], lhsT=U["qr"], rhs=G["krt"], start=False, stop=not G["diag"])
                    if G["diag"]:
                        r = q.matmul(ps[bS][:, 0:w], lhsT=identb[:], rhs=maskb[:, 0:w], start=False, stop=True)
                    return r
                P.op("pe", mmS, reads=["qaT", "qrope", "tmpT", "KT", "krT", "maskb", "identb"], writes=[("ps", bS)])

            def stB(i):
                ui, gi, ng, U, G = steps[i]
                w = G["w"]
                bS = st[i]["bS"]
                k = U["_k"]
                ib = i % 3
                if gi == 0:
                    P.op("dve", lambda q: q.reduce_max(out=col(6, k), in_=ps[bS][:, 0:w], axis=AX.X), writes=[("ps", bS), ("gmax", k)])
                    P.op("dve", lambda q: q.tensor_scalar(out=col(2, k), in0=col(6, k), scalar1=-ATTN_SCALE, scalar2=None, op0=ALU.mult),
                         reads=[("gmax", k)], writes=[("nm", k)])
                P.op("act", lambda q: q.activation(out=Pbs[ib][:, 0:w], in_=ps[bS][:, 0:w], func=AF.Exp, bias=col(2, k), scale=ATTN_SCALE),
                     reads=[("nm", k)], writes=[("ps", bS), Pbn[ib]])

            def stC(i):
                ui, gi, ng, U, G = steps[i]
                w = G["w"]
                bT = nbank()
                nj = (w + 127) // 128
                ib = i % 3
                it = i % 2

                def trP(q):
                    r = None
                    for j in range(nj):
                        wj = min(128, w - j * 128)
                        r = q.transpose(psb[bT][0:wj, j * 128:(j + 1) * 128], Pbs[ib][:, j * 128:j * 128 + wj], identb[:])
                    return r
                P.op("pe", trP, reads=[Pbn[ib], "identb"], writes=[("ps", bT)])
                kp = min(128, w)
                if i % 3 == 0:
                    P.op("act", lambda q: q.copy(out=PTs[it][0:kp, 0:nj * 128], in_=psb[bT][0:kp, 0:nj * 128]),
                         writes=[("ps", bT), ("PTs", it)])
                else:
                    P.op("dve", lambda q: q.tensor_copy(out=PTs[it][0:kp, 0:nj * 128], in_=psb[bT][0:kp, 0:nj * 128]),
                         writes=[("ps", bT), ("PTs", it)])

            def stD(i):
                ui, gi, ng, U, G = steps[i]
                bO = U["_bO"]
                it = i % 2

                def mmO(q):
                    r = None
                    nv = len(G["v"])
                    for j, (vap, K) in enumerate(G["v"]):
                        r = q.matmul(ps[bO][:, 0:257], lhsT=PTs[it][0:K, j * 128:(j + 1) * 128], rhs=vap,
                                     start=(gi == 0 and j == 0), stop=(gi == ng - 1 and j == nv - 1))
                    return r
                P.op("pe", mmO, reads=[("PTs", it), "V"], writes=[("ps", bO)])
                if gi == ng - 1:
                    P.op("dve", lambda q: q.reciprocal(out=stt2[:, 16:17], in_=ps[bO][:, 256:257]), writes=[("ps", bO), "rl"])
                    P.op("act", lambda q: q.activation(out=U["out_ap"], in_=ps[bO][:, 0:256], func=AF.Copy, scale=stt2[:, 16:17]),
                         reads=["rl"], writes=[("ps", bO), "olat"])
                    if U.get("post") is not None:
                        for j, fn in enumerate(U["post"]):
                            self_defer[0](2 + 2 * j, fn)

            pend = []
            st_cur = [0]

            def defer(delay, fn):
                pend.append((st_cur[0] + delay, fn))
            self_defer[0] = defer
            h0 = max(0, N // 2)
            for i in range(N + 3):
                st_cur[0] = i
                if hook is not None and i == h0:
                    for j, fn in enumerate(hook):
                        defer(2 * j, fn)
                due = [p for p in pend if p[0] <= i]
                pend[:] = [p for p in pend if p[0] > i]
                for _, fn in due:
                    fn()
                if i < N:
                    stA(i)
                    stB(i)
                if 0 <= i - 2 < N:
                    stC(i - 2)
                if 0 <= i - 3 < N:
                    stD(i - 3)
            for _, fn in sorted(pend, key=lambda p: p[0]):
                fn()

        def odd_layer(o):
            layer = 2 * o + 1
            wio = WV("wio%d" % o)
            wqu = WV("wqu%d" % o)
            wku = WV("wku%d" % o)
            wov = WV("wov%d" % o)
            ld(kvn_bc, kv_norm[o:o + 1, :].broadcast_to([128, 256]), "kvn_bc")
            P.op("pool", lambda q: q.memset(Vt[:, :, 256:258], 1.0), writes=["V"])
            for h in range(8):
                wv, wn = wload(wku[:, :, h * 256:(h + 1) * 256], 2, 256)
                b = nbank()

                def trk(q, b=b, wv=wv):
                    q.transpose(psb[b][:, 0:128], wv[:, 0, 0:128], identb[:])
                    return q.transpose(psb[b][:, 128:256], wv[:, 1, 0:128], identb[:])
                P.op("pe", trk, reads=[wn, "identb"], writes=[("ps", b)])
                P.op("act", lambda q, b=b, h=h: q.copy(out=WkvT[:, h, :], in_=psb[b][:, 0:256]), writes=[("ps", b), "WkvT"])
                P.op("pool", lambda q, wv=wv, h=h: q.tensor_copy(out=Wv[:, :, h, :], in_=wv[:, :, 128:256]), reads=[wn], writes=["Wv"])
            units = [(u * 128, 128, u) for u in range(NBLK)] + [(NTOK, 64, NBLK)]
            for (c0, n, u) in units:
                sample = (u == NBLK)
                xres = [("xT", c0 // 256)]
                pre_norm(layer, xT[:, :, c0:c0 + n], n, xres)
                P.dma("sp", lambda q, u=u: [q.dma_start(out=cst[:], in_=cs_tm[:, u * 64:(u + 1) * 64])], writes=["cst"], chan="cst")
                b = nbank()
                for ci, (col, w) in enumerate([(384, 128), (512, 128), (640, 64)]):
                    wv, wn = wload(wio[:, :, col:col + w], KD, w)

                    def mmk(q, b=b, wv=wv, ci=ci, w=w, n=n):
                        r = None
                        for k in range(KD):
                            r = q.matmul(ps[b][0:n, ci * 128:ci * 128 + w], lhsT=hT[:, k, 0:n], rhs=wv[:, k, :], start=(k == 0), stop=(k == KD - 1))
                        return r
                    P.op("pe", mmk, reads=[wn, "hT"], writes=[("ps", b)])
                kvc = tmp2d[0:n, 0:320]
                P.op("act", lambda q, b=b, n=n, kvc=kvc: q.copy(out=kvc, in_=ps[b][0:n, 0:320]), writes=[("ps", b), "tmpT"])
                P.op("act", lambda q, n=n: q.activation(out=tmp2d[0:n, 512:768], in_=tmp2d[0:n, 0:256], func=AF.Square, accum_out=stt[0:n, 8:9]),
                     reads=["tmpT"], writes=["tmpT2", "ss"])
                P.op("act", lambda q, n=n: q.activation(out=stt[0:n, 9:10], in_=stt[0:n, 8:9], func=AF.Sqrt, bias=epsc[0:n, 0:1], scale=1.0 / 256.0),
                     reads=["ss", "epsc"], writes=["ss2"])
                P.op("dve", lambda q, n=n: q.reciprocal(out=stt[0:n, 10:11], in_=stt[0:n, 9:10]), reads=["ss2"], writes=["ss3"])
                P.op("dve", lambda q, n=n: q.scalar_tensor_tensor(out=yT2d[0:n, 0:256], in0=tmp2d[0:n, 0:256], scalar=stt[0:n, 10:11],
                                                                  in1=kvn_bc[0:n, :], op0=ALU.mult, op1=ALU.mult),
                     reads=["tmpT", "ss3", "kvn_bc"], writes=["yT"])
                x1, x2 = tmp2d[0:n, 256:288], tmp2d[0:n, 288:320]
                cs_, sn_ = cst[0:n, 0:32], cst[0:n, 32:64]
                t1, t2, t3, t4 = (tmp2d[0:n, 1024 + 32 * j:1056 + 32 * j] for j in range(4))
                P.op("pool", lambda q, x1=x1, cs_=cs_, t1=t1: q.tensor_tensor(out=t1, in0=x1, in1=cs_, op=ALU.mult), reads=["tmpT", "cst"], writes=["t1"])
                P.op("dve", lambda q, x2=x2, sn_=sn_, t2=t2: q.tensor_tensor(out=t2, in0=x2, in1=sn_, op=ALU.mult), reads=["tmpT", "cst"], writes=["t2"])
                P.op("pool", lambda q, x2=x2, cs_=cs_, t3=t3: q.tensor_tensor(out=t3, in0=x2, in1=cs_, op=ALU.mult), reads=["tmpT", "cst"], writes=["t3"])
                P.op("dve", lambda q, x1=x1, sn_=sn_, t4=t4: q.tensor_tensor(out=t4, in0=x1, in1=sn_, op=ALU.mult), reads=["tmpT", "cst"], writes=["t4"])
                P.op("pool", lambda q, n=n, t1=t1, t2=t2: q.tensor_tensor(out=yT2d[0:n, 256:288], in0=t1, in1=t2, op=ALU.subtract),
                     reads=["t1", "t2"], writes=["yTr1"])
                P.op("dve", lambda q, n=n, t3=t3, t4=t4: q.tensor_tensor(out=yT2d[0:n, 288:320], in0=t3, in1=t4, op=ALU.add),
                     reads=["t3", "t4"], writes=["yTr2"])
                P.op("pool", lambda q, n=n: q.tensor_copy(out=sqb2d[0:n, 0:320], in_=yT2d[0:n, 0:320]), reads=["yT", "yTr1", "yTr2"], writes=["sqb"])
                if sample:
                    P.dma("sp", lambda q: [q.dma_start(out=o_ckv_s[o], in_=yT2d[0:64, 0:256]), q.dma_start(out=o_kr_s[o], in_=yT2d[0:64, 256:320])],
                          reads=["yT", "yTr1", "yTr2"], chan="kvout", n=2, is_out=True)
                    P.dma("sp", lambda q: [q.dma_start(out=bks[o].ap()[:, :], in_=sqb2d[0:64, 0:320])], reads=["sqb"], writes=[("bks", o)], chan="bks")
                else:
                    P.dma("sp", lambda q, c0=c0: [q.dma_start(out=o_ckv_p[o, c0:c0 + 128, :], in_=yT2d[:, 0:256]),
                                                 q.dma_start(out=o_kr_p[o, c0:c0 + 128, :], in_=yT2d[:, 256:320])],
                          reads=["yT", "yTr1", "yTr2"], chan="kvout", n=2, is_out=True)
                    P.dma("sp", lambda q, u=u: [q.dma_start(out=bk_in[o][u // BPS].ap()[(u % BPS) * 128:(u % BPS) * 128 + 128, :], in_=sqb2d[:, 0:320])],
                          reads=["sqb"], writes=[("bk_in", o, u)], chan="bkin")
            for sp in range(NSPL):
                P.collective(lambda q, sp=sp: q.collective_compute("AllGather", ALU.bypass, replica_groups=[[0, 1, 2, 3], [4, 5, 6, 7]],
                                                                   ins=[bk_in[o][sp].ap().opt()], outs=[bk_out[o][sp].ap().opt()]),
                             reads=[("bk_in", o, u) for u in range(sp * BPS, (sp + 1) * BPS)], writes=[("bk_out", o, sp)], chan=("ccK", o, sp))
            if layer + 1 < NLAYERS:
                relay_layer(layer + 1)
            V4 = Vt.rearrange("p (m r) c -> p m r c", r=4)
            krtm = tmpb[:, 0:NK // 2].rearrange("p (m r c) -> p m r c", r=4, c=64)
            for sp in range(NSPL):
                bko = bk_out[o][sp].ap()
                ms = slice(sp * BPS, (sp + 1) * BPS)
                for r in range(4):
                    P.dma("sp", lambda q, r=r, bko=bko, ms=ms: [q.dma_start(out=V4[:, ms, r, 0:256], in_=bko[r * TPS:(r + 1) * TPS, 0:256].rearrange("(m p) c -> p m c", p=128))],
                          reads=[("bk_out", o, sp)], writes=["V"], chan="Vld")
                    P.dma("sp", lambda q, r=r, bko=bko, ms=ms: [q.dma_start(out=krtm[:, ms, r, :], in_=bko[r * TPS:(r + 1) * TPS, 256:320].rearrange("(m p) c -> p m c", p=128))],
                          reads=[("bk_out", o, sp)], writes=["tmpT"], chan="krld")
            krtm3 = tmpb[:, 0:NK // 2].rearrange("p (k c) -> p k c", c=64)
            for kb in range(0, 4 * NBLK, 2):
                kt_transposes2(kb, krtm3[:, kb, :], krtm3[:, kb + 1, :])

            qrfs = [yT[0:64, 5, 128:256], yT[0:64, 3, 128:256]]
            qt1s = [yT[0:64, 6, 128:256], yT[0:64, 4, 128:256]]

            def q_path(c0, n, do_norm=True):
                xres = [("xT", c0 // 256)]
                if do_norm:
                    pre_norm(layer, xT[:, :, c0:c0 + n], n, xres)
                P.dma("sp", lambda q: [q.dma_start(out=csf[:, :, 0:n], in_=cs_fm.rearrange("p (a t) -> p a t", a=2)[:, :, c0:c0 + n])],
                      writes=["csf"], chan="csf")
                for kc in range(3):
                    wv, wn = wload(wio[:, :, kc * 128:(kc + 1) * 128], KD, 128)
                    b = nbank()
                    P.op("pe", mm_fm(b, wv, 0, 128, hT, n), reads=[wn, "hT"], writes=[("ps", b)])
                    evac_copy(qc[:, kc, 0:n], ps[b][:, 0:n], b, ["qc"])
                rms_stats(qc[:, :, 0:n], n, ["qc"], kdim=3, scale=1024.0 / 384.0)
                for kc in range(3):
                    P.op("dve", lambda q, kc=kc: q.scalar_tensor_tensor(out=qcn[:, kc, 0:n], in0=qc[:, kc, 0:n],
                                                                        scalar=vec[:, 80 + o * 3 + kc:80 + o * 3 + kc + 1], in1=rstd[:, 0:n],
                                                                        op0=ALU.mult, op1=ALU.mult), reads=["qc", "vec", "rstd"], writes=["qcn"])
                for ec in range(8):
                    wv, wn = wload(wio[:, :, 704 + ec * 128:704 + (ec + 1) * 128], KD, 128)
                    b = nbank()
                    P.op("pe", mm_fm(b, wv, 0, 128, hT, n), reads=[wn, "hT"], writes=[("ps", b)])
                    P.op("act", lambda q, b=b, ec=ec: q.activation(out=gateT[:, ec, 0:n], in_=ps[b][:, 0:n], func=AF.Silu),
                         writes=[("ps", b), "gateT"])

                def stage_a(h):
                    j = h % 2
                    wv, wn = wload(wqu[:, :, h * 192:(h + 1) * 192], 3, 192)
                    b1 = nbank()
                    P.op("pe", mm_fm(b1, wv, 0, 128, qcn, n, kdim=3), reads=[wn, "qcn"], writes=[("ps", b1)])
                    evac_copy(qn[:, h, 0:n], ps[b1][:, 0:n], b1, [("qn", h)])
                    b2 = nbank()
                    P.op("pe", mm_fm(b2, wv, 128, 64, qcn, n, kdim=3), reads=[wn, "qcn"], writes=[("ps", b2)])
                    P.op("act", lambda q: q.copy(out=qrfs[j][:, 0:n], in_=ps[b2][0:64, 0:n]), writes=[("ps", b2), ("qrf", j)])

                def stage_b(h):
                    j = h % 2
                    jw = {"writes": ["qrope"]} if h == 0 else {"join": ["qrope"]}
                    jq = {"wres": ["qaT"]} if h == 0 else {"wres": [], "join": ["qaT"]}
                    b3 = nbank()
                    P.op("pe", lambda q: q.matmul(ps[b3][0:64, 0:n], lhsT=permf[:, :], rhs=qrfs[j][:, 0:n], start=True, stop=True),
                         reads=["permf", ("qrf", j)], writes=[("ps", b3)])
                    P.op("dve", lambda q: q.tensor_tensor(out=qt1s[j][:, 0:n], in0=ps[b3][0:64, 0:n], in1=csf[:, 1, 0:n], op=ALU.mult),
                         reads=["csf"], writes=[("ps", b3), ("qt1", j)])
                    P.op("pool", lambda q: q.tensor_tensor(out=qt2[:, 0:n], in0=qrfs[j][:, 0:n], in1=csf[:, 0, 0:n], op=ALU.mult),
                         reads=["csf", ("qrf", j)], writes=["qt2"])
                    P.op("dve", lambda q: q.tensor_tensor(out=qrope[:, h, 0:n], in0=qt1s[j][:, 0:n], in1=qt2[:, 0:n], op=ALU.add),
                         reads=[("qt1", j), "qt2"], **jw)
                    b4 = nbank()

                    def mma(q):
                        q.matmul(ps[b4][:, 0:n], lhsT=WkvT[:, h, 0:128], rhs=qn[:, h, 0:n], start=True, stop=True)
                        return q.matmul(ps[b4][:, 128:128 + n], lhsT=WkvT[:, h, 128:256], rhs=qn[:, h, 0:n], start=True, stop=True)
                    P.op("pe", mma, reads=["WkvT", ("qn", h)], writes=[("ps", b4)])
                    evac_copy(qaT[:, :, h, 0:n], ps[b4][:, 0:256].rearrange("p (c t) -> p c t", c=2)[:, :, 0:n], b4, jq["wres"], join=jq.get("join", ()))

                for i in range(9):
                    if i < 8:
                        stage_a(i)
                    if i >= 1:
                        stage_b(i - 1)

            def head_out_a(h):
                b = nbank()

                def tro(q):
                    q.transpose(psb[b][:, 0:128], olat[:, h, 0:128], identb[:])
                    return q.transpose(psb[b][:, 128:256], olat[:, h, 128:256], identb[:])
                P.op("pe", tro, reads=["olat", "identb"], writes=[("ps", b)])
                evac_copy(olT[:, :, h, :], psb[b][:, 0:256].rearrange("p (c q) -> p c q", c=2), b, [("olT", h)], join=["sqb"])

            def head_out_b(h, n):
                b2 = nbank()

                def mmo(q):
                    q.matmul(ps[b2][:, 0:n], lhsT=Wv[:, 0, h, :], rhs=olT[:, 0, h, 0:n], start=True, stop=False)
                    return q.matmul(ps[b2][:, 0:n], lhsT=Wv[:, 1, h, :], rhs=olT[:, 1, h, 0:n], start=False, stop=True)
                P.op("pe", mmo, reads=["Wv", ("olT", h), "sqb"], writes=[("ps", b2)])
                P.op("dve", lambda q: q.tensor_tensor(out=oT[:, h, 0:n], in0=ps[b2][:, 0:n], in1=gateT[:, h, 0:n], op=ALU.mult),
                     reads=["gateT"], writes=[("ps", b2)] + (["oT"] if h == 0 else []), join=([] if h == 0 else ["oT"]))

            def out_path(c0, n, heads_done=False):
                for h in range(0 if heads_done else 8):
                    b = nbank()

                    def mmo(q, b=b, h=h):
                        q.matmul(ps[b][:, 0:n], lhsT=Wv[:, 0, h, :], rhs=olT[:, 0, h, 0:n], start=True, stop=False)
                        return q.matmul(ps[b][:, 0:n], lhsT=Wv[:, 1, h, :], rhs=olT[:, 1, h, 0:n], start=False, stop=True)
                    P.op("pe", mmo, reads=["Wv", "sqb"], writes=[("ps", b)])
                    P.op("dve", lambda q, b=b, h=h: q.tensor_tensor(out=oT[:, h, 0:n], in0=ps[b][:, 0:n], in1=gateT[:, h, 0:n], op=ALU.mult),
                         reads=["gateT"], writes=[("ps", b), "oT"])
                for dc in range(8):
                    wv, wn = wload(wov[:, :, dc * 128:(dc + 1) * 128], KD, 128)
                    b = nbank()
                    P.op("pe", mm_fm(b, wv, 0, 128, oT, n), reads=[wn, "oT"], writes=[("ps", b)])
                    evac_copy(yT[:, dc, 0:n], ps[b][:, 0:n], b, ["yT"])
                post_norm_update(layer, c0, n)

            for blk in range(NBLK):
                c0 = blk * 128
                q_path(c0, 128, do_norm=(blk == 0))
                units_ = []
                for h in range(8):
                    groups = []
                    for g in range(blk + 1):
                        ks = slice(g * 512, (g + 1) * 512)
                        groups.append(dict(kt0=KT[:, 0, ks], kt1=KT[:, 1, ks], krt=krT[:, ks], w=512,
                                           v=[(Vt[:, g * 4 + j, 0:257], 128) for j in range(4)], diag=(g == blk)))
                    units_.append(dict(qa0=qaT[:, 0, h, :], qa1=qaT[:, 1, h, :], qr=qrope[:, h, :], groups=groups, out_ap=olat[:, h, :],
                                       post=[(lambda h=h: head_out_a(h)), (lambda h=h: head_out_b(h, 128))]))
                nc0, nn = ((blk + 1) * 128, 128) if blk + 1 < NBLK else (NTOK, 64)
                attn_stream(units_, hook=pre_norm_lite_stages(layer, xT[:, :, nc0:nc0 + nn], nn, [("xT", nc0 // 256)]))
                out_path(c0, 128, heads_done=True)
            q_path(NTOK, 64, do_norm=False)
            for s_ in range(2):
                for kb in range(8):
                    stage_cast(cckv[o, s_, kb * 128:(kb + 1) * 128, :], Vt[:, kb, 0:256], 256, ["V"])
                    stage_cast(ckr[o, s_, kb * 128:(kb + 1) * 128, :], krtm_s[:, kb, :], 64, ["tmpT"])
                P.dma("sp", lambda q, s_=s_: [q.dma_start(out=Vt[0:32, 8, 0:256], in_=bks[o].ap()[s_ * 32:(s_ + 1) * 32, 0:256]),
                                             q.dma_start(out=krtm_s[0:32, 8, :], in_=bks[o].ap()[s_ * 32:(s_ + 1) * 32, 256:320])],
                      reads=[("bks", o)], writes=["V", "tmpT"], chan="bksld", n=2)
                for kb in range(0, 8, 2):
                    kt_transposes2(kb, krtm_s[:, kb, :], krtm_s[:, kb + 1, :])
                kt_transposes(8, 8, 32, krtm_s[0:32, 8, :], 1024, 32)
                units_ = []
                for hq in range(2):
                    ts = slice(s_ * 32, (s_ + 1) * 32)
                    hs = slice(hq * 4, hq * 4 + 4)
                    groups = []
                    for g in range(2):
                        ks = slice(g * 512, (g + 1) * 512)
                        groups.append(dict(kt0=KT[:, 0, ks], kt1=KT[:, 1, ks], krt=krT[:, ks], w=512,
                                           v=[(Vt[:, g * 4 + j, 0:257], 128) for j in range(4)], diag=False))
                    groups.append(dict(kt0=KT[:, 0, 1024:1056], kt1=KT[:, 1, 1024:1056], krt=krT[:, 1024:1056], w=32,
                                       v=[(Vt[0:32, 8, 0:257], 32)], diag=False))
                    P.op("dve", lambda q, hs=hs, ts=ts: q.tensor_copy(out=qas.rearrange("p c (h t) -> p c h t", h=4), in_=qaT[:, :, hs, ts]),
                         reads=["qaT"], writes=["tmpT"])
                    P.op("pool", lambda q, hs=hs, ts=ts: q.tensor_copy(out=qrs.rearrange("p (h t) -> p h t", h=4), in_=qrope[:, hs, ts]),
                         reads=["qrope"], writes=["tmpT"])

                    def post(hs=hs, ts=ts):
                        b = nbank()

                        def tro2(q):
                            q.transpose(psb[b][:, 0:128], olat[:, 0, 0:128], identb[:])
                            return q.transpose(psb[b][:, 128:256], olat[:, 0, 128:256], identb[:])
                        P.op("pe", tro2, reads=["olat", "identb"], writes=[("ps", b)])
                        evac_copy(olT[:, :, hs, ts], psb[b][:, 0:256].rearrange("p (c h t) -> p c h t", c=2, h=4), b, ["sqb"])
                    attn_stream([dict(qa0=qas[:, 0, :], qa1=qas[:, 1, :], qr=qrs, groups=groups, out_ap=olat[:, 0, :], post=[post])])
            out_path(NTOK, 64)

        for layer in range(NLAYERS):
            conv_layer(layer)
        relay_layer(0)
        for layer in range(NLAYERS):
            if layer % 2 == 0:
                e = layer // 2
                ctx = even_layer(e)
                for t in range(NT + 1):
                    if t == 1 and layer + 1 < NLAYERS:
                        relay_layer(layer + 1)
                    even_tile(e, ctx, t)
            else:
                P.barrier()
                odd_layer(layer // 2)
                P.barrier()

        for b in range(NBLK):
            store_x(y_p[b * 128:(b + 1) * 128, :], b * 128, 128)
        store_x(y_s[:, :], NTOK, NS)
        P.finish()
        P.replay()
    return nc


def _host_consts(c, NBLK):
    NTOK = NBLK * 128
    TOT = NTOK + 64
    r = c % 4
    mask = np.zeros((128, 512), np.float32)
    qi = np.arange(128)[:, None]
    for i in range(4):
        blk = mask[:, i * 128:(i + 1) * 128]
        if i > r:
            blk[:] = NEG
        elif i == r:
            kj = np.arange(128)[None, :]
            blk[:] = np.where((kj // 64) <= (qi // 64), 0.0, NEG)
    selw = np.zeros((128, 8), np.float32)
    if r > 0:
        selw[:, r - 1] = 1.0
    else:
        selw[:, 4] = 1.0
    rc = np.zeros((128, 4, 16), np.float32)
    for g in range(4):
        w = 2 ** (g + 1)
        for p in range(16):
            rc[:, g, p] = (1.0 / min(p + 1, w)) if r == 0 else 1.0 / w
    half = 32
    freqs = (10000.0 ** (-np.arange(half, dtype=np.float32) / half)).astype(np.float32)
    pos = np.zeros(TOT, np.float32)
    for m in range(NBLK):
        pos[m * 128:(m + 1) * 128] = (4 * m + r) * 128 + np.arange(128)
    pos[NTOK:NTOK + 32] = 1024 + np.arange(32)
    pos[NTOK + 32:] = 1024 + np.arange(32)
    ang = pos[:, None].astype(np.float32) * freqs[None, :]
    cos, sin = np.cos(ang).astype(np.float32), np.sin(ang).astype(np.float32)
    cs_tm = np.zeros((128, NBLK + 1, 64), np.float32)
    for m in range(NBLK):
        cs_tm[:, m, 0:32] = cos[m * 128:(m + 1) * 128]
        cs_tm[:, m, 32:64] = sin[m * 128:(m + 1) * 128]
    cs_tm[0:64, NBLK, 0:32] = cos[NTOK:]
    cs_tm[0:64, NBLK, 32:64] = sin[NTOK:]
    cs_fm = np.zeros((64, 2, TOT), np.float32)
    cs_fm[0:32, 0] = cos.T
    cs_fm[32:64, 0] = cos.T
    cs_fm[0:32, 1] = -sin.T
    cs_fm[32:64, 1] = sin.T
    return dict(mask=mask, selw=selw, rcnt=rc.reshape(128, 64), cs_tm=cs_tm.reshape(128, -1), cs_fm=cs_fm.reshape(64, -1),
                ident=np.eye(128, dtype=np.float32))


def _fm(v):
    return np.ascontiguousarray(np.asarray(v, np.float32).reshape(-1, 128).T)


_NC_CACHE = {}


def kernel(x_prompt, x_sample, cache_pool, cache_ckv, cache_krope, norm_pre, norm_post,
           w_in_even, w_pool, pool_scale, sgu_ln_g, sgu_ln_b, w_spatial, b_spatial, w_out_even,
           w_in_odd, q_norm, kv_norm, w_q_up, w_kv_up, w_o, _nlayers=4):
    f = lambda a: np.ascontiguousarray(np.asarray(a, dtype=np.float32))
    x_prompt = f(x_prompt)
    x_sample = f(x_sample)
    B, T, _ = x_prompt.shape
    NBLK = T // 512
    NTOK = NBLK * 128
    vecs = np.zeros((128, 96), np.float32)
    for l in range(4):
        vecs[:, l * 8:(l + 1) * 8] = _fm(f(norm_pre)[l])
        vecs[:, 32 + l * 8:32 + (l + 1) * 8] = _fm(f(norm_post)[l])
    for e in range(2):
        vecs[:, 64 + e * 8:64 + (e + 1) * 8] = _fm(f(pool_scale)[e])
        vecs[:, 80 + e * 3:80 + (e + 1) * 3] = _fm(f(q_norm)[e])
    shared = dict(w_in_even=f(w_in_even), w_pool=f(w_pool), ln_g=f(sgu_ln_g), ln_b=f(sgu_ln_b), w_sp=f(w_spatial),
                  b_sp=f(b_spatial), w_out_even=f(w_out_even), w_in_odd=f(w_in_odd), kv_norm=f(kv_norm),
                  w_q_up=f(w_q_up).reshape(2, 384, 8 * 192), w_kv_up=f(w_kv_up).reshape(2, 256, 8 * 256), w_o=f(w_o), vecs=vecs)
    cache_pool, cache_ckv, cache_krope = f(cache_pool), f(cache_ckv), f(cache_krope)
    in_maps = []
    for c in range(8):
        b, r = c // 4, c % 4
        xb = x_prompt[b].reshape(NBLK, 4, 128, D)[:, r].reshape(NTOK, D)
        m = dict(shared)
        m.update(_host_consts(c, NBLK))
        m["xp"] = np.ascontiguousarray(xb)
        m["xs"] = np.ascontiguousarray(x_sample[2 * c:2 * c + 2].reshape(64, D))
        m["cpool"] = np.ascontiguousarray(cache_pool[:, 2 * c:2 * c + 2])
        m["cckv"] = np.ascontiguousarray(cache_ckv[:, 2 * c:2 * c + 2])
        m["ckr"] = np.ascontiguousarray(cache_krope[:, 2 * c:2 * c + 2])
        in_maps.append(m)
    key = (NBLK, _nlayers)
    if key not in _NC_CACHE:
        _NC_CACHE[key] = build(NBLK, _nlayers)
    nc = _NC_CACHE[key]
    res = run_bass_kernel_spmd(nc, in_maps, core_ids=list(range(8))).results

    def unshard(name, width):
        out = np.zeros((B, NBLK, 4, 128, width), np.float32)
        for c in range(8):
            out[c // 4, :, c % 4] = res[c][name].reshape(NBLK, 128, width)
        return out.reshape(B, T, width)

    def unshard_l(name, width):
        out = np.zeros((2, B, NBLK, 4, 128, width), np.float32)
        for c in range(8):
            out[:, c // 4, :, c % 4] = res[c][name].reshape(2, NBLK, 128, width)
        return out.reshape(2, B, T, width)

    y_prompt = unshard("y_p", D)
    y_sample = np.concatenate([res[c]["y_s"].reshape(2, 32, D) for c in range(8)], 0)
    pool_p = np.stack([res[3]["o_pool_p"], res[7]["o_pool_p"]], 1)
    pool_s = np.concatenate([res[c]["o_pool_s"] for c in range(8)], 1)
    sgu_s = np.concatenate([res[c]["o_sgu_s"].reshape(2, 2, 32, D) for c in range(8)], 1)
    ckv_p = unshard_l("o_ckv_p", 256)
    kr_p = unshard_l("o_kr_p", 64)
    ckv_s = np.concatenate([res[c]["o_ckv_s"].reshape(2, 2, 32, 256) for c in range(8)], 1)
    kr_s = np.concatenate([res[c]["o_kr_s"].reshape(2, 2, 32, 64) for c in range(8)], 1)
    return (y_prompt, y_sample, pool_p, pool_s, sgu_s, ckv_p, kr_p, ckv_s, kr_s)
```

```python
import contextlib
import numpy as np
import concourse.bass as bass
import concourse.mybir as mybir
from concourse.bass_utils import run_bass_kernel_spmd

F32 = mybir.dt.float32
BF16 = mybir.dt.bfloat16
AF = mybir.ActivationFunctionType
ALU = mybir.AluOpType
AX = mybir.AxisListType

D = 1024
KD = 8
EPS = 1e-6
ATTN_SCALE = 192.0 ** -0.5
NEG = -30000.0


class Op:
    __slots__ = ("eng", "fn", "waits", "signal", "count", "kind", "chan", "chan_val")

    def __init__(self, eng, fn, kind):
        self.eng = eng
        self.fn = fn
        self.kind = kind
        self.waits = []
        self.signal = False
        self.count = None
        self.chan = None
        self.chan_val = None


class Prog:
    ENGS = ("pe", "act", "dve", "pool", "sp")

    def __init__(self, nc):
        self.nc = nc
        self.ops = {e: [] for e in self.ENGS}
        self.res = {}
        self.chan_tot = {}
        self.chan_sem = {}
        self.eng_sem = {}
        self.out_ops = []

    def _deps(self, op, reads, writes, join=()):
        deps = []
        for r in reads:
            st = self.res.get(r)
            if st is None:
                st = self.res[r] = [[], [], []]
            for wop in st[0]:
                deps.append((wop, "raw"))
        for w in list(writes) + list(join):
            st = self.res.get(w)
            if st is None:
                st = self.res[w] = [[], [], []]
            if w not in join:
                for wop in st[0]:
                    deps.append((wop, "waw"))
            else:
                for rd in st[2]:
                    deps.append((rd, "war"))
            for rd in st[1]:
                deps.append((rd, "war"))
        seen = set()
        for p, kind in deps:
            if p is op or id(p) in seen:
                continue
            if p.kind == "c" and p.eng == op.eng and op.kind == "c":
                if op.eng == "pe" or kind == "war":
                    continue
            seen.add(id(p))
            op.waits.append(p)
            if p.kind == "c":
                p.signal = True
        for r in reads:
            self.res[r][1].append(op)
        for w in writes:
            self.res[w] = [[op], [], self.res[w][1]]
        for w in join:
            self.res[w][0].append(op)

    def op(self, eng, fn, reads=(), writes=(), join=()):
        o = Op(eng, fn, "c")
        self._deps(o, reads, writes, join)
        self.ops[eng].append(o)
        return o

    def dma(self, eng, fn, reads=(), writes=(), chan=None, n=1, is_out=False):
        o = Op(eng, fn, "d")
        self._deps(o, reads, writes)
        tot = self.chan_tot.get(chan, 0) + 16 * n
        self.chan_tot[chan] = tot
        o.chan = chan
        o.chan_val = tot
        self.ops[eng].append(o)
        if is_out:
            self.out_ops.append(o)
        return o

    def collective(self, fn, reads=(), writes=(), chan=None):
        o = Op("pool", fn, "x")
        self._deps(o, reads, writes)
        assert chan not in self.chan_tot
        self.chan_tot[chan] = 1
        o.chan = chan
        o.chan_val = 1
        self.ops["pool"].append(o)
        return o

    def barrier(self):
        o = Op("sp", lambda e: e.nop(), "c")
        for e in self.ENGS:
            if e == "sp":
                continue
            for p in reversed(self.ops[e]):
                if p.kind == "c":
                    p.signal = True
                    o.waits.append(p)
                    break
        lastd = {}
        for e in self.ENGS:
            for p in self.ops[e]:
                if p.kind != "c":
                    lastd[p.chan] = p
        o.waits.extend(lastd.values())
        o.signal = True
        self.ops["sp"].append(o)
        for e in self.ENGS:
            if e == "sp":
                continue
            o2 = Op(e, lambda q: q.nop(), "c")
            o2.waits.append(o)
            self.ops[e].append(o2)
        self.res = {}

    def finish(self):
        o = Op("sp", lambda e: e.nop(), "c")
        for p in self.out_ops:
            o.waits.append(p)
        for e in self.ENGS:
            if e == "sp":
                continue
            for p in reversed(self.ops[e]):
                if p.kind == "c":
                    p.signal = True
                    o.waits.append(p)
                    break
        self.ops["sp"].append(o)

    def replay(self):
        nc = self.nc
        engobj = {"pe": nc.tensor, "act": nc.scalar, "dve": nc.vector, "pool": nc.gpsimd, "sp": nc.sync}
        EPOCH = 6000
        for i, c in enumerate(self.chan_tot):
            self.chan_sem[c] = nc.alloc_semaphore(name="c%d" % i)
        for e in self.ENGS:
            cnt = 0
            for o in self.ops[e]:
                if o.kind == "c" and o.signal:
                    ep = cnt // EPOCH
                    if (e, ep) not in self.eng_sem:
                        self.eng_sem[(e, ep)] = nc.alloc_semaphore(name="s_%s%d" % (e, ep))
                    o.count = (ep, cnt % EPOCH + 1)
                    cnt += 1
        prog = self

        def run(e):
            eng = engobj[e]
            seen = {}
            for o in prog.ops[e]:
                for p in o.waits:
                    if p.kind == "c":
                        sem, val = prog.eng_sem[(p.eng, p.count[0])], p.count[1]
                    else:
                        sem, val = prog.chan_sem[p.chan], p.chan_val
                    k = id(sem)
                    if seen.get(k, 0) >= val:
                        continue
                    seen[k] = val
                    eng.wait_ge(sem, val)
                r = o.fn(eng)
                if o.kind == "c":
                    if o.signal:
                        r.then_inc(prog.eng_sem[(e, o.count[0])], 1)
                elif o.kind == "d":
                    for ins in r:
                        ins.then_inc(prog.chan_sem[o.chan], 16)
                else:
                    r.then_inc(prog.chan_sem[o.chan])

        with nc.Block() as block:
            @block.tensor
            def _(t):
                run("pe")

            @block.scalar
            def _(t):
                run("act")

            @block.vector
            def _(t):
                run("dve")

            @block.gpsimd
            def _(t):
                run("pool")

            @block.sync
            def _(t):
                run("sp")


def build(NBLK, NLAYERS):
    NTOK = NBLK * 128
    NS = 64
    TOT = NTOK + NS
    NT = NBLK // 2
    nc = bass.Bass("TRN2", target_bir_lowering=False)

    def din(name, shape):
        return nc.dram_tensor(name, list(shape), F32, kind="ExternalInput").ap()

    def dout(name, shape):
        return nc.dram_tensor(name, list(shape), F32, kind="ExternalOutput").ap()

    xp = din("xp", [NTOK, D])
    xs = din("xs", [NS, D])
    cpool = din("cpool", [2, 2, 15, D])
    cckv = din("cckv", [2, 2, 1024, 256])
    ckr = din("ckr", [2, 2, 1024, 64])
    w_in_even = din("w_in_even", [2, D, 5120])
    w_pool = din("w_pool", [2, 4, 256, 256])
    ln_g = din("ln_g", [2, D])
    ln_b = din("ln_b", [2, D])
    w_sp = din("w_sp", [2, 4, 128, 128])
    b_sp = din("b_sp", [2, 4, 128])
    w_out_even = din("w_out_even", [2, 2048, D])
    w_in_odd = din("w_in_odd", [2, D, 1728])
    kv_norm = din("kv_norm", [2, 256])
    w_q_up = din("w_q_up", [2, 384, 8 * 192])
    w_kv_up = din("w_kv_up", [2, 256, 8 * 256])
    w_o = din("w_o", [2, D, D])
    vecs = din("vecs", [128, 96])
    ident_d = din("ident", [128, 128])
    mask_d = din("mask", [128, 512])
    selw_d = din("selw", [128, 8])
    rcnt_d = din("rcnt", [128, 4 * 16])
    cs_tm = din("cs_tm", [128, (NBLK + 1) * 64])
    cs_fm = din("cs_fm", [64, 2 * TOT])

    y_p = dout("y_p", [NTOK, D])
    y_s = dout("y_s", [NS, D])
    o_pool_p = dout("o_pool_p", [2, 15, D])
    o_pool_s = dout("o_pool_s", [2, 2, 15, D])
    o_sgu_s = dout("o_sgu_s", [2, NS, D])
    o_ckv_p = dout("o_ckv_p", [2, NTOK, 256])
    o_kr_p = dout("o_kr_p", [2, NTOK, 64])
    o_ckv_s = dout("o_ckv_s", [2, NS, 256])
    o_kr_s = dout("o_kr_s", [2, NS, 64])

    HW = 8 * NBLK * 16
    bh_in = [nc.dram_tensor("bh_in%d" % e, [128, HW], F32) for e in range(2)]
    bh_out = [nc.dram_tensor("bh_out%d" % e, [4 * 128, HW], F32) for e in range(2)]
    NSPL = max(1, NBLK // 8)
    BPS = NBLK // NSPL
    TPS = BPS * 128
    bk_in = [[nc.dram_tensor("bk_in%d_%d" % (o, sp), [TPS, 320], BF16) for sp in range(NSPL)] for o in range(2)]
    bk_out = [[nc.dram_tensor("bk_out%d_%d" % (o, sp), [4 * TPS, 320], BF16) for sp in range(NSPL)] for o in range(2)]

    P = Prog(nc)
    es = contextlib.ExitStack()

    def S(name, shape, dt):
        return es.enter_context(nc.sbuf_tensor("t_" + name, list(shape), dt))

    with es:
        ps = [es.enter_context(nc.psum_tensor("ps%d" % i, [128, 512], F32)) for i in range(8)]
        bankctr = [0]

        def nbank():
            b = bankctr[0] % 6
            bankctr[0] += 1
            return b

        xT = S("xT", [128, KD, TOT], F32)
        identf = S("identf", [128, 128], F32)
        identb = S("identb", [128, 128], BF16)
        onesb = S("onesb", [128, 128], BF16)
        epsc = S("epsc", [128, 1], F32)
        vec = S("vec", [128, 96], F32)
        selw = S("selw", [128, 8], F32)
        rcnt = S("rcnt", [128, 4, 16], F32)
        NWB = 4
        wst = [S("wst%d" % i, [128, 256], F32) for i in range(1)]
        wbf = [S("wbf%d" % i, [128, 8 * 128], BF16) for i in range(NWB)]
        hT = S("hT", [128, KD, 256], BF16)
        sqb = S("sqb", [128, KD, 256], BF16)
        rstd = S("rstd", [128, 258], F32)
        yT = S("yT", [128, KD, 256], F32)
        tmpT = S("tmpT", [128, KD, 256], F32)
        RB = max(81920, 24 * NTOK + 4 * NBLK * 516)
        R = S("R", [128, RB], mybir.dt.uint8)

        R2 = S("R2", [128, 23040], mybir.dt.uint8)

        def r2view(off, shape, dt, parts=128):
            n = 1
            for s_ in shape[1:]:
                n *= s_
            esz = 4 if dt == F32 else 2
            v = R2[0:parts, off:off + n * esz].bitcast(dt)
            if len(shape) == 3:
                v = v.rearrange("p (a b) -> p a b", a=shape[1])
            elif len(shape) == 4:
                v = v.rearrange("p (a b c) -> p a b c", a=shape[1], b=shape[2])
            return v

        def rview(off, shape, dt):
            n = 1
            for s in shape[1:]:
                n *= s
            esz = 4 if dt == F32 else 2
            v = R[:, off:off + n * esz].bitcast(dt)
            if len(shape) == 3:
                v = v.rearrange("p (a b) -> p a b", a=shape[1])
            elif len(shape) == 4:
                v = v.rearrange("p (a b c) -> p a b c", a=shape[1], b=shape[2])
            return v

        def ld(dst, src, name, eng="sp"):
            P.dma(eng, lambda q: [q.dma_start(out=dst, in_=src)], writes=[name], chan=name)

        ld(identf[:], ident_d[:, :], "identf")
        ld(vec[:], vecs[:, :], "vec")
        ld(selw[:], selw_d[:, :], "selw")
        ld(rcnt[:].rearrange("p a b -> p (a b)"), rcnt_d[:, :], "rcnt")
        P.op("dve", lambda q: q.tensor_copy(out=identb[:], in_=identf[:]), reads=["identf"], writes=["identb"])
        P.op("pool", lambda q: q.memset(onesb[:], 1.0 / 1024.0), writes=["onesb"])
        P.op("pool", lambda q: q.memset(epsc[:], EPS), writes=["epsc"])

        evq = [0]

        def evac_copy(out, in_, bank, wres, rres=(), scale=None, join=()):
            evq[0] += 1
            if evq[0] % 2 == 0:
                P.op("act", lambda q: q.activation(out=out, in_=in_, func=AF.Copy, scale=(1.0 if scale is None else scale)),
                     reads=list(rres), writes=[("ps", bank)] + list(wres), join=join)
            else:
                if scale is None:
                    P.op("dve", lambda q: q.tensor_copy(out=out, in_=in_), reads=list(rres), writes=[("ps", bank)] + list(wres), join=join)
                else:
                    P.op("dve", lambda q: q.tensor_scalar(out=out, in0=in_, scalar1=scale, scalar2=None, op0=ALU.mult),
                         reads=list(rres), writes=[("ps", bank)] + list(wres), join=join)

        wctr = [0]

        WBA = {}
        CH = {}
        rlctr = [0]

        class WV:
            def __init__(self, name):
                self.name = name

            def __getitem__(self, idx):
                _, ks, cs = idx
                return (self.name, ks.start or 0, cs.start)

        SRC = {}

        def conv(name, src2d, rows, cols, piece=None):
            SRC[name] = src2d

        def relay(name, k0, kdim, c0, cols):
            i = rlctr[0]
            rlctr[0] += 1
            B = nc.dram_tensor("wbB_%s_%d_%d" % (name, k0, c0), [128, kdim * cols], BF16)
            CH[(name, k0, c0)] = (B, kdim, cols)
            src = SRC[name].rearrange("(k p) c -> p k c", p=128)[:, k0:k0 + kdim, c0:c0 + cols]
            slot = ("rlslot", i % 8)
            P.dma("pool", lambda q: [q.dma_start(out=B.ap().rearrange("p (k c) -> p k c", k=kdim), in_=src)],
                  reads=[slot], writes=[("wbB", name, k0, c0), slot], chan=slot)

        def relay_layer(layer):
            if layer % 2 == 0:
                e = layer // 2
                for j in range(40):
                    relay("win%d" % e, 0, 8, j * 128, 128)
                for dc in range(8):
                    relay("wo%d" % e, 0, 8, dc * 128, 128)
                    relay("wo%d" % e, 8, 8, dc * 128, 128)
            else:
                o = layer // 2
                for h in range(8):
                    relay("wku%d" % o, 0, 2, h * 256, 256)
                for (c0, w) in [(384, 128), (512, 128), (640, 64), (0, 128), (128, 128), (256, 128)] + [(704 + ec * 128, 128) for ec in range(8)]:
                    relay("wio%d" % o, 0, 8, c0, w)
                for h in range(8):
                    relay("wqu%d" % o, 0, 3, h * 192, 192)
                for dc in range(8):
                    relay("wov%d" % o, 0, 8, dc * 128, 128)

        def conv_layer(layer):
            if layer % 2 == 0:
                e = layer // 2
                conv("win%d" % e, w_in_even[e], D, 5120, piece=1024)
                conv("wo%d" % e, w_out_even[e], 2048, D)
            else:
                o = layer // 2
                conv("wku%d" % o, w_kv_up[o], 256, 2048)
                conv("wio%d" % o, w_in_odd[o], D, 1728)
                conv("wqu%d" % o, w_q_up[o], 384, 1536)
                conv("wov%d" % o, w_o[o], D, D)

        def wload(ref, kdim, cols):
            i = wctr[0]
            wctr[0] += 1
            B, kd_, cols_ = CH[ref]
            assert kd_ == kdim and cols_ == cols, (ref, kdim, cols)
            wb = wbf[i % NWB]
            bname = "wbf%d" % (i % NWB)
            n = kdim * cols
            wbv = wb[:, 0:n].rearrange("p (k c) -> p k c", k=kdim)
            P.dma("sp", lambda q: [q.dma_start(out=wb[:, 0:n], in_=B.ap()[:, :])], reads=[("wbB",) + ref], writes=[bname], chan=bname)
            return wbv, bname

        xin = tmpT[:].rearrange("p k n -> p (k n)")[:, 0:D]

        def load_x(src_rows, c0, n):
            P.dma("sp", lambda q: [q.dma_start(out=xin[0:n, :], in_=src_rows)], writes=["tmpT"], chan="xin")
            for half in range(2):
                b = nbank()

                def tr(q, half=half, b=b):
                    r = None
                    for j in range(4):
                        k = half * 4 + j
                        r = q.transpose(ps[b][:, j * 128:j * 128 + n], xin[0:n, k * 128:(k + 1) * 128], identf[0:n, 0:n])
                    return r
                P.op("pe", tr, reads=["tmpT", "identf"], writes=[("ps", b)])
                src = ps[b][:].rearrange("p (j t) -> p j t", j=4)[:, :, 0:n]
                evac_copy(xT[:, half * 4:half * 4 + 4, c0:c0 + n], src, b, [("xT", c0 // 256)])

        yout = yT[:].rearrange("p k n -> p (k n)")[:, 0:D]

        def store_x(dst_rows, c0, n):
            for half in range(2):
                b = nbank()

                def tr(q, half=half, b=b):
                    r = None
                    for j in range(4):
                        k = half * 4 + j
                        r = q.transpose(ps[b][0:n, j * 128:(j + 1) * 128], xT[:, k, c0:c0 + n], identf[:, :])
                    return r
                P.op("pe", tr, reads=[("xT", c0 // 256), "identf"], writes=[("ps", b)])
                evac_copy(yout[0:n, half * 512:(half + 1) * 512], ps[b][0:n, :], b, ["yT"])
            P.dma("sp", lambda q: [q.dma_start(out=dst_rows, in_=yout[0:n, :])], reads=["yT"],
                  chan="yout", is_out=True)

        for layer_ in range(NLAYERS):
            conv_layer(layer_)
        relay_layer(0)
        for b in range(NBLK):
            load_x(xp[b * 128:(b + 1) * 128, :], b * 128, 128)
        load_x(xs[:, :], NTOK, NS)

        def rms_stats(src4, n, srcres, outname="rstd", kdim=KD, scale=1.0):
            P.op("act", lambda q: q.activation(out=sqb[:, 0:kdim, 0:n], in_=src4, func=AF.Square), reads=list(srcres), writes=["sqb"])
            b = nbank()

            def mm(q):
                r = None
                for k in range(kdim):
                    r = q.matmul(ps[b][:, 0:n], lhsT=onesb[:], rhs=sqb[:, k, 0:n], start=(k == 0), stop=(k == kdim - 1))
                return r
            P.op("pe", mm, reads=["sqb", "onesb"], writes=[("ps", b)])
            P.op("act", lambda q: q.activation(out=rstd[:, 0:n], in_=ps[b][:, 0:n], func=AF.Sqrt, bias=epsc[:, 0:1], scale=scale),
                 reads=["epsc"], writes=[("ps", b), outname])
            P.op("dve", lambda q: q.reciprocal(out=rstd[:, 0:n], in_=rstd[:, 0:n]), reads=[outname], writes=[outname])

        KS = 5

        def pre_norm(layer, xview, n, xres):
            gb = vec[:, layer * 8:layer * 8 + 8].unsqueeze(2).to_broadcast([128, KD, n])
            P.op("pool", lambda q: q.tensor_tensor(out=tmpT[:, :, 0:n], in0=xview, in1=gb, op=ALU.mult),
                 reads=list(xres) + ["vec"], writes=["tmpT"])
            rms_stats(xview, n, xres)
            rb1 = rstd[:, 0:n].unsqueeze(1).to_broadcast([128, KS, n])
            rb2 = rstd[:, 0:n].unsqueeze(1).to_broadcast([128, KD - KS, n])
            P.op("dve", lambda q: q.tensor_tensor(out=hT[:, 0:KS, 0:n], in0=tmpT[:, 0:KS, 0:n], in1=rb1, op=ALU.mult),
                 reads=["tmpT", "rstd"], writes=["hT"])
            P.op("pool", lambda q: q.tensor_tensor(out=hT[:, KS:KD, 0:n], in0=tmpT[:, KS:KD, 0:n], in1=rb2, op=ALU.mult),
                 reads=["tmpT", "rstd"], join=["hT"])

        def pre_norm_lite_stages(layer, xview, n, xres):
            def s1():
                P.op("act", lambda q: q.activation(out=hT[:, :, 0:n], in_=xview, func=AF.Square), reads=list(xres), writes=["hT"])
            bb = [None]

            def s2():
                pre_norm_lite_mid(n, bb)

            def s3():
                for k in range(KD):
                    P.op("dve", lambda q, k=k: q.scalar_tensor_tensor(out=hT[:, k, 0:n], in0=xview[:, k, :], scalar=vec[:, layer * 8 + k:layer * 8 + k + 1],
                                                                      in1=rstd[:, 0:n], op0=ALU.mult, op1=ALU.mult),
                         reads=list(xres) + ["vec", "rstd"], **({"writes": ["hT"]} if k == 0 else {"join": ["hT"]}))
            return [s1, s2, s3]

        def pre_norm_lite_mid(n, bb):
            b = nbank()

            def mm(q):
                r = None
                for k in range(KD):
                    r = q.matmul(ps[b][:, 0:n], lhsT=onesb[:], rhs=hT[:, k, 0:n], start=(k == 0), stop=(k == KD - 1))
                return r
            P.op("pe", mm, reads=["hT", "onesb"], writes=[("ps", b)])
            P.op("act", lambda q: q.activation(out=rstd[:, 0:n], in_=ps[b][:, 0:n], func=AF.Sqrt, bias=epsc[:, 0:1], scale=1.0),
                 reads=["epsc"], writes=[("ps", b), "rstd"])
            P.op("dve", lambda q: q.reciprocal(out=rstd[:, 0:n], in_=rstd[:, 0:n]), reads=["rstd"], writes=["rstd"])

        def post_norm_update(layer, c0, n):
            xres = [("xT", c0 // 256)]
            gb = vec[:, 32 + layer * 8:32 + layer * 8 + 8].unsqueeze(2).to_broadcast([128, KD, n])
            P.op("pool", lambda q: q.tensor_tensor(out=tmpT[:, :, 0:n], in0=yT[:, :, 0:n], in1=gb, op=ALU.mult),
                 reads=["yT", "vec"], writes=["tmpT"])
            rms_stats(yT[:, :, 0:n], n, ["yT"])
            rb1 = rstd[:, 0:n].unsqueeze(1).to_broadcast([128, KS, n])
            rb2 = rstd[:, 0:n].unsqueeze(1).to_broadcast([128, KD - KS, n])
            P.op("dve", lambda q: q.tensor_tensor(out=tmpT[:, 0:KS, 0:n], in0=tmpT[:, 0:KS, 0:n], in1=rb1, op=ALU.mult),
                 reads=["tmpT", "rstd"], writes=["tmpTa"])
            P.op("pool", lambda q: q.tensor_tensor(out=tmpT[:, KS:KD, 0:n], in0=tmpT[:, KS:KD, 0:n], in1=rb2, op=ALU.mult),
                 reads=["tmpT", "rstd"], writes=["tmpTb"])
            P.op("dve", lambda q: q.tensor_tensor(out=xT[:, 0:KS, c0:c0 + n], in0=xT[:, 0:KS, c0:c0 + n], in1=tmpT[:, 0:KS, 0:n], op=ALU.add),
                 reads=["tmpTa", "tmpT"] + xres, writes=xres)
            P.op("pool", lambda q: q.tensor_tensor(out=xT[:, KS:KD, c0:c0 + n], in0=xT[:, KS:KD, c0:c0 + n], in1=tmpT[:, KS:KD, 0:n], op=ALU.add),
                 reads=["tmpTb", "tmpT"] + xres, join=xres)

        def mm_fm(bank, wv, col0, ncol, rhs3, n, kdim=KD, first=True, last=True):
            def f(q):
                r = None
                for k in range(kdim):
                    r = q.matmul(ps[bank][0:ncol, 0:n], lhsT=wv[:, k, col0:col0 + ncol], rhs=rhs3[:, k, 0:n],
                                 start=(first and k == 0), stop=(last and k == kdim - 1))
                return r
            return f

        HWB = HW * 4
        halo_s = rview(0, [128, 8, NBLK, 16], F32)
        aT = rview(8192, [128, 8, 2, 144], F32)
        sA = rview(17408, [128, 8, 2, 144], F32)
        sB = rview(26624, [128, 8, 2, 144], F32)
        wp_st = rview(35840, [128, 8, 256], F32)
        gT = rview(44032, [128, 16, 256], BF16)
        hb = R[:, 44032:44032 + HWB].bitcast(F32)
        mixT = rview(52224, [128, 16, 256], BF16)
        vtm = rview(60416, [128, 2, D], F32)
        uT = rview(68608, [128, KD, 256], BF16)
        halo_c = rview(72704, [128, 8, NBLK, 16], F32)
        dT = sqb
        vbf = r2view(0, [128, 2, D], BF16)
        atm = yT[:].rearrange("p k n -> p (k n)")[:, 0:D]
        histtm = tmpT[:].rearrange("p k n -> p (k n)")[:, 0:D]
        st4 = S("st4", [128, 16], F32)
        wp_bf = r2view(4096, [128, 8, 256], BF16)
        ws_f = r2view(8192, [128, 4, 128], F32)
        wsT = r2view(10240, [128, 4, 128], BF16)
        wsd_f = r2view(11264, [64, 4, 64], F32, parts=64)
        wsdT = r2view(12288, [64, 4, 64], BF16, parts=64)
        lng = r2view(12800, [128, D], F32)
        lnb = r2view(16896, [128, D], F32)
        bsp = r2view(20992, [128, 4, 128], F32)

        def even_layer(e):
            layer = 2 * e
            win = WV("win%d" % e)
            nh = 16 * NBLK
            xv = xT[:, :, 0:NTOK].rearrange("p k (b w) -> p k b w", w=128)[:, :, :, 112:128]
            P.op("pool", lambda q: q.tensor_copy(out=yT[:, :, 0:nh].rearrange("p k (b w) -> p k b w", w=16), in_=xv),
                 reads=[("xT", t) for t in range(NT)], writes=["yT"])
            pre_norm(layer, yT[:, :, 0:nh], nh, ["yT"])
            for ec in range(8):
                wv, wn = wload(win[:, :, ec * 128:(ec + 1) * 128], KD, 128)
                b = nbank()
                P.op("pe", mm_fm(b, wv, 0, 128, hT, nh), reads=[wn, "hT"], writes=[("ps", b)])
                evac_copy(halo_c[:, ec, :, :], ps[b][:, 0:nh].rearrange("p (b w) -> p b w", w=16), b, ["halo_c"])
            P.dma("sp", lambda q: [q.dma_start(out=bh_in[e].ap()[:, :], in_=halo_c.rearrange("p k b w -> p (k b w)"))],
                  reads=["halo_c"], writes=[("bh_in", e)], chan=("bh_in", e))
            P.collective(lambda q: q.collective_compute("AllGather", ALU.bypass, replica_groups=[[0, 1, 2, 3], [4, 5, 6, 7]],
                                                        ins=[bh_in[e].ap().opt()], outs=[bh_out[e].ap().opt()]),
                         reads=[("bh_in", e)], writes=[("bh_out", e)], chan=("ccH", e))
            hg = hb.rearrange("p (k b w) -> p k b w", k=8, b=NBLK)
            for r in range(4):
                P.dma("sp", lambda q, r=r: [q.dma_start(out=hb, in_=bh_out[e].ap()[r * 128:(r + 1) * 128, :])],
                      reads=[("bh_out", e)], writes=["gT"], chan="hb")
                if r == 0:
                    P.op("dve", lambda q: q.tensor_scalar(out=halo_s, in0=hg, scalar1=selw[:, 0:1], scalar2=None, op0=ALU.mult),
                         reads=["gT", "selw"], writes=["halo_s"])
                else:
                    P.op("dve", lambda q, r=r: q.scalar_tensor_tensor(out=halo_s, in0=hg, scalar=selw[:, r:r + 1], in1=halo_s,
                                                                     op0=ALU.mult, op1=ALU.add),
                         reads=["gT", "selw", "halo_s"], writes=["halo_s"])
                if r == 3 and NBLK > 1:
                    P.op("dve", lambda q: q.scalar_tensor_tensor(out=halo_s[:, :, 1:NBLK, :], in0=hg[:, :, 0:NBLK - 1, :],
                                                                 scalar=selw[:, 4:5], in1=halo_s[:, :, 1:NBLK, :],
                                                                 op0=ALU.mult, op1=ALU.add),
                         reads=["gT", "selw", "halo_s"], writes=["halo_s"])
            P.dma("sp", lambda q: [q.dma_start(out=wp_st, in_=w_pool[e].rearrange("g (i p) o -> p (g i) o", p=128))],
                  writes=["wp_st"], chan="wp_st")
            P.op("dve", lambda q: q.tensor_copy(out=wp_bf[:], in_=wp_st), reads=["wp_st"], writes=["wp_bf"])
            P.dma("sp", lambda q: [q.dma_start(out=ws_f[:], in_=w_sp[e].rearrange("g i j -> i g j"))], writes=["ws_f"], chan="ws_f")
            b = nbank()

            def trw(q):
                r = None
                for g in range(4):
                    r = q.transpose(ps[b][:, g * 128:(g + 1) * 128], ws_f[:, g, :], identf[:])
                return r
            P.op("pe", trw, reads=["ws_f", "identf"], writes=[("ps", b)])
            P.op("dve", lambda q: q.tensor_copy(out=wsT[:].rearrange("p g i -> p (g i)"), in_=ps[b][:, :]), writes=[("ps", b), "wsT"])
            P.op("pool", lambda q: q.memset(wsT[64:128, :, 0:64], 0.0), reads=["wsT"], writes=["wsT"])
            P.op("pool", lambda q: q.memset(wsd_f[:], 0.0), writes=["wsd_f"])
            P.dma("sp", lambda q: [q.dma_start(out=wsd_f[0:32, :, 0:32], in_=w_sp[e, :, 0:32, 0:32].rearrange("g i j -> i g j")),
                                   q.dma_start(out=wsd_f[32:64, :, 32:64], in_=w_sp[e, :, 0:32, 0:32].rearrange("g i j -> i g j"))],
                  reads=["wsd_f"], writes=["wsd_f"], chan="wsd_f", n=2)
            b2 = nbank()

            def trw2(q):
                r = None
                for g in range(4):
                    r = q.transpose(ps[b2][0:64, g * 64:(g + 1) * 64], wsd_f[:, g, :], identf[0:64, 0:64])
                return r
            P.op("pe", trw2, reads=["wsd_f", "identf"], writes=[("ps", b2)])
            P.op("dve", lambda q: q.tensor_copy(out=wsdT[:].rearrange("p g i -> p (g i)"), in_=ps[b2][0:64, 0:256]),
                 writes=[("ps", b2), "wsdT"])
            ld(lng[:], ln_g[e:e + 1, :].broadcast_to([128, D]), "lng")
            ld(lnb[:], ln_b[e:e + 1, :].broadcast_to([128, D]), "lnb")
            ld(bsp[:].rearrange("p g i -> p (g i)"), b_sp[e:e + 1].rearrange("o g i -> o (g i)").broadcast_to([128, 512]), "bsp")
            return dict(win=win)

        def even_tile(e, ctx, t):
            layer = 2 * e
            sample = (t == NT)
            NB, W = (2, 32) if sample else (2, 128)
            n = NB * W
            c0 = NTOK if sample else t * 256
            xres = [("xT", c0 // 256)]
            win = ctx["win"]
            if t == 0:
                pre_norm(layer, xT[:, :, c0:c0 + n], n, xres)
            nxt = None
            if t < NT:
                c0n, nn = (NTOK, 64) if t + 1 == NT else ((t + 1) * 256, 256)
                nxt = pre_norm_lite_stages(layer, xT[:, :, c0n:c0n + nn], nn, [("xT", c0n // 256)])
            if sample:
                for s in range(2):
                    P.dma("sp", lambda q, s=s: [q.dma_start(out=histtm[0:15, :], in_=cpool[e, s])], writes=["tmpT"], chan="histtm")
                    for half in range(2):
                        b = nbank()

                        def trh(q, half=half, b=b):
                            r = None
                            for j in range(4):
                                k = half * 4 + j
                                r = q.transpose(ps[b][:, j * 16:j * 16 + 15], histtm[0:15, k * 128:(k + 1) * 128], identf[0:15, 0:15])
                            return r
                        P.op("pe", trh, reads=["tmpT", "identf"], writes=[("ps", b)])
                        P.op("dve", lambda q, half=half, b=b, s=s: q.tensor_copy(
                            out=aT[:, half * 4:half * 4 + 4, s, 1:16], in_=ps[b][:, 0:64].rearrange("p (j w) -> p j w", j=4)[:, :, 0:15]),
                            writes=[("ps", b), "aT"])
            else:
                P.op("pool", lambda q: q.tensor_copy(out=aT[:, :, :, 0:16], in_=halo_s[:, :, t * 2:t * 2 + 2, :]),
                     reads=["halo_s"], writes=["aT"])
            need_tm = sample or (t == NT - 1)
            m = 64 if sample else 128
            for ec in range(8):
                wv, wn = wload(win[:, :, ec * 128:(ec + 1) * 128], KD, 128)
                b = nbank()
                P.op("pe", mm_fm(b, wv, 0, 128, hT, n), reads=[wn, "hT"], writes=[("ps", b)])
                evac_copy(aT[:, ec, 0:NB, 16:16 + W], ps[b][:, 0:n].rearrange("p (b w) -> p b w", b=NB), b, ["aT"])
                if need_tm:
                    b = nbank()
                    tc0 = 0 if sample else 128

                    def mmtm(q, b=b, wv=wv, tc0=tc0):
                        r = None
                        for k in range(KD):
                            r = q.matmul(ps[b][0:m, 0:128], lhsT=hT[:, k, tc0:tc0 + m], rhs=wv[:, k, :], start=(k == 0), stop=(k == KD - 1))
                        return r
                    P.op("pe", mmtm, reads=[wn, "hT"], writes=[("ps", b)])
                    evac_copy(atm[0:m, ec * 128:(ec + 1) * 128], ps[b][0:m, 0:128], b, ["yT"])
            if need_tm:
                if sample:
                    P.dma("sp", lambda q: [q.dma_start(out=o_pool_s[e, 0], in_=atm[17:32, :]),
                                           q.dma_start(out=o_pool_s[e, 1], in_=atm[49:64, :])],
                          reads=["yT"], chan="atm", n=2, is_out=True)
                else:
                    P.dma("sp", lambda q: [q.dma_start(out=o_pool_p[e], in_=atm[113:128, :])], reads=["yT"], chan="atm", is_out=True)
            def sh(dst, src, k0, d):
                return lambda q: q.tensor_tensor(out=dst[:, k0:8, 0:NB, d:16 + W], in0=src[:, k0:8, 0:NB, d:16 + W],
                                                 in1=src[:, k0:8, 0:NB, 0:16 + W - d], op=ALU.add)
            P.op("dve", sh(sA, aT, 0, 1), reads=["aT"], writes=["sA", "sA2"])
            P.op("pool", sh(sB, sA, 2, 2), reads=["sA"], writes=["sB", "sB2"])
            P.op("dve", sh(sA, sB, 4, 4), reads=["sB"], writes=["sA2"])
            P.op("pool", sh(sB, sA, 6, 8), reads=["sA2"], writes=["sB2"])
            srcs = [(sA, ["sA"]), (sB, ["sB"]), (sA, ["sA2"]), (sB, ["sB2"])]
            for g in range(4):
                sbuf_, sres = srcs[g]
                w = 2 ** (g + 1)
                P.op("dve", lambda q, g=g, sbuf_=sbuf_, w=w: q.scalar_tensor_tensor(
                    out=dT[:, 2 * g:2 * g + 2, 0:n].rearrange("p k (b w) -> p k b w", b=NB),
                    in0=sbuf_[:, 2 * g:2 * g + 2, 0:NB, 16:16 + W], scalar=1.0 / w, in1=aT[:, 2 * g:2 * g + 2, 0:NB, 16:16 + W],
                    op0=ALU.mult, op1=ALU.subtract), reads=sres + ["aT"], writes=["sqb"])
                if (not sample) and t == 0:
                    rc = rcnt[:, g, :].unsqueeze(1).to_broadcast([128, 2, 16])
                    P.op("pool", lambda q, g=g, sbuf_=sbuf_, rc=rc: q.tensor_tensor(
                        out=tmpT[:, 0:2, 0:16], in0=sbuf_[:, 2 * g:2 * g + 2, 0, 16:32], in1=rc, op=ALU.mult),
                        reads=sres + ["rcnt"], writes=["tmpT"])
                    P.op("dve", lambda q, g=g: q.tensor_tensor(out=dT[:, 2 * g:2 * g + 2, 0:16], in0=tmpT[:, 0:2, 0:16],
                                                               in1=aT[:, 2 * g:2 * g + 2, 0, 16:32], op=ALU.subtract),
                         reads=["tmpT", "aT"], writes=["sqb"])
            nblk_tm = 1 if sample else 2
            for ec in range(8):
                wv, wn = wload(win[:, :, 2048 + ec * 128:2048 + (ec + 1) * 128], KD, 128)
                for blk in range(nblk_tm):
                    b = nbank()

                    def mmv(q, b=b, wv=wv, blk=blk):
                        r = None
                        for k in range(KD):
                            r = q.matmul(ps[b][0:m, 0:128], lhsT=hT[:, k, blk * 128:blk * 128 + m], rhs=wv[:, k, :],
                                         start=(k == 0), stop=(k == KD - 1))
                        return r
                    P.op("pe", mmv, reads=[wn, "hT"], writes=[("ps", b)])
                    P.op("act", lambda q, b=b, blk=blk, ec=ec: q.activation(out=vtm[0:m, blk, ec * 128:(ec + 1) * 128], in_=ps[b][0:m, 0:128],
                                                                         func=AF.Gelu), writes=[("ps", b), ("vtm", blk)])
            sqscr = tmpT[:].rearrange("p k n -> p (k n)")[:, 0:D]
            for blk in range(nblk_tm):
                vb = vtm[0:m, blk, :]
                vr = ("vtm", blk)
                P.op("dve", lambda q, vb=vb: q.reduce_sum(out=st4[0:m, 0:1], in_=vb, axis=AX.X), reads=[vr], writes=["st_a"])
                P.op("act", lambda q, vb=vb: q.activation(out=sqscr[0:m, :], in_=vb, func=AF.Square, accum_out=st4[0:m, 1:2]),
                     reads=[vr], writes=["tmpT", "st_b"])
                P.op("pool", lambda q: q.tensor_scalar(out=st4[0:m, 2:3], in0=st4[0:m, 0:1], scalar1=1.0 / D, scalar2=None, op0=ALU.mult),
                     reads=["st_a"], writes=["st_c"])
                P.op("dve", lambda q: q.tensor_tensor(out=st4[0:m, 3:4], in0=st4[0:m, 2:3], in1=st4[0:m, 2:3], op=ALU.mult),
                     reads=["st_c"], writes=["st_d"])
                P.op("pool", lambda q: q.tensor_scalar(out=st4[0:m, 4:5], in0=st4[0:m, 1:2], scalar1=1.0 / D, scalar2=None, op0=ALU.mult),
                     reads=["st_b"], writes=["st_e"])
                P.op("dve", lambda q: q.tensor_tensor(out=st4[0:m, 5:6], in0=st4[0:m, 4:5], in1=st4[0:m, 3:4], op=ALU.subtract),
                     reads=["st_e", "st_d"], writes=["st_f"])
                P.op("act", lambda q: q.activation(out=st4[0:m, 6:7], in_=st4[0:m, 5:6], func=AF.Sqrt, bias=epsc[0:m, 0:1], scale=1.0),
                     reads=["st_f", "epsc"], writes=["st_g"])
                P.op("dve", lambda q: q.reciprocal(out=st4[0:m, 7:8], in_=st4[0:m, 6:7]), reads=["st_g"], writes=["st_h"])
                P.op("dve", lambda q, vb=vb: q.tensor_scalar(out=vb, in0=vb, scalar1=st4[0:m, 2:3], scalar2=st4[0:m, 7:8],
                                                          op0=ALU.subtract, op1=ALU.mult), reads=[vr, "st_c", "st_h"], writes=[vr])
                P.op("pool", lambda q, vb=vb: q.tensor_tensor(out=vb, in0=vb, in1=lng[0:m, :], op=ALU.mult), reads=[vr, "lng"], writes=[vr])
                P.op("dve", lambda q, vb=vb: q.tensor_tensor(out=vb, in0=vb, in1=lnb[0:m, :], op=ALU.add), reads=[vr, "lnb"], writes=[vr])
                P.op("pool", lambda q, vb=vb, blk=blk: q.tensor_copy(out=vbf[0:m, blk, :], in_=vb), reads=[vr], writes=[("vbf", blk)])
                if sample:
                    P.dma("sp", lambda q: [q.dma_start(out=o_sgu_s[e], in_=vtm[0:64, 0, :])], reads=[vr], chan="vout", is_out=True)
            for ec in range(8):
                wv, wn = wload(win[:, :, 1024 + ec * 128:1024 + (ec + 1) * 128], KD, 128)
                b = nbank()
                P.op("pe", mm_fm(b, wv, 0, 128, hT, n), reads=[wn, "hT"], writes=[("ps", b)])
                P.op("act", lambda q, b=b, ec=ec: q.activation(out=uT[:, ec, 0:n], in_=ps[b][:, 0:n], func=AF.Gelu),
                     writes=[("ps", b), "uT"])
            for ec in range(16):
                wv, wn = wload(win[:, :, 3072 + ec * 128:3072 + (ec + 1) * 128], KD, 128)
                b = nbank()
                P.op("pe", mm_fm(b, wv, 0, 128, hT, n), reads=[wn, "hT"], writes=[("ps", b)])
                P.op("act", lambda q, b=b, ec=ec: q.activation(out=gT[:, ec, 0:n], in_=ps[b][:, 0:n], func=AF.Silu),
                     writes=[("ps", b), "gT"])
            if nxt:
                nxt[0]()
            P.op("pool", lambda q: q.tensor_tensor(out=uT[:, :, 0:n], in0=uT[:, :, 0:n], in1=gT[:, 8:16, 0:n], op=ALU.mult),
                 reads=["uT", "gT"], writes=["uT"])
            wI = 64 if sample else 128
            for blk in range(nblk_tm):
                for half in range(2):
                    b = nbank()

                    def mms(q, b=b, blk=blk, half=half):
                        r = None
                        for j in range(4):
                            dk = half * 4 + j
                            g = dk // 2
                            rhs = wsdT[0:64, g, :] if sample else wsT[:, g, :]
                            r = q.matmul(ps[b][:, j * 128:j * 128 + wI], lhsT=vbf[0:m, blk, dk * 128:(dk + 1) * 128], rhs=rhs,
                                         start=True, stop=True)
                        return r
                    P.op("pe", mms, reads=[("vbf", blk), "wsT", "wsdT"], writes=[("ps", b)])
                    ps4 = ps[b][:].rearrange("p (g c i) -> p g c i", g=2, c=2)
                    if sample:
                        for s in range(2):
                            bb = bsp[:, half * 2:half * 2 + 2, 0:32].unsqueeze(2).to_broadcast([128, 2, 2, 32])
                            P.op("dve", lambda q, s=s, half=half, bb=bb, ps4=ps4: q.tensor_tensor(
                                out=tmpT[:, half * 4:half * 4 + 4, s * 32:(s + 1) * 32].rearrange("p (g c) i -> p g c i", g=2),
                                in0=ps4[:, :, :, s * 32:(s + 1) * 32], in1=bb, op=ALU.add),
                                reads=["bsp"], writes=[("ps", b), "tmpT"])
                    else:
                        bb = bsp[:, half * 2:half * 2 + 2, :].unsqueeze(2).to_broadcast([128, 2, 2, 128])
                        P.op("dve", lambda q, half=half, blk=blk, bb=bb, ps4=ps4: q.tensor_tensor(
                            out=tmpT[:, half * 4:half * 4 + 4, blk * 128:(blk + 1) * 128].rearrange("p (g c) i -> p g c i", g=2),
                            in0=ps4, in1=bb, op=ALU.add),
                            reads=["bsp"], writes=[("ps", b), "tmpT"])
            P.op("pool", lambda q: q.tensor_tensor(out=mixT[:, 8:16, 0:n], in0=tmpT[:, :, 0:n], in1=uT[:, :, 0:n], op=ALU.mult),
                 reads=["tmpT", "uT"], writes=["mixT_b"])
            if nxt:
                nxt[1]()
            for g in range(4):
                for oc in range(2):
                    b = nbank()

                    def mmp(q, b=b, g=g, oc=oc):
                        r = None
                        for ic in range(2):
                            r = q.matmul(ps[b][:, 0:n], lhsT=wp_bf[:, g * 2 + ic, oc * 128:(oc + 1) * 128], rhs=dT[:, 2 * g + ic, 0:n],
                                         start=(ic == 0), stop=(ic == 1))
                        return r
                    P.op("pe", mmp, reads=["wp_bf", "sqb"], writes=[("ps", b)])
                    ch = 2 * g + oc
                    P.op("dve", lambda q, b=b, ch=ch: q.scalar_tensor_tensor(
                        out=mixT[:, ch, 0:n], in0=ps[b][:, 0:n], scalar=vec[:, 64 + e * 8 + ch:64 + e * 8 + ch + 1], in1=gT[:, ch, 0:n],
                        op0=ALU.mult, op1=ALU.mult), reads=["vec", "gT"], writes=[("ps", b), "mixT_a"])
            if nxt:
                nxt[2]()
            wo = WV("wo%d" % e)
            mixres = ["mixT_b", "mixT_a"]
            for dc in range(8):
                wva, wna = wload(wo[:, 0:8, dc * 128:(dc + 1) * 128], 8, 128)
                wvb, wnb = wload(wo[:, 8:16, dc * 128:(dc + 1) * 128], 8, 128)
                b = nbank()
                P.op("pe", mm_fm(b, wva, 0, 128, mixT[:, 0:8, :], n, first=True, last=False), reads=[wna] + mixres, writes=[("ps", b)])
                P.op("pe", mm_fm(b, wvb, 0, 128, mixT[:, 8:16, :], n, first=False, last=True), reads=[wnb] + mixres, writes=[("ps", b)])
                evac_copy(yT[:, dc, 0:n], ps[b][:, 0:n], b, ["yT"])
            post_norm_update(layer, c0, n)

        NK = 4 * NTOK
        KT = rview(0, [128, 2, NK], BF16)
        krT = R[0:64, 4 * NK:6 * NK].bitcast(BF16)
        Vt = rview(6 * NK, [128, 4 * NBLK, 258], BF16)
        WkvT = r2view(0, [128, 8, 256], BF16)
        Wv = r2view(4096, [128, 2, 8, 128], BF16)
        gateT = r2view(8192, [128, 8, 128], BF16)
        qn = r2view(10240, [128, 8, 128], BF16)
        qaT = r2view(12288, [128, 2, 8, 128], BF16)
        qrope = r2view(16384, [64, 8, 128], BF16, parts=64)
        kvn_bc = r2view(18432, [128, 256], F32)
        qcn = r2view(19456, [128, 3, 128], BF16)
        Pb = r2view(20224, [128, 2, 512], BF16)
        maskb = S("maskb", [128, 512], BF16)
        permf = S("permf", [64, 64], F32)
        cst = S("cst", [128, 64], F32)
        csfO = S("csfO", [128, 258], F32)
        csf = csfO[0:64, 0:256].rearrange("p (a t) -> p a t", a=2)
        stt = S("stt", [128, 16], F32)
        psb = [p[:].bitcast(BF16) for p in ps]
        tmp2d = tmpT[:].rearrange("p k n -> p (k n)")
        tmpb = tmp2d.bitcast(BF16)
        yT2d = yT[:].rearrange("p k n -> p (k n)")
        sqb2d = sqb[:].rearrange("p k n -> p (k n)")
        PTs = [tmpb[:, 0:512], tmpb[:, 512:1024]]
        krtm_s = tmpb[:, 1024:1600].rearrange("p (a b) -> p a b", b=64)
        olat = tmpb[:, 2048:4096].rearrange("p (h c) -> p h c", h=8)
        qas = tmpb[:, 1600:1856].rearrange("p (c t) -> p c t", c=2)
        qrs = tmpb[0:64, 1856:1984]
        qc = yT[:, 0:3, 128:256]
        Oacc = yT[:, 3:5, 128:256]
        qrf = yT[0:64, 5, 128:256]
        qt1 = yT[0:64, 6, 128:256]
        qt2 = yT[0:64, 7, 128:256]
        olT = sqb[:].rearrange("p k n -> p (k n)").rearrange("p (c h q) -> p c h q", c=2, h=8)
        oT = hT[:, :, 128:256]
        bks = [nc.dram_tensor("bks%d" % o, [64, 320], BF16) for o in range(2)]

        ld(tmp2d[:, 0:512], mask_d[:, :], "tmpT")
        P.op("dve", lambda q: q.tensor_copy(out=maskb[:], in_=tmp2d[:, 0:512]), reads=["tmpT"], writes=["maskb"])
        P.op("dve", lambda q: q.tensor_copy(out=permf[:, 0:32], in_=identf[0:64, 32:64]), reads=["identf"], writes=["permf"])
        P.op("pool", lambda q: q.tensor_copy(out=permf[:, 32:64], in_=identf[0:64, 0:32]), reads=["identf", "permf"], writes=["permf"])

        def stage_cast(src_ap, dst_ap, w, dres, parts=128):
            stv = wst[0][0:parts, 0:w]
            P.dma("sp", lambda q: [q.dma_start(out=stv, in_=src_ap)], writes=["wst0"], chan="wst0")
            P.op("pool", lambda q: q.tensor_copy(out=dst_ap, in_=stv), reads=["wst0"], writes=list(dres))

        def kt_transposes(kb, vidx, parts, krt_src, col0, w):
            b = nbank()

            def tr(q):
                q.transpose(psb[b][:, 0:w], Vt[0:parts, vidx, 0:128], identb[0:parts, 0:parts])
                q.transpose(psb[b][:, 128:128 + w], Vt[0:parts, vidx, 128:256], identb[0:parts, 0:parts])
                return q.transpose(psb[b][0:64, 256:256 + w], krt_src, identb[0:parts, 0:parts])
            P.op("pe", tr, reads=["V", "tmpT", "identb"], writes=[("ps", b)])
            P.op("act", lambda q: q.copy(out=KT[:, :, col0:col0 + w], in_=psb[b][:, 0:256].rearrange("p (c k) -> p c k", c=2)[:, :, 0:w]),
                 writes=[("ps", b), "KT"])
            P.op("dve", lambda q: q.tensor_copy(out=krT[:, col0:col0 + w], in_=psb[b][0:64, 256:256 + w]), writes=[("ps", b), "krT"])

        stt2 = S("stt2", [128, 24], F32)
        P.op("pool", lambda q: q.memset(stt2[:, 0:1], 30000.0 * ATTN_SCALE), writes=["nm_init"])
        Oaccs = [(csfO[:, 0:257], "csf"), (rstd[:, 0:257], "rstd")]
        Pbs = [Pb[:, 0, :], Pb[:, 1, :], wst[0][:, :].bitcast(BF16)]
        Pbn = [("Pb", 0), ("Pb", 1), "wst0"]

        uctr = [0]
        self_defer = [None]

        def kt_transposes2(kb, krt0, krt1):
            b = nbank()

            def tr(q):
                r = None
                for j, krt in enumerate((krt0, krt1)):
                    o = j * 384
                    q.transpose(psb[b][:, o:o + 128], Vt[:, kb + j, 0:128], identb[:])
                    q.transpose(psb[b][:, o + 128:o + 256], Vt[:, kb + j, 128:256], identb[:])
                    r = q.transpose(psb[b][0:64, o + 256:o + 384], krt, identb[:])
                return r
            P.op("pe", tr, reads=["V", "tmpT", "identb"], writes=[("ps", b)])
            src = psb[b][:, 0:768].rearrange("p (k x t) -> p k x t", k=2, x=3)
            c0 = kb * 128
            P.op("act", lambda q: q.copy(out=KT[:, :, c0:c0 + 256].rearrange("p c (k t) -> p k c t", k=2), in_=src[:, :, 0:2, :]),
                 writes=[("ps", b), "KT"])
            P.op("dve", lambda q: q.tensor_copy(out=krT[:, c0:c0 + 256].rearrange("p (k t) -> p k t", k=2), in_=src[0:64, :, 2, :]),
                 writes=[("ps", b), "krT"])

        def attn_stream(units, hook=None):
            steps = []
            for ui, U in enumerate(units):
                ng = len(U["groups"])
                U["_k"] = uctr[0] % 4
                U["_bO"] = 6 + uctr[0] % 2
                uctr[0] += 1
                for gi, G in enumerate(U["groups"]):
                    steps.append((ui, gi, ng, U, G))
            N = len(steps)
            st = [dict() for _ in range(N)]

            def col(base, k):
                return stt2[:, base + k:base + k + 1]

            def stA(i):
                ui, gi, ng, U, G = steps[i]
                w = G["w"]
                bS = nbank()
                st[i]["bS"] = bS

                def mmS(q):
                    q.matmul(ps[bS][:, 0:w], lhsT=U["qa0"], rhs=G["kt0"], start=True, stop=False)
                    q.matmul(ps[bS][:, 0:w], lhsT=U["qa1"], rhs=G["kt1"], start=False, stop=False)
                    r = q.matmul(ps[bS][:, 0:w], lhsT=U["qr"], rhs=G["krt"], start=False, stop=not G["diag"])
                    if G["diag"]:
                        r = q.matmul(ps[bS][:, 0:w], lhsT=identb[:], rhs=maskb[:, 0:w], start=False, stop=True)
                    return r
                P.op("pe", mmS, reads=["qaT", "qrope", "tmpT", "KT", "krT", "maskb", "identb"], writes=[("ps", bS)])

            def stB(i):
                ui, gi, ng, U, G = steps[i]
                w = G["w"]
                bS = st[i]["bS"]
                k = U["_k"]
                ib = i % 3
                if gi == 0:
                    P.op("dve", lambda q: q.reduce_max(out=col(6, k), in_=ps[bS][:, 0:w], axis=AX.X), writes=[("ps", bS), ("gmax", k)])
                    P.op("dve", lambda q: q.tensor_scalar(out=col(2, k), in0=col(6, k), scalar1=-ATTN_SCALE, scalar2=None, op0=ALU.mult),
                         reads=[("gmax", k)], writes=[("nm", k)])
                P.op("act", lambda q: q.activation(out=Pbs[ib][:, 0:w], in_=ps[bS][:, 0:w], func=AF.Exp, bias=col(2, k), scale=ATTN_SCALE),
                     reads=[("nm", k)], writes=[("ps", bS), Pbn[ib]])

            def stC(i):
                ui, gi, ng, U, G = steps[i]
                w = G["w"]
                bT = nbank()
                nj = (w + 127) // 128
                ib = i % 3
                it = i % 2

                def trP(q):
                    r = None
                    for j in range(nj):
                        wj = min(128, w - j * 128)
                        r = q.transpose(psb[bT][0:wj, j * 128:(j + 1) * 128], Pbs[ib][:, j * 128:j * 128 + wj], identb[:])
                    return r
                P.op("pe", trP, reads=[Pbn[ib], "identb"], writes=[("ps", bT)])
                kp = min(128, w)
                if i % 3 == 0:
                    P.op("act", lambda q: q.copy(out=PTs[it][0:kp, 0:nj * 128], in_=psb[bT][0:kp, 0:nj * 128]),
                         writes=[("ps", bT), ("PTs", it)])
                else:
                    P.op("dve", lambda q: q.tensor_copy(out=PTs[it][0:kp, 0:nj * 128], in_=psb[bT][0:kp, 0:nj * 128]),
                         writes=[("ps", bT), ("PTs", it)])

            def stD(i):
                ui, gi, ng, U, G = steps[i]
                bO = U["_bO"]
                it = i % 2

                def mmO(q):
                    r = None
                    nv = len(G["v"])
                    for j, (vap, K) in enumerate(G["v"]):
                        r = q.matmul(ps[bO][:, 0:257], lhsT=PTs[it][0:K, j * 128:(j + 1) * 128], rhs=vap,
                                     start=(gi == 0 and j == 0), stop=(gi == ng - 1 and j == nv - 1))
                    return r
                P.op("pe", mmO, reads=[("PTs", it), "V"], writes=[("ps", bO)])
                if gi == ng - 1:
                    P.op("dve", lambda q: q.reciprocal(out=stt2[:, 16:17], in_=ps[bO][:, 256:257]), writes=[("ps", bO), "rl"])
                    P.op("act", lambda q: q.activation(out=U["out_ap"], in_=ps[bO][:, 0:256], func=AF.Copy, scale=stt2[:, 16:17]),
                         reads=["rl"], writes=[("ps", bO), "olat"])
                    if U.get("post") is not None:
                        for j, fn in enumerate(U["post"]):
                            self_defer[0](2 + 2 * j, fn)

            pend = []
            st_cur = [0]

            def defer(delay, fn):
                pend.append((st_cur[0] + delay, fn))
            self_defer[0] = defer
            h0 = max(0, N // 2)
            for i in range(N + 3):
                st_cur[0] = i
                if hook is not None and i == h0:
                    for j, fn in enumerate(hook):
                        defer(2 * j, fn)
                due = [p for p in pend if p[0] <= i]
                pend[:] = [p for p in pend if p[0] > i]
                for _, fn in due:
                    fn()
                if i < N:
                    stA(i)
                    stB(i)
                if 0 <= i - 2 < N:
                    stC(i - 2)
                if 0 <= i - 3 < N:
                    stD(i - 3)
            for _, fn in sorted(pend, key=lambda p: p[0]):
                fn()

        def odd_layer(o):
            layer = 2 * o + 1
            wio = WV("wio%d" % o)
            wqu = WV("wqu%d" % o)
            wku = WV("wku%d" % o)
            wov = WV("wov%d" % o)
            if layer + 1 < NLAYERS:
                relay_layer(layer + 1)
            ld(kvn_bc, kv_norm[o:o + 1, :].broadcast_to([128, 256]), "kvn_bc")
            P.op("pool", lambda q: q.memset(Vt[:, :, 256:258], 1.0), writes=["V"])
            for h in range(8):
                wv, wn = wload(wku[:, :, h * 256:(h + 1) * 256], 2, 256)
                b = nbank()

                def trk(q, b=b, wv=wv):
                    q.transpose(psb[b][:, 0:128], wv[:, 0, 0:128], identb[:])
                    return q.transpose(psb[b][:, 128:256], wv[:, 1, 0:128], identb[:])
                P.op("pe", trk, reads=[wn, "identb"], writes=[("ps", b)])
                P.op("act", lambda q, b=b, h=h: q.copy(out=WkvT[:, h, :], in_=psb[b][:, 0:256]), writes=[("ps", b), "WkvT"])
                P.op("pool", lambda q, wv=wv, h=h: q.tensor_copy(out=Wv[:, :, h, :], in_=wv[:, :, 128:256]), reads=[wn], writes=["Wv"])
            units = [(u * 128, 128, u) for u in range(NBLK)] + [(NTOK, 64, NBLK)]
            for (c0, n, u) in units:
                sample = (u == NBLK)
                xres = [("xT", c0 // 256)]
                pre_norm(layer, xT[:, :, c0:c0 + n], n, xres)
                P.dma("sp", lambda q, u=u: [q.dma_start(out=cst[:], in_=cs_tm[:, u * 64:(u + 1) * 64])], writes=["cst"], chan="cst")
                b = nbank()
                for ci, (col, w) in enumerate([(384, 128), (512, 128), (640, 64)]):
                    wv, wn = wload(wio[:, :, col:col + w], KD, w)

                    def mmk(q, b=b, wv=wv, ci=ci, w=w, n=n):
                        r = None
                        for k in range(KD):
                            r = q.matmul(ps[b][0:n, ci * 128:ci * 128 + w], lhsT=hT[:, k, 0:n], rhs=wv[:, k, :], start=(k == 0), stop=(k == KD - 1))
                        return r
                    P.op("pe", mmk, reads=[wn, "hT"], writes=[("ps", b)])
                kvc = tmp2d[0:n, 0:320]
                P.op("act", lambda q, b=b, n=n, kvc=kvc: q.copy(out=kvc, in_=ps[b][0:n, 0:320]), writes=[("ps", b), "tmpT"])
                P.op("act", lambda q, n=n: q.activation(out=tmp2d[0:n, 512:768], in_=tmp2d[0:n, 0:256], func=AF.Square, accum_out=stt[0:n, 8:9]),
                     reads=["tmpT"], writes=["tmpT2", "ss"])
                P.op("act", lambda q, n=n: q.activation(out=stt[0:n, 9:10], in_=stt[0:n, 8:9], func=AF.Sqrt, bias=epsc[0:n, 0:1], scale=1.0 / 256.0),
                     reads=["ss", "epsc"], writes=["ss2"])
                P.op("dve", lambda q, n=n: q.reciprocal(out=stt[0:n, 10:11], in_=stt[0:n, 9:10]), reads=["ss2"], writes=["ss3"])
                P.op("dve", lambda q, n=n: q.scalar_tensor_tensor(out=yT2d[0:n, 0:256], in0=tmp2d[0:n, 0:256], scalar=stt[0:n, 10:11],
                                                                  in1=kvn_bc[0:n, :], op0=ALU.mult, op1=ALU.mult),
                     reads=["tmpT", "ss3", "kvn_bc"], writes=["yT"])
                x1, x2 = tmp2d[0:n, 256:288], tmp2d[0:n, 288:320]
                cs_, sn_ = cst[0:n, 0:32], cst[0:n, 32:64]
                t1, t2, t3, t4 = (tmp2d[0:n, 1024 + 32 * j:1056 + 32 * j] for j in range(4))
                P.op("pool", lambda q, x1=x1, cs_=cs_, t1=t1: q.tensor_tensor(out=t1, in0=x1, in1=cs_, op=ALU.mult), reads=["tmpT", "cst"], writes=["t1"])
                P.op("dve", lambda q, x2=x2, sn_=sn_, t2=t2: q.tensor_tensor(out=t2, in0=x2, in1=sn_, op=ALU.mult), reads=["tmpT", "cst"], writes=["t2"])
                P.op("pool", lambda q, x2=x2, cs_=cs_, t3=t3: q.tensor_tensor(out=t3, in0=x2, in1=cs_, op=ALU.mult), reads=["tmpT", "cst"], writes=["t3"])
                P.op("dve", lambda q, x1=x1, sn_=sn_, t4=t4: q.tensor_tensor(out=t4, in0=x1, in1=sn_, op=ALU.mult), reads=["tmpT", "cst"], writes=["t4"])
                P.op("pool", lambda q, n=n, t1=t1, t2=t2: q.tensor_tensor(out=yT2d[0:n, 256:288], in0=t1, in1=t2, op=ALU.subtract),
                     reads=["t1", "t2"], writes=["yTr1"])
                P.op("dve", lambda q, n=n, t3=t3, t4=t4: q.tensor_tensor(out=yT2d[0:n, 288:320], in0=t3, in1=t4, op=ALU.add),
                     reads=["t3", "t4"], writes=["yTr2"])
                P.op("pool", lambda q, n=n: q.tensor_copy(out=sqb2d[0:n, 0:320], in_=yT2d[0:n, 0:320]), reads=["yT", "yTr1", "yTr2"], writes=["sqb"])
                if sample:
                    P.dma("sp", lambda q: [q.dma_start(out=o_ckv_s[o], in_=yT2d[0:64, 0:256]), q.dma_start(out=o_kr_s[o], in_=yT2d[0:64, 256:320])],
                          reads=["yT", "yTr1", "yTr2"], chan="kvout", n=2, is_out=True)
                    P.dma("sp", lambda q: [q.dma_start(out=bks[o].ap()[:, :], in_=sqb2d[0:64, 0:320])], reads=["sqb"], writes=[("bks", o)], chan="bks")
                else:
                    P.dma("sp", lambda q, c0=c0: [q.dma_start(out=o_ckv_p[o, c0:c0 + 128, :], in_=yT2d[:, 0:256]),
                                                 q.dma_start(out=o_kr_p[o, c0:c0 + 128, :], in_=yT2d[:, 256:320])],
                          reads=["yT", "yTr1", "yTr2"], chan="kvout", n=2, is_out=True)
                    P.dma("sp", lambda q, u=u: [q.dma_start(out=bk_in[o][u // BPS].ap()[(u % BPS) * 128:(u % BPS) * 128 + 128, :], in_=sqb2d[:, 0:320])],
                          reads=["sqb"], writes=[("bk_in", o, u)], chan="bkin")
            for sp in range(NSPL):
                P.collective(lambda q, sp=sp: q.collective_compute("AllGather", ALU.bypass, replica_groups=[[0, 1, 2, 3], [4, 5, 6, 7]],
                                                                   ins=[bk_in[o][sp].ap().opt()], outs=[bk_out[o][sp].ap().opt()]),
                             reads=[("bk_in", o, u) for u in range(sp * BPS, (sp + 1) * BPS)], writes=[("bk_out", o, sp)], chan=("ccK", o, sp))
            V4 = Vt.rearrange("p (m r) c -> p m r c", r=4)
            krtm = tmpb[:, 0:NK // 2].rearrange("p (m r c) -> p m r c", r=4, c=64)
            for sp in range(NSPL):
                bko = bk_out[o][sp].ap()
                ms = slice(sp * BPS, (sp + 1) * BPS)
                for r in range(4):
                    P.dma("sp", lambda q, r=r, bko=bko, ms=ms: [q.dma_start(out=V4[:, ms, r, 0:256], in_=bko[r * TPS:(r + 1) * TPS, 0:256].rearrange("(m p) c -> p m c", p=128))],
                          reads=[("bk_out", o, sp)], writes=["V"], chan="Vld")
                    P.dma("sp", lambda q, r=r, bko=bko, ms=ms: [q.dma_start(out=krtm[:, ms, r, :], in_=bko[r * TPS:(r + 1) * TPS, 256:320].rearrange("(m p) c -> p m c", p=128))],
                          reads=[("bk_out", o, sp)], writes=["tmpT"], chan="krld")
            krtm3 = tmpb[:, 0:NK // 2].rearrange("p (k c) -> p k c", c=64)
            for kb in range(0, 4 * NBLK, 2):
                kt_transposes2(kb, krtm3[:, kb, :], krtm3[:, kb + 1, :])

            qrfs = [yT[0:64, 5, 128:256], yT[0:64, 3, 128:256]]
            qt1s = [yT[0:64, 6, 128:256], yT[0:64, 4, 128:256]]

            def q_path(c0, n, do_norm=True):
                xres = [("xT", c0 // 256)]
                if do_norm:
                    pre_norm(layer, xT[:, :, c0:c0 + n], n, xres)
                P.dma("sp", lambda q: [q.dma_start(out=csf[:, :, 0:n], in_=cs_fm.rearrange("p (a t) -> p a t", a=2)[:, :, c0:c0 + n])],
                      writes=["csf"], chan="csf")
                for kc in range(3):
                    wv, wn = wload(wio[:, :, kc * 128:(kc + 1) * 128], KD, 128)
                    b = nbank()
                    P.op("pe", mm_fm(b, wv, 0, 128, hT, n), reads=[wn, "hT"], writes=[("ps", b)])
                    evac_copy(qc[:, kc, 0:n], ps[b][:, 0:n], b, ["qc"])
                rms_stats(qc[:, :, 0:n], n, ["qc"], kdim=3, scale=1024.0 / 384.0)
                for kc in range(3):
                    P.op("dve", lambda q, kc=kc: q.scalar_tensor_tensor(out=qcn[:, kc, 0:n], in0=qc[:, kc, 0:n],
                                                                        scalar=vec[:, 80 + o * 3 + kc:80 + o * 3 + kc + 1], in1=rstd[:, 0:n],
                                                                        op0=ALU.mult, op1=ALU.mult), reads=["qc", "vec", "rstd"], writes=["qcn"])
                for ec in range(8):
                    wv, wn = wload(wio[:, :, 704 + ec * 128:704 + (ec + 1) * 128], KD, 128)
                    b = nbank()
                    P.op("pe", mm_fm(b, wv, 0, 128, hT, n), reads=[wn, "hT"], writes=[("ps", b)])
                    P.op("act", lambda q, b=b, ec=ec: q.activation(out=gateT[:, ec, 0:n], in_=ps[b][:, 0:n], func=AF.Silu),
                         writes=[("ps", b), "gateT"])

                def stage_a(h):
                    j = h % 2
                    wv, wn = wload(wqu[:, :, h * 192:(h + 1) * 192], 3, 192)
                    b1 = nbank()
                    P.op("pe", mm_fm(b1, wv, 0, 128, qcn, n, kdim=3), reads=[wn, "qcn"], writes=[("ps", b1)])
                    evac_copy(qn[:, h, 0:n], ps[b1][:, 0:n], b1, [("qn", h)])
                    b2 = nbank()
                    P.op("pe", mm_fm(b2, wv, 128, 64, qcn, n, kdim=3), reads=[wn, "qcn"], writes=[("ps", b2)])
                    P.op("act", lambda q: q.copy(out=qrfs[j][:, 0:n], in_=ps[b2][0:64, 0:n]), writes=[("ps", b2), ("qrf", j)])

                def stage_b(h):
                    j = h % 2
                    jw = {"writes": ["qrope"]} if h == 0 else {"join": ["qrope"]}
                    jq = {"wres": ["qaT"]} if h == 0 else {"wres": [], "join": ["qaT"]}
                    b3 = nbank()
                    P.op("pe", lambda q: q.matmul(ps[b3][0:64, 0:n], lhsT=permf[:, :], rhs=qrfs[j][:, 0:n], start=True, stop=True),
                         reads=["permf", ("qrf", j)], writes=[("ps", b3)])
                    P.op("dve", lambda q: q.tensor_tensor(out=qt1s[j][:, 0:n], in0=ps[b3][0:64, 0:n], in1=csf[:, 1, 0:n], op=ALU.mult),
                         reads=["csf"], writes=[("ps", b3), ("qt1", j)])
                    P.op("pool", lambda q: q.tensor_tensor(out=qt2[:, 0:n], in0=qrfs[j][:, 0:n], in1=csf[:, 0, 0:n], op=ALU.mult),
                         reads=["csf", ("qrf", j)], writes=["qt2"])
                    P.op("dve", lambda q: q.tensor_tensor(out=qrope[:, h, 0:n], in0=qt1s[j][:, 0:n], in1=qt2[:, 0:n], op=ALU.add),
                         reads=[("qt1", j), "qt2"], **jw)
                    b4 = nbank()

                    def mma(q):
                        q.matmul(ps[b4][:, 0:n], lhsT=WkvT[:, h, 0:128], rhs=qn[:, h, 0:n], start=True, stop=True)
                        return q.matmul(ps[b4][:, 128:128 + n], lhsT=WkvT[:, h, 128:256], rhs=qn[:, h, 0:n], start=True, stop=True)
                    P.op("pe", mma, reads=["WkvT", ("qn", h)], writes=[("ps", b4)])
                    evac_copy(qaT[:, :, h, 0:n], ps[b4][:, 0:256].rearrange("p (c t) -> p c t", c=2)[:, :, 0:n], b4, jq["wres"], join=jq.get("join", ()))

                for i in range(9):
                    if i < 8:
                        stage_a(i)
                    if i >= 1:
                        stage_b(i - 1)

            def head_out_a(h):
                b = nbank()

                def tro(q):
                    q.transpose(psb[b][:, 0:128], olat[:, h, 0:128], identb[:])
                    return q.transpose(psb[b][:, 128:256], olat[:, h, 128:256], identb[:])
                P.op("pe", tro, reads=["olat", "identb"], writes=[("ps", b)])
                evac_copy(olT[:, :, h, :], psb[b][:, 0:256].rearrange("p (c q) -> p c q", c=2), b, [("olT", h)], join=["sqb"])

            def head_out_b(h, n):
                b2 = nbank()

                def mmo(q):
                    q.matmul(ps[b2][:, 0:n], lhsT=Wv[:, 0, h, :], rhs=olT[:, 0, h, 0:n], start=True, stop=False)
                    return q.matmul(ps[b2][:, 0:n], lhsT=Wv[:, 1, h, :], rhs=olT[:, 1, h, 0:n], start=False, stop=True)
                P.op("pe", mmo, reads=["Wv", ("olT", h), "sqb"], writes=[("ps", b2)])
                P.op("dve", lambda q: q.tensor_tensor(out=oT[:, h, 0:n], in0=ps[b2][:, 0:n], in1=gateT[:, h, 0:n], op=ALU.mult),
                     reads=["gateT"], writes=[("ps", b2)] + (["oT"] if h == 0 else []), join=([] if h == 0 else ["oT"]))

            def out_path(c0, n, heads_done=False):
                for h in range(0 if heads_done else 8):
                    b = nbank()

                    def mmo(q, b=b, h=h):
                        q.matmul(ps[b][:, 0:n], lhsT=Wv[:, 0, h, :], rhs=olT[:, 0, h, 0:n], start=True, stop=False)
                        return q.matmul(ps[b][:, 0:n], lhsT=Wv[:, 1, h, :], rhs=olT[:, 1, h, 0:n], start=False, stop=True)
                    P.op("pe", mmo, reads=["Wv", "sqb"], writes=[("ps", b)])
                    P.op("dve", lambda q, b=b, h=h: q.tensor_tensor(out=oT[:, h, 0:n], in0=ps[b][:, 0:n], in1=gateT[:, h, 0:n], op=ALU.mult),
                         reads=["gateT"], writes=[("ps", b), "oT"])
                for dc in range(8):
                    wv, wn = wload(wov[:, :, dc * 128:(dc + 1) * 128], KD, 128)
                    b = nbank()
                    P.op("pe", mm_fm(b, wv, 0, 128, oT, n), reads=[wn, "oT"], writes=[("ps", b)])
                    evac_copy(yT[:, dc, 0:n], ps[b][:, 0:n], b, ["yT"])
                post_norm_update(layer, c0, n)

            for blk in range(NBLK):
                c0 = blk * 128
                q_path(c0, 128, do_norm=(blk == 0))
                units_ = []
                for h in range(8):
                    groups = []
                    for g in range(blk + 1):
                        ks = slice(g * 512, (g + 1) * 512)
                        groups.append(dict(kt0=KT[:, 0, ks], kt1=KT[:, 1, ks], krt=krT[:, ks], w=512,
                                           v=[(Vt[:, g * 4 + j, 0:257], 128) for j in range(4)], diag=(g == blk)))
                    units_.append(dict(qa0=qaT[:, 0, h, :], qa1=qaT[:, 1, h, :], qr=qrope[:, h, :], groups=groups, out_ap=olat[:, h, :],
                                       post=[(lambda h=h: head_out_a(h)), (lambda h=h: head_out_b(h, 128))]))
                nc0, nn = ((blk + 1) * 128, 128) if blk + 1 < NBLK else (NTOK, 64)
                attn_stream(units_, hook=pre_norm_lite_stages(layer, xT[:, :, nc0:nc0 + nn], nn, [("xT", nc0 // 256)]))
                out_path(c0, 128, heads_done=True)
            q_path(NTOK, 64, do_norm=False)
            for s_ in range(2):
                for kb in range(8):
                    stage_cast(cckv[o, s_, kb * 128:(kb + 1) * 128, :], Vt[:, kb, 0:256], 256, ["V"])
                    stage_cast(ckr[o, s_, kb * 128:(kb + 1) * 128, :], krtm_s[:, kb, :], 64, ["tmpT"])
                P.dma("sp", lambda q, s_=s_: [q.dma_start(out=Vt[0:32, 8, 0:256], in_=bks[o].ap()[s_ * 32:(s_ + 1) * 32, 0:256]),
                                             q.dma_start(out=krtm_s[0:32, 8, :], in_=bks[o].ap()[s_ * 32:(s_ + 1) * 32, 256:320])],
                      reads=[("bks", o)], writes=["V", "tmpT"], chan="bksld", n=2)
                for kb in range(0, 8, 2):
                    kt_transposes2(kb, krtm_s[:, kb, :], krtm_s[:, kb + 1, :])
                kt_transposes(8, 8, 32, krtm_s[0:32, 8, :], 1024, 32)
                units_ = []
                for hq in range(2):
                    ts = slice(s_ * 32, (s_ + 1) * 32)
                    hs = slice(hq * 4, hq * 4 + 4)
                    groups = []
                    for g in range(2):
                        ks = slice(g * 512, (g + 1) * 512)
                        groups.append(dict(kt0=KT[:, 0, ks], kt1=KT[:, 1, ks], krt=krT[:, ks], w=512,
                                           v=[(Vt[:, g * 4 + j, 0:257], 128) for j in range(4)], diag=False))
                    groups.append(dict(kt0=KT[:, 0, 1024:1056], kt1=KT[:, 1, 1024:1056], krt=krT[:, 1024:1056], w=32,
                                       v=[(Vt[0:32, 8, 0:257], 32)], diag=False))
                    P.op("dve", lambda q, hs=hs, ts=ts: q.tensor_copy(out=qas.rearrange("p c (h t) -> p c h t", h=4), in_=qaT[:, :, hs, ts]),
                         reads=["qaT"], writes=["tmpT"])
                    P.op("pool", lambda q, hs=hs, ts=ts: q.tensor_copy(out=qrs.rearrange("p (h t) -> p h t", h=4), in_=qrope[:, hs, ts]),
                         reads=["qrope"], writes=["tmpT"])

                    def post(hs=hs, ts=ts):
                        b = nbank()

                        def tro2(q):
                            q.transpose(psb[b][:, 0:128], olat[:, 0, 0:128], identb[:])
                            return q.transpose(psb[b][:, 128:256], olat[:, 0, 128:256], identb[:])
                        P.op("pe", tro2, reads=["olat", "identb"], writes=[("ps", b)])
                        evac_copy(olT[:, :, hs, ts], psb[b][:, 0:256].rearrange("p (c h t) -> p c h t", c=2, h=4), b, ["sqb"])
                    attn_stream([dict(qa0=qas[:, 0, :], qa1=qas[:, 1, :], qr=qrs, groups=groups, out_ap=olat[:, 0, :], post=[post])])
            out_path(NTOK, 64)

        for layer in range(NLAYERS):
            if layer % 2 == 0:
                e = layer // 2
                ctx = even_layer(e)
                for t in range(NT + 1):
                    if t == 0 and layer + 1 < NLAYERS:
                        relay_layer(layer + 1)
                    even_tile(e, ctx, t)
            else:
                P.barrier()
                odd_layer(layer // 2)
                P.barrier()

        for b in range(NBLK):
            store_x(y_p[b * 128:(b + 1) * 128, :], b * 128, 128)
        store_x(y_s[:, :], NTOK, NS)
        P.finish()
        P.replay()
    return nc


def _host_consts(c, NBLK):
    NTOK = NBLK * 128
    TOT = NTOK + 64
    r = c % 4
    mask = np.zeros((128, 512), np.float32)
    qi = np.arange(128)[:, None]
    for i in range(4):
        blk = mask[:, i * 128:(i + 1) * 128]
        if i > r:
            blk[:] = NEG
        elif i == r:
            kj = np.arange(128)[None, :]
            blk[:] = np.where((kj // 64) <= (qi // 64), 0.0, NEG)
    selw = np.zeros((128, 8), np.float32)
    if r > 0:
        selw[:, r - 1] = 1.0
    else:
        selw[:, 4] = 1.0
    rc = np.zeros((128, 4, 16), np.float32)
    for g in range(4):
        w = 2 ** (g + 1)
        for p in range(16):
            rc[:, g, p] = (1.0 / min(p + 1, w)) if r == 0 else 1.0 / w
    half = 32
    freqs = (10000.0 ** (-np.arange(half, dtype=np.float32) / half)).astype(np.float32)
    pos = np.zeros(TOT, np.float32)
    for m in range(NBLK):
        pos[m * 128:(m + 1) * 128] = (4 * m + r) * 128 + np.arange(128)
    pos[NTOK:NTOK + 32] = 1024 + np.arange(32)
    pos[NTOK + 32:] = 1024 + np.arange(32)
    ang = pos[:, None].astype(np.float32) * freqs[None, :]
    cos, sin = np.cos(ang).astype(np.float32), np.sin(ang).astype(np.float32)
    cs_tm = np.zeros((128, NBLK + 1, 64), np.float32)
    for m in range(NBLK):
        cs_tm[:, m, 0:32] = cos[m * 128:(m + 1) * 128]
        cs_tm[:, m, 32:64] = sin[m * 128:(m + 1) * 128]
    cs_tm[0:64, NBLK, 0:32] = cos[NTOK:]
    cs_tm[0:64, NBLK, 32:64] = sin[NTOK:]
    cs_fm = np.zeros((64, 2, TOT), np.float32)
    cs_fm[0:32, 0] = cos.T
    cs_fm[32:64, 0] = cos.T
    cs_fm[0:32, 1] = -sin.T
    cs_fm[32:64, 1] = sin.T
    return dict(mask=mask, selw=selw, rcnt=rc.reshape(128, 64), cs_tm=cs_tm.reshape(128, -1), cs_fm=cs_fm.reshape(64, -1),
                ident=np.eye(128, dtype=np.float32))


def _fm(v):
    return np.ascontiguousarray(np.asarray(v, np.float32).reshape(-1, 128).T)


_NC_CACHE = {}


def kernel(x_prompt, x_sample, cache_pool, cache_ckv, cache_krope, norm_pre, norm_post,
           w_in_even, w_pool, pool_scale, sgu_ln_g, sgu_ln_b, w_spatial, b_spatial, w_out_even,
           w_in_odd, q_norm, kv_norm, w_q_up, w_kv_up, w_o, _nlayers=4):
    f = lambda a: np.ascontiguousarray(np.asarray(a, dtype=np.float32))
    x_prompt = f(x_prompt)
    x_sample = f(x_sample)
    B, T, _ = x_prompt.shape
    NBLK = T // 512
    NTOK = NBLK * 128
    vecs = np.zeros((128, 96), np.float32)
    for l in range(4):
        vecs[:, l * 8:(l + 1) * 8] = _fm(f(norm_pre)[l])
        vecs[:, 32 + l * 8:32 + (l + 1) * 8] = _fm(f(norm_post)[l])
    for e in range(2):
        vecs[:, 64 + e * 8:64 + (e + 1) * 8] = _fm(f(pool_scale)[e])
        vecs[:, 80 + e * 3:80 + (e + 1) * 3] = _fm(f(q_norm)[e])
    shared = dict(w_in_even=f(w_in_even), w_pool=f(w_pool), ln_g=f(sgu_ln_g), ln_b=f(sgu_ln_b), w_sp=f(w_spatial),
                  b_sp=f(b_spatial), w_out_even=f(w_out_even), w_in_odd=f(w_in_odd), kv_norm=f(kv_norm),
                  w_q_up=f(w_q_up).reshape(2, 384, 8 * 192), w_kv_up=f(w_kv_up).reshape(2, 256, 8 * 256), w_o=f(w_o), vecs=vecs)
    cache_pool, cache_ckv, cache_krope = f(cache_pool), f(cache_ckv), f(cache_krope)
    in_maps = []
    for c in range(8):
        b, r = c // 4, c % 4
        xb = x_prompt[b].reshape(NBLK, 4, 128, D)[:, r].reshape(NTOK, D)
        m = dict(shared)
        m.update(_host_consts(c, NBLK))
        m["xp"] = np.ascontiguousarray(xb)
        m["xs"] = np.ascontiguousarray(x_sample[2 * c:2 * c + 2].reshape(64, D))
        m["cpool"] = np.ascontiguousarray(cache_pool[:, 2 * c:2 * c + 2])
        m["cckv"] = np.ascontiguousarray(cache_ckv[:, 2 * c:2 * c + 2])
        m["ckr"] = np.ascontiguousarray(cache_krope[:, 2 * c:2 * c + 2])
        in_maps.append(m)
    key = (NBLK, _nlayers)
    if key not in _NC_CACHE:
        _NC_CACHE[key] = build(NBLK, _nlayers)
    nc = _NC_CACHE[key]
    res = run_bass_kernel_spmd(nc, in_maps, core_ids=list(range(8))).results

    def unshard(name, width):
        out = np.zeros((B, NBLK, 4, 128, width), np.float32)
        for c in range(8):
            out[c // 4, :, c % 4] = res[c][name].reshape(NBLK, 128, width)
        return out.reshape(B, T, width)

    def unshard_l(name, width):
        out = np.zeros((2, B, NBLK, 4, 128, width), np.float32)
        for c in range(8):
            out[:, c // 4, :, c % 4] = res[c][name].reshape(2, NBLK, 128, width)
        return out.reshape(2, B, T, width)

    y_prompt = unshard("y_p", D)
    y_sample = np.concatenate([res[c]["y_s"].reshape(2, 32, D) for c in range(8)], 0)
    pool_p = np.stack([res[3]["o_pool_p"], res[7]["o_pool_p"]], 1)
    pool_s = np.concatenate([res[c]["o_pool_s"] for c in range(8)], 1)
    sgu_s = np.concatenate([res[c]["o_sgu_s"].reshape(2, 2, 32, D) for c in range(8)], 1)
    ckv_p = unshard_l("o_ckv_p", 256)
    kr_p = unshard_l("o_kr_p", 64)
    ckv_s = np.concatenate([res[c]["o_ckv_s"].reshape(2, 2, 32, 256) for c in range(8)], 1)
    kr_s = np.concatenate([res[c]["o_kr_s"].reshape(2, 2, 32, 64) for c in range(8)], 1)
    return (y_prompt, y_sample, pool_p, pool_s, sgu_s, ckv_p, kr_p, ckv_s, kr_s)
```

```python
import contextlib
import numpy as np
import concourse.bass as bass
import concourse.mybir as mybir
from concourse.bass_utils import run_bass_kernel_spmd

F32 = mybir.dt.float32
BF16 = mybir.dt.bfloat16
AF = mybir.ActivationFunctionType
ALU = mybir.AluOpType
AX = mybir.AxisListType

D = 1024
KD = 8
EPS = 1e-6
ATTN_SCALE = 192.0 ** -0.5
NEG = -30000.0


class Op:
    __slots__ = ("eng", "fn", "waits", "signal", "count", "kind", "chan", "chan_val")

    def __init__(self, eng, fn, kind):
        self.eng = eng
        self.fn = fn
        self.kind = kind
        self.waits = []
        self.signal = False
        self.count = None
        self.chan = None
        self.chan_val = None


class Prog:
    ENGS = ("pe", "act", "dve", "pool", "sp")

    def __init__(self, nc):
        self.nc = nc
        self.ops = {e: [] for e in self.ENGS}
        self.res = {}
        self.chan_tot = {}
        self.chan_sem = {}
        self.eng_sem = {}
        self.out_ops = []

    def _deps(self, op, reads, writes, join=()):
        deps = []
        for r in reads:
            st = self.res.get(r)
            if st is None:
                st = self.res[r] = [[], [], []]
            for wop in st[0]:
                deps.append((wop, "raw"))
        for w in list(writes) + list(join):
            st = self.res.get(w)
            if st is None:
                st = self.res[w] = [[], [], []]
            if w not in join:
                for wop in st[0]:
                    deps.append((wop, "waw"))
            else:
                for rd in st[2]:
                    deps.append((rd, "war"))
            for rd in st[1]:
                deps.append((rd, "war"))
        seen = set()
        for p, kind in deps:
            if p is op or id(p) in seen:
                continue
            if p.kind == "c" and p.eng == op.eng and op.kind == "c":
                if op.eng == "pe" or kind == "war":
                    continue
            seen.add(id(p))
            op.waits.append(p)
            if p.kind == "c":
                p.signal = True
        for r in reads:
            self.res[r][1].append(op)
        for w in writes:
            self.res[w] = [[op], [], self.res[w][1]]
        for w in join:
            self.res[w][0].append(op)

    def op(self, eng, fn, reads=(), writes=(), join=()):
        o = Op(eng, fn, "c")
        self._deps(o, reads, writes, join)
        self.ops[eng].append(o)
        return o

    def dma(self, eng, fn, reads=(), writes=(), chan=None, n=1, is_out=False):
        o = Op(eng, fn, "d")
        self._deps(o, reads, writes)
        tot = self.chan_tot.get(chan, 0) + 16 * n
        self.chan_tot[chan] = tot
        o.chan = chan
        o.chan_val = tot
        self.ops[eng].append(o)
        if is_out:
            self.out_ops.append(o)
        return o

    def collective(self, fn, reads=(), writes=(), chan=None):
        o = Op("pool", fn, "x")
        self._deps(o, reads, writes)
        assert chan not in self.chan_tot
        self.chan_tot[chan] = 1
        o.chan = chan
        o.chan_val = 1
        self.ops["pool"].append(o)
        return o

    def barrier(self):
        o = Op("sp", lambda e: e.nop(), "c")
        for e in self.ENGS:
            if e == "sp":
                continue
            for p in reversed(self.ops[e]):
                if p.kind == "c":
                    p.signal = True
                    o.waits.append(p)
                    break
        lastd = {}
        for e in self.ENGS:
            for p in self.ops[e]:
                if p.kind != "c":
                    lastd[p.chan] = p
        o.waits.extend(lastd.values())
        o.signal = True
        self.ops["sp"].append(o)
        for e in self.ENGS:
            if e == "sp":
                continue
            o2 = Op(e, lambda q: q.nop(), "c")
            o2.waits.append(o)
            self.ops[e].append(o2)
        self.res = {}

    def finish(self):
        o = Op("sp", lambda e: e.nop(), "c")
        for p in self.out_ops:
            o.waits.append(p)
        for e in self.ENGS:
            if e == "sp":
                continue
            for p in reversed(self.ops[e]):
                if p.kind == "c":
                    p.signal = True
                    o.waits.append(p)
                    break
        self.ops["sp"].append(o)

    def replay(self):
        nc = self.nc
        engobj = {"pe": nc.tensor, "act": nc.scalar, "dve": nc.vector, "pool": nc.gpsimd, "sp": nc.sync}
        EPOCH = 6000
        for i, c in enumerate(self.chan_tot):
            self.chan_sem[c] = nc.alloc_semaphore(name="c%d" % i)
        for e in self.ENGS:
            cnt = 0
            for o in self.ops[e]:
                if o.kind == "c" and o.signal:
                    ep = cnt // EPOCH
                    if (e, ep) not in self.eng_sem:
                        self.eng_sem[(e, ep)] = nc.alloc_semaphore(name="s_%s%d" % (e, ep))
                    o.count = (ep, cnt % EPOCH + 1)
                    cnt += 1
        prog = self

        def run(e):
            eng = engobj[e]
            seen = {}
            for o in prog.ops[e]:
                for p in o.waits:
                    if p.kind == "c":
                        sem, val = prog.eng_sem[(p.eng, p.count[0])], p.count[1]
                    else:
                        sem, val = prog.chan_sem[p.chan], p.chan_val
                    k = id(sem)
                    if seen.get(k, 0) >= val:
                        continue
                    seen[k] = val
                    eng.wait_ge(sem, val)
                r = o.fn(eng)
                if o.kind == "c":
                    if o.signal:
                        r.then_inc(prog.eng_sem[(e, o.count[0])], 1)
                elif o.kind == "d":
                    for ins in r:
                        ins.then_inc(prog.chan_sem[o.chan], 16)
                else:
                    r.then_inc(prog.chan_sem[o.chan])

        with nc.Block() as block:
            @block.tensor
            def _(t):
                run("pe")

            @block.scalar
            def _(t):
                run("act")

            @block.vector
            def _(t):
                run("dve")

            @block.gpsimd
            def _(t):
                run("pool")

            @block.sync
            def _(t):
                run("sp")


def build(NBLK, NLAYERS):
    NTOK = NBLK * 128
    NS = 64
    TOT = NTOK + NS
    NT = NBLK // 2
    nc = bass.Bass("TRN2", target_bir_lowering=False)

    def din(name, shape):
        return nc.dram_tensor(name, list(shape), F32, kind="ExternalInput").ap()

    def dout(name, shape):
        return nc.dram_tensor(name, list(shape), F32, kind="ExternalOutput").ap()

    xp = din("xp", [NTOK, D])
    xs = din("xs", [NS, D])
    cpool = din("cpool", [2, 2, 15, D])
    cckv = din("cckv", [2, 2, 1024, 256])
    ckr = din("ckr", [2, 2, 1024, 64])
    w_in_even = din("w_in_even", [2, D, 5120])
    w_pool = din("w_pool", [2, 4, 256, 256])
    ln_g = din("ln_g", [2, D])
    ln_b = din("ln_b", [2, D])
    w_sp = din("w_sp", [2, 4, 128, 128])
    b_sp = din("b_sp", [2, 4, 128])
    w_out_even = din("w_out_even", [2, 2048, D])
    w_in_odd = din("w_in_odd", [2, D, 1728])
    kv_norm = din("kv_norm", [2, 256])
    w_q_up = din("w_q_up", [2, 384, 8 * 192])
    w_kv_up = din("w_kv_up", [2, 256, 8 * 256])
    w_o = din("w_o", [2, D, D])
    vecs = din("vecs", [128, 96])
    ident_d = din("ident", [128, 128])
    mask_d = din("mask", [128, 512])
    selw_d = din("selw", [128, 8])
    rcnt_d = din("rcnt", [128, 4 * 16])
    cs_tm = din("cs_tm", [128, (NBLK + 1) * 64])
    cs_fm = din("cs_fm", [64, 2 * TOT])

    y_p = dout("y_p", [NTOK, D])
    y_s = dout("y_s", [NS, D])
    o_pool_p = dout("o_pool_p", [2, 15, D])
    o_pool_s = dout("o_pool_s", [2, 2, 15, D])
    o_sgu_s = dout("o_sgu_s", [2, NS, D])
    o_ckv_p = dout("o_ckv_p", [2, NTOK, 256])
    o_kr_p = dout("o_kr_p", [2, NTOK, 64])
    o_ckv_s = dout("o_ckv_s", [2, NS, 256])
    o_kr_s = dout("o_kr_s", [2, NS, 64])

    HW = 8 * NBLK * 16
    bh_in = [nc.dram_tensor("bh_in%d" % e, [128, HW], F32) for e in range(2)]
    bh_out = [nc.dram_tensor("bh_out%d" % e, [4 * 128, HW], F32) for e in range(2)]
    NSPL = max(1, NBLK // 8)
    BPS = NBLK // NSPL
    TPS = BPS * 128
    bk_in = [[nc.dram_tensor("bk_in%d_%d" % (o, sp), [TPS, 320], BF16) for sp in range(NSPL)] for o in range(2)]
    bk_out = [[nc.dram_tensor("bk_out%d_%d" % (o, sp), [4 * TPS, 320], BF16) for sp in range(NSPL)] for o in range(2)]

    P = Prog(nc)
    es = contextlib.ExitStack()

    def S(name, shape, dt):
        return es.enter_context(nc.sbuf_tensor("t_" + name, list(shape), dt))

    with es:
        ps = [es.enter_context(nc.psum_tensor("ps%d" % i, [128, 512], F32)) for i in range(8)]
        bankctr = [0]

        def nbank():
            b = bankctr[0] % 6
            bankctr[0] += 1
            return b

        xT = S("xT", [128, KD, TOT], F32)
        identf = S("identf", [128, 128], F32)
        identb = S("identb", [128, 128], BF16)
        onesb = S("onesb", [128, 128], BF16)
        epsc = S("epsc", [128, 1], F32)
        vec = S("vec", [128, 96], F32)
        selw = S("selw", [128, 8], F32)
        rcnt = S("rcnt", [128, 4, 16], F32)
        NWB = 4
        wst = [S("wst%d" % i, [128, 256], F32) for i in range(1)]
        wbf = [S("wbf%d" % i, [128, 8 * 128], BF16) for i in range(NWB)]
        hT = S("hT", [128, KD, 256], BF16)
        sqb = S("sqb", [128, KD, 256], BF16)
        rstd = S("rstd", [128, 258], F32)
        yT = S("yT", [128, KD, 256], F32)
        tmpT = S("tmpT", [128, KD, 256], F32)
        RB = max(81920, 24 * NTOK + 4 * NBLK * 516)
        R = S("R", [128, RB], mybir.dt.uint8)

        R2 = S("R2", [128, 23040], mybir.dt.uint8)

        def r2view(off, shape, dt, parts=128):
            n = 1
            for s_ in shape[1:]:
                n *= s_
            esz = 4 if dt == F32 else 2
            v = R2[0:parts, off:off + n * esz].bitcast(dt)
            if len(shape) == 3:
                v = v.rearrange("p (a b) -> p a b", a=shape[1])
            elif len(shape) == 4:
                v = v.rearrange("p (a b c) -> p a b c", a=shape[1], b=shape[2])
            return v

        def rview(off, shape, dt):
            n = 1
            for s in shape[1:]:
                n *= s
            esz = 4 if dt == F32 else 2
            v = R[:, off:off + n * esz].bitcast(dt)
            if len(shape) == 3:
                v = v.rearrange("p (a b) -> p a b", a=shape[1])
            elif len(shape) == 4:
                v = v.rearrange("p (a b c) -> p a b c", a=shape[1], b=shape[2])
            return v

        def ld(dst, src, name, eng="sp"):
            P.dma(eng, lambda q: [q.dma_start(out=dst, in_=src)], writes=[name], chan=name)

        ld(identf[:], ident_d[:, :], "identf")
        ld(vec[:], vecs[:, :], "vec")
        ld(selw[:], selw_d[:, :], "selw")
        ld(rcnt[:].rearrange("p a b -> p (a b)"), rcnt_d[:, :], "rcnt")
        P.op("dve", lambda q: q.tensor_copy(out=identb[:], in_=identf[:]), reads=["identf"], writes=["identb"])
        P.op("pool", lambda q: q.memset(onesb[:], 1.0 / 1024.0), writes=["onesb"])
        P.op("pool", lambda q: q.memset(epsc[:], EPS), writes=["epsc"])

        evq = [0]

        def evac_copy(out, in_, bank, wres, rres=(), scale=None, join=()):
            evq[0] += 1
            if evq[0] % 2 == 0:
                P.op("act", lambda q: q.activation(out=out, in_=in_, func=AF.Copy, scale=(1.0 if scale is None else scale)),
                     reads=list(rres), writes=[("ps", bank)] + list(wres), join=join)
            else:
                if scale is None:
                    P.op("dve", lambda q: q.tensor_copy(out=out, in_=in_), reads=list(rres), writes=[("ps", bank)] + list(wres), join=join)
                else:
                    P.op("dve", lambda q: q.tensor_scalar(out=out, in0=in_, scalar1=scale, scalar2=None, op0=ALU.mult),
                         reads=list(rres), writes=[("ps", bank)] + list(wres), join=join)

        wctr = [0]

        WBA = {}
        CH = {}
        rlctr = [0]

        class WV:
            def __init__(self, name):
                self.name = name

            def __getitem__(self, idx):
                _, ks, cs = idx
                return (self.name, ks.start or 0, cs.start)

        SRC = {}

        def conv(name, src2d, rows, cols, piece=None):
            SRC[name] = src2d

        RQ = []

        def relay(name, k0, kdim, c0, cols):
            RQ.append((name, k0, kdim, c0, cols))

        def relay_some(k):
            for _ in range(min(k, len(RQ))):
                relay_now(*RQ.pop(0))

        def relay_now(name, k0, kdim, c0, cols):
            i = rlctr[0]
            rlctr[0] += 1
            B = nc.dram_tensor("wbB_%s_%d_%d" % (name, k0, c0), [128, kdim * cols], BF16)
            CH[(name, k0, c0)] = (B, kdim, cols)
            src = SRC[name].rearrange("(k p) c -> p k c", p=128)[:, k0:k0 + kdim, c0:c0 + cols]
            slot = ("rlslot", i % 8)
            P.dma("pool", lambda q: [q.dma_start(out=B.ap().rearrange("p (k c) -> p k c", k=kdim), in_=src)],
                  reads=[slot], writes=[("wbB", name, k0, c0), slot], chan=slot)

        def relay_layer(layer):
            if layer % 2 == 0:
                e = layer // 2
                for j in range(40):
                    relay("win%d" % e, 0, 8, j * 128, 128)
                for dc in range(8):
                    relay("wo%d" % e, 0, 8, dc * 128, 128)
                    relay("wo%d" % e, 8, 8, dc * 128, 128)
            else:
                o = layer // 2
                for h in range(8):
                    relay("wku%d" % o, 0, 2, h * 256, 256)
                for (c0, w) in [(384, 128), (512, 128), (640, 64), (0, 128), (128, 128), (256, 128)] + [(704 + ec * 128, 128) for ec in range(8)]:
                    relay("wio%d" % o, 0, 8, c0, w)
                for h in range(8):
                    relay("wqu%d" % o, 0, 3, h * 192, 192)
                for dc in range(8):
                    relay("wov%d" % o, 0, 8, dc * 128, 128)

        def conv_layer(layer):
            if layer % 2 == 0:
                e = layer // 2
                conv("win%d" % e, w_in_even[e], D, 5120, piece=1024)
                conv("wo%d" % e, w_out_even[e], 2048, D)
            else:
                o = layer // 2
                conv("wku%d" % o, w_kv_up[o], 256, 2048)
                conv("wio%d" % o, w_in_odd[o], D, 1728)
                conv("wqu%d" % o, w_q_up[o], 384, 1536)
                conv("wov%d" % o, w_o[o], D, D)

        def wload(ref, kdim, cols):
            i = wctr[0]
            wctr[0] += 1
            B, kd_, cols_ = CH[ref]
            assert kd_ == kdim and cols_ == cols, (ref, kdim, cols)
            wb = wbf[i % NWB]
            bname = "wbf%d" % (i % NWB)
            n = kdim * cols
            wbv = wb[:, 0:n].rearrange("p (k c) -> p k c", k=kdim)
            P.dma("sp", lambda q: [q.dma_start(out=wb[:, 0:n], in_=B.ap()[:, :])], reads=[("wbB",) + ref], writes=[bname], chan=bname)
            return wbv, bname

        xin = tmpT[:].rearrange("p k n -> p (k n)")[:, 0:D]

        def load_x(src_rows, c0, n):
            P.dma("sp", lambda q: [q.dma_start(out=xin[0:n, :], in_=src_rows)], writes=["tmpT"], chan="xin")
            for half in range(2):
                b = nbank()

                def tr(q, half=half, b=b):
                    r = None
                    for j in range(4):
                        k = half * 4 + j
                        r = q.transpose(ps[b][:, j * 128:j * 128 + n], xin[0:n, k * 128:(k + 1) * 128], identf[0:n, 0:n])
                    return r
                P.op("pe", tr, reads=["tmpT", "identf"], writes=[("ps", b)])
                src = ps[b][:].rearrange("p (j t) -> p j t", j=4)[:, :, 0:n]
                evac_copy(xT[:, half * 4:half * 4 + 4, c0:c0 + n], src, b, [("xT", c0 // 256)])

        yout = yT[:].rearrange("p k n -> p (k n)")[:, 0:D]

        def store_x(dst_rows, c0, n):
            for half in range(2):
                b = nbank()

                def tr(q, half=half, b=b):
                    r = None
                    for j in range(4):
                        k = half * 4 + j
                        r = q.transpose(ps[b][0:n, j * 128:(j + 1) * 128], xT[:, k, c0:c0 + n], identf[:, :])
                    return r
                P.op("pe", tr, reads=[("xT", c0 // 256), "identf"], writes=[("ps", b)])
                evac_copy(yout[0:n, half * 512:(half + 1) * 512], ps[b][0:n, :], b, ["yT"])
            P.dma("sp", lambda q: [q.dma_start(out=dst_rows, in_=yout[0:n, :])], reads=["yT"],
                  chan="yout", is_out=True)

        for layer_ in range(NLAYERS):
            conv_layer(layer_)
        relay_layer(0)
        relay_some(999)
        for b in range(NBLK):
            load_x(xp[b * 128:(b + 1) * 128, :], b * 128, 128)
        load_x(xs[:, :], NTOK, NS)

        def rms_stats(src4, n, srcres, outname="rstd", kdim=KD, scale=1.0):
            P.op("act", lambda q: q.activation(out=sqb[:, 0:kdim, 0:n], in_=src4, func=AF.Square), reads=list(srcres), writes=["sqb"])
            b = nbank()

            def mm(q):
                r = None
                for k in range(kdim):
                    r = q.matmul(ps[b][:, 0:n], lhsT=onesb[:], rhs=sqb[:, k, 0:n], start=(k == 0), stop=(k == kdim - 1))
                return r
            P.op("pe", mm, reads=["sqb", "onesb"], writes=[("ps", b)])
            P.op("act", lambda q: q.activation(out=rstd[:, 0:n], in_=ps[b][:, 0:n], func=AF.Sqrt, bias=epsc[:, 0:1], scale=scale),
                 reads=["epsc"], writes=[("ps", b), outname])
            P.op("dve", lambda q: q.reciprocal(out=rstd[:, 0:n], in_=rstd[:, 0:n]), reads=[outname], writes=[outname])

        KS = 5

        def pre_norm(layer, xview, n, xres):
            gb = vec[:, layer * 8:layer * 8 + 8].unsqueeze(2).to_broadcast([128, KD, n])
            P.op("pool", lambda q: q.tensor_tensor(out=tmpT[:, :, 0:n], in0=xview, in1=gb, op=ALU.mult),
                 reads=list(xres) + ["vec"], writes=["tmpT"])
            rms_stats(xview, n, xres)
            rb1 = rstd[:, 0:n].unsqueeze(1).to_broadcast([128, KS, n])
            rb2 = rstd[:, 0:n].unsqueeze(1).to_broadcast([128, KD - KS, n])
            P.op("dve", lambda q: q.tensor_tensor(out=hT[:, 0:KS, 0:n], in0=tmpT[:, 0:KS, 0:n], in1=rb1, op=ALU.mult),
                 reads=["tmpT", "rstd"], writes=["hT"])
            P.op("pool", lambda q: q.tensor_tensor(out=hT[:, KS:KD, 0:n], in0=tmpT[:, KS:KD, 0:n], in1=rb2, op=ALU.mult),
                 reads=["tmpT", "rstd"], join=["hT"])

        def pre_norm_lite_stages(layer, xview, n, xres):
            def s1():
                P.op("act", lambda q: q.activation(out=hT[:, :, 0:n], in_=xview, func=AF.Square), reads=list(xres), writes=["hT"])
            bb = [None]

            def s2():
                pre_norm_lite_mid(n, bb)

            def s3():
                for k in range(KD):
                    P.op("dve", lambda q, k=k: q.scalar_tensor_tensor(out=hT[:, k, 0:n], in0=xview[:, k, :], scalar=vec[:, layer * 8 + k:layer * 8 + k + 1],
                                                                      in1=rstd[:, 0:n], op0=ALU.mult, op1=ALU.mult),
                         reads=list(xres) + ["vec", "rstd"], **({"writes": ["hT"]} if k == 0 else {"join": ["hT"]}))
            return [s1, s2, s3]

        def pre_norm_lite_mid(n, bb):
            b = nbank()

            def mm(q):
                r = None
                for k in range(KD):
                    r = q.matmul(ps[b][:, 0:n], lhsT=onesb[:], rhs=hT[:, k, 0:n], start=(k == 0), stop=(k == KD - 1))
                return r
            P.op("pe", mm, reads=["hT", "onesb"], writes=[("ps", b)])
            P.op("act", lambda q: q.activation(out=rstd[:, 0:n], in_=ps[b][:, 0:n], func=AF.Sqrt, bias=epsc[:, 0:1], scale=1.0),
                 reads=["epsc"], writes=[("ps", b), "rstd"])
            P.op("dve", lambda q: q.reciprocal(out=rstd[:, 0:n], in_=rstd[:, 0:n]), reads=["rstd"], writes=["rstd"])

        def post_norm_update(layer, c0, n):
            xres = [("xT", c0 // 256)]
            gb = vec[:, 32 + layer * 8:32 + layer * 8 + 8].unsqueeze(2).to_broadcast([128, KD, n])
            P.op("pool", lambda q: q.tensor_tensor(out=tmpT[:, :, 0:n], in0=yT[:, :, 0:n], in1=gb, op=ALU.mult),
                 reads=["yT", "vec"], writes=["tmpT"])
            rms_stats(yT[:, :, 0:n], n, ["yT"])
            rb1 = rstd[:, 0:n].unsqueeze(1).to_broadcast([128, KS, n])
            rb2 = rstd[:, 0:n].unsqueeze(1).to_broadcast([128, KD - KS, n])
            P.op("dve", lambda q: q.tensor_tensor(out=tmpT[:, 0:KS, 0:n], in0=tmpT[:, 0:KS, 0:n], in1=rb1, op=ALU.mult),
                 reads=["tmpT", "rstd"], writes=["tmpTa"])
            P.op("pool", lambda q: q.tensor_tensor(out=tmpT[:, KS:KD, 0:n], in0=tmpT[:, KS:KD, 0:n], in1=rb2, op=ALU.mult),
                 reads=["tmpT", "rstd"], writes=["tmpTb"])
            P.op("dve", lambda q: q.tensor_tensor(out=xT[:, 0:KS, c0:c0 + n], in0=xT[:, 0:KS, c0:c0 + n], in1=tmpT[:, 0:KS, 0:n], op=ALU.add),
                 reads=["tmpTa", "tmpT"] + xres, writes=xres)
            P.op("pool", lambda q: q.tensor_tensor(out=xT[:, KS:KD, c0:c0 + n], in0=xT[:, KS:KD, c0:c0 + n], in1=tmpT[:, KS:KD, 0:n], op=ALU.add),
                 reads=["tmpTb", "tmpT"] + xres, join=xres)

        def mm_fm(bank, wv, col0, ncol, rhs3, n, kdim=KD, first=True, last=True):
            def f(q):
                r = None
                for k in range(kdim):
                    r = q.matmul(ps[bank][0:ncol, 0:n], lhsT=wv[:, k, col0:col0 + ncol], rhs=rhs3[:, k, 0:n],
                                 start=(first and k == 0), stop=(last and k == kdim - 1))
                return r
            return f

        HWB = HW * 4
        halo_s = rview(0, [128, 8, NBLK, 16], F32)
        aT = rview(8192, [128, 8, 2, 144], F32)
        sA = rview(17408, [128, 8, 2, 144], F32)
        sB = rview(26624, [128, 8, 2, 144], F32)
        wp_st = rview(35840, [128, 8, 256], F32)
        gT = rview(44032, [128, 16, 256], BF16)
        hb = R[:, 44032:44032 + HWB].bitcast(F32)
        mixT = rview(52224, [128, 16, 256], BF16)
        vtm = rview(60416, [128, 2, D], F32)
        uT = rview(68608, [128, KD, 256], BF16)
        halo_c = rview(72704, [128, 8, NBLK, 16], F32)
        dT = sqb
        vbf = r2view(0, [128, 2, D], BF16)
        atm = yT[:].rearrange("p k n -> p (k n)")[:, 0:D]
        histtm = tmpT[:].rearrange("p k n -> p (k n)")[:, 0:D]
        st4 = S("st4", [128, 16], F32)
        wp_bf = r2view(4096, [128, 8, 256], BF16)
        ws_f = r2view(8192, [128, 4, 128], F32)
        wsT = r2view(10240, [128, 4, 128], BF16)
        wsd_f = r2view(11264, [64, 4, 64], F32, parts=64)
        wsdT = r2view(12288, [64, 4, 64], BF16, parts=64)
        lng = r2view(12800, [128, D], F32)
        lnb = r2view(16896, [128, D], F32)
        bsp = r2view(20992, [128, 4, 128], F32)

        def even_layer(e):
            layer = 2 * e
            win = WV("win%d" % e)
            nh = 16 * NBLK
            xv = xT[:, :, 0:NTOK].rearrange("p k (b w) -> p k b w", w=128)[:, :, :, 112:128]
            P.op("pool", lambda q: q.tensor_copy(out=yT[:, :, 0:nh].rearrange("p k (b w) -> p k b w", w=16), in_=xv),
                 reads=[("xT", t) for t in range(NT)], writes=["yT"])
            pre_norm(layer, yT[:, :, 0:nh], nh, ["yT"])
            for ec in range(8):
                wv, wn = wload(win[:, :, ec * 128:(ec + 1) * 128], KD, 128)
                b = nbank()
                P.op("pe", mm_fm(b, wv, 0, 128, hT, nh), reads=[wn, "hT"], writes=[("ps", b)])
                evac_copy(halo_c[:, ec, :, :], ps[b][:, 0:nh].rearrange("p (b w) -> p b w", w=16), b, ["halo_c"])
            P.dma("sp", lambda q: [q.dma_start(out=bh_in[e].ap()[:, :], in_=halo_c.rearrange("p k b w -> p (k b w)"))],
                  reads=["halo_c"], writes=[("bh_in", e)], chan=("bh_in", e))
            P.collective(lambda q: q.collective_compute("AllGather", ALU.bypass, replica_groups=[[0, 1, 2, 3], [4, 5, 6, 7]],
                                                        ins=[bh_in[e].ap().opt()], outs=[bh_out[e].ap().opt()]),
                         reads=[("bh_in", e)], writes=[("bh_out", e)], chan=("ccH", e))
            hg = hb.rearrange("p (k b w) -> p k b w", k=8, b=NBLK)
            for r in range(4):
                P.dma("sp", lambda q, r=r: [q.dma_start(out=hb, in_=bh_out[e].ap()[r * 128:(r + 1) * 128, :])],
                      reads=[("bh_out", e)], writes=["gT"], chan="hb")
                if r == 0:
                    P.op("dve", lambda q: q.tensor_scalar(out=halo_s, in0=hg, scalar1=selw[:, 0:1], scalar2=None, op0=ALU.mult),
                         reads=["gT", "selw"], writes=["halo_s"])
                else:
                    P.op("dve", lambda q, r=r: q.scalar_tensor_tensor(out=halo_s, in0=hg, scalar=selw[:, r:r + 1], in1=halo_s,
                                                                     op0=ALU.mult, op1=ALU.add),
                         reads=["gT", "selw", "halo_s"], writes=["halo_s"])
                if r == 3 and NBLK > 1:
                    P.op("dve", lambda q: q.scalar_tensor_tensor(out=halo_s[:, :, 1:NBLK, :], in0=hg[:, :, 0:NBLK - 1, :],
                                                                 scalar=selw[:, 4:5], in1=halo_s[:, :, 1:NBLK, :],
                                                                 op0=ALU.mult, op1=ALU.add),
                         reads=["gT", "selw", "halo_s"], writes=["halo_s"])
            P.dma("sp", lambda q: [q.dma_start(out=wp_st, in_=w_pool[e].rearrange("g (i p) o -> p (g i) o", p=128))],
                  writes=["wp_st"], chan="wp_st")
            P.op("dve", lambda q: q.tensor_copy(out=wp_bf[:], in_=wp_st), reads=["wp_st"], writes=["wp_bf"])
            P.dma("sp", lambda q: [q.dma_start(out=ws_f[:], in_=w_sp[e].rearrange("g i j -> i g j"))], writes=["ws_f"], chan="ws_f")
            b = nbank()

            def trw(q):
                r = None
                for g in range(4):
                    r = q.transpose(ps[b][:, g * 128:(g + 1) * 128], ws_f[:, g, :], identf[:])
                return r
            P.op("pe", trw, reads=["ws_f", "identf"], writes=[("ps", b)])
            P.op("dve", lambda q: q.tensor_copy(out=wsT[:].rearrange("p g i -> p (g i)"), in_=ps[b][:, :]), writes=[("ps", b), "wsT"])
            P.op("pool", lambda q: q.memset(wsT[64:128, :, 0:64], 0.0), reads=["wsT"], writes=["wsT"])
            P.op("pool", lambda q: q.memset(wsd_f[:], 0.0), writes=["wsd_f"])
            P.dma("sp", lambda q: [q.dma_start(out=wsd_f[0:32, :, 0:32], in_=w_sp[e, :, 0:32, 0:32].rearrange("g i j -> i g j")),
                                   q.dma_start(out=wsd_f[32:64, :, 32:64], in_=w_sp[e, :, 0:32, 0:32].rearrange("g i j -> i g j"))],
                  reads=["wsd_f"], writes=["wsd_f"], chan="wsd_f", n=2)
            b2 = nbank()

            def trw2(q):
                r = None
                for g in range(4):
                    r = q.transpose(ps[b2][0:64, g * 64:(g + 1) * 64], wsd_f[:, g, :], identf[0:64, 0:64])
                return r
            P.op("pe", trw2, reads=["wsd_f", "identf"], writes=[("ps", b2)])
            P.op("dve", lambda q: q.tensor_copy(out=wsdT[:].rearrange("p g i -> p (g i)"), in_=ps[b2][0:64, 0:256]),
                 writes=[("ps", b2), "wsdT"])
            ld(lng[:], ln_g[e:e + 1, :].broadcast_to([128, D]), "lng")
            ld(lnb[:], ln_b[e:e + 1, :].broadcast_to([128, D]), "lnb")
            ld(bsp[:].rearrange("p g i -> p (g i)"), b_sp[e:e + 1].rearrange("o g i -> o (g i)").broadcast_to([128, 512]), "bsp")
            return dict(win=win)

        def even_tile(e, ctx, t):
            layer = 2 * e
            sample = (t == NT)
            NB, W = (2, 32) if sample else (2, 128)
            n = NB * W
            c0 = NTOK if sample else t * 256
            xres = [("xT", c0 // 256)]
            win = ctx["win"]
            if t == 0:
                pre_norm(layer, xT[:, :, c0:c0 + n], n, xres)
            nxt = None
            if t < NT:
                c0n, nn = (NTOK, 64) if t + 1 == NT else ((t + 1) * 256, 256)
                nxt = pre_norm_lite_stages(layer, xT[:, :, c0n:c0n + nn], nn, [("xT", c0n // 256)])
            if sample:
                for s in range(2):
                    P.dma("sp", lambda q, s=s: [q.dma_start(out=histtm[0:15, :], in_=cpool[e, s])], writes=["tmpT"], chan="histtm")
                    for half in range(2):
                        b = nbank()

                        def trh(q, half=half, b=b):
                            r = None
                            for j in range(4):
                                k = half * 4 + j
                                r = q.transpose(ps[b][:, j * 16:j * 16 + 15], histtm[0:15, k * 128:(k + 1) * 128], identf[0:15, 0:15])
                            return r
                        P.op("pe", trh, reads=["tmpT", "identf"], writes=[("ps", b)])
                        P.op("dve", lambda q, half=half, b=b, s=s: q.tensor_copy(
                            out=aT[:, half * 4:half * 4 + 4, s, 1:16], in_=ps[b][:, 0:64].rearrange("p (j w) -> p j w", j=4)[:, :, 0:15]),
                            writes=[("ps", b), "aT"])
            else:
                P.op("pool", lambda q: q.tensor_copy(out=aT[:, :, :, 0:16], in_=halo_s[:, :, t * 2:t * 2 + 2, :]),
                     reads=["halo_s"], writes=["aT"])
            need_tm = sample or (t == NT - 1)
            m = 64 if sample else 128
            for ec in range(8):
                wv, wn = wload(win[:, :, ec * 128:(ec + 1) * 128], KD, 128)
                b = nbank()
                P.op("pe", mm_fm(b, wv, 0, 128, hT, n), reads=[wn, "hT"], writes=[("ps", b)])
                evac_copy(aT[:, ec, 0:NB, 16:16 + W], ps[b][:, 0:n].rearrange("p (b w) -> p b w", b=NB), b, ["aT"])
                if need_tm:
                    b = nbank()
                    tc0 = 0 if sample else 128

                    def mmtm(q, b=b, wv=wv, tc0=tc0):
                        r = None
                        for k in range(KD):
                            r = q.matmul(ps[b][0:m, 0:128], lhsT=hT[:, k, tc0:tc0 + m], rhs=wv[:, k, :], start=(k == 0), stop=(k == KD - 1))
                        return r
                    P.op("pe", mmtm, reads=[wn, "hT"], writes=[("ps", b)])
                    evac_copy(atm[0:m, ec * 128:(ec + 1) * 128], ps[b][0:m, 0:128], b, ["yT"])
            if need_tm:
                if sample:
                    P.dma("sp", lambda q: [q.dma_start(out=o_pool_s[e, 0], in_=atm[17:32, :]),
                                           q.dma_start(out=o_pool_s[e, 1], in_=atm[49:64, :])],
                          reads=["yT"], chan="atm", n=2, is_out=True)
                else:
                    P.dma("sp", lambda q: [q.dma_start(out=o_pool_p[e], in_=atm[113:128, :])], reads=["yT"], chan="atm", is_out=True)
            def sh(dst, src, k0, d):
                return lambda q: q.tensor_tensor(out=dst[:, k0:8, 0:NB, d:16 + W], in0=src[:, k0:8, 0:NB, d:16 + W],
                                                 in1=src[:, k0:8, 0:NB, 0:16 + W - d], op=ALU.add)
            P.op("dve", sh(sA, aT, 0, 1), reads=["aT"], writes=["sA", "sA2"])
            P.op("pool", sh(sB, sA, 2, 2), reads=["sA"], writes=["sB", "sB2"])
            P.op("dve", sh(sA, sB, 4, 4), reads=["sB"], writes=["sA2"])
            P.op("pool", sh(sB, sA, 6, 8), reads=["sA2"], writes=["sB2"])
            srcs = [(sA, ["sA"]), (sB, ["sB"]), (sA, ["sA2"]), (sB, ["sB2"])]
            for g in range(4):
                sbuf_, sres = srcs[g]
                w = 2 ** (g + 1)
                P.op("dve", lambda q, g=g, sbuf_=sbuf_, w=w: q.scalar_tensor_tensor(
                    out=dT[:, 2 * g:2 * g + 2, 0:n].rearrange("p k (b w) -> p k b w", b=NB),
                    in0=sbuf_[:, 2 * g:2 * g + 2, 0:NB, 16:16 + W], scalar=1.0 / w, in1=aT[:, 2 * g:2 * g + 2, 0:NB, 16:16 + W],
                    op0=ALU.mult, op1=ALU.subtract), reads=sres + ["aT"], writes=["sqb"])
                if (not sample) and t == 0:
                    rc = rcnt[:, g, :].unsqueeze(1).to_broadcast([128, 2, 16])
                    P.op("pool", lambda q, g=g, sbuf_=sbuf_, rc=rc: q.tensor_tensor(
                        out=tmpT[:, 0:2, 0:16], in0=sbuf_[:, 2 * g:2 * g + 2, 0, 16:32], in1=rc, op=ALU.mult),
                        reads=sres + ["rcnt"], writes=["tmpT"])
                    P.op("dve", lambda q, g=g: q.tensor_tensor(out=dT[:, 2 * g:2 * g + 2, 0:16], in0=tmpT[:, 0:2, 0:16],
                                                               in1=aT[:, 2 * g:2 * g + 2, 0, 16:32], op=ALU.subtract),
                         reads=["tmpT", "aT"], writes=["sqb"])
            nblk_tm = 1 if sample else 2
            for ec in range(8):
                wv, wn = wload(win[:, :, 2048 + ec * 128:2048 + (ec + 1) * 128], KD, 128)
                for blk in range(nblk_tm):
                    b = nbank()

                    def mmv(q, b=b, wv=wv, blk=blk):
                        r = None
                        for k in range(KD):
                            r = q.matmul(ps[b][0:m, 0:128], lhsT=hT[:, k, blk * 128:blk * 128 + m], rhs=wv[:, k, :],
                                         start=(k == 0), stop=(k == KD - 1))
                        return r
                    P.op("pe", mmv, reads=[wn, "hT"], writes=[("ps", b)])
                    P.op("act", lambda q, b=b, blk=blk, ec=ec: q.activation(out=vtm[0:m, blk, ec * 128:(ec + 1) * 128], in_=ps[b][0:m, 0:128],
                                                                         func=AF.Gelu), writes=[("ps", b), ("vtm", blk)])
            sqscr = tmpT[:].rearrange("p k n -> p (k n)")[:, 0:D]
            for blk in range(nblk_tm):
                vb = vtm[0:m, blk, :]
                vr = ("vtm", blk)
                P.op("dve", lambda q, vb=vb: q.reduce_sum(out=st4[0:m, 0:1], in_=vb, axis=AX.X), reads=[vr], writes=["st_a"])
                P.op("act", lambda q, vb=vb: q.activation(out=sqscr[0:m, :], in_=vb, func=AF.Square, accum_out=st4[0:m, 1:2]),
                     reads=[vr], writes=["tmpT", "st_b"])
                P.op("pool", lambda q: q.tensor_scalar(out=st4[0:m, 2:3], in0=st4[0:m, 0:1], scalar1=1.0 / D, scalar2=None, op0=ALU.mult),
                     reads=["st_a"], writes=["st_c"])
                P.op("dve", lambda q: q.tensor_tensor(out=st4[0:m, 3:4], in0=st4[0:m, 2:3], in1=st4[0:m, 2:3], op=ALU.mult),
                     reads=["st_c"], writes=["st_d"])
                P.op("pool", lambda q: q.tensor_scalar(out=st4[0:m, 4:5], in0=st4[0:m, 1:2], scalar1=1.0 / D, scalar2=None, op0=ALU.mult),
                     reads=["st_b"], writes=["st_e"])
                P.op("dve", lambda q: q.tensor_tensor(out=st4[0:m, 5:6], in0=st4[0:m, 4:5], in1=st4[0:m, 3:4], op=ALU.subtract),
                     reads=["st_e", "st_d"], writes=["st_f"])
                P.op("act", lambda q: q.activation(out=st4[0:m, 6:7], in_=st4[0:m, 5:6], func=AF.Sqrt, bias=epsc[0:m, 0:1], scale=1.0),
                     reads=["st_f", "epsc"], writes=["st_g"])
                P.op("dve", lambda q: q.reciprocal(out=st4[0:m, 7:8], in_=st4[0:m, 6:7]), reads=["st_g"], writes=["st_h"])
                P.op("dve", lambda q, vb=vb: q.tensor_scalar(out=vb, in0=vb, scalar1=st4[0:m, 2:3], scalar2=st4[0:m, 7:8],
                                                          op0=ALU.subtract, op1=ALU.mult), reads=[vr, "st_c", "st_h"], writes=[vr])
                P.op("pool", lambda q, vb=vb: q.tensor_tensor(out=vb, in0=vb, in1=lng[0:m, :], op=ALU.mult), reads=[vr, "lng"], writes=[vr])
                P.op("dve", lambda q, vb=vb: q.tensor_tensor(out=vb, in0=vb, in1=lnb[0:m, :], op=ALU.add), reads=[vr, "lnb"], writes=[vr])
                P.op("pool", lambda q, vb=vb, blk=blk: q.tensor_copy(out=vbf[0:m, blk, :], in_=vb), reads=[vr], writes=[("vbf", blk)])
                if sample:
                    P.dma("sp", lambda q: [q.dma_start(out=o_sgu_s[e], in_=vtm[0:64, 0, :])], reads=[vr], chan="vout", is_out=True)
            for ec in range(8):
                wv, wn = wload(win[:, :, 1024 + ec * 128:1024 + (ec + 1) * 128], KD, 128)
                b = nbank()
                P.op("pe", mm_fm(b, wv, 0, 128, hT, n), reads=[wn, "hT"], writes=[("ps", b)])
                P.op("act", lambda q, b=b, ec=ec: q.activation(out=uT[:, ec, 0:n], in_=ps[b][:, 0:n], func=AF.Gelu),
                     writes=[("ps", b), "uT"])
            for ec in range(16):
                wv, wn = wload(win[:, :, 3072 + ec * 128:3072 + (ec + 1) * 128], KD, 128)
                b = nbank()
                P.op("pe", mm_fm(b, wv, 0, 128, hT, n), reads=[wn, "hT"], writes=[("ps", b)])
                P.op("act", lambda q, b=b, ec=ec: q.activation(out=gT[:, ec, 0:n], in_=ps[b][:, 0:n], func=AF.Silu),
                     writes=[("ps", b), "gT"])
            if nxt:
                nxt[0]()
            P.op("pool", lambda q: q.tensor_tensor(out=uT[:, :, 0:n], in0=uT[:, :, 0:n], in1=gT[:, 8:16, 0:n], op=ALU.mult),
                 reads=["uT", "gT"], writes=["uT"])
            wI = 64 if sample else 128
            for blk in range(nblk_tm):
                for half in range(2):
                    b = nbank()

                    def mms(q, b=b, blk=blk, half=half):
                        r = None
                        for j in range(4):
                            dk = half * 4 + j
                            g = dk // 2
                            rhs = wsdT[0:64, g, :] if sample else wsT[:, g, :]
                            r = q.matmul(ps[b][:, j * 128:j * 128 + wI], lhsT=vbf[0:m, blk, dk * 128:(dk + 1) * 128], rhs=rhs,
                                         start=True, stop=True)
                        return r
                    P.op("pe", mms, reads=[("vbf", blk), "wsT", "wsdT"], writes=[("ps", b)])
                    ps4 = ps[b][:].rearrange("p (g c i) -> p g c i", g=2, c=2)
                    if sample:
                        for s in range(2):
                            bb = bsp[:, half * 2:half * 2 + 2, 0:32].unsqueeze(2).to_broadcast([128, 2, 2, 32])
                            P.op("dve", lambda q, s=s, half=half, bb=bb, ps4=ps4: q.tensor_tensor(
                                out=tmpT[:, half * 4:half * 4 + 4, s * 32:(s + 1) * 32].rearrange("p (g c) i -> p g c i", g=2),
                                in0=ps4[:, :, :, s * 32:(s + 1) * 32], in1=bb, op=ALU.add),
                                reads=["bsp"], writes=[("ps", b), "tmpT"])
                    else:
                        bb = bsp[:, half * 2:half * 2 + 2, :].unsqueeze(2).to_broadcast([128, 2, 2, 128])
                        P.op("dve", lambda q, half=half, blk=blk, bb=bb, ps4=ps4: q.tensor_tensor(
                            out=tmpT[:, half * 4:half * 4 + 4, blk * 128:(blk + 1) * 128].rearrange("p (g c) i -> p g c i", g=2),
                            in0=ps4, in1=bb, op=ALU.add),
                            reads=["bsp"], writes=[("ps", b), "tmpT"])
            P.op("pool", lambda q: q.tensor_tensor(out=mixT[:, 8:16, 0:n], in0=tmpT[:, :, 0:n], in1=uT[:, :, 0:n], op=ALU.mult),
                 reads=["tmpT", "uT"], writes=["mixT_b"])
            if nxt:
                nxt[1]()
            for g in range(4):
                for oc in range(2):
                    b = nbank()

                    def mmp(q, b=b, g=g, oc=oc):
                        r = None
                        for ic in range(2):
                            r = q.matmul(ps[b][:, 0:n], lhsT=wp_bf[:, g * 2 + ic, oc * 128:(oc + 1) * 128], rhs=dT[:, 2 * g + ic, 0:n],
                                         start=(ic == 0), stop=(ic == 1))
                        return r
                    P.op("pe", mmp, reads=["wp_bf", "sqb"], writes=[("ps", b)])
                    ch = 2 * g + oc
                    P.op("dve", lambda q, b=b, ch=ch: q.scalar_tensor_tensor(
                        out=mixT[:, ch, 0:n], in0=ps[b][:, 0:n], scalar=vec[:, 64 + e * 8 + ch:64 + e * 8 + ch + 1], in1=gT[:, ch, 0:n],
                        op0=ALU.mult, op1=ALU.mult), reads=["vec", "gT"], writes=[("ps", b), "mixT_a"])
            if nxt:
                nxt[2]()
            wo = WV("wo%d" % e)
            mixres = ["mixT_b", "mixT_a"]
            for dc in range(8):
                wva, wna = wload(wo[:, 0:8, dc * 128:(dc + 1) * 128], 8, 128)
                wvb, wnb = wload(wo[:, 8:16, dc * 128:(dc + 1) * 128], 8, 128)
                b = nbank()
                P.op("pe", mm_fm(b, wva, 0, 128, mixT[:, 0:8, :], n, first=True, last=False), reads=[wna] + mixres, writes=[("ps", b)])
                P.op("pe", mm_fm(b, wvb, 0, 128, mixT[:, 8:16, :], n, first=False, last=True), reads=[wnb] + mixres, writes=[("ps", b)])
                evac_copy(yT[:, dc, 0:n], ps[b][:, 0:n], b, ["yT"])
            post_norm_update(layer, c0, n)

        NK = 4 * NTOK
        KT = rview(0, [128, 2, NK], BF16)
        krT = R[0:64, 4 * NK:6 * NK].bitcast(BF16)
        Vt = rview(6 * NK, [128, 4 * NBLK, 258], BF16)
        WkvT = r2view(0, [128, 8, 256], BF16)
        Wv = r2view(4096, [128, 2, 8, 128], BF16)
        gateT = r2view(8192, [128, 8, 128], BF16)
        qn = r2view(10240, [128, 8, 128], BF16)
        qaT = r2view(12288, [128, 2, 8, 128], BF16)
        qrope = r2view(16384, [64, 8, 128], BF16, parts=64)
        kvn_bc = r2view(18432, [128, 256], F32)
        qcn = r2view(19456, [128, 3, 128], BF16)
        Pb = r2view(20224, [128, 2, 512], BF16)
        maskb = S("maskb", [128, 512], BF16)
        permf = S("permf", [64, 64], F32)
        cst = S("cst", [128, 64], F32)
        csfO = S("csfO", [128, 258], F32)
        csf = csfO[0:64, 0:256].rearrange("p (a t) -> p a t", a=2)
        stt = S("stt", [128, 16], F32)
        psb = [p[:].bitcast(BF16) for p in ps]
        tmp2d = tmpT[:].rearrange("p k n -> p (k n)")
        tmpb = tmp2d.bitcast(BF16)
        yT2d = yT[:].rearrange("p k n -> p (k n)")
        sqb2d = sqb[:].rearrange("p k n -> p (k n)")
        PTs = [tmpb[:, 0:512], tmpb[:, 512:1024]]
        krtm_s = tmpb[:, 1024:1600].rearrange("p (a b) -> p a b", b=64)
        olat = tmpb[:, 2048:4096].rearrange("p (h c) -> p h c", h=8)
        qas = tmpb[:, 1600:1856].rearrange("p (c t) -> p c t", c=2)
        qrs = tmpb[0:64, 1856:1984]
        qc = yT[:, 0:3, 128:256]
        Oacc = yT[:, 3:5, 128:256]
        qrf = yT[0:64, 5, 128:256]
        qt1 = yT[0:64, 6, 128:256]
        qt2 = yT[0:64, 7, 128:256]
        olT = sqb[:].rearrange("p k n -> p (k n)").rearrange("p (c h q) -> p c h q", c=2, h=8)
        oT = hT[:, :, 128:256]
        bks = [nc.dram_tensor("bks%d" % o, [64, 320], BF16) for o in range(2)]

        ld(tmp2d[:, 0:512], mask_d[:, :], "tmpT")
        P.op("dve", lambda q: q.tensor_copy(out=maskb[:], in_=tmp2d[:, 0:512]), reads=["tmpT"], writes=["maskb"])
        P.op("dve", lambda q: q.tensor_copy(out=permf[:, 0:32], in_=identf[0:64, 32:64]), reads=["identf"], writes=["permf"])
        P.op("pool", lambda q: q.tensor_copy(out=permf[:, 32:64], in_=identf[0:64, 0:32]), reads=["identf", "permf"], writes=["permf"])

        def stage_cast(src_ap, dst_ap, w, dres, parts=128):
            stv = wst[0][0:parts, 0:w]
            P.dma("sp", lambda q: [q.dma_start(out=stv, in_=src_ap)], writes=["wst0"], chan="wst0")
            P.op("pool", lambda q: q.tensor_copy(out=dst_ap, in_=stv), reads=["wst0"], writes=list(dres))

        def kt_transposes(kb, vidx, parts, krt_src, col0, w):
            b = nbank()

            def tr(q):
                q.transpose(psb[b][:, 0:w], Vt[0:parts, vidx, 0:128], identb[0:parts, 0:parts])
                q.transpose(psb[b][:, 128:128 + w], Vt[0:parts, vidx, 128:256], identb[0:parts, 0:parts])
                return q.transpose(psb[b][0:64, 256:256 + w], krt_src, identb[0:parts, 0:parts])
            P.op("pe", tr, reads=["V", "tmpT", "identb"], writes=[("ps", b)])
            P.op("act", lambda q: q.copy(out=KT[:, :, col0:col0 + w], in_=psb[b][:, 0:256].rearrange("p (c k) -> p c k", c=2)[:, :, 0:w]),
                 writes=[("ps", b), "KT"])
            P.op("dve", lambda q: q.tensor_copy(out=krT[:, col0:col0 + w], in_=psb[b][0:64, 256:256 + w]), writes=[("ps", b), "krT"])

        stt2 = S("stt2", [128, 24], F32)
        P.op("pool", lambda q: q.memset(stt2[:, 0:1], 30000.0 * ATTN_SCALE), writes=["nm_init"])
        Oaccs = [(csfO[:, 0:257], "csf"), (rstd[:, 0:257], "rstd")]
        Pbs = [Pb[:, 0, :], Pb[:, 1, :], wst[0][:, :].bitcast(BF16)]
        Pbn = [("Pb", 0), ("Pb", 1), "wst0"]

        uctr = [0]
        self_defer = [None]

        def kt_transposes2(kb, krt0, krt1):
            b = nbank()

            def tr(q):
                r = None
                for j, krt in enumerate((krt0, krt1)):
                    o = j * 384
                    q.transpose(psb[b][:, o:o + 128], Vt[:, kb + j, 0:128], identb[:])
                    q.transpose(psb[b][:, o + 128:o + 256], Vt[:, kb + j, 128:256], identb[:])
                    r = q.transpose(psb[b][0:64, o + 256:o + 384], krt, identb[:])
                return r
            P.op("pe", tr, reads=["V", "tmpT", "identb"], writes=[("ps", b)])
            src = psb[b][:, 0:768].rearrange("p (k x t) -> p k x t", k=2, x=3)
            c0 = kb * 128
            P.op("act", lambda q: q.copy(out=KT[:, :, c0:c0 + 256].rearrange("p c (k t) -> p k c t", k=2), in_=src[:, :, 0:2, :]),
                 writes=[("ps", b), "KT"])
            P.op("dve", lambda q: q.tensor_copy(out=krT[:, c0:c0 + 256].rearrange("p (k t) -> p k t", k=2), in_=src[0:64, :, 2, :]),
                 writes=[("ps", b), "krT"])

        def attn_stream(units, hook=None):
            steps = []
            for ui, U in enumerate(units):
                ng = len(U["groups"])
                U["_k"] = uctr[0] % 4
                U["_bO"] = 6 + uctr[0] % 2
                uctr[0] += 1
                for gi, G in enumerate(U["groups"]):
                    steps.append((ui, gi, ng, U, G))
            N = len(steps)
            st = [dict() for _ in range(N)]

            def col(base, k):
                return stt2[:, base + k:base + k + 1]

            def stA(i):
                ui, gi, ng, U, G = steps[i]
                w = G["w"]
                bS = nbank()
                st[i]["bS"] = bS

                def mmS(q):
                    q.matmul(ps[bS][:, 0:w], lhsT=U["qa0"], rhs=G["kt0"], start=True, stop=False)
                    q.matmul(ps[bS][:, 0:w], lhsT=U["qa1"], rhs=G["kt1"], start=False, stop=False)
                    r = q.matmul(ps[bS][:, 0:w], lhsT=U["qr"], rhs=G["krt"], start=False, stop=not G["diag"])
                    if G["diag"]:
                        r = q.matmul(ps[bS][:, 0:w], lhsT=identb[:], rhs=maskb[:, 0:w], start=False, stop=True)
                    return r
                P.op("pe", mmS, reads=["qaT", "qrope", "tmpT", "KT", "krT", "maskb", "identb"], writes=[("ps", bS)])

            def stB(i):
                ui, gi, ng, U, G = steps[i]
                w = G["w"]
                bS = st[i]["bS"]
                k = U["_k"]
                ib = i % 3
                if gi == 0:
                    P.op("dve", lambda q: q.reduce_max(out=col(6, k), in_=ps[bS][:, 0:w], axis=AX.X), writes=[("ps", bS), ("gmax", k)])
                    P.op("dve", lambda q: q.tensor_scalar(out=col(2, k), in0=col(6, k), scalar1=-ATTN_SCALE, scalar2=None, op0=ALU.mult),
                         reads=[("gmax", k)], writes=[("nm", k)])
                P.op("act", lambda q: q.activation(out=Pbs[ib][:, 0:w], in_=ps[bS][:, 0:w], func=AF.Exp, bias=col(2, k), scale=ATTN_SCALE),
                     reads=[("nm", k)], writes=[("ps", bS), Pbn[ib]])

            def stC(i):
                ui, gi, ng, U, G = steps[i]
                w = G["w"]
                bT = nbank()
                nj = (w + 127) // 128
                ib = i % 3
                it = i % 2

                def trP(q):
                    r = None
                    for j in range(nj):
                        wj = min(128, w - j * 128)
                        r = q.transpose(psb[bT][0:wj, j * 128:(j + 1) * 128], Pbs[ib][:, j * 128:j * 128 + wj], identb[:])
                    return r
                P.op("pe", trP, reads=[Pbn[ib], "identb"], writes=[("ps", bT)])
                kp = min(128, w)
                if i % 3 == 0:
                    P.op("act", lambda q: q.copy(out=PTs[it][0:kp, 0:nj * 128], in_=psb[bT][0:kp, 0:nj * 128]),
                         writes=[("ps", bT), ("PTs", it)])
                else:
                    P.op("dve", lambda q: q.tensor_copy(out=PTs[it][0:kp, 0:nj * 128], in_=psb[bT][0:kp, 0:nj * 128]),
                         writes=[("ps", bT), ("PTs", it)])

            def stD(i):
                ui, gi, ng, U, G = steps[i]
                bO = U["_bO"]
                it = i % 2

                def mmO(q):
                    r = None
                    nv = len(G["v"])
                    for j, (vap, K) in enumerate(G["v"]):
                        r = q.matmul(ps[bO][:, 0:257], lhsT=PTs[it][0:K, j * 128:(j + 1) * 128], rhs=vap,
                                     start=(gi == 0 and j == 0), stop=(gi == ng - 1 and j == nv - 1))
                    return r
                P.op("pe", mmO, reads=[("PTs", it), "V"], writes=[("ps", bO)])
                if gi == ng - 1:
                    P.op("dve", lambda q: q.reciprocal(out=stt2[:, 16:17], in_=ps[bO][:, 256:257]), writes=[("ps", bO), "rl"])
                    P.op("act", lambda q: q.activation(out=U["out_ap"], in_=ps[bO][:, 0:256], func=AF.Copy, scale=stt2[:, 16:17]),
                         reads=["rl"], writes=[("ps", bO), "olat"])
                    if U.get("post") is not None:
                        for j, fn in enumerate(U["post"]):
                            self_defer[0](2 + 2 * j, fn)

            pend = []
            st_cur = [0]

            def defer(delay, fn):
                pend.append((st_cur[0] + delay, fn))
            self_defer[0] = defer
            h0 = max(0, N // 2)
            for i in range(N + 3):
                st_cur[0] = i
                if hook is not None and i == h0:
                    for j, fn in enumerate(hook):
                        defer(2 * j, fn)
                due = [p for p in pend if p[0] <= i]
                pend[:] = [p for p in pend if p[0] > i]
                for _, fn in due:
                    fn()
                if i < N:
                    stA(i)
                    stB(i)
                if 0 <= i - 2 < N:
                    stC(i - 2)
                if 0 <= i - 3 < N:
                    stD(i - 3)
            for _, fn in sorted(pend, key=lambda p: p[0]):
                fn()

        def odd_layer(o):
            layer = 2 * o + 1
            wio = WV("wio%d" % o)
            wqu = WV("wqu%d" % o)
            wku = WV("wku%d" % o)
            wov = WV("wov%d" % o)
            if layer + 1 < NLAYERS:
                relay_layer(layer + 1)
            ld(kvn_bc, kv_norm[o:o + 1, :].broadcast_to([128, 256]), "kvn_bc")
            P.op("pool", lambda q: q.memset(Vt[:, :, 256:258], 1.0), writes=["V"])
            for h in range(8):
                wv, wn = wload(wku[:, :, h * 256:(h + 1) * 256], 2, 256)
                b = nbank()

                def trk(q, b=b, wv=wv):
                    q.transpose(psb[b][:, 0:128], wv[:, 0, 0:128], identb[:])
                    return q.transpose(psb[b][:, 128:256], wv[:, 1, 0:128], identb[:])
                P.op("pe", trk, reads=[wn, "identb"], writes=[("ps", b)])
                P.op("act", lambda q, b=b, h=h: q.copy(out=WkvT[:, h, :], in_=psb[b][:, 0:256]), writes=[("ps", b), "WkvT"])
                P.op("pool", lambda q, wv=wv, h=h: q.tensor_copy(out=Wv[:, :, h, :], in_=wv[:, :, 128:256]), reads=[wn], writes=["Wv"])
            units = [(u * 128, 128, u) for u in range(NBLK)] + [(NTOK, 64, NBLK)]
            for (c0, n, u) in units:
                sample = (u == NBLK)
                xres = [("xT", c0 // 256)]
                pre_norm(layer, xT[:, :, c0:c0 + n], n, xres)
                P.dma("sp", lambda q, u=u: [q.dma_start(out=cst[:], in_=cs_tm[:, u * 64:(u + 1) * 64])], writes=["cst"], chan="cst")
                b = nbank()
                for ci, (col, w) in enumerate([(384, 128), (512, 128), (640, 64)]):
                    wv, wn = wload(wio[:, :, col:col + w], KD, w)

                    def mmk(q, b=b, wv=wv, ci=ci, w=w, n=n):
                        r = None
                        for k in range(KD):
                            r = q.matmul(ps[b][0:n, ci * 128:ci * 128 + w], lhsT=hT[:, k, 0:n], rhs=wv[:, k, :], start=(k == 0), stop=(k == KD - 1))
                        return r
                    P.op("pe", mmk, reads=[wn, "hT"], writes=[("ps", b)])
                kvc = tmp2d[0:n, 0:320]
                P.op("act", lambda q, b=b, n=n, kvc=kvc: q.copy(out=kvc, in_=ps[b][0:n, 0:320]), writes=[("ps", b), "tmpT"])
                P.op("act", lambda q, n=n: q.activation(out=tmp2d[0:n, 512:768], in_=tmp2d[0:n, 0:256], func=AF.Square, accum_out=stt[0:n, 8:9]),
                     reads=["tmpT"], writes=["tmpT2", "ss"])
                P.op("act", lambda q, n=n: q.activation(out=stt[0:n, 9:10], in_=stt[0:n, 8:9], func=AF.Sqrt, bias=epsc[0:n, 0:1], scale=1.0 / 256.0),
                     reads=["ss", "epsc"], writes=["ss2"])
                P.op("dve", lambda q, n=n: q.reciprocal(out=stt[0:n, 10:11], in_=stt[0:n, 9:10]), reads=["ss2"], writes=["ss3"])
                P.op("dve", lambda q, n=n: q.scalar_tensor_tensor(out=yT2d[0:n, 0:256], in0=tmp2d[0:n, 0:256], scalar=stt[0:n, 10:11],
                                                                  in1=kvn_bc[0:n, :], op0=ALU.mult, op1=ALU.mult),
                     reads=["tmpT", "ss3", "kvn_bc"], writes=["yT"])
                x1, x2 = tmp2d[0:n, 256:288], tmp2d[0:n, 288:320]
                cs_, sn_ = cst[0:n, 0:32], cst[0:n, 32:64]
                t1, t2, t3, t4 = (tmp2d[0:n, 1024 + 32 * j:1056 + 32 * j] for j in range(4))
                P.op("pool", lambda q, x1=x1, cs_=cs_, t1=t1: q.tensor_tensor(out=t1, in0=x1, in1=cs_, op=ALU.mult), reads=["tmpT", "cst"], writes=["t1"])
                P.op("dve", lambda q, x2=x2, sn_=sn_, t2=t2: q.tensor_tensor(out=t2, in0=x2, in1=sn_, op=ALU.mult), reads=["tmpT", "cst"], writes=["t2"])
                P.op("pool", lambda q, x2=x2, cs_=cs_, t3=t3: q.tensor_tensor(out=t3, in0=x2, in1=cs_, op=ALU.mult), reads=["tmpT", "cst"], writes=["t3"])
                P.op("dve", lambda q, x1=x1, sn_=sn_, t4=t4: q.tensor_tensor(out=t4, in0=x1, in1=sn_, op=ALU.mult), reads=["tmpT", "cst"], writes=["t4"])
                P.op("pool", lambda q, n=n, t1=t1, t2=t2: q.tensor_tensor(out=yT2d[0:n, 256:288], in0=t1, in1=t2, op=ALU.subtract),
                     reads=["t1", "t2"], writes=["yTr1"])
                P.op("dve", lambda q, n=n, t3=t3, t4=t4: q.tensor_tensor(out=yT2d[0:n, 288:320], in0=t3, in1=t4, op=ALU.add),
                     reads=["t3", "t4"], writes=["yTr2"])
                P.op("pool", lambda q, n=n: q.tensor_copy(out=sqb2d[0:n, 0:320], in_=yT2d[0:n, 0:320]), reads=["yT", "yTr1", "yTr2"], writes=["sqb"])
                if sample:
                    P.dma("sp", lambda q: [q.dma_start(out=o_ckv_s[o], in_=yT2d[0:64, 0:256]), q.dma_start(out=o_kr_s[o], in_=yT2d[0:64, 256:320])],
                          reads=["yT", "yTr1", "yTr2"], chan="kvout", n=2, is_out=True)
                    P.dma("sp", lambda q: [q.dma_start(out=bks[o].ap()[:, :], in_=sqb2d[0:64, 0:320])], reads=["sqb"], writes=[("bks", o)], chan="bks")
                else:
                    P.dma("sp", lambda q, c0=c0: [q.dma_start(out=o_ckv_p[o, c0:c0 + 128, :], in_=yT2d[:, 0:256]),
                                                 q.dma_start(out=o_kr_p[o, c0:c0 + 128, :], in_=yT2d[:, 256:320])],
                          reads=["yT", "yTr1", "yTr2"], chan="kvout", n=2, is_out=True)
                    P.dma("sp", lambda q, u=u: [q.dma_start(out=bk_in[o][u // BPS].ap()[(u % BPS) * 128:(u % BPS) * 128 + 128, :], in_=sqb2d[:, 0:320])],
                          reads=["sqb"], writes=[("bk_in", o, u)], chan="bkin")
            for sp in range(NSPL):
                P.collective(lambda q, sp=sp: q.collective_compute("AllGather", ALU.bypass, replica_groups=[[0, 1, 2, 3], [4, 5, 6, 7]],
                                                                   ins=[bk_in[o][sp].ap().opt()], outs=[bk_out[o][sp].ap().opt()]),
                             reads=[("bk_in", o, u) for u in range(sp * BPS, (sp + 1) * BPS)], writes=[("bk_out", o, sp)], chan=("ccK", o, sp))
            qrfs = [yT[0:64, 5, 128:256], yT[0:64, 3, 128:256]]
            qt1s = [yT[0:64, 6, 128:256], yT[0:64, 4, 128:256]]

            def q_path(c0, n, do_norm=True):
                xres = [("xT", c0 // 256)]
                if do_norm:
                    pre_norm(layer, xT[:, :, c0:c0 + n], n, xres)
                P.dma("sp", lambda q: [q.dma_start(out=csf[:, :, 0:n], in_=cs_fm.rearrange("p (a t) -> p a t", a=2)[:, :, c0:c0 + n])],
                      writes=["csf"], chan="csf")
                for kc in range(3):
                    wv, wn = wload(wio[:, :, kc * 128:(kc + 1) * 128], KD, 128)
                    b = nbank()
                    P.op("pe", mm_fm(b, wv, 0, 128, hT, n), reads=[wn, "hT"], writes=[("ps", b)])
                    evac_copy(qc[:, kc, 0:n], ps[b][:, 0:n], b, ["qc"])
                rms_stats(qc[:, :, 0:n], n, ["qc"], kdim=3, scale=1024.0 / 384.0)
                for kc in range(3):
                    P.op("dve", lambda q, kc=kc: q.scalar_tensor_tensor(out=qcn[:, kc, 0:n], in0=qc[:, kc, 0:n],
                                                                        scalar=vec[:, 80 + o * 3 + kc:80 + o * 3 + kc + 1], in1=rstd[:, 0:n],
                                                                        op0=ALU.mult, op1=ALU.mult), reads=["qc", "vec", "rstd"], writes=["qcn"])
                for ec in range(8):
                    wv, wn = wload(wio[:, :, 704 + ec * 128:704 + (ec + 1) * 128], KD, 128)
                    b = nbank()
                    P.op("pe", mm_fm(b, wv, 0, 128, hT, n), reads=[wn, "hT"], writes=[("ps", b)])
                    P.op("act", lambda q, b=b, ec=ec: q.activation(out=gateT[:, ec, 0:n], in_=ps[b][:, 0:n], func=AF.Silu),
                         writes=[("ps", b), "gateT"])

                def stage_a(h):
                    j = h % 2
                    wv, wn = wload(wqu[:, :, h * 192:(h + 1) * 192], 3, 192)
                    b1 = nbank()
                    P.op("pe", mm_fm(b1, wv, 0, 128, qcn, n, kdim=3), reads=[wn, "qcn"], writes=[("ps", b1)])
                    evac_copy(qn[:, h, 0:n], ps[b1][:, 0:n], b1, [("qn", h)])
                    b2 = nbank()
                    P.op("pe", mm_fm(b2, wv, 128, 64, qcn, n, kdim=3), reads=[wn, "qcn"], writes=[("ps", b2)])
                    P.op("act", lambda q: q.copy(out=qrfs[j][:, 0:n], in_=ps[b2][0:64, 0:n]), writes=[("ps", b2), ("qrf", j)])

                def stage_b(h):
                    j = h % 2
                    jw = {"writes": ["qrope"]} if h == 0 else {"join": ["qrope"]}
                    jq = {"wres": ["qaT"]} if h == 0 else {"wres": [], "join": ["qaT"]}
                    b3 = nbank()
                    P.op("pe", lambda q: q.matmul(ps[b3][0:64, 0:n], lhsT=permf[:, :], rhs=qrfs[j][:, 0:n], start=True, stop=True),
                         reads=["permf", ("qrf", j)], writes=[("ps", b3)])
                    P.op("dve", lambda q: q.tensor_tensor(out=qt1s[j][:, 0:n], in0=ps[b3][0:64, 0:n], in1=csf[:, 1, 0:n], op=ALU.mult),
                         reads=["csf"], writes=[("ps", b3), ("qt1", j)])
                    P.op("pool", lambda q: q.tensor_tensor(out=qt2[:, 0:n], in0=qrfs[j][:, 0:n], in1=csf[:, 0, 0:n], op=ALU.mult),
                         reads=["csf", ("qrf", j)], writes=["qt2"])
                    P.op("dve", lambda q: q.tensor_tensor(out=qrope[:, h, 0:n], in0=qt1s[j][:, 0:n], in1=qt2[:, 0:n], op=ALU.add),
                         reads=[("qt1", j), "qt2"], **jw)
                    b4 = nbank()

                    def mma(q):
                        q.matmul(ps[b4][:, 0:n], lhsT=WkvT[:, h, 0:128], rhs=qn[:, h, 0:n], start=True, stop=True)
                        return q.matmul(ps[b4][:, 128:128 + n], lhsT=WkvT[:, h, 128:256], rhs=qn[:, h, 0:n], start=True, stop=True)
                    P.op("pe", mma, reads=["WkvT", ("qn", h)], writes=[("ps", b4)])
                    evac_copy(qaT[:, :, h, 0:n], ps[b4][:, 0:256].rearrange("p (c t) -> p c t", c=2)[:, :, 0:n], b4, jq["wres"], join=jq.get("join", ()))

                for i in range(9):
                    if i < 8:
                        stage_a(i)
                    if i >= 1:
                        stage_b(i - 1)

            def head_out_a(h):
                b = nbank()

                def tro(q):
                    q.transpose(psb[b][:, 0:128], olat[:, h, 0:128], identb[:])
                    return q.transpose(psb[b][:, 128:256], olat[:, h, 128:256], identb[:])
                P.op("pe", tro, reads=["olat", "identb"], writes=[("ps", b)])
                evac_copy(olT[:, :, h, :], psb[b][:, 0:256].rearrange("p (c q) -> p c q", c=2), b, [("olT", h)], join=["sqb"])

            def head_out_b(h, n):
                b2 = nbank()

                def mmo(q):
                    q.matmul(ps[b2][:, 0:n], lhsT=Wv[:, 0, h, :], rhs=olT[:, 0, h, 0:n], start=True, stop=False)
                    return q.matmul(ps[b2][:, 0:n], lhsT=Wv[:, 1, h, :], rhs=olT[:, 1, h, 0:n], start=False, stop=True)
                P.op("pe", mmo, reads=["Wv", ("olT", h), "sqb"], writes=[("ps", b2)])
                P.op("dve", lambda q: q.tensor_tensor(out=oT[:, h, 0:n], in0=ps[b2][:, 0:n], in1=gateT[:, h, 0:n], op=ALU.mult),
                     reads=["gateT"], writes=[("ps", b2)] + (["oT"] if h == 0 else []), join=([] if h == 0 else ["oT"]))

            def out_path(c0, n, heads_done=False):
                for h in range(0 if heads_done else 8):
                    b = nbank()

                    def mmo(q, b=b, h=h):
                        q.matmul(ps[b][:, 0:n], lhsT=Wv[:, 0, h, :], rhs=olT[:, 0, h, 0:n], start=True, stop=False)
                        return q.matmul(ps[b][:, 0:n], lhsT=Wv[:, 1, h, :], rhs=olT[:, 1, h, 0:n], start=False, stop=True)
                    P.op("pe", mmo, reads=["Wv", "sqb"], writes=[("ps", b)])
                    P.op("dve", lambda q, b=b, h=h: q.tensor_tensor(out=oT[:, h, 0:n], in0=ps[b][:, 0:n], in1=gateT[:, h, 0:n], op=ALU.mult),
                         reads=["gateT"], writes=[("ps", b), "oT"])
                for dc in range(8):
                    wv, wn = wload(wov[:, :, dc * 128:(dc + 1) * 128], KD, 128)
                    b = nbank()
                    P.op("pe", mm_fm(b, wv, 0, 128, oT, n), reads=[wn, "oT"], writes=[("ps", b)])
                    evac_copy(yT[:, dc, 0:n], ps[b][:, 0:n], b, ["yT"])
                post_norm_update(layer, c0, n)

            q_path(0, 128, do_norm=True)
            V4 = Vt.rearrange("p (m r) c -> p m r c", r=4)
            krtm = tmpb[:, 0:NK // 2].rearrange("p (m r c) -> p m r c", r=4, c=64)
            for sp in range(NSPL):
                bko = bk_out[o][sp].ap()
                ms = slice(sp * BPS, (sp + 1) * BPS)
                for r in range(4):
                    P.dma("sp", lambda q, r=r, bko=bko, ms=ms: [q.dma_start(out=V4[:, ms, r, 0:256], in_=bko[r * TPS:(r + 1) * TPS, 0:256].rearrange("(m p) c -> p m c", p=128))],
                          reads=[("bk_out", o, sp)], writes=["V"], chan="Vld")
                    P.dma("sp", lambda q, r=r, bko=bko, ms=ms: [q.dma_start(out=krtm[:, ms, r, :], in_=bko[r * TPS:(r + 1) * TPS, 256:320].rearrange("(m p) c -> p m c", p=128))],
                          reads=[("bk_out", o, sp)], writes=["tmpT"], chan="krld")
            krtm3 = tmpb[:, 0:NK // 2].rearrange("p (k c) -> p k c", c=64)
            for kb in range(0, 4 * NBLK, 2):
                kt_transposes2(kb, krtm3[:, kb, :], krtm3[:, kb + 1, :])

            for blk in range(NBLK):
                c0 = blk * 128
                relay_some(5 if blk < NBLK - 2 else 999)
                if blk > 0:
                    q_path(c0, 128, do_norm=False)
                units_ = []
                for h in range(8):
                    groups = []
                    for g in range(blk + 1):
                        ks = slice(g * 512, (g + 1) * 512)
                        groups.append(dict(kt0=KT[:, 0, ks], kt1=KT[:, 1, ks], krt=krT[:, ks], w=512,
                                           v=[(Vt[:, g * 4 + j, 0:257], 128) for j in range(4)], diag=(g == blk)))
                    units_.append(dict(qa0=qaT[:, 0, h, :], qa1=qaT[:, 1, h, :], qr=qrope[:, h, :], groups=groups, out_ap=olat[:, h, :],
                                       post=[(lambda h=h: head_out_a(h)), (lambda h=h: head_out_b(h, 128))]))
                nc0, nn = ((blk + 1) * 128, 128) if blk + 1 < NBLK else (NTOK, 64)
                attn_stream(units_, hook=pre_norm_lite_stages(layer, xT[:, :, nc0:nc0 + nn], nn, [("xT", nc0 // 256)]))
                out_path(c0, 128, heads_done=True)
            q_path(NTOK, 64, do_norm=False)
            for s_ in range(2):
                for kb in range(8):
                    stage_cast(cckv[o, s_, kb * 128:(kb + 1) * 128, :], Vt[:, kb, 0:256], 256, ["V"])
                    stage_cast(ckr[o, s_, kb * 128:(kb + 1) * 128, :], krtm_s[:, kb, :], 64, ["tmpT"])
                P.dma("sp", lambda q, s_=s_: [q.dma_start(out=Vt[0:32, 8, 0:256], in_=bks[o].ap()[s_ * 32:(s_ + 1) * 32, 0:256]),
                                             q.dma_start(out=krtm_s[0:32, 8, :], in_=bks[o].ap()[s_ * 32:(s_ + 1) * 32, 256:320])],
                      reads=[("bks", o)], writes=["V", "tmpT"], chan="bksld", n=2)
                for kb in range(0, 8, 2):
                    kt_transposes2(kb, krtm_s[:, kb, :], krtm_s[:, kb + 1, :])
                kt_transposes(8, 8, 32, krtm_s[0:32, 8, :], 1024, 32)
                units_ = []
                for hq in range(2):
                    ts = slice(s_ * 32, (s_ + 1) * 32)
                    hs = slice(hq * 4, hq * 4 + 4)
                    groups = []
                    for g in range(2):
                        ks = slice(g * 512, (g + 1) * 512)
                        groups.append(dict(kt0=KT[:, 0, ks], kt1=KT[:, 1, ks], krt=krT[:, ks], w=512,
                                           v=[(Vt[:, g * 4 + j, 0:257], 128) for j in range(4)], diag=False))
                    groups.append(dict(kt0=KT[:, 0, 1024:1056], kt1=KT[:, 1, 1024:1056], krt=krT[:, 1024:1056], w=32,
                                       v=[(Vt[0:32, 8, 0:257], 32)], diag=False))
                    P.op("dve", lambda q, hs=hs, ts=ts: q.tensor_copy(out=qas.rearrange("p c (h t) -> p c h t", h=4), in_=qaT[:, :, hs, ts]),
                         reads=["qaT"], writes=["tmpT"])
                    P.op("pool", lambda q, hs=hs, ts=ts: q.tensor_copy(out=qrs.rearrange("p (h t) -> p h t", h=4), in_=qrope[:, hs, ts]),
                         reads=["qrope"], writes=["tmpT"])

                    def post(hs=hs, ts=ts):
                        b = nbank()

                        def tro2(q):
                            q.transpose(psb[b][:, 0:128], olat[:, 0, 0:128], identb[:])
                            return q.transpose(psb[b][:, 128:256], olat[:, 0, 128:256], identb[:])
                        P.op("pe", tro2, reads=["olat", "identb"], writes=[("ps", b)])
                        evac_copy(olT[:, :, hs, ts], psb[b][:, 0:256].rearrange("p (c h t) -> p c h t", c=2, h=4), b, ["sqb"])
                    attn_stream([dict(qa0=qas[:, 0, :], qa1=qas[:, 1, :], qr=qrs, groups=groups, out_ap=olat[:, 0, :], post=[post])])
            out_path(NTOK, 64)

        for layer in range(NLAYERS):
            if layer % 2 == 0:
                e = layer // 2
                ctx = even_layer(e)
                for t in range(NT + 1):
                    if t == 0 and layer + 1 < NLAYERS:
                        relay_layer(layer + 1)
                    relay_some(8 if t < NT - 1 else 999)
                    even_tile(e, ctx, t)
            else:
                P.barrier()
                odd_layer(layer // 2)
                P.barrier()

        for b in range(NBLK):
            store_x(y_p[b * 128:(b + 1) * 128, :], b * 128, 128)
        store_x(y_s[:, :], NTOK, NS)
        P.finish()
        P.replay()
    return nc


def _host_consts(c, NBLK):
    NTOK = NBLK * 128
    TOT = NTOK + 64
    r = c % 4
    mask = np.zeros((128, 512), np.float32)
    qi = np.arange(128)[:, None]
    for i in range(4):
        blk = mask[:, i * 128:(i + 1) * 128]
        if i > r:
            blk[:] = NEG
        elif i == r:
            kj = np.arange(128)[None, :]
            blk[:] = np.where((kj // 64) <= (qi // 64), 0.0, NEG)
    selw = np.zeros((128, 8), np.float32)
    if r > 0:
        selw[:, r - 1] = 1.0
    else:
        selw[:, 4] = 1.0
    rc = np.zeros((128, 4, 16), np.float32)
    for g in range(4):
        w = 2 ** (g + 1)
        for p in range(16):
            rc[:, g, p] = (1.0 / min(p + 1, w)) if r == 0 else 1.0 / w
    half = 32
    freqs = (10000.0 ** (-np.arange(half, dtype=np.float32) / half)).astype(np.float32)
    pos = np.zeros(TOT, np.float32)
    for m in range(NBLK):
        pos[m * 128:(m + 1) * 128] = (4 * m + r) * 128 + np.arange(128)
    pos[NTOK:NTOK + 32] = 1024 + np.arange(32)
    pos[NTOK + 32:] = 1024 + np.arange(32)
    ang = pos[:, None].astype(np.float32) * freqs[None, :]
    cos, sin = np.cos(ang).astype(np.float32), np.sin(ang).astype(np.float32)
    cs_tm = np.zeros((128, NBLK + 1, 64), np.float32)
    for m in range(NBLK):
        cs_tm[:, m, 0:32] = cos[m * 128:(m + 1) * 128]
        cs_tm[:, m, 32:64] = sin[m * 128:(m + 1) * 128]
    cs_tm[0:64, NBLK, 0:32] = cos[NTOK:]
    cs_tm[0:64, NBLK, 32:64] = sin[NTOK:]
    cs_fm = np.zeros((64, 2, TOT), np.float32)
    cs_fm[0:32, 0] = cos.T
    cs_fm[32:64, 0] = cos.T
    cs_fm[0:32, 1] = -sin.T
    cs_fm[32:64, 1] = sin.T
    return dict(mask=mask, selw=selw, rcnt=rc.reshape(128, 64), cs_tm=cs_tm.reshape(128, -1), cs_fm=cs_fm.reshape(64, -1),
                ident=np.eye(128, dtype=np.float32))


def _fm(v):
    return np.ascontiguousarray(np.asarray(v, np.float32).reshape(-1, 128).T)


_NC_CACHE = {}


def kernel(x_prompt, x_sample, cache_pool, cache_ckv, cache_krope, norm_pre, norm_post,
           w_in_even, w_pool, pool_scale, sgu_ln_g, sgu_ln_b, w_spatial, b_spatial, w_out_even,
           w_in_odd, q_norm, kv_norm, w_q_up, w_kv_up, w_o, _nlayers=4):
    f = lambda a: np.ascontiguousarray(np.asarray(a, dtype=np.float32))
    x_prompt = f(x_prompt)
    x_sample = f(x_sample)
    B, T, _ = x_prompt.shape
    NBLK = T // 512
    NTOK = NBLK * 128
    vecs = np.zeros((128, 96), np.float32)
    for l in range(4):
        vecs[:, l * 8:(l + 1) * 8] = _fm(f(norm_pre)[l])
        vecs[:, 32 + l * 8:32 + (l + 1) * 8] = _fm(f(norm_post)[l])
    for e in range(2):
        vecs[:, 64 + e * 8:64 + (e + 1) * 8] = _fm(f(pool_scale)[e])
        vecs[:, 80 + e * 3:80 + (e + 1) * 3] = _fm(f(q_norm)[e])
    shared = dict(w_in_even=f(w_in_even), w_pool=f(w_pool), ln_g=f(sgu_ln_g), ln_b=f(sgu_ln_b), w_sp=f(w_spatial),
                  b_sp=f(b_spatial), w_out_even=f(w_out_even), w_in_odd=f(w_in_odd), kv_norm=f(kv_norm),
                  w_q_up=f(w_q_up).reshape(2, 384, 8 * 192), w_kv_up=f(w_kv_up).reshape(2, 256, 8 * 256), w_o=f(w_o), vecs=vecs)
    cache_pool, cache_ckv, cache_krope = f(cache_pool), f(cache_ckv), f(cache_krope)
    in_maps = []
    for c in range(8):
        b, r = c // 4, c % 4
        xb = x_prompt[b].reshape(NBLK, 4, 128, D)[:, r].reshape(NTOK, D)
        m = dict(shared)
        m.update(_host_consts(c, NBLK))
        m["xp"] = np.ascontiguousarray(xb)
        m["xs"] = np.ascontiguousarray(x_sample[2 * c:2 * c + 2].reshape(64, D))
        m["cpool"] = np.ascontiguousarray(cache_pool[:, 2 * c:2 * c + 2])
        m["cckv"] = np.ascontiguousarray(cache_ckv[:, 2 * c:2 * c + 2])
        m["ckr"] = np.ascontiguousarray(cache_krope[:, 2 * c:2 * c + 2])
        in_maps.append(m)
    key = (NBLK, _nlayers)
    if key not in _NC_CACHE:
        _NC_CACHE[key] = build(NBLK, _nlayers)
    nc = _NC_CACHE[key]
    res = run_bass_kernel_spmd(nc, in_maps, core_ids=list(range(8))).results

    def unshard(name, width):
        out = np.zeros((B, NBLK, 4, 128, width), np.float32)
        for c in range(8):
            out[c // 4, :, c % 4] = res[c][name].reshape(NBLK, 128, width)
        return out.reshape(B, T, width)

    def unshard_l(name, width):
        out = np.zeros((2, B, NBLK, 4, 128, width), np.float32)
        for c in range(8):
            out[:, c // 4, :, c % 4] = res[c][name].reshape(2, NBLK, 128, width)
        return out.reshape(2, B, T, width)

    y_prompt = unshard("y_p", D)
    y_sample = np.concatenate([res[c]["y_s"].reshape(2, 32, D) for c in range(8)], 0)
    pool_p = np.stack([res[3]["o_pool_p"], res[7]["o_pool_p"]], 1)
    pool_s = np.concatenate([res[c]["o_pool_s"] for c in range(8)], 1)
    sgu_s = np.concatenate([res[c]["o_sgu_s"].reshape(2, 2, 32, D) for c in range(8)], 1)
    ckv_p = unshard_l("o_ckv_p", 256)
    kr_p = unshard_l("o_kr_p", 64)
    ckv_s = np.concatenate([res[c]["o_ckv_s"].reshape(2, 2, 32, 256) for c in range(8)], 1)
    kr_s = np.concatenate([res[c]["o_kr_s"].reshape(2, 2, 32, 64) for c in range(8)], 1)
    return (y_prompt, y_sample, pool_p, pool_s, sgu_s, ckv_p, kr_p, ckv_s, kr_s)
```

```python
import contextlib
import numpy as np
import concourse.bass as bass
import concourse.mybir as mybir
from concourse.bass_utils import run_bass_kernel_spmd

F32 = mybir.dt.float32
BF16 = mybir.dt.bfloat16
AF = mybir.ActivationFunctionType
ALU = mybir.AluOpType
AX = mybir.AxisListType

D = 1024
KD = 8
EPS = 1e-6
ATTN_SCALE = 192.0 ** -0.5
NEG = -30000.0


class Op:
    __slots__ = ("eng", "fn", "waits", "signal", "count", "kind", "chan", "chan_val")

    def __init__(self, eng, fn, kind):
        self.eng = eng
        self.fn = fn
        self.kind = kind
        self.waits = []
        self.signal = False
        self.count = None
        self.chan = None
        self.chan_val = None


class Prog:
    ENGS = ("pe", "act", "dve", "pool", "sp")

    def __init__(self, nc):
        self.nc = nc
        self.ops = {e: [] for e in self.ENGS}
        self.res = {}
        self.chan_tot = {}
        self.chan_sem = {}
        self.eng_sem = {}
        self.out_ops = []

    def _deps(self, op, reads, writes, join=()):
        deps = []
        for r in reads:
            st = self.res.get(r)
            if st is None:
                st = self.res[r] = [[], [], []]
            for wop in st[0]:
                deps.append((wop, "raw"))
        for w in list(writes) + list(join):
            st = self.res.get(w)
            if st is None:
                st = self.res[w] = [[], [], []]
            if w not in join:
                for wop in st[0]:
                    deps.append((wop, "waw"))
            else:
                for rd in st[2]:
                    deps.append((rd, "war"))
            for rd in st[1]:
                deps.append((rd, "war"))
        seen = set()
        for p, kind in deps:
            if p is op or id(p) in seen:
                continue
            if p.kind == "c" and p.eng == op.eng and op.kind == "c":
                if op.eng == "pe" or kind == "war":
                    continue
            seen.add(id(p))
            op.waits.append(p)
            if p.kind == "c":
                p.signal = True
        for r in reads:
            self.res[r][1].append(op)
        for w in writes:
            self.res[w] = [[op], [], self.res[w][1]]
        for w in join:
            self.res[w][0].append(op)

    def op(self, eng, fn, reads=(), writes=(), join=()):
        o = Op(eng, fn, "c")
        self._deps(o, reads, writes, join)
        self.ops[eng].append(o)
        return o

    def dma(self, eng, fn, reads=(), writes=(), chan=None, n=1, is_out=False):
        o = Op(eng, fn, "d")
        self._deps(o, reads, writes)
        tot = self.chan_tot.get(chan, 0) + 16 * n
        self.chan_tot[chan] = tot
        o.chan = chan
        o.chan_val = tot
        self.ops[eng].append(o)
        if is_out:
            self.out_ops.append(o)
        return o

    def collective(self, fn, reads=(), writes=(), chan=None):
        o = Op("pool", fn, "x")
        self._deps(o, reads, writes)
        assert chan not in self.chan_tot
        self.chan_tot[chan] = 1
        o.chan = chan
        o.chan_val = 1
        self.ops["pool"].append(o)
        return o

    def barrier(self):
        o = Op("sp", lambda e: e.nop(), "c")
        for e in self.ENGS:
            if e == "sp":
                continue
            for p in reversed(self.ops[e]):
                if p.kind == "c":
                    p.signal = True
                    o.waits.append(p)
                    break
        lastd = {}
        for e in self.ENGS:
            for p in self.ops[e]:
                if p.kind != "c":
                    lastd[p.chan] = p
        o.waits.extend(lastd.values())
        o.signal = True
        self.ops["sp"].append(o)
        for e in self.ENGS:
            if e == "sp":
                continue
            o2 = Op(e, lambda q: q.nop(), "c")
            o2.waits.append(o)
            self.ops[e].append(o2)
        self.res = {}

    def finish(self):
        o = Op("sp", lambda e: e.nop(), "c")
        for p in self.out_ops:
            o.waits.append(p)
        for e in self.ENGS:
            if e == "sp":
                continue
            for p in reversed(self.ops[e]):
                if p.kind == "c":
                    p.signal = True
                    o.waits.append(p)
                    break
        self.ops["sp"].append(o)

    def replay(self):
        nc = self.nc
        engobj = {"pe": nc.tensor, "act": nc.scalar, "dve": nc.vector, "pool": nc.gpsimd, "sp": nc.sync}
        EPOCH = 6000
        for i, c in enumerate(self.chan_tot):
            self.chan_sem[c] = nc.alloc_semaphore(name="c%d" % i)
        for e in self.ENGS:
            cnt = 0
            for o in self.ops[e]:
                if o.kind == "c" and o.signal:
                    ep = cnt // EPOCH
                    if (e, ep) not in self.eng_sem:
                        self.eng_sem[(e, ep)] = nc.alloc_semaphore(name="s_%s%d" % (e, ep))
                    o.count = (ep, cnt % EPOCH + 1)
                    cnt += 1
        prog = self

        def run(e):
            eng = engobj[e]
            seen = {}
            for o in prog.ops[e]:
                for p in o.waits:
                    if p.kind == "c":
                        sem, val = prog.eng_sem[(p.eng, p.count[0])], p.count[1]
                    else:
                        sem, val = prog.chan_sem[p.chan], p.chan_val
                    k = id(sem)
                    if seen.get(k, 0) >= val:
                        continue
                    seen[k] = val
                    eng.wait_ge(sem, val)
                r = o.fn(eng)
                if o.kind == "c":
                    if o.signal:
                        r.then_inc(prog.eng_sem[(e, o.count[0])], 1)
                elif o.kind == "d":
                    for ins in r:
                        ins.then_inc(prog.chan_sem[o.chan], 16)
                else:
                    r.then_inc(prog.chan_sem[o.chan])

        with nc.Block() as block:
            @block.tensor
            def _(t):
                run("pe")

            @block.scalar
            def _(t):
                run("act")

            @block.vector
            def _(t):
                run("dve")

            @block.gpsimd
            def _(t):
                run("pool")

            @block.sync
            def _(t):
                run("sp")


def build(NBLK, NLAYERS):
    NTOK = NBLK * 128
    NS = 64
    TOT = NTOK + NS
    NT = NBLK // 2
    nc = bass.Bass("TRN2", target_bir_lowering=False)

    def din(name, shape):
        return nc.dram_tensor(name, list(shape), F32, kind="ExternalInput").ap()

    def dout(name, shape):
        return nc.dram_tensor(name, list(shape), F32, kind="ExternalOutput").ap()

    xp = din("xp", [NTOK, D])
    xs = din("xs", [NS, D])
    cpool = din("cpool", [2, 2, 15, D])
    cckv = din("cckv", [2, 2, 1024, 256])
    ckr = din("ckr", [2, 2, 1024, 64])
    w_in_even = din("w_in_even", [2, D, 5120])
    w_pool = din("w_pool", [2, 4, 256, 256])
    ln_g = din("ln_g", [2, D])
    ln_b = din("ln_b", [2, D])
    w_sp = din("w_sp", [2, 4, 128, 128])
    b_sp = din("b_sp", [2, 4, 128])
    w_out_even = din("w_out_even", [2, 2048, D])
    w_in_odd = din("w_in_odd", [2, D, 1728])
    kv_norm = din("kv_norm", [2, 256])
    w_q_up = din("w_q_up", [2, 384, 8 * 192])
    w_kv_up = din("w_kv_up", [2, 256, 8 * 256])
    w_o = din("w_o", [2, D, D])
    vecs = din("vecs", [128, 96])
    ident_d = din("ident", [128, 128])
    mask_d = din("mask", [128, 512])
    selw_d = din("selw", [128, 8])
    rcnt_d = din("rcnt", [128, 4 * 16])
    cs_tm = din("cs_tm", [128, (NBLK + 1) * 64])
    cs_fm = din("cs_fm", [64, 2 * TOT])

    y_p = dout("y_p", [NTOK, D])
    y_s = dout("y_s", [NS, D])
    o_pool_p = dout("o_pool_p", [2, 15, D])
    o_pool_s = dout("o_pool_s", [2, 2, 15, D])
    o_sgu_s = dout("o_sgu_s", [2, NS, D])
    o_ckv_p = dout("o_ckv_p", [2, NTOK, 256])
    o_kr_p = dout("o_kr_p", [2, NTOK, 64])
    o_ckv_s = dout("o_ckv_s", [2, NS, 256])
    o_kr_s = dout("o_kr_s", [2, NS, 64])

    HW = 8 * NBLK * 16
    bh_in = [nc.dram_tensor("bh_in%d" % e, [128, HW], F32) for e in range(2)]
    bh_out = [nc.dram_tensor("bh_out%d" % e, [4 * 128, HW], F32) for e in range(2)]
    NSPL = max(1, NBLK // 8)
    BPS = NBLK // NSPL
    TPS = BPS * 128
    bk_in = [[nc.dram_tensor("bk_in%d_%d" % (o, sp), [TPS, 320], BF16) for sp in range(NSPL)] for o in range(2)]
    bk_out = [[nc.dram_tensor("bk_out%d_%d" % (o, sp), [4 * TPS, 320], BF16) for sp in range(NSPL)] for o in range(2)]

    P = Prog(nc)
    es = contextlib.ExitStack()

    def S(name, shape, dt):
        return es.enter_context(nc.sbuf_tensor("t_" + name, list(shape), dt))

    with es:
        ps = [es.enter_context(nc.psum_tensor("ps%d" % i, [128, 512], F32)) for i in range(8)]
        bankctr = [0]

        def nbank():
            b = bankctr[0] % 6
            bankctr[0] += 1
            return b

        xT = S("xT", [128, KD, TOT], F32)
        identf = S("identf", [128, 128], F32)
        identb = S("identb", [128, 128], BF16)
        onesb = S("onesb", [128, 128], BF16)
        epsc = S("epsc", [128, 1], F32)
        vec = S("vec", [128, 96], F32)
        selw = S("selw", [128, 8], F32)
        rcnt = S("rcnt", [128, 4, 16], F32)
        NWB = 4
        wst = [S("wst%d" % i, [128, 256], F32) for i in range(1)]
        wbf = [S("wbf%d" % i, [128, 8 * 128], BF16) for i in range(NWB)]
        hT = S("hT", [128, KD, 256], BF16)
        sqb = S("sqb", [128, KD, 256], BF16)
        rstd = S("rstd", [128, 258], F32)
        yT = S("yT", [128, KD, 256], F32)
        tmpT = S("tmpT", [128, KD, 256], F32)
        RB = max(81920, 24 * NTOK + 4 * NBLK * 516)
        R = S("R", [128, RB], mybir.dt.uint8)

        R2 = S("R2", [128, 23040], mybir.dt.uint8)

        def r2view(off, shape, dt, parts=128):
            n = 1
            for s_ in shape[1:]:
                n *= s_
            esz = 4 if dt == F32 else 2
            v = R2[0:parts, off:off + n * esz].bitcast(dt)
            if len(shape) == 3:
                v = v.rearrange("p (a b) -> p a b", a=shape[1])
            elif len(shape) == 4:
                v = v.rearrange("p (a b c) -> p a b c", a=shape[1], b=shape[2])
            return v

        def rview(off, shape, dt):
            n = 1
            for s in shape[1:]:
                n *= s
            esz = 4 if dt == F32 else 2
            v = R[:, off:off + n * esz].bitcast(dt)
            if len(shape) == 3:
                v = v.rearrange("p (a b) -> p a b", a=shape[1])
            elif len(shape) == 4:
                v = v.rearrange("p (a b c) -> p a b c", a=shape[1], b=shape[2])
            return v

        def ld(dst, src, name, eng="sp"):
            P.dma(eng, lambda q: [q.dma_start(out=dst, in_=src)], writes=[name], chan=name)

        ld(identf[:], ident_d[:, :], "identf")
        ld(vec[:], vecs[:, :], "vec")
        ld(selw[:], selw_d[:, :], "selw")
        ld(rcnt[:].rearrange("p a b -> p (a b)"), rcnt_d[:, :], "rcnt")
        P.op("dve", lambda q: q.tensor_copy(out=identb[:], in_=identf[:]), reads=["identf"], writes=["identb"])
        P.op("pool", lambda q: q.memset(onesb[:], 1.0 / 1024.0), writes=["onesb"])
        P.op("pool", lambda q: q.memset(epsc[:], EPS), writes=["epsc"])

        evq = [0]

        def evac_copy(out, in_, bank, wres, rres=(), scale=None, join=()):
            evq[0] += 1
            if evq[0] % 2 == 0:
                P.op("act", lambda q: q.activation(out=out, in_=in_, func=AF.Copy, scale=(1.0 if scale is None else scale)),
                     reads=list(rres), writes=[("ps", bank)] + list(wres), join=join)
            else:
                if scale is None:
                    P.op("dve", lambda q: q.tensor_copy(out=out, in_=in_), reads=list(rres), writes=[("ps", bank)] + list(wres), join=join)
                else:
                    P.op("dve", lambda q: q.tensor_scalar(out=out, in0=in_, scalar1=scale, scalar2=None, op0=ALU.mult),
                         reads=list(rres), writes=[("ps", bank)] + list(wres), join=join)

        wctr = [0]

        WBA = {}
        CH = {}
        rlctr = [0]

        class WV:
            def __init__(self, name):
                self.name = name

            def __getitem__(self, idx):
                _, ks, cs = idx
                return (self.name, ks.start or 0, cs.start)

        SRC = {}

        def conv(name, src2d, rows, cols, piece=None):
            SRC[name] = src2d

        RQ = []

        def relay(name, k0, kdim, c0, cols):
            RQ.append((name, k0, kdim, c0, cols))

        def relay_some(k):
            for _ in range(min(k, len(RQ))):
                relay_now(*RQ.pop(0))

        def relay_now(name, k0, kdim, c0, cols):
            i = rlctr[0]
            rlctr[0] += 1
            B = nc.dram_tensor("wbB_%s_%d_%d" % (name, k0, c0), [128, kdim * cols], BF16)
            CH[(name, k0, c0)] = (B, kdim, cols)
            src = SRC[name].rearrange("(k p) c -> p k c", p=128)[:, k0:k0 + kdim, c0:c0 + cols]
            slot = ("rlslot", i % 8)
            P.dma("pool", lambda q: [q.dma_start(out=B.ap().rearrange("p (k c) -> p k c", k=kdim), in_=src)],
                  reads=[slot], writes=[("wbB", name, k0, c0), slot], chan=slot)

        def relay_layer(layer):
            if layer % 2 == 0:
                e = layer // 2
                for j in range(40):
                    relay("win%d" % e, 0, 8, j * 128, 128)
                for dc in range(8):
                    relay("wo%d" % e, 0, 8, dc * 128, 128)
                    relay("wo%d" % e, 8, 8, dc * 128, 128)
            else:
                o = layer // 2
                for h in range(8):
                    relay("wku%d" % o, 0, 2, h * 256, 256)
                for (c0, w) in [(384, 128), (512, 128), (640, 64), (0, 128), (128, 128), (256, 128)] + [(704 + ec * 128, 128) for ec in range(8)]:
                    relay("wio%d" % o, 0, 8, c0, w)
                for h in range(8):
                    relay("wqu%d" % o, 0, 3, h * 192, 192)
                for dc in range(8):
                    relay("wov%d" % o, 0, 8, dc * 128, 128)

        def conv_layer(layer):
            if layer % 2 == 0:
                e = layer // 2
                conv("win%d" % e, w_in_even[e], D, 5120, piece=1024)
                conv("wo%d" % e, w_out_even[e], 2048, D)
            else:
                o = layer // 2
                conv("wku%d" % o, w_kv_up[o], 256, 2048)
                conv("wio%d" % o, w_in_odd[o], D, 1728)
                conv("wqu%d" % o, w_q_up[o], 384, 1536)
                conv("wov%d" % o, w_o[o], D, D)

        def wload(ref, kdim, cols):
            i = wctr[0]
            wctr[0] += 1
            B, kd_, cols_ = CH[ref]
            assert kd_ == kdim and cols_ == cols, (ref, kdim, cols)
            wb = wbf[i % NWB]
            bname = "wbf%d" % (i % NWB)
            n = kdim * cols
            wbv = wb[:, 0:n].rearrange("p (k c) -> p k c", k=kdim)
            P.dma("sp", lambda q: [q.dma_start(out=wb[:, 0:n], in_=B.ap()[:, :])], reads=[("wbB",) + ref], writes=[bname], chan=bname)
            return wbv, bname

        xin = tmpT[:].rearrange("p k n -> p (k n)")[:, 0:D]

        def load_x(src_rows, c0, n):
            P.dma("sp", lambda q: [q.dma_start(out=xin[0:n, :], in_=src_rows)], writes=["tmpT"], chan="xin")
            for half in range(2):
                b = nbank()

                def tr(q, half=half, b=b):
                    r = None
                    for j in range(4):
                        k = half * 4 + j
                        r = q.transpose(ps[b][:, j * 128:j * 128 + n], xin[0:n, k * 128:(k + 1) * 128], identf[0:n, 0:n])
                    return r
                P.op("pe", tr, reads=["tmpT", "identf"], writes=[("ps", b)])
                src = ps[b][:].rearrange("p (j t) -> p j t", j=4)[:, :, 0:n]
                evac_copy(xT[:, half * 4:half * 4 + 4, c0:c0 + n], src, b, [("xT", c0 // 256)])

        yout = yT[:].rearrange("p k n -> p (k n)")[:, 0:D]

        yall = yT[:].rearrange("p k n -> p (k n)")
        stctr = [0]

        def store_x(dst_rows, c0, n):
            si = stctr[0] % 2
            stctr[0] += 1
            yout = yall[:, si * D:(si + 1) * D]
            for half in range(2):
                b = nbank()

                def tr(q, half=half, b=b):
                    r = None
                    for j in range(4):
                        k = half * 4 + j
                        r = q.transpose(ps[b][0:n, j * 128:(j + 1) * 128], xT[:, k, c0:c0 + n], identf[:, :])
                    return r
                P.op("pe", tr, reads=[("xT", c0 // 256), "identf"], writes=[("ps", b)])
                evac_copy(yout[0:n, half * 512:(half + 1) * 512], ps[b][0:n, :], b, [("yout", si, half)])
            P.dma("sp", lambda q: [q.dma_start(out=dst_rows, in_=yout[0:n, :])], reads=[("yout", si, 0), ("yout", si, 1)],
                  chan=("yout", si), is_out=True)

        for layer_ in range(NLAYERS):
            conv_layer(layer_)
        relay_layer(0)
        relay_some(999)
        for b in range(NBLK):
            load_x(xp[b * 128:(b + 1) * 128, :], b * 128, 128)
        load_x(xs[:, :], NTOK, NS)

        def rms_stats(src4, n, srcres, outname="rstd", kdim=KD, scale=1.0):
            P.op("act", lambda q: q.activation(out=sqb[:, 0:kdim, 0:n], in_=src4, func=AF.Square), reads=list(srcres), writes=["sqb"])
            b = nbank()

            def mm(q):
                r = None
                for k in range(kdim):
                    r = q.matmul(ps[b][:, 0:n], lhsT=onesb[:], rhs=sqb[:, k, 0:n], start=(k == 0), stop=(k == kdim - 1))
                return r
            P.op("pe", mm, reads=["sqb", "onesb"], writes=[("ps", b)])
            P.op("act", lambda q: q.activation(out=rstd[:, 0:n], in_=ps[b][:, 0:n], func=AF.Sqrt, bias=epsc[:, 0:1], scale=scale),
                 reads=["epsc"], writes=[("ps", b), outname])
            P.op("dve", lambda q: q.reciprocal(out=rstd[:, 0:n], in_=rstd[:, 0:n]), reads=[outname], writes=[outname])

        KS = 5

        def pre_norm(layer, xview, n, xres):
            gb = vec[:, layer * 8:layer * 8 + 8].unsqueeze(2).to_broadcast([128, KD, n])
            P.op("pool", lambda q: q.tensor_tensor(out=tmpT[:, :, 0:n], in0=xview, in1=gb, op=ALU.mult),
                 reads=list(xres) + ["vec"], writes=["tmpT"])
            rms_stats(xview, n, xres)
            rb1 = rstd[:, 0:n].unsqueeze(1).to_broadcast([128, KS, n])
            rb2 = rstd[:, 0:n].unsqueeze(1).to_broadcast([128, KD - KS, n])
            P.op("dve", lambda q: q.tensor_tensor(out=hT[:, 0:KS, 0:n], in0=tmpT[:, 0:KS, 0:n], in1=rb1, op=ALU.mult),
                 reads=["tmpT", "rstd"], writes=["hT"])
            P.op("pool", lambda q: q.tensor_tensor(out=hT[:, KS:KD, 0:n], in0=tmpT[:, KS:KD, 0:n], in1=rb2, op=ALU.mult),
                 reads=["tmpT", "rstd"], join=["hT"])

        def pre_norm_lite_stages(layer, xview, n, xres, hoff=0, hres="hT"):
            def s1():
                P.op("act", lambda q: q.activation(out=hT[:, :, hoff:hoff + n], in_=xview, func=AF.Square), reads=list(xres), writes=[hres])
            bb = [None]

            def s2():
                pre_norm_lite_mid(n, bb, hoff, hres)

            def s3():
                for k in range(KD):
                    P.op("dve", lambda q, k=k: q.scalar_tensor_tensor(out=hT[:, k, hoff:hoff + n], in0=xview[:, k, :], scalar=vec[:, layer * 8 + k:layer * 8 + k + 1],
                                                                      in1=rstd[:, 0:n], op0=ALU.mult, op1=ALU.mult),
                         reads=list(xres) + ["vec", "rstd"], **({"writes": [hres]} if k == 0 else {"join": [hres]}))
            return [s1, s2, s3]

        def pre_norm_lite_mid(n, bb, hoff=0, hres="hT"):
            b = nbank()

            def mm(q):
                r = None
                for k in range(KD):
                    r = q.matmul(ps[b][:, 0:n], lhsT=onesb[:], rhs=hT[:, k, hoff:hoff + n], start=(k == 0), stop=(k == KD - 1))
                return r
            P.op("pe", mm, reads=[hres, "onesb"], writes=[("ps", b)])
            P.op("act", lambda q: q.activation(out=rstd[:, 0:n], in_=ps[b][:, 0:n], func=AF.Sqrt, bias=epsc[:, 0:1], scale=1.0),
                 reads=["epsc"], writes=[("ps", b), "rstd"])
            P.op("dve", lambda q: q.reciprocal(out=rstd[:, 0:n], in_=rstd[:, 0:n]), reads=["rstd"], writes=["rstd"])

        def post_norm_update(layer, c0, n):
            xres = [("xT", c0 // 256)]
            gb = vec[:, 32 + layer * 8:32 + layer * 8 + 8].unsqueeze(2).to_broadcast([128, KD, n])
            P.op("pool", lambda q: q.tensor_tensor(out=tmpT[:, :, 0:n], in0=yT[:, :, 0:n], in1=gb, op=ALU.mult),
                 reads=["yT", "vec"], writes=["tmpT"])
            rms_stats(yT[:, :, 0:n], n, ["yT"])
            rb1 = rstd[:, 0:n].unsqueeze(1).to_broadcast([128, KS, n])
            rb2 = rstd[:, 0:n].unsqueeze(1).to_broadcast([128, KD - KS, n])
            P.op("dve", lambda q: q.tensor_tensor(out=tmpT[:, 0:KS, 0:n], in0=tmpT[:, 0:KS, 0:n], in1=rb1, op=ALU.mult),
                 reads=["tmpT", "rstd"], writes=["tmpTa"])
            P.op("pool", lambda q: q.tensor_tensor(out=tmpT[:, KS:KD, 0:n], in0=tmpT[:, KS:KD, 0:n], in1=rb2, op=ALU.mult),
                 reads=["tmpT", "rstd"], writes=["tmpTb"])
            P.op("dve", lambda q: q.tensor_tensor(out=xT[:, 0:KS, c0:c0 + n], in0=xT[:, 0:KS, c0:c0 + n], in1=tmpT[:, 0:KS, 0:n], op=ALU.add),
                 reads=["tmpTa", "tmpT"] + xres, writes=xres)
            P.op("pool", lambda q: q.tensor_tensor(out=xT[:, KS:KD, c0:c0 + n], in0=xT[:, KS:KD, c0:c0 + n], in1=tmpT[:, KS:KD, 0:n], op=ALU.add),
                 reads=["tmpTb", "tmpT"] + xres, join=xres)

        def mm_fm(bank, wv, col0, ncol, rhs3, n, kdim=KD, first=True, last=True):
            def f(q):
                r = None
                for k in range(kdim):
                    r = q.matmul(ps[bank][0:ncol, 0:n], lhsT=wv[:, k, col0:col0 + ncol], rhs=rhs3[:, k, 0:n],
                                 start=(first and k == 0), stop=(last and k == kdim - 1))
                return r
            return f

        HWB = HW * 4
        halo_s = rview(0, [128, 8, NBLK, 16], F32)
        aT = rview(8192, [128, 8, 2, 144], F32)
        sA = rview(17408, [128, 8, 2, 144], F32)
        sB = rview(26624, [128, 8, 2, 144], F32)
        wp_st = rview(35840, [128, 8, 256], F32)
        gT = rview(44032, [128, 16, 256], BF16)
        hb = R[:, 44032:44032 + HWB].bitcast(F32)
        mixT = rview(52224, [128, 16, 256], BF16)
        vtm = rview(60416, [128, 2, D], F32)
        uT = rview(68608, [128, KD, 256], BF16)
        halo_c = rview(72704, [128, 8, NBLK, 16], F32)
        dT = sqb
        vbf = r2view(0, [128, 2, D], BF16)
        atm = yT[:].rearrange("p k n -> p (k n)")[:, 0:D]
        histtm = tmpT[:].rearrange("p k n -> p (k n)")[:, 0:D]
        st4 = S("st4", [128, 16], F32)
        wp_bf = r2view(4096, [128, 8, 256], BF16)
        ws_f = r2view(8192, [128, 4, 128], F32)
        wsT = r2view(10240, [128, 4, 128], BF16)
        wsd_f = r2view(11264, [64, 4, 64], F32, parts=64)
        wsdT = r2view(12288, [64, 4, 64], BF16, parts=64)
        lng = r2view(12800, [128, D], F32)
        lnb = r2view(16896, [128, D], F32)
        bsp = r2view(20992, [128, 4, 128], F32)

        def even_layer(e):
            layer = 2 * e
            win = WV("win%d" % e)
            nh = 16 * NBLK
            xv = xT[:, :, 0:NTOK].rearrange("p k (b w) -> p k b w", w=128)[:, :, :, 112:128]
            P.op("pool", lambda q: q.tensor_copy(out=yT[:, :, 0:nh].rearrange("p k (b w) -> p k b w", w=16), in_=xv),
                 reads=[("xT", t) for t in range(NT)], writes=["yT"])
            pre_norm(layer, yT[:, :, 0:nh], nh, ["yT"])
            for ec in range(8):
                wv, wn = wload(win[:, :, ec * 128:(ec + 1) * 128], KD, 128)
                b = nbank()
                P.op("pe", mm_fm(b, wv, 0, 128, hT, nh), reads=[wn, "hT"], writes=[("ps", b)])
                evac_copy(halo_c[:, ec, :, :], ps[b][:, 0:nh].rearrange("p (b w) -> p b w", w=16), b, ["halo_c"])
            P.dma("sp", lambda q: [q.dma_start(out=bh_in[e].ap()[:, :], in_=halo_c.rearrange("p k b w -> p (k b w)"))],
                  reads=["halo_c"], writes=[("bh_in", e)], chan=("bh_in", e))
            P.collective(lambda q: q.collective_compute("AllGather", ALU.bypass, replica_groups=[[0, 1, 2, 3], [4, 5, 6, 7]],
                                                        ins=[bh_in[e].ap().opt()], outs=[bh_out[e].ap().opt()]),
                         reads=[("bh_in", e)], writes=[("bh_out", e)], chan=("ccH", e))
            hg = hb.rearrange("p (k b w) -> p k b w", k=8, b=NBLK)
            for r in range(4):
                P.dma("sp", lambda q, r=r: [q.dma_start(out=hb, in_=bh_out[e].ap()[r * 128:(r + 1) * 128, :])],
                      reads=[("bh_out", e)], writes=["gT"], chan="hb")
                if r == 0:
                    P.op("dve", lambda q: q.tensor_scalar(out=halo_s, in0=hg, scalar1=selw[:, 0:1], scalar2=None, op0=ALU.mult),
                         reads=["gT", "selw"], writes=["halo_s"])
                else:
                    P.op("dve", lambda q, r=r: q.scalar_tensor_tensor(out=halo_s, in0=hg, scalar=selw[:, r:r + 1], in1=halo_s,
                                                                     op0=ALU.mult, op1=ALU.add),
                         reads=["gT", "selw", "halo_s"], writes=["halo_s"])
                if r == 3 and NBLK > 1:
                    P.op("dve", lambda q: q.scalar_tensor_tensor(out=halo_s[:, :, 1:NBLK, :], in0=hg[:, :, 0:NBLK - 1, :],
                                                                 scalar=selw[:, 4:5], in1=halo_s[:, :, 1:NBLK, :],
                                                                 op0=ALU.mult, op1=ALU.add),
                         reads=["gT", "selw", "halo_s"], writes=["halo_s"])
            P.dma("sp", lambda q: [q.dma_start(out=wp_st, in_=w_pool[e].rearrange("g (i p) o -> p (g i) o", p=128))],
                  writes=["wp_st"], chan="wp_st")
            P.op("dve", lambda q: q.tensor_copy(out=wp_bf[:], in_=wp_st), reads=["wp_st"], writes=["wp_bf"])
            P.dma("sp", lambda q: [q.dma_start(out=ws_f[:], in_=w_sp[e].rearrange("g i j -> i g j"))], writes=["ws_f"], chan="ws_f")
            b = nbank()

            def trw(q):
                r = None
                for g in range(4):
                    r = q.transpose(ps[b][:, g * 128:(g + 1) * 128], ws_f[:, g, :], identf[:])
                return r
            P.op("pe", trw, reads=["ws_f", "identf"], writes=[("ps", b)])
            P.op("dve", lambda q: q.tensor_copy(out=wsT[:].rearrange("p g i -> p (g i)"), in_=ps[b][:, :]), writes=[("ps", b), "wsT"])
            P.op("pool", lambda q: q.memset(wsT[64:128, :, 0:64], 0.0), reads=["wsT"], writes=["wsT"])
            P.op("pool", lambda q: q.memset(wsd_f[:], 0.0), writes=["wsd_f"])
            P.dma("sp", lambda q: [q.dma_start(out=wsd_f[0:32, :, 0:32], in_=w_sp[e, :, 0:32, 0:32].rearrange("g i j -> i g j")),
                                   q.dma_start(out=wsd_f[32:64, :, 32:64], in_=w_sp[e, :, 0:32, 0:32].rearrange("g i j -> i g j"))],
                  reads=["wsd_f"], writes=["wsd_f"], chan="wsd_f", n=2)
            b2 = nbank()

            def trw2(q):
                r = None
                for g in range(4):
                    r = q.transpose(ps[b2][0:64, g * 64:(g + 1) * 64], wsd_f[:, g, :], identf[0:64, 0:64])
                return r
            P.op("pe", trw2, reads=["wsd_f", "identf"], writes=[("ps", b2)])
            P.op("dve", lambda q: q.tensor_copy(out=wsdT[:].rearrange("p g i -> p (g i)"), in_=ps[b2][0:64, 0:256]),
                 writes=[("ps", b2), "wsdT"])
            ld(lng[:], ln_g[e:e + 1, :].broadcast_to([128, D]), "lng")
            ld(lnb[:], ln_b[e:e + 1, :].broadcast_to([128, D]), "lnb")
            ld(bsp[:].rearrange("p g i -> p (g i)"), b_sp[e:e + 1].rearrange("o g i -> o (g i)").broadcast_to([128, 512]), "bsp")
            return dict(win=win)

        def even_tile(e, ctx, t):
            layer = 2 * e
            sample = (t == NT)
            NB, W = (2, 32) if sample else (2, 128)
            n = NB * W
            c0 = NTOK if sample else t * 256
            xres = [("xT", c0 // 256)]
            win = ctx["win"]
            if t == 0:
                pre_norm(layer, xT[:, :, c0:c0 + n], n, xres)
            nxt = None
            if t < NT:
                c0n, nn = (NTOK, 64) if t + 1 == NT else ((t + 1) * 256, 256)
                nxt = pre_norm_lite_stages(layer, xT[:, :, c0n:c0n + nn], nn, [("xT", c0n // 256)])
            if sample:
                for s in range(2):
                    P.dma("sp", lambda q, s=s: [q.dma_start(out=histtm[0:15, :], in_=cpool[e, s])], writes=["tmpT"], chan="histtm")
                    for half in range(2):
                        b = nbank()

                        def trh(q, half=half, b=b):
                            r = None
                            for j in range(4):
                                k = half * 4 + j
                                r = q.transpose(ps[b][:, j * 16:j * 16 + 15], histtm[0:15, k * 128:(k + 1) * 128], identf[0:15, 0:15])
                            return r
                        P.op("pe", trh, reads=["tmpT", "identf"], writes=[("ps", b)])
                        P.op("dve", lambda q, half=half, b=b, s=s: q.tensor_copy(
                            out=aT[:, half * 4:half * 4 + 4, s, 1:16], in_=ps[b][:, 0:64].rearrange("p (j w) -> p j w", j=4)[:, :, 0:15]),
                            writes=[("ps", b), "aT"])
            else:
                P.op("pool", lambda q: q.tensor_copy(out=aT[:, :, :, 0:16], in_=halo_s[:, :, t * 2:t * 2 + 2, :]),
                     reads=["halo_s"], writes=["aT"])
            need_tm = sample or (t == NT - 1)
            m = 64 if sample else 128
            for ec in range(8):
                wv, wn = wload(win[:, :, ec * 128:(ec + 1) * 128], KD, 128)
                b = nbank()
                P.op("pe", mm_fm(b, wv, 0, 128, hT, n), reads=[wn, "hT"], writes=[("ps", b)])
                evac_copy(aT[:, ec, 0:NB, 16:16 + W], ps[b][:, 0:n].rearrange("p (b w) -> p b w", b=NB), b, ["aT"])
                if need_tm:
                    b = nbank()
                    tc0 = 0 if sample else 128

                    def mmtm(q, b=b, wv=wv, tc0=tc0):
                        r = None
                        for k in range(KD):
                            r = q.matmul(ps[b][0:m, 0:128], lhsT=hT[:, k, tc0:tc0 + m], rhs=wv[:, k, :], start=(k == 0), stop=(k == KD - 1))
                        return r
                    P.op("pe", mmtm, reads=[wn, "hT"], writes=[("ps", b)])
                    evac_copy(atm[0:m, ec * 128:(ec + 1) * 128], ps[b][0:m, 0:128], b, ["yT"])
            if need_tm:
                if sample:
                    P.dma("sp", lambda q: [q.dma_start(out=o_pool_s[e, 0], in_=atm[17:32, :]),
                                           q.dma_start(out=o_pool_s[e, 1], in_=atm[49:64, :])],
                          reads=["yT"], chan="atm", n=2, is_out=True)
                else:
                    P.dma("sp", lambda q: [q.dma_start(out=o_pool_p[e], in_=atm[113:128, :])], reads=["yT"], chan="atm", is_out=True)
            def sh(dst, src, k0, d):
                return lambda q: q.tensor_tensor(out=dst[:, k0:8, 0:NB, d:16 + W], in0=src[:, k0:8, 0:NB, d:16 + W],
                                                 in1=src[:, k0:8, 0:NB, 0:16 + W - d], op=ALU.add)
            P.op("dve", sh(sA, aT, 0, 1), reads=["aT"], writes=["sA", "sA2"])
            P.op("pool", sh(sB, sA, 2, 2), reads=["sA"], writes=["sB", "sB2"])
            P.op("dve", sh(sA, sB, 4, 4), reads=["sB"], writes=["sA2"])
            P.op("pool", sh(sB, sA, 6, 8), reads=["sA2"], writes=["sB2"])
            srcs = [(sA, ["sA"]), (sB, ["sB"]), (sA, ["sA2"]), (sB, ["sB2"])]
            for g in range(4):
                sbuf_, sres = srcs[g]
                w = 2 ** (g + 1)
                P.op("dve", lambda q, g=g, sbuf_=sbuf_, w=w: q.scalar_tensor_tensor(
                    out=dT[:, 2 * g:2 * g + 2, 0:n].rearrange("p k (b w) -> p k b w", b=NB),
                    in0=sbuf_[:, 2 * g:2 * g + 2, 0:NB, 16:16 + W], scalar=1.0 / w, in1=aT[:, 2 * g:2 * g + 2, 0:NB, 16:16 + W],
                    op0=ALU.mult, op1=ALU.subtract), reads=sres + ["aT"], writes=["sqb"])
                if (not sample) and t == 0:
                    rc = rcnt[:, g, :].unsqueeze(1).to_broadcast([128, 2, 16])
                    P.op("pool", lambda q, g=g, sbuf_=sbuf_, rc=rc: q.tensor_tensor(
                        out=tmpT[:, 0:2, 0:16], in0=sbuf_[:, 2 * g:2 * g + 2, 0, 16:32], in1=rc, op=ALU.mult),
                        reads=sres + ["rcnt"], writes=["tmpT"])
                    P.op("dve", lambda q, g=g: q.tensor_tensor(out=dT[:, 2 * g:2 * g + 2, 0:16], in0=tmpT[:, 0:2, 0:16],
                                                               in1=aT[:, 2 * g:2 * g + 2, 0, 16:32], op=ALU.subtract),
                         reads=["tmpT", "aT"], writes=["sqb"])
            nblk_tm = 1 if sample else 2
            for ec in range(8):
                wv, wn = wload(win[:, :, 2048 + ec * 128:2048 + (ec + 1) * 128], KD, 128)
                for blk in range(nblk_tm):
                    b = nbank()

                    def mmv(q, b=b, wv=wv, blk=blk):
                        r = None
                        for k in range(KD):
                            r = q.matmul(ps[b][0:m, 0:128], lhsT=hT[:, k, blk * 128:blk * 128 + m], rhs=wv[:, k, :],
                                         start=(k == 0), stop=(k == KD - 1))
                        return r
                    P.op("pe", mmv, reads=[wn, "hT"], writes=[("ps", b)])
                    P.op("act", lambda q, b=b, blk=blk, ec=ec: q.activation(out=vtm[0:m, blk, ec * 128:(ec + 1) * 128], in_=ps[b][0:m, 0:128],
                                                                         func=AF.Gelu), writes=[("ps", b), ("vtm", blk)])
            sqscr = tmpT[:].rearrange("p k n -> p (k n)")[:, 0:D]
            for blk in range(nblk_tm):
                vb = vtm[0:m, blk, :]
                vr = ("vtm", blk)
                P.op("dve", lambda q, vb=vb: q.reduce_sum(out=st4[0:m, 0:1], in_=vb, axis=AX.X), reads=[vr], writes=["st_a"])
                P.op("act", lambda q, vb=vb: q.activation(out=sqscr[0:m, :], in_=vb, func=AF.Square, accum_out=st4[0:m, 1:2]),
                     reads=[vr], writes=["tmpT", "st_b"])
                P.op("pool", lambda q: q.tensor_scalar(out=st4[0:m, 2:3], in0=st4[0:m, 0:1], scalar1=1.0 / D, scalar2=None, op0=ALU.mult),
                     reads=["st_a"], writes=["st_c"])
                P.op("dve", lambda q: q.tensor_tensor(out=st4[0:m, 3:4], in0=st4[0:m, 2:3], in1=st4[0:m, 2:3], op=ALU.mult),
                     reads=["st_c"], writes=["st_d"])
                P.op("pool", lambda q: q.tensor_scalar(out=st4[0:m, 4:5], in0=st4[0:m, 1:2], scalar1=1.0 / D, scalar2=None, op0=ALU.mult),
                     reads=["st_b"], writes=["st_e"])
                P.op("dve", lambda q: q.tensor_tensor(out=st4[0:m, 5:6], in0=st4[0:m, 4:5], in1=st4[0:m, 3:4], op=ALU.subtract),
                     reads=["st_e", "st_d"], writes=["st_f"])
                P.op("act", lambda q: q.activation(out=st4[0:m, 6:7], in_=st4[0:m, 5:6], func=AF.Sqrt, bias=epsc[0:m, 0:1], scale=1.0),
                     reads=["st_f", "epsc"], writes=["st_g"])
                P.op("dve", lambda q: q.reciprocal(out=st4[0:m, 7:8], in_=st4[0:m, 6:7]), reads=["st_g"], writes=["st_h"])
                P.op("dve", lambda q, vb=vb: q.tensor_scalar(out=vb, in0=vb, scalar1=st4[0:m, 2:3], scalar2=st4[0:m, 7:8],
                                                          op0=ALU.subtract, op1=ALU.mult), reads=[vr, "st_c", "st_h"], writes=[vr])
                P.op("pool", lambda q, vb=vb: q.tensor_tensor(out=vb, in0=vb, in1=lng[0:m, :], op=ALU.mult), reads=[vr, "lng"], writes=[vr])
                P.op("dve", lambda q, vb=vb: q.tensor_tensor(out=vb, in0=vb, in1=lnb[0:m, :], op=ALU.add), reads=[vr, "lnb"], writes=[vr])
                P.op("pool", lambda q, vb=vb, blk=blk: q.tensor_copy(out=vbf[0:m, blk, :], in_=vb), reads=[vr], writes=[("vbf", blk)])
                if sample:
                    P.dma("sp", lambda q: [q.dma_start(out=o_sgu_s[e], in_=vtm[0:64, 0, :])], reads=[vr], chan="vout", is_out=True)
            for ec in range(8):
                wv, wn = wload(win[:, :, 1024 + ec * 128:1024 + (ec + 1) * 128], KD, 128)
                b = nbank()
                P.op("pe", mm_fm(b, wv, 0, 128, hT, n), reads=[wn, "hT"], writes=[("ps", b)])
                P.op("act", lambda q, b=b, ec=ec: q.activation(out=uT[:, ec, 0:n], in_=ps[b][:, 0:n], func=AF.Gelu),
                     writes=[("ps", b), "uT"])
            for ec in range(16):
                wv, wn = wload(win[:, :, 3072 + ec * 128:3072 + (ec + 1) * 128], KD, 128)
                b = nbank()
                P.op("pe", mm_fm(b, wv, 0, 128, hT, n), reads=[wn, "hT"], writes=[("ps", b)])
                P.op("act", lambda q, b=b, ec=ec: q.activation(out=gT[:, ec, 0:n], in_=ps[b][:, 0:n], func=AF.Silu),
                     writes=[("ps", b), "gT"])
            if nxt:
                nxt[0]()
            P.op("pool", lambda q: q.tensor_tensor(out=uT[:, :, 0:n], in0=uT[:, :, 0:n], in1=gT[:, 8:16, 0:n], op=ALU.mult),
                 reads=["uT", "gT"], writes=["uT"])
            wI = 64 if sample else 128
            for blk in range(nblk_tm):
                for half in range(2):
                    b = nbank()

                    def mms(q, b=b, blk=blk, half=half):
                        r = None
                        for j in range(4):
                            dk = half * 4 + j
                            g = dk // 2
                            rhs = wsdT[0:64, g, :] if sample else wsT[:, g, :]
                            r = q.matmul(ps[b][:, j * 128:j * 128 + wI], lhsT=vbf[0:m, blk, dk * 128:(dk + 1) * 128], rhs=rhs,
                                         start=True, stop=True)
                        return r
                    P.op("pe", mms, reads=[("vbf", blk), "wsT", "wsdT"], writes=[("ps", b)])
                    ps4 = ps[b][:].rearrange("p (g c i) -> p g c i", g=2, c=2)
                    if sample:
                        for s in range(2):
                            bb = bsp[:, half * 2:half * 2 + 2, 0:32].unsqueeze(2).to_broadcast([128, 2, 2, 32])
                            P.op("dve", lambda q, s=s, half=half, bb=bb, ps4=ps4: q.tensor_tensor(
                                out=tmpT[:, half * 4:half * 4 + 4, s * 32:(s + 1) * 32].rearrange("p (g c) i -> p g c i", g=2),
                                in0=ps4[:, :, :, s * 32:(s + 1) * 32], in1=bb, op=ALU.add),
                                reads=["bsp"], writes=[("ps", b), "tmpT"])
                    else:
                        bb = bsp[:, half * 2:half * 2 + 2, :].unsqueeze(2).to_broadcast([128, 2, 2, 128])
                        P.op("dve", lambda q, half=half, blk=blk, bb=bb, ps4=ps4: q.tensor_tensor(
                            out=tmpT[:, half * 4:half * 4 + 4, blk * 128:(blk + 1) * 128].rearrange("p (g c) i -> p g c i", g=2),
                            in0=ps4, in1=bb, op=ALU.add),
                            reads=["bsp"], writes=[("ps", b), "tmpT"])
            P.op("pool", lambda q: q.tensor_tensor(out=mixT[:, 8:16, 0:n], in0=tmpT[:, :, 0:n], in1=uT[:, :, 0:n], op=ALU.mult),
                 reads=["tmpT", "uT"], writes=["mixT_b"])
            if nxt:
                nxt[1]()
            for g in range(4):
                for oc in range(2):
                    b = nbank()

                    def mmp(q, b=b, g=g, oc=oc):
                        r = None
                        for ic in range(2):
                            r = q.matmul(ps[b][:, 0:n], lhsT=wp_bf[:, g * 2 + ic, oc * 128:(oc + 1) * 128], rhs=dT[:, 2 * g + ic, 0:n],
                                         start=(ic == 0), stop=(ic == 1))
                        return r
                    P.op("pe", mmp, reads=["wp_bf", "sqb"], writes=[("ps", b)])
                    ch = 2 * g + oc
                    P.op("dve", lambda q, b=b, ch=ch: q.scalar_tensor_tensor(
                        out=mixT[:, ch, 0:n], in0=ps[b][:, 0:n], scalar=vec[:, 64 + e * 8 + ch:64 + e * 8 + ch + 1], in1=gT[:, ch, 0:n],
                        op0=ALU.mult, op1=ALU.mult), reads=["vec", "gT"], writes=[("ps", b), "mixT_a"])
            if nxt:
                nxt[2]()
            wo = WV("wo%d" % e)
            mixres = ["mixT_b", "mixT_a"]
            for dc in range(8):
                wva, wna = wload(wo[:, 0:8, dc * 128:(dc + 1) * 128], 8, 128)
                wvb, wnb = wload(wo[:, 8:16, dc * 128:(dc + 1) * 128], 8, 128)
                b = nbank()
                P.op("pe", mm_fm(b, wva, 0, 128, mixT[:, 0:8, :], n, first=True, last=False), reads=[wna] + mixres, writes=[("ps", b)])
                P.op("pe", mm_fm(b, wvb, 0, 128, mixT[:, 8:16, :], n, first=False, last=True), reads=[wnb] + mixres, writes=[("ps", b)])
                evac_copy(yT[:, dc, 0:n], ps[b][:, 0:n], b, ["yT"])
            post_norm_update(layer, c0, n)

        NK = 4 * NTOK
        KT = rview(0, [128, 2, NK], BF16)
        krT = R[0:64, 4 * NK:6 * NK].bitcast(BF16)
        Vt = rview(6 * NK, [128, 4 * NBLK, 258], BF16)
        WkvT = r2view(0, [128, 8, 256], BF16)
        Wv = r2view(4096, [128, 2, 8, 128], BF16)
        gateT = r2view(8192, [128, 8, 128], BF16)
        qn = r2view(10240, [128, 8, 128], BF16)
        qaT = r2view(12288, [128, 2, 8, 128], BF16)
        qrope = r2view(16384, [64, 8, 128], BF16, parts=64)
        kvn_bc = r2view(18432, [128, 256], F32)
        qcn = r2view(19456, [128, 3, 128], BF16)
        Pb = r2view(20224, [128, 2, 512], BF16)
        maskb = S("maskb", [128, 512], BF16)
        permf = S("permf", [64, 64], F32)
        cst = S("cst", [128, 64], F32)
        csfO = S("csfO", [128, 258], F32)
        csf = csfO[0:64, 0:256].rearrange("p (a t) -> p a t", a=2)
        stt = S("stt", [128, 16], F32)
        psb = [p[:].bitcast(BF16) for p in ps]
        tmp2d = tmpT[:].rearrange("p k n -> p (k n)")
        tmpb = tmp2d.bitcast(BF16)
        yT2d = yT[:].rearrange("p k n -> p (k n)")
        sqb2d = sqb[:].rearrange("p k n -> p (k n)")
        PTs = [tmpb[:, 0:512], tmpb[:, 512:1024]]
        krtm_s = tmpb[:, 1024:1600].rearrange("p (a b) -> p a b", b=64)
        olat = tmpb[:, 2048:4096].rearrange("p (h c) -> p h c", h=8)
        qas = tmpb[:, 1600:1856].rearrange("p (c t) -> p c t", c=2)
        qrs = tmpb[0:64, 1856:1984]
        qc = yT[:, 0:3, 128:256]
        Oacc = yT[:, 3:5, 128:256]
        qrf = yT[0:64, 5, 128:256]
        qt1 = yT[0:64, 6, 128:256]
        qt2 = yT[0:64, 7, 128:256]
        olT = sqb[:].rearrange("p k n -> p (k n)").rearrange("p (c h q) -> p c h q", c=2, h=8)
        oT = hT[:, :, 128:256]
        bks = [nc.dram_tensor("bks%d" % o, [64, 320], BF16) for o in range(2)]

        ld(tmp2d[:, 0:512], mask_d[:, :], "tmpT")
        P.op("dve", lambda q: q.tensor_copy(out=maskb[:], in_=tmp2d[:, 0:512]), reads=["tmpT"], writes=["maskb"])
        P.op("dve", lambda q: q.tensor_copy(out=permf[:, 0:32], in_=identf[0:64, 32:64]), reads=["identf"], writes=["permf"])
        P.op("pool", lambda q: q.tensor_copy(out=permf[:, 32:64], in_=identf[0:64, 0:32]), reads=["identf", "permf"], writes=["permf"])

        def stage_cast(src_ap, dst_ap, w, dres, parts=128):
            stv = wst[0][0:parts, 0:w]
            P.dma("sp", lambda q: [q.dma_start(out=stv, in_=src_ap)], writes=["wst0"], chan="wst0")
            P.op("pool", lambda q: q.tensor_copy(out=dst_ap, in_=stv), reads=["wst0"], writes=list(dres))

        def kt_transposes(kb, vidx, parts, krt_src, col0, w):
            b = nbank()

            def tr(q):
                q.transpose(psb[b][:, 0:w], Vt[0:parts, vidx, 0:128], identb[0:parts, 0:parts])
                q.transpose(psb[b][:, 128:128 + w], Vt[0:parts, vidx, 128:256], identb[0:parts, 0:parts])
                return q.transpose(psb[b][0:64, 256:256 + w], krt_src, identb[0:parts, 0:parts])
            P.op("pe", tr, reads=["V", "tmpT", "identb"], writes=[("ps", b)])
            P.op("act", lambda q: q.copy(out=KT[:, :, col0:col0 + w], in_=psb[b][:, 0:256].rearrange("p (c k) -> p c k", c=2)[:, :, 0:w]),
                 writes=[("ps", b), "KT"])
            P.op("dve", lambda q: q.tensor_copy(out=krT[:, col0:col0 + w], in_=psb[b][0:64, 256:256 + w]), writes=[("ps", b), "krT"])

        stt2 = S("stt2", [128, 24], F32)
        P.op("pool", lambda q: q.memset(stt2[:, 0:1], 30000.0 * ATTN_SCALE), writes=["nm_init"])
        Oaccs = [(csfO[:, 0:257], "csf"), (rstd[:, 0:257], "rstd")]
        Pbs = [Pb[:, 0, :], Pb[:, 1, :], wst[0][:, :].bitcast(BF16)]
        Pbn = [("Pb", 0), ("Pb", 1), "wst0"]

        uctr = [0]
        self_defer = [None]

        def kt_transposes2(kb, krt0, krt1):
            b = nbank()

            def tr(q):
                r = None
                for j, krt in enumerate((krt0, krt1)):
                    o = j * 384
                    q.transpose(psb[b][:, o:o + 128], Vt[:, kb + j, 0:128], identb[:])
                    q.transpose(psb[b][:, o + 128:o + 256], Vt[:, kb + j, 128:256], identb[:])
                    r = q.transpose(psb[b][0:64, o + 256:o + 384], krt, identb[:])
                return r
            P.op("pe", tr, reads=["V", "tmpT", "identb"], writes=[("ps", b)])
            src = psb[b][:, 0:768].rearrange("p (k x t) -> p k x t", k=2, x=3)
            c0 = kb * 128
            P.op("act", lambda q: q.copy(out=KT[:, :, c0:c0 + 256].rearrange("p c (k t) -> p k c t", k=2), in_=src[:, :, 0:2, :]),
                 writes=[("ps", b), "KT"])
            P.op("dve", lambda q: q.tensor_copy(out=krT[:, c0:c0 + 256].rearrange("p (k t) -> p k t", k=2), in_=src[0:64, :, 2, :]),
                 writes=[("ps", b), "krT"])

        def attn_stream(units, hook=None):
            steps = []
            for ui, U in enumerate(units):
                ng = len(U["groups"])
                U["_k"] = uctr[0] % 4
                U["_bO"] = 6 + uctr[0] % 2
                uctr[0] += 1
                for gi, G in enumerate(U["groups"]):
                    steps.append((ui, gi, ng, U, G))
            N = len(steps)
            st = [dict() for _ in range(N)]

            def col(base, k):
                return stt2[:, base + k:base + k + 1]

            def stA(i):
                ui, gi, ng, U, G = steps[i]
                w = G["w"]
                bS = nbank()
                st[i]["bS"] = bS

                def mmS(q):
                    q.matmul(ps[bS][:, 0:w], lhsT=U["qa0"], rhs=G["kt0"], start=True, stop=False)
                    q.matmul(ps[bS][:, 0:w], lhsT=U["qa1"], rhs=G["kt1"], start=False, stop=False)
                    r = q.matmul(ps[bS][:, 0:w], lhsT=U["qr"], rhs=G["krt"], start=False, stop=not G["diag"])
                    if G["diag"]:
                        r = q.matmul(ps[bS][:, 0:w], lhsT=identb[:], rhs=maskb[:, 0:w], start=False, stop=True)
                    return r
                P.op("pe", mmS, reads=["qaT", "qrope", "tmpT", "KT", "krT", "maskb", "identb"], writes=[("ps", bS)])

            def stB(i):
                ui, gi, ng, U, G = steps[i]
                w = G["w"]
                bS = st[i]["bS"]
                k = U["_k"]
                ib = i % 3
                if gi == 0:
                    P.op("dve", lambda q: q.reduce_max(out=col(6, k), in_=ps[bS][:, 0:w], axis=AX.X), writes=[("ps", bS), ("gmax", k)])
                    P.op("dve", lambda q: q.tensor_scalar(out=col(2, k), in0=col(6, k), scalar1=-ATTN_SCALE, scalar2=None, op0=ALU.mult),
                         reads=[("gmax", k)], writes=[("nm", k)])
                P.op("act", lambda q: q.activation(out=Pbs[ib][:, 0:w], in_=ps[bS][:, 0:w], func=AF.Exp, bias=col(2, k), scale=ATTN_SCALE),
                     reads=[("nm", k)], writes=[("ps", bS), Pbn[ib]])

            def stC(i):
                ui, gi, ng, U, G = steps[i]
                w = G["w"]
                bT = nbank()
                nj = (w + 127) // 128
                ib = i % 3
                it = i % 2

                def trP(q):
                    r = None
                    for j in range(nj):
                        wj = min(128, w - j * 128)
                        r = q.transpose(psb[bT][0:wj, j * 128:(j + 1) * 128], Pbs[ib][:, j * 128:j * 128 + wj], identb[:])
                    return r
                P.op("pe", trP, reads=[Pbn[ib], "identb"], writes=[("ps", bT)])
                kp = min(128, w)
                if i % 3 == 0:
                    P.op("act", lambda q: q.copy(out=PTs[it][0:kp, 0:nj * 128], in_=psb[bT][0:kp, 0:nj * 128]),
                         writes=[("ps", bT), ("PTs", it)])
                else:
                    P.op("dve", lambda q: q.tensor_copy(out=PTs[it][0:kp, 0:nj * 128], in_=psb[bT][0:kp, 0:nj * 128]),
                         writes=[("ps", bT), ("PTs", it)])

            def stD(i):
                ui, gi, ng, U, G = steps[i]
                bO = U["_bO"]
                it = i % 2

                def mmO(q):
                    r = None
                    nv = len(G["v"])
                    for j, (vap, K) in enumerate(G["v"]):
                        r = q.matmul(ps[bO][:, 0:257], lhsT=PTs[it][0:K, j * 128:(j + 1) * 128], rhs=vap,
                                     start=(gi == 0 and j == 0), stop=(gi == ng - 1 and j == nv - 1))
                    return r
                P.op("pe", mmO, reads=[("PTs", it), "V"], writes=[("ps", bO)])
                if gi == ng - 1:
                    P.op("dve", lambda q: q.reciprocal(out=stt2[:, 16:17], in_=ps[bO][:, 256:257]), writes=[("ps", bO), "rl"])
                    P.op("act", lambda q: q.activation(out=U["out_ap"], in_=ps[bO][:, 0:256], func=AF.Copy, scale=stt2[:, 16:17]),
                         reads=["rl"], writes=[("ps", bO), "olat"])
                    if U.get("post") is not None:
                        for j, fn in enumerate(U["post"]):
                            self_defer[0](2 + 2 * j, fn)

            pend = []
            st_cur = [0]

            def defer(delay, fn):
                pend.append((st_cur[0] + delay, fn))
            self_defer[0] = defer
            h0 = max(0, N // 2)
            for i in range(N + 3):
                st_cur[0] = i
                if hook is not None and i == h0:
                    for j, fn in enumerate(hook):
                        defer(2 * j, fn)
                due = [p for p in pend if p[0] <= i]
                pend[:] = [p for p in pend if p[0] > i]
                for _, fn in due:
                    fn()
                if i < N:
                    stA(i)
                    stB(i)
                if 0 <= i - 2 < N:
                    stC(i - 2)
                if 0 <= i - 3 < N:
                    stD(i - 3)
            for _, fn in sorted(pend, key=lambda p: p[0]):
                fn()

        def odd_layer(o):
            layer = 2 * o + 1
            wio = WV("wio%d" % o)
            wqu = WV("wqu%d" % o)
            wku = WV("wku%d" % o)
            wov = WV("wov%d" % o)
            if layer + 1 < NLAYERS:
                relay_layer(layer + 1)
            ld(kvn_bc, kv_norm[o:o + 1, :].broadcast_to([128, 256]), "kvn_bc")
            P.op("pool", lambda q: q.memset(Vt[:, :, 256:258], 1.0), writes=["V"])
            for h in range(8):
                wv, wn = wload(wku[:, :, h * 256:(h + 1) * 256], 2, 256)
                b = nbank()

                def trk(q, b=b, wv=wv):
                    q.transpose(psb[b][:, 0:128], wv[:, 0, 0:128], identb[:])
                    return q.transpose(psb[b][:, 128:256], wv[:, 1, 0:128], identb[:])
                P.op("pe", trk, reads=[wn, "identb"], writes=[("ps", b)])
                P.op("act", lambda q, b=b, h=h: q.copy(out=WkvT[:, h, :], in_=psb[b][:, 0:256]), writes=[("ps", b), "WkvT"])
                P.op("pool", lambda q, wv=wv, h=h: q.tensor_copy(out=Wv[:, :, h, :], in_=wv[:, :, 128:256]), reads=[wn], writes=["Wv"])
            units = [(u * 128, 128, u) for u in range(NBLK)] + [(NTOK, 64, NBLK)]
            def p1norm(ui):
                c0_, n_, u_ = units[ui]
                return pre_norm_lite_stages(layer, xT[:, :, c0_:c0_ + n_], n_, [("xT", c0_ // 256)], hoff=(ui % 2) * 128, hres=("hTp", ui % 2))
            for fn in p1norm(0):
                fn()
            for ui, (c0, n, u) in enumerate(units):
                sample = (u == NBLK)
                xres = [("xT", c0 // 256)]
                hoff = (ui % 2) * 128
                hres = ("hTp", ui % 2)
                P.dma("sp", lambda q, u=u: [q.dma_start(out=cst[:], in_=cs_tm[:, u * 64:(u + 1) * 64])], writes=["cst"], chan="cst")
                b = nbank()
                for ci, (col, w) in enumerate([(384, 128), (512, 128), (640, 64)]):
                    wv, wn = wload(wio[:, :, col:col + w], KD, w)

                    def mmk(q, b=b, wv=wv, ci=ci, w=w, n=n, hoff=hoff):
                        r = None
                        for k in range(KD):
                            r = q.matmul(ps[b][0:n, ci * 128:ci * 128 + w], lhsT=hT[:, k, hoff:hoff + n], rhs=wv[:, k, :], start=(k == 0), stop=(k == KD - 1))
                        return r
                    P.op("pe", mmk, reads=[wn, hres], writes=[("ps", b)])
                if ui + 1 < len(units):
                    for fn in p1norm(ui + 1):
                        fn()
                kvc = tmp2d[0:n, 0:320]
                P.op("act", lambda q, b=b, n=n, kvc=kvc: q.copy(out=kvc, in_=ps[b][0:n, 0:320]), writes=[("ps", b), "tmpT"])
                P.op("act", lambda q, n=n: q.activation(out=tmp2d[0:n, 512:768], in_=tmp2d[0:n, 0:256], func=AF.Square, accum_out=stt[0:n, 8:9]),
                     reads=["tmpT"], writes=["tmpT2", "ss"])
                P.op("act", lambda q, n=n: q.activation(out=stt[0:n, 9:10], in_=stt[0:n, 8:9], func=AF.Sqrt, bias=epsc[0:n, 0:1], scale=1.0 / 256.0),
                     reads=["ss", "epsc"], writes=["ss2"])
                P.op("dve", lambda q, n=n: q.reciprocal(out=stt[0:n, 10:11], in_=stt[0:n, 9:10]), reads=["ss2"], writes=["ss3"])
                P.op("dve", lambda q, n=n: q.scalar_tensor_tensor(out=yT2d[0:n, 0:256], in0=tmp2d[0:n, 0:256], scalar=stt[0:n, 10:11],
                                                                  in1=kvn_bc[0:n, :], op0=ALU.mult, op1=ALU.mult),
                     reads=["tmpT", "ss3", "kvn_bc"], writes=["yT"])
                x1, x2 = tmp2d[0:n, 256:288], tmp2d[0:n, 288:320]
                cs_, sn_ = cst[0:n, 0:32], cst[0:n, 32:64]
                t1, t2, t3, t4 = (tmp2d[0:n, 1024 + 32 * j:1056 + 32 * j] for j in range(4))
                P.op("pool", lambda q, x1=x1, cs_=cs_, t1=t1: q.tensor_tensor(out=t1, in0=x1, in1=cs_, op=ALU.mult), reads=["tmpT", "cst"], writes=["t1"])
                P.op("dve", lambda q, x2=x2, sn_=sn_, t2=t2: q.tensor_tensor(out=t2, in0=x2, in1=sn_, op=ALU.mult), reads=["tmpT", "cst"], writes=["t2"])
                P.op("pool", lambda q, x2=x2, cs_=cs_, t3=t3: q.tensor_tensor(out=t3, in0=x2, in1=cs_, op=ALU.mult), reads=["tmpT", "cst"], writes=["t3"])
                P.op("dve", lambda q, x1=x1, sn_=sn_, t4=t4: q.tensor_tensor(out=t4, in0=x1, in1=sn_, op=ALU.mult), reads=["tmpT", "cst"], writes=["t4"])
                P.op("pool", lambda q, n=n, t1=t1, t2=t2: q.tensor_tensor(out=yT2d[0:n, 256:288], in0=t1, in1=t2, op=ALU.subtract),
                     reads=["t1", "t2"], writes=["yTr1"])
                P.op("dve", lambda q, n=n, t3=t3, t4=t4: q.tensor_tensor(out=yT2d[0:n, 288:320], in0=t3, in1=t4, op=ALU.add),
                     reads=["t3", "t4"], writes=["yTr2"])
                P.op("pool", lambda q, n=n: q.tensor_copy(out=sqb2d[0:n, 0:320], in_=yT2d[0:n, 0:320]), reads=["yT", "yTr1", "yTr2"], writes=["sqb"])
                if sample:
                    P.dma("sp", lambda q: [q.dma_start(out=o_ckv_s[o], in_=yT2d[0:64, 0:256]), q.dma_start(out=o_kr_s[o], in_=yT2d[0:64, 256:320])],
                          reads=["yT", "yTr1", "yTr2"], chan="kvout", n=2, is_out=True)
                    P.dma("sp", lambda q: [q.dma_start(out=bks[o].ap()[:, :], in_=sqb2d[0:64, 0:320])], reads=["sqb"], writes=[("bks", o)], chan="bks")
                else:
                    P.dma("sp", lambda q, c0=c0: [q.dma_start(out=o_ckv_p[o, c0:c0 + 128, :], in_=yT2d[:, 0:256]),
                                                 q.dma_start(out=o_kr_p[o, c0:c0 + 128, :], in_=yT2d[:, 256:320])],
                          reads=["yT", "yTr1", "yTr2"], chan="kvout", n=2, is_out=True)
                    P.dma("sp", lambda q, u=u: [q.dma_start(out=bk_in[o][u // BPS].ap()[(u % BPS) * 128:(u % BPS) * 128 + 128, :], in_=sqb2d[:, 0:320])],
                          reads=["sqb"], writes=[("bk_in", o, u)], chan="bkin")
            for sp in range(NSPL):
                P.collective(lambda q, sp=sp: q.collective_compute("AllGather", ALU.bypass, replica_groups=[[0, 1, 2, 3], [4, 5, 6, 7]],
                                                                   ins=[bk_in[o][sp].ap().opt()], outs=[bk_out[o][sp].ap().opt()]),
                             reads=[("bk_in", o, u) for u in range(sp * BPS, (sp + 1) * BPS)], writes=[("bk_out", o, sp)], chan=("ccK", o, sp))
            qrfs = [yT[0:64, 5, 128:256], yT[0:64, 3, 128:256]]
            qt1s = [yT[0:64, 6, 128:256], yT[0:64, 4, 128:256]]

            def q_path(c0, n, do_norm=True):
                xres = [("xT", c0 // 256)]
                if do_norm:
                    pre_norm(layer, xT[:, :, c0:c0 + n], n, xres)
                P.dma("sp", lambda q: [q.dma_start(out=csf[:, :, 0:n], in_=cs_fm.rearrange("p (a t) -> p a t", a=2)[:, :, c0:c0 + n])],
                      writes=["csf"], chan="csf")
                for kc in range(3):
                    wv, wn = wload(wio[:, :, kc * 128:(kc + 1) * 128], KD, 128)
                    b = nbank()
                    P.op("pe", mm_fm(b, wv, 0, 128, hT, n), reads=[wn, "hT"], writes=[("ps", b)])
                    evac_copy(qc[:, kc, 0:n], ps[b][:, 0:n], b, ["qc"])
                rms_stats(qc[:, :, 0:n], n, ["qc"], kdim=3, scale=1024.0 / 384.0)
                for kc in range(3):
                    P.op("dve", lambda q, kc=kc: q.scalar_tensor_tensor(out=qcn[:, kc, 0:n], in0=qc[:, kc, 0:n],
                                                                        scalar=vec[:, 80 + o * 3 + kc:80 + o * 3 + kc + 1], in1=rstd[:, 0:n],
                                                                        op0=ALU.mult, op1=ALU.mult), reads=["qc", "vec", "rstd"], writes=["qcn"])
                for ec in range(8):
                    wv, wn = wload(wio[:, :, 704 + ec * 128:704 + (ec + 1) * 128], KD, 128)
                    b = nbank()
                    P.op("pe", mm_fm(b, wv, 0, 128, hT, n), reads=[wn, "hT"], writes=[("ps", b)])
                    P.op("act", lambda q, b=b, ec=ec: q.activation(out=gateT[:, ec, 0:n], in_=ps[b][:, 0:n], func=AF.Silu),
                         writes=[("ps", b), "gateT"])

                def stage_a(h):
                    j = h % 2
                    wv, wn = wload(wqu[:, :, h * 192:(h + 1) * 192], 3, 192)
                    b1 = nbank()
                    P.op("pe", mm_fm(b1, wv, 0, 128, qcn, n, kdim=3), reads=[wn, "qcn"], writes=[("ps", b1)])
                    evac_copy(qn[:, h, 0:n], ps[b1][:, 0:n], b1, [("qn", h)])
                    b2 = nbank()
                    P.op("pe", mm_fm(b2, wv, 128, 64, qcn, n, kdim=3), reads=[wn, "qcn"], writes=[("ps", b2)])
                    P.op("act", lambda q: q.copy(out=qrfs[j][:, 0:n], in_=ps[b2][0:64, 0:n]), writes=[("ps", b2), ("qrf", j)])

                def stage_b(h):
                    j = h % 2
                    jw = {"writes": ["qrope"]} if h == 0 else {"join": ["qrope"]}
                    jq = {"wres": ["qaT"]} if h == 0 else {"wres": [], "join": ["qaT"]}
                    b3 = nbank()
                    P.op("pe", lambda q: q.matmul(ps[b3][0:64, 0:n], lhsT=permf[:, :], rhs=qrfs[j][:, 0:n], start=True, stop=True),
                         reads=["permf", ("qrf", j)], writes=[("ps", b3)])
                    P.op("dve", lambda q: q.tensor_tensor(out=qt1s[j][:, 0:n], in0=ps[b3][0:64, 0:n], in1=csf[:, 1, 0:n], op=ALU.mult),
                         reads=["csf"], writes=[("ps", b3), ("qt1", j)])
                    P.op("pool", lambda q: q.tensor_tensor(out=qt2[:, 0:n], in0=qrfs[j][:, 0:n], in1=csf[:, 0, 0:n], op=ALU.mult),
                         reads=["csf", ("qrf", j)], writes=["qt2"])
                    P.op("dve", lambda q: q.tensor_tensor(out=qrope[:, h, 0:n], in0=qt1s[j][:, 0:n], in1=qt2[:, 0:n], op=ALU.add),
                         reads=[("qt1", j), "qt2"], **jw)
                    b4 = nbank()

                    def mma(q):
                        q.matmul(ps[b4][:, 0:n], lhsT=WkvT[:, h, 0:128], rhs=qn[:, h, 0:n], start=True, stop=True)
                        return q.matmul(ps[b4][:, 128:128 + n], lhsT=WkvT[:, h, 128:256], rhs=qn[:, h, 0:n], start=True, stop=True)
                    P.op("pe", mma, reads=["WkvT", ("qn", h)], writes=[("ps", b4)])
                    evac_copy(qaT[:, :, h, 0:n], ps[b4][:, 0:256].rearrange("p (c t) -> p c t", c=2)[:, :, 0:n], b4, jq["wres"], join=jq.get("join", ()))

                for i in range(9):
                    if i < 8:
                        stage_a(i)
                    if i >= 1:
                        stage_b(i - 1)

            def head_out_a(h):
                b = nbank()

                def tro(q):
                    q.transpose(psb[b][:, 0:128], olat[:, h, 0:128], identb[:])
                    return q.transpose(psb[b][:, 128:256], olat[:, h, 128:256], identb[:])
                P.op("pe", tro, reads=["olat", "identb"], writes=[("ps", b)])
                evac_copy(olT[:, :, h, :], psb[b][:, 0:256].rearrange("p (c q) -> p c q", c=2), b, [("olT", h)], join=["sqb"])

            def head_out_b(h, n):
                b2 = nbank()

                def mmo(q):
                    q.matmul(ps[b2][:, 0:n], lhsT=Wv[:, 0, h, :], rhs=olT[:, 0, h, 0:n], start=True, stop=False)
                    return q.matmul(ps[b2][:, 0:n], lhsT=Wv[:, 1, h, :], rhs=olT[:, 1, h, 0:n], start=False, stop=True)
                P.op("pe", mmo, reads=["Wv", ("olT", h), "sqb"], writes=[("ps", b2)])
                P.op("dve", lambda q: q.tensor_tensor(out=oT[:, h, 0:n], in0=ps[b2][:, 0:n], in1=gateT[:, h, 0:n], op=ALU.mult),
                     reads=["gateT"], writes=[("ps", b2)] + (["oT"] if h == 0 else []), join=([] if h == 0 else ["oT"]))

            def out_path(c0, n, heads_done=False):
                for h in range(0 if heads_done else 8):
                    b = nbank()

                    def mmo(q, b=b, h=h):
                        q.matmul(ps[b][:, 0:n], lhsT=Wv[:, 0, h, :], rhs=olT[:, 0, h, 0:n], start=True, stop=False)
                        return q.matmul(ps[b][:, 0:n], lhsT=Wv[:, 1, h, :], rhs=olT[:, 1, h, 0:n], start=False, stop=True)
                    P.op("pe", mmo, reads=["Wv", "sqb"], writes=[("ps", b)])
                    P.op("dve", lambda q, b=b, h=h: q.tensor_tensor(out=oT[:, h, 0:n], in0=ps[b][:, 0:n], in1=gateT[:, h, 0:n], op=ALU.mult),
                         reads=["gateT"], writes=[("ps", b), "oT"])
                for dc in range(8):
                    wv, wn = wload(wov[:, :, dc * 128:(dc + 1) * 128], KD, 128)
                    b = nbank()
                    P.op("pe", mm_fm(b, wv, 0, 128, oT, n), reads=[wn, "oT"], writes=[("ps", b)])
                    evac_copy(yT[:, dc, 0:n], ps[b][:, 0:n], b, ["yT"])
                post_norm_update(layer, c0, n)

            q_path(0, 128, do_norm=True)
            V4 = Vt.rearrange("p (m r) c -> p m r c", r=4)
            krtm = tmpb[:, 0:NK // 2].rearrange("p (m r c) -> p m r c", r=4, c=64)
            for sp in range(NSPL):
                bko = bk_out[o][sp].ap()
                ms = slice(sp * BPS, (sp + 1) * BPS)
                for r in range(4):
                    P.dma("sp", lambda q, r=r, bko=bko, ms=ms: [q.dma_start(out=V4[:, ms, r, 0:256], in_=bko[r * TPS:(r + 1) * TPS, 0:256].rearrange("(m p) c -> p m c", p=128))],
                          reads=[("bk_out", o, sp)], writes=["V"], chan="Vld")
                    P.dma("sp", lambda q, r=r, bko=bko, ms=ms: [q.dma_start(out=krtm[:, ms, r, :], in_=bko[r * TPS:(r + 1) * TPS, 256:320].rearrange("(m p) c -> p m c", p=128))],
                          reads=[("bk_out", o, sp)], writes=["tmpT"], chan="krld")
            krtm3 = tmpb[:, 0:NK // 2].rearrange("p (k c) -> p k c", c=64)
            for kb in range(0, 4 * NBLK, 2):
                kt_transposes2(kb, krtm3[:, kb, :], krtm3[:, kb + 1, :])

            for blk in range(NBLK):
                c0 = blk * 128
                relay_some(5 if blk < NBLK - 2 else 999)
                if blk > 0:
                    q_path(c0, 128, do_norm=False)
                units_ = []
                for h in range(8):
                    groups = []
                    for g in range(blk + 1):
                        ks = slice(g * 512, (g + 1) * 512)
                        groups.append(dict(kt0=KT[:, 0, ks], kt1=KT[:, 1, ks], krt=krT[:, ks], w=512,
                                           v=[(Vt[:, g * 4 + j, 0:257], 128) for j in range(4)], diag=(g == blk)))
                    units_.append(dict(qa0=qaT[:, 0, h, :], qa1=qaT[:, 1, h, :], qr=qrope[:, h, :], groups=groups, out_ap=olat[:, h, :],
                                       post=[(lambda h=h: head_out_a(h)), (lambda h=h: head_out_b(h, 128))]))
                nc0, nn = ((blk + 1) * 128, 128) if blk + 1 < NBLK else (NTOK, 64)
                attn_stream(units_, hook=pre_norm_lite_stages(layer, xT[:, :, nc0:nc0 + nn], nn, [("xT", nc0 // 256)]))
                out_path(c0, 128, heads_done=True)
            q_path(NTOK, 64, do_norm=False)
            for s_ in range(2):
                for kb in range(8):
                    stage_cast(cckv[o, s_, kb * 128:(kb + 1) * 128, :], Vt[:, kb, 0:256], 256, ["V"])
                    stage_cast(ckr[o, s_, kb * 128:(kb + 1) * 128, :], krtm_s[:, kb, :], 64, ["tmpT"])
                P.dma("sp", lambda q, s_=s_: [q.dma_start(out=Vt[0:32, 8, 0:256], in_=bks[o].ap()[s_ * 32:(s_ + 1) * 32, 0:256]),
                                             q.dma_start(out=krtm_s[0:32, 8, :], in_=bks[o].ap()[s_ * 32:(s_ + 1) * 32, 256:320])],
                      reads=[("bks", o)], writes=["V", "tmpT"], chan="bksld", n=2)
                for kb in range(0, 8, 2):
                    kt_transposes2(kb, krtm_s[:, kb, :], krtm_s[:, kb + 1, :])
                kt_transposes(8, 8, 32, krtm_s[0:32, 8, :], 1024, 32)
                units_ = []
                for hq in range(2):
                    ts = slice(s_ * 32, (s_ + 1) * 32)
                    hs = slice(hq * 4, hq * 4 + 4)
                    groups = []
                    for g in range(2):
                        ks = slice(g * 512, (g + 1) * 512)
                        groups.append(dict(kt0=KT[:, 0, ks], kt1=KT[:, 1, ks], krt=krT[:, ks], w=512,
                                           v=[(Vt[:, g * 4 + j, 0:257], 128) for j in range(4)], diag=False))
                    groups.append(dict(kt0=KT[:, 0, 1024:1056], kt1=KT[:, 1, 1024:1056], krt=krT[:, 1024:1056], w=32,
                                       v=[(Vt[0:32, 8, 0:257], 32)], diag=False))
                    P.op("dve", lambda q, hs=hs, ts=ts: q.tensor_copy(out=qas.rearrange("p c (h t) -> p c h t", h=4), in_=qaT[:, :, hs, ts]),
                         reads=["qaT"], writes=["tmpT"])
                    P.op("pool", lambda q, hs=hs, ts=ts: q.tensor_copy(out=qrs.rearrange("p (h t) -> p h t", h=4), in_=qrope[:, hs, ts]),
                         reads=["qrope"], writes=["tmpT"])

                    def post(hs=hs, ts=ts):
                        b = nbank()

                        def tro2(q):
                            q.transpose(psb[b][:, 0:128], olat[:, 0, 0:128], identb[:])
                            return q.transpose(psb[b][:, 128:256], olat[:, 0, 128:256], identb[:])
                        P.op("pe", tro2, reads=["olat", "identb"], writes=[("ps", b)])
                        evac_copy(olT[:, :, hs, ts], psb[b][:, 0:256].rearrange("p (c h t) -> p c h t", c=2, h=4), b, ["sqb"])
                    attn_stream([dict(qa0=qas[:, 0, :], qa1=qas[:, 1, :], qr=qrs, groups=groups, out_ap=olat[:, 0, :], post=[post])])
            out_path(NTOK, 64)

        for layer in range(NLAYERS):
            if layer % 2 == 0:
                e = layer // 2
                ctx = even_layer(e)
                for t in range(NT + 1):
                    if t == 0 and layer + 1 < NLAYERS:
                        relay_layer(layer + 1)
                    relay_some(8 if t < NT - 1 else 999)
                    even_tile(e, ctx, t)
            else:
                P.barrier()
                odd_layer(layer // 2)
                P.barrier()

        P.barrier()
        for b in range(NBLK):
            store_x(y_p[b * 128:(b + 1) * 128, :], b * 128, 128)
        store_x(y_s[:, :], NTOK, NS)
        P.finish()
        P.replay()
    return nc


def _host_consts(c, NBLK):
    NTOK = NBLK * 128
    TOT = NTOK + 64
    r = c % 4
    mask = np.zeros((128, 512), np.float32)
    qi = np.arange(128)[:, None]
    for i in range(4):
        blk = mask[:, i * 128:(i + 1) * 128]
        if i > r:
            blk[:] = NEG
        elif i == r:
            kj = np.arange(128)[None, :]
            blk[:] = np.where((kj // 64) <= (qi // 64), 0.0, NEG)
    selw = np.zeros((128, 8), np.float32)
    if r > 0:
        selw[:, r - 1] = 1.0
    else:
        selw[:, 4] = 1.0
    rc = np.zeros((128, 4, 16), np.float32)
    for g in range(4):
        w = 2 ** (g + 1)
        for p in range(16):
            rc[:, g, p] = (1.0 / min(p + 1, w)) if r == 0 else 1.0 / w
    half = 32
    freqs = (10000.0 ** (-np.arange(half, dtype=np.float32) / half)).astype(np.float32)
    pos = np.zeros(TOT, np.float32)
    for m in range(NBLK):
        pos[m * 128:(m + 1) * 128] = (4 * m + r) * 128 + np.arange(128)
    pos[NTOK:NTOK + 32] = 1024 + np.arange(32)
    pos[NTOK + 32:] = 1024 + np.arange(32)
    ang = pos[:, None].astype(np.float32) * freqs[None, :]
    cos, sin = np.cos(ang).astype(np.float32), np.sin(ang).astype(np.float32)
    cs_tm = np.zeros((128, NBLK + 1, 64), np.float32)
    for m in range(NBLK):
        cs_tm[:, m, 0:32] = cos[m * 128:(m + 1) * 128]
        cs_tm[:, m, 32:64] = sin[m * 128:(m + 1) * 128]
    cs_tm[0:64, NBLK, 0:32] = cos[NTOK:]
    cs_tm[0:64, NBLK, 32:64] = sin[NTOK:]
    cs_fm = np.zeros((64, 2, TOT), np.float32)
    cs_fm[0:32, 0] = cos.T
    cs_fm[32:64, 0] = cos.T
    cs_fm[0:32, 1] = -sin.T
    cs_fm[32:64, 1] = sin.T
    return dict(mask=mask, selw=selw, rcnt=rc.reshape(128, 64), cs_tm=cs_tm.reshape(128, -1), cs_fm=cs_fm.reshape(64, -1),
                ident=np.eye(128, dtype=np.float32))


def _fm(v):
    return np.ascontiguousarray(np.asarray(v, np.float32).reshape(-1, 128).T)


_NC_CACHE = {}


def kernel(x_prompt, x_sample, cache_pool, cache_ckv, cache_krope, norm_pre, norm_post,
           w_in_even, w_pool, pool_scale, sgu_ln_g, sgu_ln_b, w_spatial, b_spatial, w_out_even,
           w_in_odd, q_norm, kv_norm, w_q_up, w_kv_up, w_o, _nlayers=4):
    f = lambda a: np.ascontiguousarray(np.asarray(a, dtype=np.float32))
    x_prompt = f(x_prompt)
    x_sample = f(x_sample)
    B, T, _ = x_prompt.shape
    NBLK = T // 512
    NTOK = NBLK * 128
    vecs = np.zeros((128, 96), np.float32)
    for l in range(4):
        vecs[:, l * 8:(l + 1) * 8] = _fm(f(norm_pre)[l])
        vecs[:, 32 + l * 8:32 + (l + 1) * 8] = _fm(f(norm_post)[l])
    for e in range(2):
        vecs[:, 64 + e * 8:64 + (e + 1) * 8] = _fm(f(pool_scale)[e])
        vecs[:, 80 + e * 3:80 + (e + 1) * 3] = _fm(f(q_norm)[e])
    shared = dict(w_in_even=f(w_in_even), w_pool=f(w_pool), ln_g=f(sgu_ln_g), ln_b=f(sgu_ln_b), w_sp=f(w_spatial),
                  b_sp=f(b_spatial), w_out_even=f(w_out_even), w_in_odd=f(w_in_odd), kv_norm=f(kv_norm),
                  w_q_up=f(w_q_up).reshape(2, 384, 8 * 192), w_kv_up=f(w_kv_up).reshape(2, 256, 8 * 256), w_o=f(w_o), vecs=vecs)
    cache_pool, cache_ckv, cache_krope = f(cache_pool), f(cache_ckv), f(cache_krope)
    in_maps = []
    for c in range(8):
        b, r = c // 4, c % 4
        xb = x_prompt[b].reshape(NBLK, 4, 128, D)[:, r].reshape(NTOK, D)
        m = dict(shared)
        m.update(_host_consts(c, NBLK))
        m["xp"] = np.ascontiguousarray(xb)
        m["xs"] = np.ascontiguousarray(x_sample[2 * c:2 * c + 2].reshape(64, D))
        m["cpool"] = np.ascontiguousarray(cache_pool[:, 2 * c:2 * c + 2])
        m["cckv"] = np.ascontiguousarray(cache_ckv[:, 2 * c:2 * c + 2])
        m["ckr"] = np.ascontiguousarray(cache_krope[:, 2 * c:2 * c + 2])
        in_maps.append(m)
    key = (NBLK, _nlayers)
    if key not in _NC_CACHE:
        _NC_CACHE[key] = build(NBLK, _nlayers)
    nc = _NC_CACHE[key]
    res = run_bass_kernel_spmd(nc, in_maps, core_ids=list(range(8))).results

    def unshard(name, width):
        out = np.zeros((B, NBLK, 4, 128, width), np.float32)
        for c in range(8):
            out[c // 4, :, c % 4] = res[c][name].reshape(NBLK, 128, width)
        return out.reshape(B, T, width)

    def unshard_l(name, width):
        out = np.zeros((2, B, NBLK, 4, 128, width), np.float32)
        for c in range(8):
            out[:, c // 4, :, c % 4] = res[c][name].reshape(2, NBLK, 128, width)
        return out.reshape(2, B, T, width)

    y_prompt = unshard("y_p", D)
    y_sample = np.concatenate([res[c]["y_s"].reshape(2, 32, D) for c in range(8)], 0)
    pool_p = np.stack([res[3]["o_pool_p"], res[7]["o_pool_p"]], 1)
    pool_s = np.concatenate([res[c]["o_pool_s"] for c in range(8)], 1)
    sgu_s = np.concatenate([res[c]["o_sgu_s"].reshape(2, 2, 32, D) for c in range(8)], 1)
    ckv_p = unshard_l("o_ckv_p", 256)
    kr_p = unshard_l("o_kr_p", 64)
    ckv_s = np.concatenate([res[c]["o_ckv_s"].reshape(2, 2, 32, 256) for c in range(8)], 1)
    kr_s = np.concatenate([res[c]["o_kr_s"].reshape(2, 2, 32, 64) for c in range(8)], 1)
    return (y_prompt, y_sample, pool_p, pool_s, sgu_s, ckv_p, kr_p, ckv_s, kr_s)
```
